# Optimizing a Trainium2 kernel written in Bass

```python
import jax, jax.numpy as jnp
from jax import lax
import numpy as np

D_MODEL = 1024
BATCH = 2
SEQ = 8192
DEPTH = 1

PLE_DIM = 256
MIX_WIDTH = D_MODEL
HEAD_DIM = 64
ATT_WIDTH = MIX_WIDTH // 2
N_ATT_HEADS = ATT_WIDTH // HEAD_DIM
SG_WIDTH = MIX_WIDTH - ATT_WIDTH
N_SG_GROUPS = 8
SG_GROUP_DIM = SG_WIDTH // N_SG_GROUPS
CHUNK = 128
Q_BLOCK = 128
D_FF = 4 * D_MODEL
IN_PROJ_WIDTH = 3 * ATT_WIDTH + N_ATT_HEADS + 2 * SG_WIDTH
EPS = 1e-6

kernel_name = "hybrid_fox_gmlp_sandwich_ple"


def rms_norm(x, g):
    xf = x.astype(jnp.float32)
    y = xf * lax.rsqrt(jnp.mean(xf * xf, axis=-1, keepdims=True) + EPS)
    return (y * g.astype(jnp.float32)).astype(x.dtype)


def layer_norm(x, g, b):
    xf = x.astype(jnp.float32)
    mu = jnp.mean(xf, axis=-1, keepdims=True)
    xc = xf - mu
    y = xc * lax.rsqrt(jnp.mean(xc * xc, axis=-1, keepdims=True) + EPS)
    return (y * g.astype(jnp.float32) + b.astype(jnp.float32)).astype(x.dtype)


def forgetting_attention(q, k, v, f_logit):
    B, S, H, Dh = q.shape
    nb = S // Q_BLOCK
    log_f = jax.nn.log_sigmoid(f_logit.astype(jnp.float32))
    c = jnp.transpose(jnp.cumsum(log_f, axis=1), (0, 2, 1))
    qh = jnp.transpose(q, (0, 2, 1, 3)).astype(jnp.float32) * (Dh ** -0.5)
    kh = jnp.transpose(k, (0, 2, 1, 3)).astype(jnp.float32)
    vh = jnp.transpose(v, (0, 2, 1, 3)).astype(jnp.float32)
    q_blocks = jnp.transpose(qh.reshape(B, H, nb, Q_BLOCK, Dh), (2, 0, 1, 3, 4))
    cq_blocks = jnp.transpose(c.reshape(B, H, nb, Q_BLOCK), (2, 0, 1, 3))
    k_pos = jnp.arange(S)

    def one_block(args):
        qb, cqb, bi = args
        q_pos = bi * Q_BLOCK + jnp.arange(Q_BLOCK)
        logits = (jnp.einsum('bhqd,bhkd->bhqk', qb, kh)
                  + cqb[..., :, None] - c[..., None, :])
        causal = k_pos[None, :] <= q_pos[:, None]
        logits = jnp.where(causal, logits, -jnp.inf)
        probs = jax.nn.softmax(logits, axis=-1)
        return jnp.einsum('bhqk,bhkd->bhqd', probs, vh)

    out = lax.map(one_block, (q_blocks, cq_blocks, jnp.arange(nb)))
    out = jnp.transpose(out, (1, 0, 3, 2, 4)).reshape(B, S, H * Dh)
    return out.astype(q.dtype)


def chunked_spatial_gating(u, v, ln_g, ln_b, w_s, b_s):
    B, S, _ = u.shape
    nc = S // CHUNK
    u = jax.nn.gelu(u)
    v = layer_norm(jax.nn.gelu(v), ln_g, ln_b)
    vc = v.reshape(B, nc, CHUNK, N_SG_GROUPS, SG_GROUP_DIM)
    mask = jnp.tril(jnp.ones((CHUNK, CHUNK), dtype=w_s.dtype))
    w = w_s * mask[None]
    mixed = (jnp.einsum('gts,bcsgd->bctgd', w, vc)
             + jnp.transpose(b_s)[None, None, :, :, None])
    return u * mixed.reshape(B, S, SG_WIDTH)


def setup_inputs(seed: int = 0) -> dict:
    key = jax.random.key(seed)
    ks = jax.random.split(key, 24)
    n = jax.random.normal
    f32 = jnp.float32

    def gain(k, shape):
        return 1.0 + 0.05 * n(k, shape, f32)

    x = n(ks[0], (BATCH, SEQ, D_MODEL), f32)
    p = n(ks[1], (DEPTH, BATCH, SEQ, PLE_DIM), f32)
    w_in = n(ks[2], (DEPTH, D_MODEL, IN_PROJ_WIDTH), f32) * D_MODEL ** -0.5
    f_bias = (jnp.linspace(1.0, 5.0, N_ATT_HEADS, dtype=f32)[None, :]
              + 0.1 * n(ks[3], (DEPTH, N_ATT_HEADS), f32))
    sg_ln_g = gain(ks[4], (DEPTH, SG_WIDTH))
    sg_ln_b = 0.02 * n(ks[5], (DEPTH, SG_WIDTH), f32)
    sg_w = n(ks[6], (DEPTH, N_SG_GROUPS, CHUNK, CHUNK), f32) * CHUNK ** -0.5
    sg_b = 1.0 + 0.02 * n(ks[7], (DEPTH, N_SG_GROUPS, CHUNK), f32)
    att_out_g = gain(ks[8], (DEPTH, ATT_WIDTH))
    sg_out_g = gain(ks[9], (DEPTH, SG_WIDTH))
    w_out = n(ks[10], (DEPTH, MIX_WIDTH, D_MODEL), f32) * MIX_WIDTH ** -0.5
    pre_mix_g = gain(ks[11], (DEPTH, D_MODEL))
    post_mix_g = gain(ks[12], (DEPTH, D_MODEL))
    pre_ffn_g = gain(ks[13], (DEPTH, D_MODEL))
    post_ffn_g = gain(ks[14], (DEPTH, D_MODEL))
    w_ff1 = n(ks[15], (DEPTH, D_MODEL, D_FF), f32) * D_MODEL ** -0.5
    w_ff2 = n(ks[16], (DEPTH, D_FF, D_MODEL), f32) * D_FF ** -0.5
    ple_w = n(ks[17], (DEPTH, PLE_DIM, D_MODEL), f32) * PLE_DIM ** -0.5
    ple_gate_w = n(ks[18], (DEPTH, D_MODEL, D_MODEL), f32) * D_MODEL ** -0.5
    ple_gate_b = 0.02 * n(ks[19], (DEPTH, D_MODEL), f32)
    return {"x": x, "p": p, "w_in": w_in, "f_bias": f_bias,
            "sg_ln_g": sg_ln_g, "sg_ln_b": sg_ln_b, "sg_w": sg_w, "sg_b": sg_b,
            "att_out_g": att_out_g, "sg_out_g": sg_out_g, "w_out": w_out,
            "pre_mix_g": pre_mix_g, "post_mix_g": post_mix_g,
            "pre_ffn_g": pre_ffn_g, "post_ffn_g": post_ffn_g,
            "w_ff1": w_ff1, "w_ff2": w_ff2,
            "ple_w": ple_w, "ple_gate_w": ple_gate_w, "ple_gate_b": ple_gate_b}


def reference(x, p, w_in, f_bias, sg_ln_g, sg_ln_b, sg_w, sg_b, att_out_g, sg_out_g,
              w_out, pre_mix_g, post_mix_g, pre_ffn_g, post_ffn_g, w_ff1, w_ff2,
              ple_w, ple_gate_w, ple_gate_b):
    B, S, _ = x.shape
    splits = [ATT_WIDTH, 2 * ATT_WIDTH, 3 * ATT_WIDTH,
              3 * ATT_WIDTH + N_ATT_HEADS, 3 * ATT_WIDTH + N_ATT_HEADS + SG_WIDTH]
    h = x
    for i in range(DEPTH):
        a = rms_norm(h, pre_mix_g[i])
        z = a @ w_in[i]
        q, k, v, f, u_sg, v_sg = jnp.split(z, splits, axis=-1)
        q = q.reshape(B, S, N_ATT_HEADS, HEAD_DIM)
        k = k.reshape(B, S, N_ATT_HEADS, HEAD_DIM)
        v = v.reshape(B, S, N_ATT_HEADS, HEAD_DIM)
        y_att = forgetting_attention(q, k, v, f + f_bias[i])
        y_sg = chunked_spatial_gating(u_sg, v_sg, sg_ln_g[i], sg_ln_b[i],
                                      sg_w[i], sg_b[i])
        y = jnp.concatenate([rms_norm(y_att, att_out_g[i]),
                             rms_norm(y_sg, sg_out_g[i])], axis=-1)
        h = h + rms_norm(y @ w_out[i], post_mix_g[i])
        c = rms_norm(h, pre_ffn_g[i])
        ff = jnp.square(jax.nn.relu(c @ w_ff1[i])) @ w_ff2[i]
        h = h + rms_norm(ff, post_ffn_g[i])
        gate = jax.nn.sigmoid(h @ ple_gate_w[i] + ple_gate_b[i])
        h = h + gate * (p[i] @ ple_w[i])
    return h
```

```python
import numpy as np
import concourse.bass as bass
import concourse.mybir as mybir
from concourse.bass_utils import run_bass_kernel_spmd

F32 = mybir.dt.float32
BF16 = mybir.dt.bfloat16
AF = mybir.ActivationFunctionType
ALU = mybir.AluOpType
AX = mybir.AxisListType

EPS = 1e-6
NEG = -30000.0


class _Ins:
    __slots__ = ("eng", "idx", "fn", "deps", "signal", "is_dma", "dma_sem", "dma_val", "sig_val", "epoch")

    def __init__(self, eng, idx, fn, is_dma):
        self.eng = eng
        self.idx = idx
        self.fn = fn
        self.deps = set()
        self.signal = False
        self.is_dma = is_dma
        self.dma_sem = None
        self.dma_val = 0
        self.sig_val = 0
        self.epoch = 0


class Prog:
    ENGS = ("pe", "act", "dve", "pool", "sp")

    def __init__(self, nc, n_dma_sems=24):
        self.nc = nc
        self.q = {e: [] for e in self.ENGS}
        self.lastw = {}
        self.readers = {}
        self.n_dma_sems = n_dma_sems
        self.dma_count = 0
        self.dma_last = [None] * n_dma_sems
        self.dma_pools = {"sp": (0, n_dma_sems - 8), "act": (0, n_dma_sems - 8), "pool": (n_dma_sems - 8, 8)}
        self.dma_pool_cnt = {"sp": 0, "act": 0, "pool": 0}
        self.dma_sem_uses = [0] * n_dma_sems
        self.epoch = 0

    def op(self, eng, fn, r=(), w=(), dma=False):
        ins = _Ins(eng, len(self.q[eng]), fn, dma)
        ins.epoch = self.epoch
        deps = ins.deps
        for k in r:
            lw = self.lastw.get(k)
            if lw is not None:
                deps.add(lw)
        for k in w:
            lw = self.lastw.get(k)
            if lw is not None:
                deps.add(lw)
            rd = self.readers.get(k)
            if rd:
                for x in rd[0].values():
                    deps.add(x)
                for x in rd[1]:
                    deps.add(x)
        if dma:
            base, cnt = self.dma_pools[eng]
            pk = "sp" if eng in ("sp", "act") else "pool"
            s = base + self.dma_pool_cnt[pk] % cnt
            self.dma_pool_cnt[pk] += 1
            prev = self.dma_last[s]
            if prev is not None:
                deps.add(prev)
            self.dma_sem_uses[s] += 1
            ins.dma_sem = s
            ins.dma_val = 16 * self.dma_sem_uses[s]
            self.dma_last[s] = ins
            self.dma_count += 1
        deps.discard(ins)
        for k in w:
            self.lastw[k] = ins
            self.readers[k] = ({}, [])
        for k in r:
            rd = self.readers.setdefault(k, ({}, []))
            if dma:
                rd[1].append(ins)
            else:
                rd[0][eng] = ins
        self.q[eng].append(ins)
        return ins

    def barrier(self):
        lasts = []
        for e in self.ENGS:
            for ins in reversed(self.q[e]):
                if not ins.is_dma and ins.fn is not None:
                    lasts.append(ins)
                    break
        dmas = [d for d in self.dma_last if d is not None]
        for e in self.ENGS:
            ins = _Ins(e, len(self.q[e]), None, False)
            ins.epoch = self.epoch
            ins.deps = set(lasts) | set(dmas)
            self.q[e].append(ins)
        self.lastw.clear()
        self.readers.clear()
        self.epoch += 1

    def wait_all(self, eng, instrs):
        ins = _Ins(eng, len(self.q[eng]), None, False)
        ins.epoch = self.epoch
        ins.deps = set(instrs)
        self.q[eng].append(ins)

    def emit(self):
        nc = self.nc
        for e in self.ENGS:
            for ins in self.q[e]:
                for d in ins.deps:
                    if not d.is_dma:
                        d.signal = True
        counts = {}
        for e in self.ENGS:
            c = 0
            ep = 0
            mx = 0
            for ins in self.q[e]:
                if ins.epoch != ep:
                    ep = ins.epoch
                    c = 0
                if (not ins.is_dma) and ins.signal and ins.fn is not None:
                    c += 1
                    ins.sig_val = c
                    mx = max(mx, c)
            counts[e] = mx
        self.counts = counts
        nep = self.epoch + 1
        import contextlib

        with contextlib.ExitStack() as st:
            esem = {(e, ep): st.enter_context(nc.semaphore("s_%s%d" % (e, ep))) for e in self.ENGS for ep in range(nep)}
            dsem = [st.enter_context(nc.semaphore("s_dma%d" % i)) for i in range(self.n_dma_sems)]
            block = st.enter_context(nc.Block())

            def run(e, eng):
                known = {}
                for ins in self.q[e]:
                    waits = {}
                    for d in ins.deps:
                        if d.is_dma:
                            key = ("d", d.dma_sem)
                            val = d.dma_val
                        else:
                            if d.fn is None:
                                continue
                            if d.eng == e and not ins.is_dma:
                                if e == "pe" or ins.idx - d.idx >= 3:
                                    continue
                            key = ("e", (d.eng, d.epoch))
                            val = d.sig_val
                        if val > waits.get(key, 0):
                            waits[key] = val
                    for key, val in waits.items():
                        if known.get(key, 0) >= val:
                            continue
                        sem = dsem[key[1]] if key[0] == "d" else esem[key[1]]
                        eng.wait_ge(sem, val)
                        known[key] = val
                    if ins.fn is None:
                        continue
                    bi = ins.fn(eng)
                    if ins.is_dma:
                        bi.then_inc(dsem[ins.dma_sem], 16)
                    elif ins.signal:
                        bi.then_inc(esem[(e, ins.epoch)], 1)

            @block.tensor
            def _(eng):
                run("pe", eng)

            @block.scalar
            def _(eng):
                run("act", eng)

            @block.vector
            def _(eng):
                run("dve", eng)

            @block.gpsimd
            def _(eng):
                run("pool", eng)

            @block.sync
            def _(eng):
                run("sp", eng)


class Cfg:
    def __init__(self, D=1024, S=8192, PLE=256):
        self.D, self.S, self.PLE = D, S, PLE
        self.KC = D // 128
        self.AW = D // 2
        self.H = self.AW // 64
        self.SGW = D // 2
        self.G = self.SGW // 64
        self.DFF = 4 * D
        self.FC = self.DFF // 128
        self.IPW = 3 * self.AW + self.H + 2 * self.SGW
        self.NB = S // 128
        self.J = self.NB // 8
        self.NO = 2 * self.J
        self.HH = self.H // 2
        self.PK = PLE // 128
        self.DH = max(1, D // 512)
        self.DW = min(D, 512)
        NO, J, NB = self.NO, self.J, self.NB
        self.groups = [list(range(i, min(i + 4, NO))) for i in range(0, NO, 4)]
        self.lo = [4 * j for j in range(J)] + [NB - 4 - 4 * j for j in reversed(range(J))]
        self.mtype = [0] * J + [1] * J

    def owned_blocks(self, r):
        J, NB = self.J, self.NB
        return [4 * j + r for j in range(J)] + [NB - 1 - 4 * j - r for j in reversed(range(J))]


class Arena:
    def __init__(self, ap, total):
        self.A = ap
        self.total = total
        self.top = 0

    def alloc(self, shape, dt):
        n = int(np.prod(shape[1:]))
        ne = n * (2 if dt == F32 else 1)
        ne = (ne + 15) // 16 * 16
        assert self.top + ne <= self.total, ("SBUF arena overflow", self.top, ne, self.total)
        v = self.A[:, self.top:self.top + (n * (2 if dt == F32 else 1))]
        self.top += ne
        if dt == F32:
            v = v.bitcast(F32)
        if len(shape) > 2:
            names = " ".join("a%d" % i for i in range(len(shape) - 1))
            kw = {"a%d" % i: int(shape[i + 1]) for i in range(len(shape) - 1)}
            v = v.rearrange("p (%s) -> p %s" % (names, names), **kw)
        if shape[0] < 128:
            v = v[0:shape[0]]
        return v


def _ap(t):
    return t.ap() if hasattr(t, "ap") else t[:]


def build_program(cfg, debug=False):
    c = cfg
    D, S, KC, AW, H, HH, SGW, G, NB, NO = c.D, c.S, c.KC, c.AW, c.H, c.HH, c.SGW, c.G, c.NB, c.NO
    DFF, FC, PLE, PK, DH, DW = c.DFF, c.FC, c.PLE, c.PK, c.DH, c.DW
    NG = len(c.groups)
    HP = H // 2
    HPP = HH // 2
    nc = bass.Bass("TRN2", target_bir_lowering=False)

    def din(name, shape):
        return nc.dram_tensor(name, list(shape), F32, kind="ExternalInput").ap()

    xfull = din("xfull", [S, D])
    xown = din("xown", [NO * 128, D])
    pown = din("pown", [NO * 128, PLE])
    w_in = din("w_in", [D, c.IPW])
    w_out = din("w_out", [D, D])
    w_ff1 = din("w_ff1", [D, DFF])
    w_ff2 = din("w_ff2", [DFF, D])
    ple_w = din("ple_w", [PLE, D])
    gate_w = din("gate_w", [D, D])
    gpm_d = din("gpm", [128, KC])
    gpf_d = din("gpf", [128, KC])
    gcat_d = din("gcat", [128, KC])
    gpmix_d = din("gpmix", [128, D])
    gpffn_d = din("gpffn", [128, D])
    gateb_d = din("gateb", [128, D])
    lng_d = din("lng", [128, SGW])
    lnb_d = din("lnb", [128, SGW])
    fbias_d = din("fbias", [128, H])
    sgwT_d = din("sgwT", [128, G, 128])
    sgbT_d = din("sgbT", [128, G])
    ident_d = din("ident", [128, 128])
    U_d = din("U", [128, 128])
    maskT_d = din("maskT", [128, 8, 128])
    sel_d = din("sel", [HH, HH, 128])
    LTfull_d = din("LTfull", [NB, NB])
    LTown_d = din("LTown", [NB, NO])
    out_d = nc.dram_tensor("out", [NO * 128, D], F32, kind="ExternalOutput").ap()
    h1_d = nc.dram_tensor("h1_scr", [NO * 128, D], F32).ap()
    h2_d = nc.dram_tensor("h2_scr", [NO * 128, D], F32).ap()
    dbg = {}

    total = (nc.sbuf_bytes_remaining - 2048) // 2
    total = total // 16 * 16
    arena_t = nc.alloc_sbuf_tensor("arena", [128, total], BF16)
    AR = Arena(_ap(arena_t), total)
    banks = [_ap(nc.alloc_psum_tensor("bank%d" % i, [128, 512], F32)) for i in range(8)]
    banksb = [b.bitcast(BF16) for b in banks]

    P = Prog(nc)
    rr = [0]

    def wq():
        return "sp"

    identf = AR.alloc([128, 128], F32)
    identb = AR.alloc([128, 128], BF16)
    Uf = AR.alloc([128, 128], F32)
    onesf = AR.alloc([128, 128], F32)
    maskb = AR.alloc([128, 8, 128], BF16)
    selb = AR.alloc([128, HH, 128], BF16)
    LTfull = AR.alloc([128, NB], F32)
    LTown = AR.alloc([128, NO], F32)
    gpm = AR.alloc([128, KC], F32)
    gpf = AR.alloc([128, KC], F32)
    gcat = AR.alloc([128, KC], F32)
    fbias = AR.alloc([128, H], F32)
    stats = AR.alloc([128, 64], F32)
    const_top = AR.top
    QT = AR.alloc([128, HP, NO * 128], BF16)
    within_own = AR.alloc([128, NO, H], F32)
    yatt = AR.alloc([128, NO, AW], F32)
    persist_top = AR.top

    def ld(dst, src, key, eng="sp"):
        return P.op(eng, lambda e: e.dma_start(out=dst, in_=src), w=[key], dma=True)

    ld(identf, ident_d, "identf")
    ld(Uf, U_d, "Uf")
    ld(LTfull[0:NB], LTfull_d, "LTfull")
    ld(LTown[0:NB], LTown_d, "LTown")
    ld(gpm, gpm_d, "gpm")
    ld(gpf, gpf_d, "gpf")
    ld(gcat, gcat_d, "gcat")
    ld(fbias, fbias_d, "fbias")
    P.op("pool", lambda e: e.dma_start(out=maskb, in_=maskT_d), w=["maskb"], dma=True)
    P.op("pool", lambda e: e.dma_start(out=selb[0:HH], in_=sel_d), w=["selb"], dma=True)
    P.op("dve", lambda e: e.tensor_copy(out=identb, in_=identf), r=["identf"], w=["identb"])
    P.op("dve", lambda e: e.memset(onesf, 1.0), w=["onesf"])

    scnt = [0]

    def stat_slot(n=1):
        s = scnt[0]
        scnt[0] = (scnt[0] + n) % 60
        if s + n > 60:
            s = 0
            scnt[0] = n
        return s

    def load_w(dst, src2d, key, kcn):
        for kc in range(kcn):
            P.op("pool", lambda e, kc=kc: e.dma_start(out=dst[:, kc, :], in_=src2d[kc * 128:(kc + 1) * 128, :]),
                 w=[key + str(kc)], dma=True)
        return [key + str(kc) for kc in range(kcn)]

    def rsqrt_ops(ss_ap, out_ap, n, scale, rkeys, wkeys):
        tmpslot = stat_slot(n)
        tmp = stats[:, tmpslot:tmpslot + n]
        tks = ["st%d" % (tmpslot + j) for j in range(n)]
        P.op("act", lambda e: e.activation(out=tmp, in_=ss_ap, func=AF.Ln, scale=scale, bias=EPS), r=rkeys, w=tks)
        P.op("act", lambda e: e.activation(out=out_ap, in_=tmp, func=AF.Exp, scale=-0.5), r=tks, w=wkeys)

    class Front:
        def __init__(self, nx=3, tpbanks=(0, 1)):
            self.xt = [AR.alloc([128, D], F32) for _ in range(nx)]
            self.xn = [AR.alloc([128, D], BF16) for _ in range(2)]
            self.i = 0
            self.tpb = tpbanks

        def run(self, rows_ap, gain, dst, dkey, norm=True, dkeys_extra=()):
            i = self.i
            self.i += 1
            xt = self.xt[i % len(self.xt)]
            xk = "xt%d" % (i % len(self.xt))
            xn = self.xn[i % 2]
            nk = "xn%d" % (i % 2)
            tb = self.tpb[i % len(self.tpb)]
            tpv = banksb[tb][:, 0:KC * 128].rearrange("p (k t) -> p k t", t=128)
            tk = "bank%d" % tb
            P.op("sp", lambda e: e.dma_start(out=xt, in_=rows_ap), w=[xk], dma=True)
            if norm:
                sl = stat_slot(2)
                ss = stats[:, sl:sl + 1]
                rs = stats[:, sl + 1:sl + 2]
                sk = "st%d" % sl
                rk = "st%d" % (sl + 1)
                P.op("act", lambda e: e.activation(out=xn, in_=xt, func=AF.Square, accum_out=ss), r=[xk], w=[nk, sk])
                rsqrt_ops(ss, rs, 1, 1.0 / D, [sk], [rk])
                P.op("dve", lambda e: e.tensor_scalar(out=xn, in0=xt, scalar1=rs, scalar2=None, op0=ALU.mult),
                     r=[xk, rk], w=[nk])
            else:
                P.op("dve", lambda e: e.tensor_copy(out=xn, in_=xt), r=[xk], w=[nk])
            for kc in range(KC):
                P.op("pe", lambda e, kc=kc: e.transpose(out=tpv[:, kc, :], in_=xn[:, kc * 128:(kc + 1) * 128], identity=identb),
                     r=[nk, "identb"], w=[tk])
            if gain is not None:
                gk = gain[1]
                gb = gain[0].unsqueeze(2).broadcast_to([128, KC, 128])
                P.op("dve", lambda e: e.tensor_tensor(out=dst, in0=tpv, in1=gb, op=ALU.mult), r=[tk, gk], w=[dkey])
            else:
                P.op("act", lambda e: e.activation(out=dst, in_=tpv, func=AF.Copy), r=[tk], w=[dkey])
            return xt, xk

    def softplus_neg(dst, src, n_keys_r, wkey, tmp):
        P.op("act", lambda e: e.activation(out=tmp, in_=src, func=AF.Exp, scale=-1.0), r=n_keys_r, w=[wkey + "_e"])
        P.op("act", lambda e: e.activation(out=dst, in_=tmp, func=AF.Ln, scale=1.0, bias=1.0), r=[wkey + "_e"], w=[wkey])

    qcol = 0
    kcol = AW
    vcol = 2 * AW
    fcol = 3 * AW
    ucol = 3 * AW + H
    vscol = ucol + SGW
    w_in_r = w_in

    mark = AR.top
    Wq = AR.alloc([128, KC, AW], BF16)
    Wfa = AR.alloc([128, KC, H], BF16)
    fr = Front(nx=3, tpbanks=(0, 1))
    aTq = [AR.alloc([128, KC, 512], BF16) for _ in range(2)]
    fown = AR.alloc([128, NO, H], F32)
    spo = AR.alloc([128, NO, H], F32)
    spo_e = AR.alloc([128, NO, H], F32)
    wqk = load_w(Wq, w_in_r[:, qcol:qcol + AW], "Wq", KC)
    wfk = load_w(Wfa, w_in_r[:, fcol:fcol + H], "Wfa", KC)
    for g, blocks in enumerate(c.groups):
        aT = aTq[g % 2]
        ak = "aTq%d" % (g % 2)
        N = len(blocks) * 128
        for ti, ob in enumerate(blocks):
            fr.run(xown[ob * 128:(ob + 1) * 128, :], (gpm, "gpm"), aT[:, :, ti * 128:(ti + 1) * 128], ak)
            for kc in range(KC):
                P.op("pe", lambda e, kc=kc, ti=ti, aT=aT: e.matmul(banks[4][:, 0:H], lhsT=aT[:, kc, ti * 128:(ti + 1) * 128], rhs=Wfa[:, kc, :],
                                                                   start=(kc == 0), stop=(kc == KC - 1)), r=[ak, wfk[kc]], w=["bank4"])
            P.op("dve", lambda e, ob=ob: e.tensor_tensor(out=fown[:, ob, :], in0=banks[4][:, 0:H], in1=fbias, op=ALU.add),
                 r=["bank4", "fbias"], w=["fown"])
        for hp in range(HP):
            qb = 2 + (hp % 2)
            for kc in range(KC):
                P.op("pe", lambda e, kc=kc, hp=hp, aT=aT, qb=qb, N=N: e.matmul(banks[qb][:, 0:N], lhsT=Wq[:, kc, hp * 128:(hp + 1) * 128], rhs=aT[:, kc, 0:N],
                                                                              start=(kc == 0), stop=(kc == KC - 1)), r=[ak, wqk[kc]], w=["bank%d" % qb])
            P.op("act", lambda e, hp=hp, qb=qb, g=g, N=N: e.activation(out=QT[:, hp, g * 512:g * 512 + N], in_=banks[qb][:, 0:N], func=AF.Copy, scale=0.125),
                 r=["bank%d" % qb], w=["QT"])
    softplus_neg(spo, fown, ["fown"], "spo", spo_e)
    P.op("pe", lambda e: e.matmul(banks[5][:, 0:NO * H], lhsT=Uf, rhs=spo.rearrange("p a b -> p (a b)"), start=True, stop=True),
         r=["Uf", "spo"], w=["bank5"])
    P.op("dve", lambda e: e.tensor_copy(out=within_own.rearrange("p a b -> p (a b)"), in_=banks[5][:, 0:NO * H]), r=["bank5"], w=["within_own"])
    P.barrier()
    AR.top = mark

    KT = AR.alloc([128, HPP, S], BF16)
    Vaug = AR.alloc([128, NB, HH, 65], BF16)
    biasT = AR.alloc([128, NG, NB, HH], F32)
    R8 = AR.alloc([128, NO * 128], BF16)
    attn_top = AR.top
    P.op("dve", lambda e: e.memset(Vaug[:, :, :, 64:65], 1.0), w=["Vones"])

    for hs in range(2):
        mark = AR.top
        Wk = AR.alloc([128, KC, HH * 64], BF16)
        Wv = AR.alloc([128, KC, HH * 64], BF16)
        Wf = AR.alloc([128, KC, HH], BF16)
        fr = Front(nx=3, tpbanks=(0, 1))
        aTa = [AR.alloc([128, KC, 512], BF16) for _ in range(2)]
        fsb = AR.alloc([128, NB, HH], F32)
        spf = AR.alloc([128, NB, HH], F32)
        spf_e = AR.alloc([128, NB, HH], F32)
        wsb = AR.alloc([128, NB, HH], F32)
        Cpos = AR.alloc([128, NB, HH], F32)
        totT = AR.alloc([128, HH], F32)
        rhs_full = AR.alloc([128, NB, HH], F32)
        rhs_own = AR.alloc([128, NO, HH], F32)
        pexo = AR.alloc([128, NO, HH], F32)
        rt1 = AR.alloc([128, NO, HH], F32)
        Rtok = AR.alloc([128, NO, HH], F32)
        wkk = load_w(Wk, w_in_r[:, kcol + hs * HH * 64: kcol + (hs + 1) * HH * 64], "Wk", KC)
        wvk = load_w(Wv, w_in_r[:, vcol + hs * HH * 64: vcol + (hs + 1) * HH * 64], "Wv", KC)
        wfk2 = load_w(Wf, w_in_r[:, fcol + hs * HH: fcol + (hs + 1) * HH], "Wf", KC)
        nst = (NB + 3) // 4
        for st in range(nst):
            aT = aTa[st % 2]
            ak = "aTa%d" % (st % 2)
            tiles = list(range(st * 4, min(NB, st * 4 + 4)))
            N = len(tiles) * 128
            for ti, t in enumerate(tiles):
                fr.run(xfull[t * 128:(t + 1) * 128, :], (gpm, "gpm"), aT[:, :, ti * 128:(ti + 1) * 128], ak)
                vb = 2 + (t % 2)
                for kc in range(KC):
                    P.op("pe", lambda e, kc=kc, ti=ti, aT=aT, vb=vb: e.matmul(banks[vb][:, 0:HH * 64], lhsT=aT[:, kc, ti * 128:(ti + 1) * 128], rhs=Wv[:, kc, :],
                                                                             start=(kc == 0), stop=(kc == KC - 1)), r=[ak, wvk[kc]], w=["bank%d" % vb])
                P.op("act", lambda e, t=t, vb=vb: e.activation(out=Vaug[:, t, :, 0:64], in_=banks[vb][:, 0:HH * 64].rearrange("p (h d) -> p h d", d=64), func=AF.Copy),
                     r=["bank%d" % vb], w=["V%d" % t])
                for kc in range(KC):
                    P.op("pe", lambda e, kc=kc, ti=ti, aT=aT: e.matmul(banks[4][:, 0:HH], lhsT=aT[:, kc, ti * 128:(ti + 1) * 128], rhs=Wf[:, kc, :],
                                                                       start=(kc == 0), stop=(kc == KC - 1)), r=[ak, wfk2[kc]], w=["bank4"])
                P.op("dve", lambda e, t=t, hs=hs: e.tensor_tensor(out=fsb[:, t, :], in0=banks[4][:, 0:HH], in1=fbias[:, hs * HH:(hs + 1) * HH], op=ALU.add),
                     r=["bank4", "fbias"], w=["fsb"])
            for hpl in range(HPP):
                kb_ = 5 + (hpl % 2)
                for kc in range(KC):
                    P.op("pe", lambda e, kc=kc, hpl=hpl, aT=aT, kb_=kb_, N=N: e.matmul(banks[kb_][:, 0:N], lhsT=Wk[:, kc, hpl * 128:(hpl + 1) * 128], rhs=aT[:, kc, 0:N],
                                                                                      start=(kc == 0), stop=(kc == KC - 1)), r=[ak, wkk[kc]], w=["bank%d" % kb_])
                P.op("act", lambda e, hpl=hpl, kb_=kb_, st=st, N=N: e.activation(out=KT[:, hpl, st * 512:st * 512 + N], in_=banks[kb_][:, 0:N], func=AF.Copy),
                     r=["bank%d" % kb_], w=["KT%d" % st])

        softplus_neg(spf, fsb, ["fsb"], "spf", spf_e)
        spf2 = spf.rearrange("p a b -> p (a b)")
        P.op("pe", lambda e: e.matmul(banks[0][:, 0:NB * HH], lhsT=Uf, rhs=spf2, start=True, stop=True), r=["Uf", "spf"], w=["bank0"])
        P.op("dve", lambda e: e.tensor_copy(out=wsb.rearrange("p a b -> p (a b)"), in_=banks[0][:, 0:NB * HH]), r=["bank0"], w=["wsb"])
        for hh in range(HH):
            P.op("pe", lambda e, hh=hh: e.matmul(banks[1][0:NB, hh:hh + 1], lhsT=spf[:, :, hh], rhs=onesf[:, 0:1], start=True, stop=True),
                 r=["spf", "onesf"], w=["bank1"])
        P.op("dve", lambda e: e.tensor_copy(out=totT[0:NB, :], in_=banks[1][0:NB, 0:HH]), r=["bank1"], w=["totT"])
        P.op("dve", lambda e: e.tensor_tensor(out=rhs_full[0:NB], in0=LTfull[0:NB].unsqueeze(2).broadcast_to([NB, NB, HH]),
                                              in1=totT[0:NB].unsqueeze(1).broadcast_to([NB, NB, HH]), op=ALU.mult), r=["LTfull", "totT"], w=["rhs_full"])
        P.op("dve", lambda e: e.tensor_tensor(out=rhs_own[0:NB], in0=LTown[0:NB].unsqueeze(2).broadcast_to([NB, NO, HH]),
                                              in1=totT[0:NB].unsqueeze(1).broadcast_to([NB, NO, HH]), op=ALU.mult), r=["LTown", "totT"], w=["rhs_own"])
        P.op("pe", lambda e: e.matmul(banks[2][:, 0:NB * HH], lhsT=onesf[0:NB, :], rhs=rhs_full[0:NB].rearrange("p a b -> p (a b)"), start=True, stop=True),
             r=["onesf", "rhs_full"], w=["bank2"])
        P.op("pe", lambda e: e.matmul(banks[3][:, 0:NO * HH], lhsT=onesf[0:NB, :], rhs=rhs_own[0:NB].rearrange("p a b -> p (a b)"), start=True, stop=True),
             r=["onesf", "rhs_own"], w=["bank3"])
        P.op("dve", lambda e: e.tensor_tensor(out=Cpos.rearrange("p a b -> p (a b)"), in0=banks[2][:, 0:NB * HH], in1=wsb.rearrange("p a b -> p (a b)"), op=ALU.add),
             r=["bank2", "wsb"], w=["Cpos"])
        P.op("dve", lambda e: e.tensor_copy(out=pexo.rearrange("p a b -> p (a b)"), in_=banks[3][:, 0:NO * HH]), r=["bank3"], w=["pexo"])
        for g, blocks in enumerate(c.groups):
            g0 = blocks[0]
            nb = len(blocks)
            P.op("dve", lambda e, g=g, g0=g0: e.tensor_tensor(out=biasT[:, g, :, :], in0=Cpos, in1=pexo[:, g0:g0 + 1, :].broadcast_to([128, NB, HH]), op=ALU.subtract),
                 r=["Cpos", "pexo"], w=["biasT"])
            P.op("dve", lambda e, g0=g0, nb=nb: e.tensor_tensor(out=rt1[:, g0:g0 + nb, :], in0=pexo[:, g0:g0 + 1, :].broadcast_to([128, nb, HH]), in1=pexo[:, g0:g0 + nb, :], op=ALU.subtract),
                 r=["pexo"], w=["rt1"])
            P.op("dve", lambda e, g0=g0, nb=nb, hs=hs: e.tensor_tensor(out=Rtok[:, g0:g0 + nb, :], in0=rt1[:, g0:g0 + nb, :], in1=within_own[:, g0:g0 + nb, hs * HH:(hs + 1) * HH], op=ALU.subtract),
                 r=["rt1", "within_own"], w=["Rtok"])
            for ti, ob in enumerate(blocks):
                P.op("pe", lambda e, ti=ti, ob=ob: e.matmul(banks[4][0:HH, ti * 128:(ti + 1) * 128], lhsT=Rtok[:, ob, :], rhs=identf, start=True, stop=True),
                     r=["Rtok", "identf"], w=["bank4"])
            P.op("dve", lambda e, g0=g0, nb=nb: e.tensor_copy(out=R8[0:HH, g0 * 128:(g0 + nb) * 128], in_=banks[4][0:HH, 0:nb * 128]), r=["bank4"], w=["R8"])

        mark2 = AR.top
        pts = [AR.alloc([128, 512], BF16) for _ in range(3)]
        osb = [AR.alloc([128, 512], F32) for _ in range(2)]
        rc = AR.alloc([128, 8], F32)
        items = []
        for hh in range(HH):
            for g, blocks in enumerate(c.groups):
                kmax = c.lo[blocks[-1]] + 3
                for kb in range(kmax + 1):
                    items.append((hh, g, kb, kb == kmax))
        gcount = {}
        gi = [0]

        def emit_S(i):
            hh, g, kb, last = items[i]
            blocks = c.groups[g]
            h = hs * HH + hh
            hpg = h // 2
            e_ = h % 2
            hpl = hh // 2
            fa = 0
            while c.lo[blocks[fa]] + 3 < kb:
                fa += 1
            nact = len(blocks) - fa
            N = nact * 128
            c0 = blocks[fa] * 128
            sb = i % 2
            sk = "bank%d" % sb
            msk = [(bi, kb - c.lo[ob]) for bi, ob in enumerate(blocks) if bi >= fa and c.lo[ob] <= kb <= c.lo[ob] + 3]
            P.op("pe", lambda e: e.matmul(banks[sb][:, 0:N], lhsT=KT[e_ * 64:(e_ + 1) * 64, hpl, kb * 128:(kb + 1) * 128],
                                          rhs=QT[e_ * 64:(e_ + 1) * 64, hpg, c0:c0 + N], start=True, stop=False),
                 r=["KT%d" % (kb // 4), "QT"], w=[sk])
            P.op("pe", lambda e: e.matmul(banks[sb][:, 0:N], lhsT=selb[0:HH, hh, :], rhs=R8[0:HH, c0:c0 + N], start=False, stop=(len(msk) == 0)),
                 r=["selb", "R8"], w=[sk])
            for mi, (bi, i4) in enumerate(msk):
                mt = c.mtype[blocks[bi]] * 4 + i4
                P.op("pe", lambda e, bi=bi, mt=mt, mi=mi: e.matmul(banks[sb][:, (bi - fa) * 128:(bi - fa + 1) * 128], lhsT=identb, rhs=maskb[:, mt, :],
                                                                 start=False, stop=(mi == len(msk) - 1)), r=["identb", "maskb"], w=[sk])
            pt = pts[i % 3]
            pk = "pt%d" % (i % 3)
            P.op("act", lambda e: e.activation(out=pt[:, 0:N], in_=banks[sb][:, 0:N], func=AF.Exp, bias=biasT[:, g, kb, hh:hh + 1], scale=1.0),
                 r=[sk, "biasT"], w=[pk])

        def emit_PV(i):
            hh, g, kb, last = items[i]
            blocks = c.groups[g]
            h = hs * HH + hh
            fa = 0
            while c.lo[blocks[fa]] + 3 < kb:
                fa += 1
            N = (len(blocks) - fa) * 128
            if kb == 0:
                gcount[(hh, g)] = gi[0]
                gi[0] += 1
            ob_ = 2 + gcount[(hh, g)] % 2
            ok = "bank%d" % ob_
            pt = pts[i % 3]
            pk = "pt%d" % (i % 3)
            P.op("pe", lambda e: e.matmul(banks[ob_][0:65, fa * 128:fa * 128 + N], lhsT=Vaug[:, kb, hh, :], rhs=pt[:, 0:N], start=(kb == 0), stop=last),
                 r=["V%d" % kb, "Vones", pk], w=[ok])
            if last:
                nb = len(blocks)
                os_ = osb[gcount[(hh, g)] % 2]
                osk = "osb%d" % (gcount[(hh, g)] % 2)
                P.op("dve", lambda e: e.tensor_copy(out=os_[0:65, 0:nb * 128], in_=banks[ob_][0:65, 0:nb * 128]), r=[ok], w=[osk])
                for ti, ob in enumerate(blocks):
                    P.op("pe", lambda e, ti=ti: e.matmul(banks[4][:, ti * 65:(ti + 1) * 65], lhsT=os_[0:65, ti * 128:(ti + 1) * 128], rhs=identf[0:65, 0:65], start=True, stop=True),
                         r=[osk, "identf"], w=["bank4"])
                o3 = banks[4][:, 0:nb * 65].rearrange("p (b x) -> p b x", x=65)
                P.op("dve", lambda e: e.reciprocal(out=rc[:, 0:nb], in_=o3[:, :, 64]), r=["bank4"], w=["rc"])
                for ti, ob in enumerate(blocks):
                    P.op("dve", lambda e, ti=ti, ob=ob: e.tensor_scalar(out=yatt[:, ob, h * 64:(h + 1) * 64], in0=o3[:, ti, 0:64], scalar1=rc[:, ti:ti + 1], scalar2=None, op0=ALU.mult),
                         r=["bank4", "rc"], w=["yatt"])

        for i in range(len(items)):
            emit_S(i)
            if i >= 1:
                emit_PV(i - 1)
        emit_PV(len(items) - 1)
        P.barrier()
        AR.top = mark

    AR.top = attn_top - 0
    AR.top = persist_top
    C0 = 0.7978845608028654
    C1_ = 0.044715
    Wu = AR.alloc([128, KC, SGW], BF16)
    Wvs = AR.alloc([128, KC, SGW], BF16)
    Wo = AR.alloc([128, KC, D], BF16)
    wsT = AR.alloc([128, G, 128], BF16)
    wsTf = AR.alloc([128, G, 128], F32)
    sgb = AR.alloc([128, G], F32)
    lng = AR.alloc([128, SGW], F32)
    lnb = AR.alloc([128, SGW], F32)
    gpmix = AR.alloc([128, D], F32)
    fr = Front(nx=3, tpbanks=(0, 1))
    aTc = [AR.alloc([128, KC, 128], BF16) for _ in range(2)]
    x2 = AR.alloc([128, SGW], F32)
    tA = AR.alloc([128, SGW], F32)
    tB = AR.alloc([128, SGW], F32)
    gu = AR.alloc([128, SGW], F32)
    gv = AR.alloc([128, SGW], F32)
    xc = AR.alloc([128, SGW], F32)
    vln = AR.alloc([128, SGW], F32)
    vlnb = AR.alloc([128, SGW], BF16)
    tmix = AR.alloc([128, SGW], F32)
    ysg = AR.alloc([128, SGW], F32)
    junk1 = AR.alloc([128, D], BF16)
    yn = AR.alloc([128, D], BF16)
    ynT = AR.alloc([128, KC, 128], BF16)
    h1t = [AR.alloc([128, D], F32) for _ in range(2)]
    wuk = load_w(Wu, w_in_r[:, ucol:ucol + SGW], "Wu", KC)
    wvsk = load_w(Wvs, w_in_r[:, vscol:vscol + SGW], "Wvs", KC)
    wok = load_w(Wo, w_out, "Wo", KC)
    ld(wsTf, sgwT_d, "wsTf")
    ld(sgb, sgbT_d, "sgb")
    ld(lng, lng_d, "lng")
    ld(lnb, lnb_d, "lnb")
    ld(gpmix, gpmix_d, "gpmix")
    P.op("dve", lambda e: e.tensor_tensor(out=wsT, in0=wsTf, in1=Uf.unsqueeze(1).broadcast_to([128, G, 128]), op=ALU.mult), r=["wsTf", "Uf"], w=["wsT"])

    def gelu2(dst, ps, pk, dk):
        P.op("act", lambda e: e.activation(out=x2, in_=ps, func=AF.Square), r=[pk], w=["x2"])
        P.op("dve", lambda e: e.tensor_scalar(out=tA, in0=x2, scalar1=C1_, scalar2=1.0, op0=ALU.mult, op1=ALU.add), r=["x2"], w=["tA"])
        P.op("dve", lambda e: e.tensor_tensor(out=tB, in0=tA, in1=ps, op=ALU.mult), r=["tA", pk], w=["tB"])
        P.op("act", lambda e: e.activation(out=tA, in_=tB, func=AF.Tanh, scale=C0), r=["tB"], w=["tA"])
        P.op("dve", lambda e: e.scalar_tensor_tensor(out=dst, in0=tA, scalar=1.0, in1=ps, op0=ALU.add, op1=ALU.mult), r=["tA", pk], w=[dk])

    for ob in range(NO):
        i2 = ob % 2
        aT = aTc[i2]
        ak = "aTc%d" % i2
        xt, xk = fr.run(xown[ob * 128:(ob + 1) * 128, :], (gpm, "gpm"), aT, ak)
        for kc in range(KC):
            P.op("pe", lambda e, kc=kc, aT=aT: e.matmul(banks[2][:, 0:SGW], lhsT=aT[:, kc, :], rhs=Wu[:, kc, :], start=(kc == 0), stop=(kc == KC - 1)),
                 r=[ak, wuk[kc]], w=["bank2"])
        for kc in range(KC):
            P.op("pe", lambda e, kc=kc, aT=aT: e.matmul(banks[3][:, 0:SGW], lhsT=aT[:, kc, :], rhs=Wvs[:, kc, :], start=(kc == 0), stop=(kc == KC - 1)),
                 r=[ak, wvsk[kc]], w=["bank3"])
        gelu2(gu, banks[2][:, 0:SGW], "bank2", "gu")
        gelu2(gv, banks[3][:, 0:SGW], "bank3", "gv")
        sl = stat_slot(6)
        s1 = stats[:, sl:sl + 1]
        nm = stats[:, sl + 1:sl + 2]
        s2 = stats[:, sl + 2:sl + 3]
        r2 = stats[:, sl + 3:sl + 4]
        k_ = ["st%d" % (sl + j) for j in range(6)]
        P.op("dve", lambda e, s1=s1: e.reduce_sum(out=s1, in_=gv, axis=AX.X), r=["gv"], w=[k_[0]])
        P.op("dve", lambda e, s1=s1, nm=nm: e.tensor_scalar(out=nm, in0=s1, scalar1=-0.5 / SGW, scalar2=None, op0=ALU.mult), r=[k_[0]], w=[k_[1]])
        P.op("dve", lambda e, nm=nm: e.tensor_scalar(out=xc, in0=gv, scalar1=0.5, scalar2=nm, op0=ALU.mult, op1=ALU.add), r=["gv", k_[1]], w=["xc"])
        P.op("act", lambda e, s2=s2: e.activation(out=junk1[:, 0:SGW], in_=xc, func=AF.Square, accum_out=s2), r=["xc"], w=["junk1", k_[2]])
        rsqrt_ops(s2, r2, 1, 1.0 / SGW, [k_[2]], [k_[3]])
        P.op("dve", lambda e, r2=r2: e.scalar_tensor_tensor(out=vln, in0=xc, scalar=r2, in1=lng, op0=ALU.mult, op1=ALU.mult), r=["xc", k_[3], "lng"], w=["vln"])
        P.op("dve", lambda e: e.tensor_tensor(out=vlnb, in0=vln, in1=lnb, op=ALU.add), r=["vln", "lnb"], w=["vlnb"])
        for g8 in range(G):
            P.op("pe", lambda e, g8=g8: e.matmul(banks[4][:, g8 * 64:(g8 + 1) * 64], lhsT=wsT[:, g8, :], rhs=vlnb[:, g8 * 64:(g8 + 1) * 64], start=True, stop=True),
                 r=["wsT", "vlnb"], w=["bank4"])
        P.op("dve", lambda e: e.tensor_tensor(out=tmix.rearrange("p (g d) -> p g d", d=64), in0=banks[4][:, 0:SGW].rearrange("p (g d) -> p g d", d=64),
                                              in1=sgb.unsqueeze(2).broadcast_to([128, G, 64]), op=ALU.add), r=["bank4", "sgb"], w=["tmix"])
        P.op("dve", lambda e: e.scalar_tensor_tensor(out=ysg, in0=gu, scalar=0.5, in1=tmix, op0=ALU.mult, op1=ALU.mult), r=["gu", "tmix"], w=["ysg"])
        sl2 = stat_slot(4)
        k2 = ["st%d" % (sl2 + j) for j in range(4)]
        ssq = stats[:, sl2:sl2 + 2]
        rsq = stats[:, sl2 + 2:sl2 + 4]
        P.op("act", lambda e, ssq=ssq, ob=ob: e.activation(out=junk1[:, 0:AW], in_=yatt[:, ob, :], func=AF.Square, accum_out=ssq[:, 0:1]), r=["yatt"], w=["junk1", k2[0]])
        P.op("act", lambda e, ssq=ssq: e.activation(out=junk1[:, 0:SGW], in_=ysg, func=AF.Square, accum_out=ssq[:, 1:2]), r=["ysg"], w=["junk1", k2[1]])
        rsqrt_ops(ssq, rsq, 2, 1.0 / AW, [k2[0], k2[1]], [k2[2], k2[3]])
        P.op("dve", lambda e, rsq=rsq, ob=ob: e.tensor_scalar(out=yn[:, 0:AW], in0=yatt[:, ob, :], scalar1=rsq[:, 0:1], scalar2=None, op0=ALU.mult), r=["yatt", k2[2]], w=["yn"])
        P.op("dve", lambda e, rsq=rsq: e.tensor_scalar(out=yn[:, AW:D], in0=ysg, scalar1=rsq[:, 1:2], scalar2=None, op0=ALU.mult), r=["ysg", k2[3]], w=["yn"])
        tpv = banksb[5][:, 0:KC * 128].rearrange("p (k t) -> p k t", t=128)
        for kc in range(KC):
            P.op("pe", lambda e, kc=kc, tpv=tpv: e.transpose(out=tpv[:, kc, :], in_=yn[:, kc * 128:(kc + 1) * 128], identity=identb), r=["yn", "identb"], w=["bank5"])
        P.op("dve", lambda e, tpv=tpv: e.tensor_tensor(out=ynT, in0=tpv, in1=gcat.unsqueeze(2).broadcast_to([128, KC, 128]), op=ALU.mult), r=["bank5", "gcat"], w=["ynT"])
        sl3 = stat_slot(4)
        k3 = ["st%d" % (sl3 + j) for j in range(4)]
        for dh in range(DH):
            ob_ = 6 + dh
            for kc in range(KC):
                P.op("pe", lambda e, kc=kc, dh=dh, ob_=ob_: e.matmul(banks[ob_][:, 0:DW], lhsT=ynT[:, kc, :], rhs=Wo[:, kc, dh * DW:(dh + 1) * DW], start=(kc == 0), stop=(kc == KC - 1)),
                     r=["ynT", wok[kc]], w=["bank%d" % ob_])
            P.op("act", lambda e, dh=dh, ob_=ob_, sl3=sl3: e.activation(out=junk1[:, 0:DW], in_=banks[ob_][:, 0:DW], func=AF.Square, accum_out=stats[:, sl3 + dh:sl3 + dh + 1]),
                 r=["bank%d" % ob_], w=["junk1", k3[dh]])
        if DH == 2:
            P.op("dve", lambda e, sl3=sl3: e.tensor_tensor(out=stats[:, sl3 + 2:sl3 + 3], in0=stats[:, sl3:sl3 + 1], in1=stats[:, sl3 + 1:sl3 + 2], op=ALU.add), r=[k3[0], k3[1]], w=[k3[2]])
            sso = stats[:, sl3 + 2:sl3 + 3]
            ssk = k3[2]
        else:
            sso = stats[:, sl3:sl3 + 1]
            ssk = k3[0]
        rso = stats[:, sl3 + 3:sl3 + 4]
        rsqrt_ops(sso, rso, 1, 1.0 / D, [ssk], [k3[3]])
        h1 = h1t[i2]
        hk = "h1t%d" % i2
        for dh in range(DH):
            ob_ = 6 + dh
            P.op("dve", lambda e, dh=dh, ob_=ob_, rso=rso, h1=h1: e.scalar_tensor_tensor(out=h1[:, dh * DW:(dh + 1) * DW], in0=banks[ob_][:, 0:DW], scalar=rso, in1=gpmix[:, dh * DW:(dh + 1) * DW], op0=ALU.mult, op1=ALU.mult),
                 r=["bank%d" % ob_, k3[3], "gpmix"], w=[hk])
        P.op("dve", lambda e, h1=h1, xt=xt: e.tensor_tensor(out=h1, in0=h1, in1=xt, op=ALU.add), r=[hk, xk], w=[hk])
        P.op("sp", lambda e, h1=h1, ob=ob: e.dma_start(out=h1_d[ob * 128:(ob + 1) * 128, :], in_=h1), r=[hk], w=["h1d%d" % ob], dma=True)
    P.barrier()
    AR.top = const_top

    W1 = AR.alloc([128, KC, DFF], BF16)
    W2 = AR.alloc([128, FC, D], BF16)
    HT = AR.alloc([128, FC, 512], BF16)
    cT = AR.alloc([128, KC, 512], BF16)
    gpffn = AR.alloc([128, D], F32)
    fr = Front(nx=2, tpbanks=(0, 1))
    rtmp = [AR.alloc([128, 512], BF16) for _ in range(2)]
    h1r = [AR.alloc([128, D], F32) for _ in range(2)]
    o2t = [AR.alloc([128, D], F32) for _ in range(1)]
    junk2 = AR.alloc([128, 512], BF16)
    w1k = load_w(W1, w_ff1, "W1", KC)
    w2k = load_w(W2, w_ff2, "W2", FC)
    ld(gpffn, gpffn_d, "gpffn")
    tcount = 0
    for g, blocks in enumerate(c.groups):
        N = len(blocks) * 128
        for ti, ob in enumerate(blocks):
            fr.run(h1_d[ob * 128:(ob + 1) * 128, :], (gpf, "gpf"), cT[:, :, ti * 128:(ti + 1) * 128], "cT")
        for fc in range(FC):
            hb = 2 + fc % 2
            for kc in range(KC):
                P.op("pe", lambda e, kc=kc, fc=fc, hb=hb, N=N: e.matmul(banks[hb][:, 0:N], lhsT=W1[:, kc, fc * 128:(fc + 1) * 128], rhs=cT[:, kc, 0:N], start=(kc == 0), stop=(kc == KC - 1)),
                     r=["cT", w1k[kc]], w=["bank%d" % hb])
            rt = rtmp[fc % 2]
            rk = "rtmp%d" % (fc % 2)
            P.op("act", lambda e, hb=hb, rt=rt, N=N: e.activation(out=rt[:, 0:N], in_=banks[hb][:, 0:N], func=AF.Relu), r=["bank%d" % hb], w=[rk])
            P.op("dve", lambda e, fc=fc, rt=rt, N=N: e.tensor_tensor(out=HT[:, fc, 0:N], in0=rt[:, 0:N], in1=rt[:, 0:N], op=ALU.mult), r=[rk], w=["HT"])
        for ti, ob in enumerate(blocks):
            i2 = tcount % 2
            tcount += 1
            h1 = h1r[i2]
            hk = "h1r%d" % i2
            P.op("sp", lambda e, h1=h1, ob=ob: e.dma_start(out=h1, in_=h1_d[ob * 128:(ob + 1) * 128, :]), r=["h1d%d" % ob], w=[hk], dma=True)
            sl3 = stat_slot(4)
            k3 = ["st%d" % (sl3 + j) for j in range(4)]
            for dh in range(DH):
                ob_ = 4 + 2 * i2 + dh
                for fc in range(FC):
                    P.op("pe", lambda e, fc=fc, dh=dh, ob_=ob_, ti=ti: e.matmul(banks[ob_][:, 0:DW], lhsT=HT[:, fc, ti * 128:(ti + 1) * 128], rhs=W2[:, fc, dh * DW:(dh + 1) * DW], start=(fc == 0), stop=(fc == FC - 1)),
                         r=["HT", w2k[fc]], w=["bank%d" % ob_])
                P.op("act", lambda e, dh=dh, ob_=ob_, sl3=sl3: e.activation(out=junk2[:, 0:DW], in_=banks[ob_][:, 0:DW], func=AF.Square, accum_out=stats[:, sl3 + dh:sl3 + dh + 1]),
                     r=["bank%d" % ob_], w=["junk2", k3[dh]])
            if DH == 2:
                P.op("dve", lambda e, sl3=sl3: e.tensor_tensor(out=stats[:, sl3 + 2:sl3 + 3], in0=stats[:, sl3:sl3 + 1], in1=stats[:, sl3 + 1:sl3 + 2], op=ALU.add), r=[k3[0], k3[1]], w=[k3[2]])
                sso = stats[:, sl3 + 2:sl3 + 3]
                ssk = k3[2]
            else:
                sso = stats[:, sl3:sl3 + 1]
                ssk = k3[0]
            rso = stats[:, sl3 + 3:sl3 + 4]
            rsqrt_ops(sso, rso, 1, 1.0 / D, [ssk], [k3[3]])
            o2 = o2t[0]
            ok2 = "o2t0"
            for dh in range(DH):
                ob_ = 4 + 2 * i2 + dh
                P.op("dve", lambda e, dh=dh, ob_=ob_, rso=rso, o2=o2: e.scalar_tensor_tensor(out=o2[:, dh * DW:(dh + 1) * DW], in0=banks[ob_][:, 0:DW], scalar=rso, in1=gpffn[:, dh * DW:(dh + 1) * DW], op0=ALU.mult, op1=ALU.mult),
                     r=["bank%d" % ob_, k3[3], "gpffn"], w=[ok2])
            P.op("dve", lambda e, o2=o2, h1=h1: e.tensor_tensor(out=h1, in0=o2, in1=h1, op=ALU.add), r=[ok2, hk], w=[hk])
            P.op("sp", lambda e, h1=h1, ob=ob: e.dma_start(out=h2_d[ob * 128:(ob + 1) * 128, :], in_=h1), r=[hk], w=["h2d%d" % ob], dma=True)
    P.barrier()
    AR.top = const_top

    Wg = AR.alloc([128, KC, D], BF16)
    Wpe = AR.alloc([128, PK, D], BF16)
    gateb = AR.alloc([128, D], F32)
    fr = Front(nx=3, tpbanks=(0, 1))
    h2T = [AR.alloc([128, KC, 128], BF16) for _ in range(2)]
    pts_ = [AR.alloc([128, PLE], F32) for _ in range(2)]
    pbs = [AR.alloc([128, PLE], BF16) for _ in range(2)]
    pTs = [AR.alloc([128, PK, 128], BF16) for _ in range(2)]
    zt = AR.alloc([128, D], F32)
    gt = AR.alloc([128, D], F32)
    outs = [AR.alloc([128, D], F32) for _ in range(2)]
    wgk = load_w(Wg, gate_w, "Wg", KC)
    wpk = load_w(Wpe, ple_w, "Wpe", PK)
    ld(gateb, gateb_d, "gateb")
    out_dmas = []
    for ob in range(NO):
        i2 = ob % 2
        xt, xk = fr.run(h2_d[ob * 128:(ob + 1) * 128, :], None, h2T[i2], "h2T%d" % i2, norm=False)
        pt_, pb_, pT_ = pts_[i2], pbs[i2], pTs[i2]
        P.op("sp", lambda e, pt_=pt_, ob=ob: e.dma_start(out=pt_, in_=pown[ob * 128:(ob + 1) * 128, :]), w=["pt_%d" % i2], dma=True)
        P.op("dve", lambda e, pt_=pt_, pb_=pb_: e.tensor_copy(out=pb_, in_=pt_), r=["pt_%d" % i2], w=["pb%d" % i2])
        tpv3 = banksb[2][:, 0:PK * 128].rearrange("p (k t) -> p k t", t=128)
        for k2_ in range(PK):
            P.op("pe", lambda e, k2_=k2_, pb_=pb_, tpv3=tpv3: e.transpose(out=tpv3[:, k2_, :], in_=pb_[:, k2_ * 128:(k2_ + 1) * 128], identity=identb), r=["pb%d" % i2, "identb"], w=["bank2"])
        P.op("act", lambda e, pT_=pT_, tpv3=tpv3: e.activation(out=pT_, in_=tpv3, func=AF.Copy), r=["bank2"], w=["pT%d" % i2])
        for dh in range(DH):
            gb_ = 4 + dh
            pb2 = 6 + dh
            for kc in range(KC):
                P.op("pe", lambda e, kc=kc, dh=dh, gb_=gb_, i2=i2: e.matmul(banks[gb_][:, 0:DW], lhsT=h2T[i2][:, kc, :], rhs=Wg[:, kc, dh * DW:(dh + 1) * DW], start=(kc == 0), stop=(kc == KC - 1)),
                     r=["h2T%d" % i2, wgk[kc]], w=["bank%d" % gb_])
            for k2_ in range(PK):
                P.op("pe", lambda e, k2_=k2_, dh=dh, pb2=pb2, pT_=pT_: e.matmul(banks[pb2][:, 0:DW], lhsT=pT_[:, k2_, :], rhs=Wpe[:, k2_, dh * DW:(dh + 1) * DW], start=(k2_ == 0), stop=(k2_ == PK - 1)),
                     r=["pT%d" % i2, wpk[k2_]], w=["bank%d" % pb2])
            P.op("dve", lambda e, dh=dh, gb_=gb_: e.tensor_tensor(out=zt[:, dh * DW:(dh + 1) * DW], in0=banks[gb_][:, 0:DW], in1=gateb[:, dh * DW:(dh + 1) * DW], op=ALU.add),
                 r=["bank%d" % gb_, "gateb"], w=["zt"])
        P.op("act", lambda e: e.activation(out=gt, in_=zt, func=AF.Tanh, scale=0.5), r=["zt"], w=["gt"])
        P.op("dve", lambda e: e.tensor_scalar(out=gt, in0=gt, scalar1=0.5, scalar2=0.5, op0=ALU.mult, op1=ALU.add), r=["gt"], w=["gt"])
        o_ = outs[i2]
        okk = "outs%d" % i2
        for dh in range(DH):
            pb2 = 6 + dh
            P.op("dve", lambda e, dh=dh, pb2=pb2, o_=o_: e.tensor_tensor(out=o_[:, dh * DW:(dh + 1) * DW], in0=gt[:, dh * DW:(dh + 1) * DW], in1=banks[pb2][:, 0:DW], op=ALU.mult),
                 r=["gt", "bank%d" % pb2], w=[okk])
        P.op("dve", lambda e, o_=o_, xt=xt: e.tensor_tensor(out=o_, in0=o_, in1=xt, op=ALU.add), r=[okk, xk], w=[okk])
        out_dmas.append(P.op("sp", lambda e, o_=o_, ob=ob: e.dma_start(out=out_d[ob * 128:(ob + 1) * 128, :], in_=o_), r=[okk], dma=True))
    P.wait_all("sp", out_dmas)
    P.emit()
    return nc, P


def make_core_inputs(cfg, core, x, p, w_in, f_bias, sg_ln_g, sg_ln_b, sg_w, sg_b, att_out_g, sg_out_g,
                     w_out, pre_mix_g, post_mix_g, pre_ffn_g, post_ffn_g, w_ff1, w_ff2, ple_w, ple_gate_w, ple_gate_b):
    c = cfg
    b, r = core // 4, core % 4
    f32 = np.float32
    blocks = c.owned_blocks(r)
    rows = np.concatenate([np.arange(bl * 128, (bl + 1) * 128) for bl in blocks])

    def fm(v):
        return np.ascontiguousarray(np.asarray(v, f32).reshape(c.KC, 128).T)

    def rep(v):
        return np.ascontiguousarray(np.broadcast_to(np.asarray(v, f32).reshape(1, -1), (128, np.asarray(v).size)))

    k = np.arange(128)[:, None]
    q = np.arange(128)[None, :]
    tri = np.where(k > q, NEG, 0.0).astype(f32)
    full = np.full((128, 128), NEG, f32)
    zero = np.zeros((128, 128), f32)
    maskT = np.zeros((128, 8, 128), f32)
    for i in range(4):
        maskT[:, i, :] = zero if i < r else (tri if i == r else full)
        maskT[:, 4 + i, :] = zero if i < 3 - r else (tri if i == 3 - r else full)
    sel = np.zeros((c.HH, c.HH, 128), f32)
    for hh in range(c.HH):
        sel[hh, hh, :] = 1.0
    LTfull = (np.arange(c.NB)[:, None] < np.arange(c.NB)[None, :]).astype(f32)
    LTown = (np.arange(c.NB)[:, None] < np.asarray(blocks)[None, :]).astype(f32)
    xb = np.asarray(x[b], f32)
    return {
        "xfull": np.ascontiguousarray(xb),
        "xown": np.ascontiguousarray(xb[rows]),
        "pown": np.ascontiguousarray(np.asarray(p[0, b], f32)[rows]),
        "w_in": np.ascontiguousarray(np.asarray(w_in[0], f32)),
        "w_out": np.ascontiguousarray(np.asarray(w_out[0], f32)),
        "w_ff1": np.ascontiguousarray(np.asarray(w_ff1[0], f32)),
        "w_ff2": np.ascontiguousarray(np.asarray(w_ff2[0], f32)),
        "ple_w": np.ascontiguousarray(np.asarray(ple_w[0], f32)),
        "gate_w": np.ascontiguousarray(np.asarray(ple_gate_w[0], f32)),
        "gpm": fm(pre_mix_g[0]),
        "gpf": fm(pre_ffn_g[0]),
        "gcat": fm(np.concatenate([np.asarray(att_out_g[0]), np.asarray(sg_out_g[0])])),
        "gpmix": rep(post_mix_g[0]),
        "gpffn": rep(post_ffn_g[0]),
        "gateb": rep(ple_gate_b[0]),
        "lng": rep(sg_ln_g[0]),
        "lnb": rep(sg_ln_b[0]),
        "fbias": rep(f_bias[0]),
        "sgwT": np.ascontiguousarray(np.transpose(np.asarray(sg_w[0], f32), (2, 0, 1))),
        "sgbT": np.ascontiguousarray(np.asarray(sg_b[0], f32).T),
        "ident": np.eye(128, dtype=f32),
        "U": (np.arange(128)[:, None] <= np.arange(128)[None, :]).astype(f32),
        "maskT": maskT,
        "sel": sel,
        "LTfull": LTfull,
        "LTown": LTown,
    }, rows


_CACHE = {}


def kernel(**inputs):
    x = np.asarray(inputs["x"])
    B, S, D = x.shape
    PLE = np.asarray(inputs["p"]).shape[-1]
    cfg = Cfg(D=D, S=S, PLE=PLE)
    key = (D, S, PLE)
    if key not in _CACHE:
        _CACHE[key] = build_program(cfg)
    nc, _ = _CACHE[key]
    in_maps, rows_all = [], []
    for core in range(8):
        m, rows = make_core_inputs(cfg, core, **inputs)
        in_maps.append(m)
        rows_all.append(rows)
    res = run_bass_kernel_spmd(nc, in_maps, core_ids=list(range(8)))
    out = np.zeros((B, S, D), np.float32)
    for core in range(8):
        out[core // 4, rows_all[core], :] = np.asarray(res.results[core]["out"], np.float32)
    return out
```

```python
import numpy as np
import concourse.bass as bass
import concourse.mybir as mybir
from concourse.bass_utils import run_bass_kernel_spmd

F32 = mybir.dt.float32
BF16 = mybir.dt.bfloat16
AF = mybir.ActivationFunctionType
ALU = mybir.AluOpType
AX = mybir.AxisListType

EPS = 1e-6
NEG = -30000.0


class _Ins:
    __slots__ = ("eng", "idx", "fn", "deps", "signal", "is_dma", "dma_sem", "dma_val", "sig_val", "epoch")

    def __init__(self, eng, idx, fn, is_dma):
        self.eng = eng
        self.idx = idx
        self.fn = fn
        self.deps = set()
        self.signal = False
        self.is_dma = is_dma
        self.dma_sem = None
        self.dma_val = 0
        self.sig_val = 0
        self.epoch = 0


class Prog:
    ENGS = ("pe", "act", "dve", "pool", "sp")

    def __init__(self, nc, n_dma_sems=24):
        self.nc = nc
        self.q = {e: [] for e in self.ENGS}
        self.lastw = {}
        self.readers = {}
        self.n_dma_sems = n_dma_sems
        self.dma_count = 0
        self.dma_last = [None] * n_dma_sems
        self.dma_pools = {"sp": (0, n_dma_sems - 8), "act": (0, n_dma_sems - 8), "pool": (n_dma_sems - 8, 8)}
        self.dma_pool_cnt = {"sp": 0, "act": 0, "pool": 0}
        self.dma_sem_uses = [0] * n_dma_sems
        self.epoch = 0

    def op(self, eng, fn, r=(), w=(), dma=False):
        ins = _Ins(eng, len(self.q[eng]), fn, dma)
        ins.epoch = self.epoch
        deps = ins.deps
        if any(k.startswith("bank") for k in r):
            w = list(w) + [k for k in r if k.startswith("bank") and k not in w]
            r = [k for k in r if not k.startswith("bank")]
        for k in r:
            lw = self.lastw.get(k)
            if lw is not None:
                deps.add(lw)
        for k in w:
            lw = self.lastw.get(k)
            if lw is not None:
                deps.add(lw)
            rd = self.readers.get(k)
            if rd:
                for x in rd[0].values():
                    deps.add(x)
                for x in rd[1]:
                    deps.add(x)
        if dma:
            base, cnt = self.dma_pools[eng]
            pk = "sp" if eng in ("sp", "act") else "pool"
            s = base + self.dma_pool_cnt[pk] % cnt
            self.dma_pool_cnt[pk] += 1
            prev = self.dma_last[s]
            if prev is not None:
                deps.add(prev)
            self.dma_sem_uses[s] += 1
            ins.dma_sem = s
            ins.dma_val = 16 * self.dma_sem_uses[s]
            self.dma_last[s] = ins
            self.dma_count += 1
        deps.discard(ins)
        for k in w:
            self.lastw[k] = ins
            self.readers[k] = ({}, [])
        for k in r:
            rd = self.readers.setdefault(k, ({}, []))
            if dma:
                rd[1].append(ins)
            else:
                rd[0][eng] = ins
        self.q[eng].append(ins)
        return ins

    def barrier(self):
        lasts = []
        for e in self.ENGS:
            for ins in reversed(self.q[e]):
                if not ins.is_dma and ins.fn is not None:
                    lasts.append(ins)
                    break
        dmas = [d for d in self.dma_last if d is not None]
        for e in self.ENGS:
            ins = _Ins(e, len(self.q[e]), None, False)
            ins.epoch = self.epoch
            ins.deps = set(lasts) | set(dmas)
            self.q[e].append(ins)
        self.lastw.clear()
        self.readers.clear()
        self.epoch += 1

    def wait_all(self, eng, instrs):
        ins = _Ins(eng, len(self.q[eng]), None, False)
        ins.epoch = self.epoch
        ins.deps = set(instrs)
        self.q[eng].append(ins)

    def emit(self):
        nc = self.nc
        for e in self.ENGS:
            for ins in self.q[e]:
                for d in ins.deps:
                    if not d.is_dma:
                        d.signal = True
        counts = {}
        for e in self.ENGS:
            c = 0
            ep = 0
            mx = 0
            for ins in self.q[e]:
                if ins.epoch != ep:
                    ep = ins.epoch
                    c = 0
                if (not ins.is_dma) and ins.signal and ins.fn is not None:
                    c += 1
                    ins.sig_val = c
                    mx = max(mx, c)
            counts[e] = mx
        self.counts = counts
        nep = self.epoch + 1
        import contextlib

        with contextlib.ExitStack() as st:
            esem = {(e, ep): st.enter_context(nc.semaphore("s_%s%d" % (e, ep))) for e in self.ENGS for ep in range(nep)}
            dsem = [st.enter_context(nc.semaphore("s_dma%d" % i)) for i in range(self.n_dma_sems)]
            block = st.enter_context(nc.Block())

            def run(e, eng):
                known = {}
                for ins in self.q[e]:
                    waits = {}
                    for d in ins.deps:
                        if d.is_dma:
                            key = ("d", d.dma_sem)
                            val = d.dma_val
                        else:
                            if d.fn is None:
                                continue
                            if d.eng == e and not ins.is_dma:
                                if e == "pe" or ins.idx - d.idx >= 3:
                                    continue
                            key = ("e", (d.eng, d.epoch))
                            val = d.sig_val
                        if val > waits.get(key, 0):
                            waits[key] = val
                    for key, val in waits.items():
                        if known.get(key, 0) >= val:
                            continue
                        sem = dsem[key[1]] if key[0] == "d" else esem[key[1]]
                        eng.wait_ge(sem, val)
                        known[key] = val
                    if ins.fn is None:
                        continue
                    bi = ins.fn(eng)
                    if ins.is_dma:
                        bi.then_inc(dsem[ins.dma_sem], 16)
                    elif ins.signal:
                        bi.then_inc(esem[(e, ins.epoch)], 1)

            @block.tensor
            def _(eng):
                run("pe", eng)

            @block.scalar
            def _(eng):
                run("act", eng)

            @block.vector
            def _(eng):
                run("dve", eng)

            @block.gpsimd
            def _(eng):
                run("pool", eng)

            @block.sync
            def _(eng):
                run("sp", eng)


class Cfg:
    def __init__(self, D=1024, S=8192, PLE=256):
        self.D, self.S, self.PLE = D, S, PLE
        self.KC = D // 128
        self.AW = D // 2
        self.H = self.AW // 64
        self.SGW = D // 2
        self.G = self.SGW // 64
        self.DFF = 4 * D
        self.FC = self.DFF // 128
        self.IPW = 3 * self.AW + self.H + 2 * self.SGW
        self.NB = S // 128
        self.J = self.NB // 8
        self.NO = 2 * self.J
        self.HH = self.H // 2
        self.PK = PLE // 128
        self.DH = max(1, D // 512)
        self.DW = min(D, 512)
        NO, J, NB = self.NO, self.J, self.NB
        self.groups = [list(range(i, min(i + 4, NO))) for i in range(0, NO, 4)]
        self.lo = [4 * j for j in range(J)] + [NB - 4 - 4 * j for j in reversed(range(J))]
        self.mtype = [0] * J + [1] * J

    def owned_blocks(self, r):
        J, NB = self.J, self.NB
        return [4 * j + r for j in range(J)] + [NB - 1 - 4 * j - r for j in reversed(range(J))]


class Arena:
    def __init__(self, ap, total):
        self.A = ap
        self.total = total
        self.top = 0

    def alloc(self, shape, dt):
        n = int(np.prod(shape[1:]))
        ne = n * (2 if dt == F32 else 1)
        ne = (ne + 15) // 16 * 16
        assert self.top + ne <= self.total, ("SBUF arena overflow", self.top, ne, self.total)
        v = self.A[:, self.top:self.top + (n * (2 if dt == F32 else 1))]
        self.top += ne
        if dt == F32:
            v = v.bitcast(F32)
        if len(shape) > 2:
            names = " ".join("a%d" % i for i in range(len(shape) - 1))
            kw = {"a%d" % i: int(shape[i + 1]) for i in range(len(shape) - 1)}
            v = v.rearrange("p (%s) -> p %s" % (names, names), **kw)
        if shape[0] < 128:
            v = v[0:shape[0]]
        return v


def _ap(t):
    return t.ap() if hasattr(t, "ap") else t[:]


def build_program(cfg, debug=False):
    c = cfg
    D, S, KC, AW, H, HH, SGW, G, NB, NO = c.D, c.S, c.KC, c.AW, c.H, c.HH, c.SGW, c.G, c.NB, c.NO
    DFF, FC, PLE, PK, DH, DW = c.DFF, c.FC, c.PLE, c.PK, c.DH, c.DW
    NG = len(c.groups)
    HP = H // 2
    HPP = HH // 2
    nc = bass.Bass("TRN2", target_bir_lowering=False)

    def din(name, shape):
        return nc.dram_tensor(name, list(shape), F32, kind="ExternalInput").ap()

    xfull = din("xfull", [S, D])
    xown = din("xown", [NO * 128, D])
    pown = din("pown", [NO * 128, PLE])
    w_in = din("w_in", [D, c.IPW])
    w_out = din("w_out", [D, D])
    w_ff1 = din("w_ff1", [D, DFF])
    w_ff2 = din("w_ff2", [DFF, D])
    ple_w = din("ple_w", [PLE, D])
    gate_w = din("gate_w", [D, D])
    gpm_d = din("gpm", [128, KC])
    gpf_d = din("gpf", [128, KC])
    gcat_d = din("gcat", [128, KC])
    gpmix_d = din("gpmix", [128, D])
    gpffn_d = din("gpffn", [128, D])
    gateb_d = din("gateb", [128, D])
    lng_d = din("lng", [128, SGW])
    lnb_d = din("lnb", [128, SGW])
    fbias_d = din("fbias", [128, H])
    sgwT_d = din("sgwT", [128, G, 128])
    sgbT_d = din("sgbT", [128, G])
    ident_d = din("ident", [128, 128])
    U_d = din("U", [128, 128])
    maskT_d = din("maskT", [128, 8, 128])
    sel_d = din("sel", [HH, HH, 128])
    LTfull_d = din("LTfull", [NB, NB])
    LTown_d = din("LTown", [NB, NO])
    out_d = nc.dram_tensor("out", [NO * 128, D], F32, kind="ExternalOutput").ap()
    h1_d = nc.dram_tensor("h1_scr", [NO * 128, D], F32).ap()
    h2_d = nc.dram_tensor("h2_scr", [NO * 128, D], F32).ap()
    dbg = {}

    total = (nc.sbuf_bytes_remaining - 2048) // 2
    total = total // 16 * 16
    arena_t = nc.alloc_sbuf_tensor("arena", [128, total], BF16)
    AR = Arena(_ap(arena_t), total)
    banks = [_ap(nc.alloc_psum_tensor("bank%d" % i, [128, 512], F32)) for i in range(8)]
    banksb = [b.bitcast(BF16) for b in banks]

    P = Prog(nc)
    rr = [0]

    def wq():
        return "sp"

    identf = AR.alloc([128, 128], F32)
    identb = AR.alloc([128, 128], BF16)
    Uf = AR.alloc([128, 128], F32)
    onesf = AR.alloc([128, 128], F32)
    maskb = AR.alloc([128, 8, 128], BF16)
    selb = AR.alloc([128, HH, 128], BF16)
    LTfull = AR.alloc([128, NB], F32)
    LTown = AR.alloc([128, NO], F32)
    gpm = AR.alloc([128, KC], F32)
    gpf = AR.alloc([128, KC], F32)
    gcat = AR.alloc([128, KC], F32)
    fbias = AR.alloc([128, H], F32)
    stats = AR.alloc([128, 64], F32)
    const_top = AR.top
    QT = AR.alloc([128, HP, NO * 128], BF16)
    within_own = AR.alloc([128, NO, H], F32)
    yatt = AR.alloc([128, NO, AW], F32)
    persist_top = AR.top

    def ld(dst, src, key, eng="sp"):
        return P.op(eng, lambda e: e.dma_start(out=dst, in_=src), w=[key], dma=True)

    ld(identf, ident_d, "identf")
    ld(Uf, U_d, "Uf")
    ld(LTfull[0:NB], LTfull_d, "LTfull")
    ld(LTown[0:NB], LTown_d, "LTown")
    ld(gpm, gpm_d, "gpm")
    ld(gpf, gpf_d, "gpf")
    ld(gcat, gcat_d, "gcat")
    ld(fbias, fbias_d, "fbias")
    P.op("pool", lambda e: e.dma_start(out=maskb, in_=maskT_d), w=["maskb"], dma=True)
    P.op("pool", lambda e: e.dma_start(out=selb[0:HH], in_=sel_d), w=["selb"], dma=True)
    P.op("dve", lambda e: e.tensor_copy(out=identb, in_=identf), r=["identf"], w=["identb"])
    P.op("dve", lambda e: e.memset(onesf, 1.0), w=["onesf"])

    scnt = [0]

    def stat_slot(n=1):
        s = scnt[0]
        scnt[0] = (scnt[0] + n) % 60
        if s + n > 60:
            s = 0
            scnt[0] = n
        return s

    def load_w(dst, src2d, key, kcn):
        for kc in range(kcn):
            P.op("pool", lambda e, kc=kc: e.dma_start(out=dst[:, kc, :], in_=src2d[kc * 128:(kc + 1) * 128, :]),
                 w=[key + str(kc)], dma=True)
        return [key + str(kc) for kc in range(kcn)]

    def rsqrt_ops(ss_ap, out_ap, n, scale, rkeys, wkeys):
        tmpslot = stat_slot(n)
        tmp = stats[:, tmpslot:tmpslot + n]
        tks = ["st%d" % (tmpslot + j) for j in range(n)]
        P.op("act", lambda e: e.activation(out=tmp, in_=ss_ap, func=AF.Ln, scale=scale, bias=EPS), r=rkeys, w=tks)
        P.op("act", lambda e: e.activation(out=out_ap, in_=tmp, func=AF.Exp, scale=-0.5), r=tks, w=wkeys)

    class Front:
        def __init__(self, nx=3, tpbanks=(0, 1)):
            self.xt = [AR.alloc([128, D], F32) for _ in range(nx)]
            self.xn = [AR.alloc([128, D], BF16) for _ in range(2)]
            self.i = 0
            self.tpb = tpbanks

        def run(self, rows_ap, gain, dst, dkey, norm=True):
            g = self.gen(rows_ap, gain, dst, dkey, norm)
            for _ in g:
                pass
            return self.last

        def gen(self, rows_ap, gain, dst, dkey, norm=True):
            i = self.i
            self.i += 1
            xt = self.xt[i % len(self.xt)]
            xk = "xt%d_%d" % (id(self) % 1000, i % len(self.xt))
            xn = self.xn[i % 2]
            nk = "xn%d_%d" % (id(self) % 1000, i % 2)
            tb = self.tpb[i % len(self.tpb)]
            tpv = banksb[tb][:, 0:KC * 128].rearrange("p (k t) -> p k t", t=128)
            tk = "bank%d" % tb
            self.last = (xt, xk)
            P.op("sp", lambda e: e.dma_start(out=xt, in_=rows_ap), w=[xk], dma=True)
            yield
            if norm:
                sl = stat_slot(2)
                ss = stats[:, sl:sl + 1]
                rs = stats[:, sl + 1:sl + 2]
                sk = "st%d" % sl
                rk = "st%d" % (sl + 1)
                P.op("act", lambda e: e.activation(out=xn, in_=xt, func=AF.Square, accum_out=ss), r=[xk], w=[nk, sk])
                yield
                rsqrt_ops(ss, rs, 1, 1.0 / D, [sk], [rk])
                yield
                P.op("dve", lambda e: e.tensor_scalar(out=xn, in0=xt, scalar1=rs, scalar2=None, op0=ALU.mult),
                     r=[xk, rk], w=[nk])
            else:
                P.op("dve", lambda e: e.tensor_copy(out=xn, in_=xt), r=[xk], w=[nk])
            yield
            for kc in range(KC):
                P.op("pe", lambda e, kc=kc: e.transpose(out=tpv[:, kc, :], in_=xn[:, kc * 128:(kc + 1) * 128], identity=identb),
                     r=[nk, "identb"], w=[tk])
            yield
            if gain is not None:
                gk = gain[1]
                gb = gain[0].unsqueeze(2).broadcast_to([128, KC, 128])
                P.op("dve", lambda e: e.tensor_tensor(out=dst, in0=tpv, in1=gb, op=ALU.mult), r=[tk, gk], w=[dkey])
            else:
                P.op("act", lambda e: e.activation(out=dst, in_=tpv, func=AF.Copy), r=[tk], w=[dkey])
            yield

    def interleave(gens, depth, period):
        it = iter(gens)
        active = []
        rounds = 0
        done = False
        while True:
            if not done and len(active) < depth and (rounds % period == 0 or not active):
                try:
                    active.append(next(it))
                except StopIteration:
                    done = True
            if not active:
                if done:
                    break
                continue
            for g in list(active):
                try:
                    next(g)
                except StopIteration:
                    active.remove(g)
            rounds += 1

    def softplus_neg(dst, src, n_keys_r, wkey, tmp):
        P.op("act", lambda e: e.activation(out=tmp, in_=src, func=AF.Exp, scale=-1.0), r=n_keys_r, w=[wkey + "_e"])
        P.op("act", lambda e: e.activation(out=dst, in_=tmp, func=AF.Ln, scale=1.0, bias=1.0), r=[wkey + "_e"], w=[wkey])

    qcol = 0
    kcol = AW
    vcol = 2 * AW
    fcol = 3 * AW
    ucol = 3 * AW + H
    vscol = ucol + SGW
    w_in_r = w_in

    mark = AR.top
    Wq = AR.alloc([128, KC, AW], BF16)
    Wfa = AR.alloc([128, KC, H], BF16)
    fr = Front(nx=3, tpbanks=(0, 1))
    aTq = [AR.alloc([128, KC, 512], BF16) for _ in range(2)]
    fown = AR.alloc([128, NO, H], F32)
    spo = AR.alloc([128, NO, H], F32)
    spo_e = AR.alloc([128, NO, H], F32)
    wqk = load_w(Wq, w_in_r[:, qcol:qcol + AW], "Wq", KC)
    wfk = load_w(Wfa, w_in_r[:, fcol:fcol + H], "Wfa", KC)
    for g, blocks in enumerate(c.groups):
        aT = aTq[g % 2]
        ak = "aTq%d" % (g % 2)
        N = len(blocks) * 128
        for ti, ob in enumerate(blocks):
            fr.run(xown[ob * 128:(ob + 1) * 128, :], (gpm, "gpm"), aT[:, :, ti * 128:(ti + 1) * 128], ak)
            for kc in range(KC):
                P.op("pe", lambda e, kc=kc, ti=ti, aT=aT: e.matmul(banks[4][:, 0:H], lhsT=aT[:, kc, ti * 128:(ti + 1) * 128], rhs=Wfa[:, kc, :],
                                                                   start=(kc == 0), stop=(kc == KC - 1)), r=[ak, wfk[kc]], w=["bank4"])
            P.op("dve", lambda e, ob=ob: e.tensor_tensor(out=fown[:, ob, :], in0=banks[4][:, 0:H], in1=fbias, op=ALU.add),
                 r=["bank4", "fbias"], w=["fown"])
        for hp in range(HP):
            qb = 2 + (hp % 2)
            for kc in range(KC):
                P.op("pe", lambda e, kc=kc, hp=hp, aT=aT, qb=qb, N=N: e.matmul(banks[qb][:, 0:N], lhsT=Wq[:, kc, hp * 128:(hp + 1) * 128], rhs=aT[:, kc, 0:N],
                                                                              start=(kc == 0), stop=(kc == KC - 1)), r=[ak, wqk[kc]], w=["bank%d" % qb])
            P.op("act", lambda e, hp=hp, qb=qb, g=g, N=N: e.activation(out=QT[:, hp, g * 512:g * 512 + N], in_=banks[qb][:, 0:N], func=AF.Copy, scale=0.125),
                 r=["bank%d" % qb], w=["QT"])
    softplus_neg(spo, fown, ["fown"], "spo", spo_e)
    P.op("pe", lambda e: e.matmul(banks[5][:, 0:NO * H], lhsT=Uf, rhs=spo.rearrange("p a b -> p (a b)"), start=True, stop=True),
         r=["Uf", "spo"], w=["bank5"])
    P.op("dve", lambda e: e.tensor_copy(out=within_own.rearrange("p a b -> p (a b)"), in_=banks[5][:, 0:NO * H]), r=["bank5"], w=["within_own"])
    P.barrier()
    AR.top = mark

    KT = AR.alloc([128, HPP, S], BF16)
    Vaug = AR.alloc([128, NB, HH, 65], BF16)
    biasT = AR.alloc([128, NG, NB, HH], F32)
    R8 = AR.alloc([128, NO * 128], BF16)
    attn_top = AR.top
    P.op("dve", lambda e: e.memset(Vaug[:, :, :, 64:65], 1.0), w=["Vones"])

    for hs in range(2):
        mark = AR.top
        Wk = AR.alloc([128, KC, HH * 64], BF16)
        Wv = AR.alloc([128, KC, HH * 64], BF16)
        Wf = AR.alloc([128, KC, HH], BF16)
        fr = Front(nx=3, tpbanks=(0, 1))
        aTa = [AR.alloc([128, KC, 512], BF16) for _ in range(2)]
        fsb = AR.alloc([128, NB, HH], F32)
        spf = AR.alloc([128, NB, HH], F32)
        spf_e = AR.alloc([128, NB, HH], F32)
        wsb = AR.alloc([128, NB, HH], F32)
        Cpos = AR.alloc([128, NB, HH], F32)
        totT = AR.alloc([128, HH], F32)
        rhs_full = AR.alloc([128, NB, HH], F32)
        rhs_own = AR.alloc([128, NO, HH], F32)
        pexo = AR.alloc([128, NO, HH], F32)
        rt1 = AR.alloc([128, NO, HH], F32)
        Rtok = AR.alloc([128, NO, HH], F32)
        wkk = load_w(Wk, w_in_r[:, kcol + hs * HH * 64: kcol + (hs + 1) * HH * 64], "Wk", KC)
        wvk = load_w(Wv, w_in_r[:, vcol + hs * HH * 64: vcol + (hs + 1) * HH * 64], "Wv", KC)
        wfk2 = load_w(Wf, w_in_r[:, fcol + hs * HH: fcol + (hs + 1) * HH], "Wf", KC)
        nst = (NB + 3) // 4

        def tileA(t, hs=hs, Wk=Wk, Wv=Wv, Wf=Wf, fsb=fsb, aTa=aTa, fr=fr, wkk=wkk, wvk=wvk, wfk2=wfk2):
            st, ti = t // 4, t % 4
            aT = aTa[st % 2]
            aks = ["aTa%d_%d" % (st % 2, j) for j in range(4)]
            ak = aks[ti]
            for _ in fr.gen(xfull[t * 128:(t + 1) * 128, :], (gpm, "gpm"), aT[:, :, ti * 128:(ti + 1) * 128], ak):
                yield
            vb = 2 + (t % 2)
            vk = "bank%d" % vb
            for kc in range(KC):
                P.op("pe", lambda e, kc=kc: e.matmul(banks[vb][:, 0:HH * 64], lhsT=aT[:, kc, ti * 128:(ti + 1) * 128], rhs=Wv[:, kc, :],
                                                     start=(kc == 0), stop=(kc == KC - 1)), r=[ak, wvk[kc]], w=[vk])
            fb_ = 4 if t % 2 == 0 else 7
            fk = "bank%d" % fb_
            for kc in range(KC):
                P.op("pe", lambda e, kc=kc: e.matmul(banks[fb_][:, 0:HH], lhsT=aT[:, kc, ti * 128:(ti + 1) * 128], rhs=Wf[:, kc, :],
                                                     start=(kc == 0), stop=(kc == KC - 1)), r=[ak, wfk2[kc]], w=[fk])
            yield
            P.op("act", lambda e: e.activation(out=Vaug[:, t, :, 0:64], in_=banks[vb][:, 0:HH * 64].rearrange("p (h d) -> p h d", d=64), func=AF.Copy),
                 r=[vk], w=["V%d" % t])
            yield
            P.op("dve", lambda e: e.tensor_tensor(out=fsb[:, t, :], in0=banks[fb_][:, 0:HH], in1=fbias[:, hs * HH:(hs + 1) * HH], op=ALU.add),
                 r=[fk, "fbias"], w=["fsb"])
            yield
            if ti == 3 or t == NB - 1:
                N = (ti + 1) * 128
                for hpl in range(HPP):
                    kb_ = 5 + (hpl % 2)
                    for kc in range(KC):
                        P.op("pe", lambda e, kc=kc, hpl=hpl, kb_=kb_: e.matmul(banks[kb_][:, 0:N], lhsT=Wk[:, kc, hpl * 128:(hpl + 1) * 128], rhs=aT[:, kc, 0:N],
                                                                               start=(kc == 0), stop=(kc == KC - 1)), r=aks[0:ti + 1] + [wkk[kc]], w=["bank%d" % kb_])
                    yield
                    P.op("act", lambda e, hpl=hpl, kb_=kb_: e.activation(out=KT[:, hpl, st * 512:st * 512 + N], in_=banks[kb_][:, 0:N], func=AF.Copy),
                         r=["bank%d" % kb_], w=["KT%d" % st])
                    yield

        interleave((tileA(t) for t in range(NB)), 3, 3)

        softplus_neg(spf, fsb, ["fsb"], "spf", spf_e)
        spf2 = spf.rearrange("p a b -> p (a b)")
        P.op("pe", lambda e: e.matmul(banks[0][:, 0:NB * HH], lhsT=Uf, rhs=spf2, start=True, stop=True), r=["Uf", "spf"], w=["bank0"])
        P.op("dve", lambda e: e.tensor_copy(out=wsb.rearrange("p a b -> p (a b)"), in_=banks[0][:, 0:NB * HH]), r=["bank0"], w=["wsb"])
        for hh in range(HH):
            P.op("pe", lambda e, hh=hh: e.matmul(banks[1][0:NB, hh:hh + 1], lhsT=spf[:, :, hh], rhs=onesf[:, 0:1], start=True, stop=True),
                 r=["spf", "onesf"], w=["bank1"])
        P.op("dve", lambda e: e.tensor_copy(out=totT[0:NB, :], in_=banks[1][0:NB, 0:HH]), r=["bank1"], w=["totT"])
        P.op("dve", lambda e: e.tensor_tensor(out=rhs_full[0:NB], in0=LTfull[0:NB].unsqueeze(2).broadcast_to([NB, NB, HH]),
                                              in1=totT[0:NB].unsqueeze(1).broadcast_to([NB, NB, HH]), op=ALU.mult), r=["LTfull", "totT"], w=["rhs_full"])
        P.op("dve", lambda e: e.tensor_tensor(out=rhs_own[0:NB], in0=LTown[0:NB].unsqueeze(2).broadcast_to([NB, NO, HH]),
                                              in1=totT[0:NB].unsqueeze(1).broadcast_to([NB, NO, HH]), op=ALU.mult), r=["LTown", "totT"], w=["rhs_own"])
        P.op("pe", lambda e: e.matmul(banks[2][:, 0:NB * HH], lhsT=onesf[0:NB, :], rhs=rhs_full[0:NB].rearrange("p a b -> p (a b)"), start=True, stop=True),
             r=["onesf", "rhs_full"], w=["bank2"])
        P.op("pe", lambda e: e.matmul(banks[3][:, 0:NO * HH], lhsT=onesf[0:NB, :], rhs=rhs_own[0:NB].rearrange("p a b -> p (a b)"), start=True, stop=True),
             r=["onesf", "rhs_own"], w=["bank3"])
        P.op("dve", lambda e: e.tensor_tensor(out=Cpos.rearrange("p a b -> p (a b)"), in0=banks[2][:, 0:NB * HH], in1=wsb.rearrange("p a b -> p (a b)"), op=ALU.add),
             r=["bank2", "wsb"], w=["Cpos"])
        P.op("dve", lambda e: e.tensor_copy(out=pexo.rearrange("p a b -> p (a b)"), in_=banks[3][:, 0:NO * HH]), r=["bank3"], w=["pexo"])
        for g, blocks in enumerate(c.groups):
            g0 = blocks[0]
            nb = len(blocks)
            P.op("dve", lambda e, g=g, g0=g0: e.tensor_tensor(out=biasT[:, g, :, :], in0=Cpos, in1=pexo[:, g0:g0 + 1, :].broadcast_to([128, NB, HH]), op=ALU.subtract),
                 r=["Cpos", "pexo"], w=["biasT"])
            P.op("dve", lambda e, g0=g0, nb=nb: e.tensor_tensor(out=rt1[:, g0:g0 + nb, :], in0=pexo[:, g0:g0 + 1, :].broadcast_to([128, nb, HH]), in1=pexo[:, g0:g0 + nb, :], op=ALU.subtract),
                 r=["pexo"], w=["rt1"])
            P.op("dve", lambda e, g0=g0, nb=nb, hs=hs: e.tensor_tensor(out=Rtok[:, g0:g0 + nb, :], in0=rt1[:, g0:g0 + nb, :], in1=within_own[:, g0:g0 + nb, hs * HH:(hs + 1) * HH], op=ALU.subtract),
                 r=["rt1", "within_own"], w=["Rtok"])
            for ti, ob in enumerate(blocks):
                P.op("pe", lambda e, ti=ti, ob=ob: e.matmul(banks[4][0:HH, ti * 128:(ti + 1) * 128], lhsT=Rtok[:, ob, :], rhs=identf, start=True, stop=True),
                     r=["Rtok", "identf"], w=["bank4"])
            P.op("dve", lambda e, g0=g0, nb=nb: e.tensor_copy(out=R8[0:HH, g0 * 128:(g0 + nb) * 128], in_=banks[4][0:HH, 0:nb * 128]), r=["bank4"], w=["R8"])

        mark2 = AR.top
        NPT = 6
        pts = [AR.alloc([128, 512], BF16) for _ in range(NPT)]
        osb = [AR.alloc([128, 512], F32) for _ in range(2)]
        rc = AR.alloc([128, 8], F32)
        kmaxs = [c.lo[blocks[-1]] + 3 for blocks in c.groups]
        batches = []
        for hh in range(HH):
            for kb in range(NB):
                act_g = [g for g in range(NG) if kb <= kmaxs[g]]
                for j in range(0, len(act_g), 2):
                    batches.append((hh, kb, act_g[j:j + 2]))
        epi = [0]

        def geom(g, kb):
            blocks = c.groups[g]
            fa = 0
            while c.lo[blocks[fa]] + 3 < kb:
                fa += 1
            N = (len(blocks) - fa) * 128
            c0 = blocks[fa] * 128
            msk = [(bi, kb - c.lo[ob]) for bi, ob in enumerate(blocks) if bi >= fa and c.lo[ob] <= kb <= c.lo[ob] + 3]
            return blocks, fa, N, c0, msk

        def emit_S(i):
            hh, kb, gs = batches[i]
            h = hs * HH + hh
            hpg, e_, hpl = h // 2, h % 2, hh // 2
            info = []
            for j, g in enumerate(gs):
                blocks, fa, N, c0, msk = geom(g, kb)
                sb = 2 * (i % 2) + j
                info.append((g, blocks, fa, N, c0, msk, sb))
            for (g, blocks, fa, N, c0, msk, sb) in info:
                P.op("pe", lambda e, N=N, c0=c0, sb=sb: e.matmul(banks[sb][:, 0:N], lhsT=KT[e_ * 64:(e_ + 1) * 64, hpl, kb * 128:(kb + 1) * 128],
                                                                rhs=QT[e_ * 64:(e_ + 1) * 64, hpg, c0:c0 + N], start=True, stop=False),
                     r=["KT%d" % (kb // 4), "QT"], w=["bank%d" % sb])
            for (g, blocks, fa, N, c0, msk, sb) in info:
                P.op("pe", lambda e, N=N, c0=c0, sb=sb, nm=len(msk): e.matmul(banks[sb][:, 0:N], lhsT=selb[0:HH, hh, :], rhs=R8[0:HH, c0:c0 + N], start=False, stop=(nm == 0)),
                     r=["selb", "R8"], w=["bank%d" % sb])
            for (g, blocks, fa, N, c0, msk, sb) in info:
                for mi, (bi, i4) in enumerate(msk):
                    mt = c.mtype[blocks[bi]] * 4 + i4
                    P.op("pe", lambda e, bi=bi, mt=mt, mi=mi, fa=fa, sb=sb, nm=len(msk): e.matmul(banks[sb][:, (bi - fa) * 128:(bi - fa + 1) * 128], lhsT=identb, rhs=maskb[:, mt, :],
                                                                                              start=False, stop=(mi == nm - 1)), r=["identb", "maskb"], w=["bank%d" % sb])
            for j, (g, blocks, fa, N, c0, msk, sb) in enumerate(info):
                pi = (2 * i + j) % NPT
                P.op("act", lambda e, N=N, sb=sb, g=g, pi=pi: e.activation(out=pts[pi][:, 0:N], in_=banks[sb][:, 0:N], func=AF.Exp, bias=biasT[:, g, kb, hh:hh + 1], scale=1.0),
                     r=["bank%d" % sb, "biasT"], w=["pt%d" % pi])

        def emit_PV(i):
            hh, kb, gs = batches[i]
            h = hs * HH + hh
            for j, g in enumerate(gs):
                blocks, fa, N, c0, msk = geom(g, kb)
                ob_ = 4 + g
                ok = "bank%d" % ob_
                pi = (2 * i + j) % NPT
                last = (kb == kmaxs[g])
                P.op("pe", lambda e, fa=fa, N=N, ob_=ob_, pi=pi, last=last: e.matmul(banks[ob_][0:65, fa * 128:fa * 128 + N], lhsT=Vaug[:, kb, hh, :], rhs=pts[pi][:, 0:N], start=(kb == 0), stop=last),
                     r=["V%d" % kb, "Vones", "pt%d" % pi], w=[ok])
            for j, g in enumerate(gs):
                if kb != kmaxs[g]:
                    continue
                blocks = c.groups[g]
                ob_ = 4 + g
                ok = "bank%d" % ob_
                nb = len(blocks)
                ei = epi[0] % 2
                epi[0] += 1
                os_ = osb[ei]
                osk = "osb%d" % ei
                tb = 2 * (i % 2)
                tk = "bank%d" % tb
                P.op("dve", lambda e, os_=os_, ob_=ob_, nb=nb: e.tensor_copy(out=os_[0:65, 0:nb * 128], in_=banks[ob_][0:65, 0:nb * 128]), r=[ok], w=[osk])
                for ti, ob in enumerate(blocks):
                    P.op("pe", lambda e, ti=ti, os_=os_, tb=tb: e.matmul(banks[tb][:, ti * 65:(ti + 1) * 65], lhsT=os_[0:65, ti * 128:(ti + 1) * 128], rhs=identf[0:65, 0:65], start=True, stop=True),
                         r=[osk, "identf"], w=[tk])
                o3 = banks[tb][:, 0:nb * 65].rearrange("p (b x) -> p b x", x=65)
                P.op("dve", lambda e, o3=o3, nb=nb: e.reciprocal(out=rc[:, 0:nb], in_=o3[:, :, 64]), r=[tk], w=["rc"])
                for ti, ob in enumerate(blocks):
                    P.op("dve", lambda e, ti=ti, ob=ob, o3=o3, h=h: e.tensor_scalar(out=yatt[:, ob, h * 64:(h + 1) * 64], in0=o3[:, ti, 0:64], scalar1=rc[:, ti:ti + 1], scalar2=None, op0=ALU.mult),
                         r=[tk, "rc"], w=["yatt"])

        for i in range(len(batches)):
            emit_S(i)
            if i >= 1:
                emit_PV(i - 1)
        emit_PV(len(batches) - 1)
        P.barrier()
        AR.top = mark

    AR.top = attn_top - 0
    AR.top = persist_top
    C0 = 0.7978845608028654
    C1_ = 0.044715
    Wu = AR.alloc([128, KC, SGW], BF16)
    Wvs = AR.alloc([128, KC, SGW], BF16)
    Wo = AR.alloc([128, KC, D], BF16)
    wsT = AR.alloc([128, G, 128], BF16)
    wsTf = AR.alloc([128, G, 128], F32)
    sgb = AR.alloc([128, G], F32)
    lng = AR.alloc([128, SGW], F32)
    lnb = AR.alloc([128, SGW], F32)
    gpmix = AR.alloc([128, D], F32)
    fr = Front(nx=3, tpbanks=(0, 1))
    aTc = [AR.alloc([128, KC, 128], BF16) for _ in range(2)]
    TN = ["x2", "tA", "tB", "gu", "gv", "xc", "vln", "tmix", "ysg"]
    TT = [{n: AR.alloc([128, SGW], F32) for n in TN} for _ in range(2)]
    for p_ in range(2):
        TT[p_]["vlnb"] = AR.alloc([128, SGW], BF16)
        TT[p_]["junk"] = AR.alloc([128, D], BF16)
        TT[p_]["yn"] = AR.alloc([128, D], BF16)
        TT[p_]["ynT"] = AR.alloc([128, KC, 128], BF16)
        TT[p_]["h1"] = AR.alloc([128, D], F32)
    wuk = load_w(Wu, w_in_r[:, ucol:ucol + SGW], "Wu", KC)
    wvsk = load_w(Wvs, w_in_r[:, vscol:vscol + SGW], "Wvs", KC)
    wok = load_w(Wo, w_out, "Wo", KC)
    ld(wsTf, sgwT_d, "wsTf")
    ld(sgb, sgbT_d, "sgb")
    ld(lng, lng_d, "lng")
    ld(lnb, lnb_d, "lnb")
    ld(gpmix, gpmix_d, "gpmix")
    P.op("dve", lambda e: e.tensor_tensor(out=wsT, in0=wsTf, in1=Uf.unsqueeze(1).broadcast_to([128, G, 128]), op=ALU.mult), r=["wsTf", "Uf"], w=["wsT"])

    def gelu2(T, p, dst, dk, ps, pk):
        x2, tA, tB = T["x2"], T["tA"], T["tB"]
        kx, ka, kb2 = "x2_%d" % p, "tA_%d" % p, "tB_%d" % p
        P.op("act", lambda e: e.activation(out=x2, in_=ps, func=AF.Square), r=[pk], w=[kx])
        yield
        P.op("dve", lambda e: e.tensor_scalar(out=tA, in0=x2, scalar1=C1_, scalar2=1.0, op0=ALU.mult, op1=ALU.add), r=[kx], w=[ka])
        yield
        P.op("dve", lambda e: e.tensor_tensor(out=tB, in0=tA, in1=ps, op=ALU.mult), r=[ka, pk], w=[kb2])
        yield
        P.op("act", lambda e: e.activation(out=tA, in_=tB, func=AF.Tanh, scale=C0), r=[kb2], w=[ka])
        yield
        P.op("dve", lambda e: e.scalar_tensor_tensor(out=dst, in0=tA, scalar=1.0, in1=ps, op0=ALU.add, op1=ALU.mult), r=[ka, pk], w=[dk])
        yield

    def tileC1(ob):
        p = ob % 2
        T = TT[p]
        aT = aTc[p]
        ak = "aTc%d" % p
        bX, bY = 2 + 3 * p, 3 + 3 * p
        bT = 4 + 3 * p
        kX, kY, kT = "bank%d" % bX, "bank%d" % bY, "bank%d" % bT
        K = lambda n: "%s_%d" % (n, p)
        gu, gv, xc, vln, vlnb, tmix, ysg, junk, yn, ynT, h1 = (T[n] for n in ("gu", "gv", "xc", "vln", "vlnb", "tmix", "ysg", "junk", "yn", "ynT", "h1"))
        for _ in fr.gen(xown[ob * 128:(ob + 1) * 128, :], (gpm, "gpm"), aT, ak):
            yield
        xt, xk = fr.last
        for kc in range(KC):
            P.op("pe", lambda e, kc=kc: e.matmul(banks[bX][:, 0:SGW], lhsT=aT[:, kc, :], rhs=Wu[:, kc, :], start=(kc == 0), stop=(kc == KC - 1)),
                 r=[ak, wuk[kc]], w=[kX])
        for kc in range(KC):
            P.op("pe", lambda e, kc=kc: e.matmul(banks[bY][:, 0:SGW], lhsT=aT[:, kc, :], rhs=Wvs[:, kc, :], start=(kc == 0), stop=(kc == KC - 1)),
                 r=[ak, wvsk[kc]], w=[kY])
        yield
        for _ in gelu2(T, p, gv, K("gv"), banks[bY][:, 0:SGW], kY):
            yield
        for _ in gelu2(T, p, gu, K("gu"), banks[bX][:, 0:SGW], kX):
            yield
        sl = stat_slot(6)
        s1 = stats[:, sl:sl + 1]
        nm = stats[:, sl + 1:sl + 2]
        s2 = stats[:, sl + 2:sl + 3]
        r2 = stats[:, sl + 3:sl + 4]
        k_ = ["st%d" % (sl + j) for j in range(6)]
        P.op("dve", lambda e: e.reduce_sum(out=s1, in_=gv, axis=AX.X), r=[K("gv")], w=[k_[0]])
        yield
        P.op("dve", lambda e: e.tensor_scalar(out=nm, in0=s1, scalar1=-0.5 / SGW, scalar2=None, op0=ALU.mult), r=[k_[0]], w=[k_[1]])
        yield
        P.op("dve", lambda e: e.tensor_scalar(out=xc, in0=gv, scalar1=0.5, scalar2=nm, op0=ALU.mult, op1=ALU.add), r=[K("gv"), k_[1]], w=[K("xc")])
        yield
        P.op("act", lambda e: e.activation(out=junk[:, 0:SGW], in_=xc, func=AF.Square, accum_out=s2), r=[K("xc")], w=[K("junk"), k_[2]])
        yield
        rsqrt_ops(s2, r2, 1, 1.0 / SGW, [k_[2]], [k_[3]])
        yield
        P.op("dve", lambda e: e.scalar_tensor_tensor(out=vln, in0=xc, scalar=r2, in1=lng, op0=ALU.mult, op1=ALU.mult), r=[K("xc"), k_[3], "lng"], w=[K("vln")])
        yield
        P.op("dve", lambda e: e.tensor_tensor(out=vlnb, in0=vln, in1=lnb, op=ALU.add), r=[K("vln"), "lnb"], w=[K("vlnb")])
        yield
        for g8 in range(G):
            P.op("pe", lambda e, g8=g8: e.matmul(banks[bY][:, g8 * 64:(g8 + 1) * 64], lhsT=wsT[:, g8, :], rhs=vlnb[:, g8 * 64:(g8 + 1) * 64], start=True, stop=True),
                 r=["wsT", K("vlnb")], w=[kY])
        yield
        P.op("dve", lambda e: e.tensor_tensor(out=tmix.rearrange("p (g d) -> p g d", d=64), in0=banks[bY][:, 0:SGW].rearrange("p (g d) -> p g d", d=64),
                                              in1=sgb.unsqueeze(2).broadcast_to([128, G, 64]), op=ALU.add), r=[kY, "sgb"], w=[K("tmix")])
        yield
        P.op("dve", lambda e: e.scalar_tensor_tensor(out=ysg, in0=gu, scalar=0.5, in1=tmix, op0=ALU.mult, op1=ALU.mult), r=[K("gu"), K("tmix")], w=[K("ysg")])
        yield
        sl2 = stat_slot(4)
        k2 = ["st%d" % (sl2 + j) for j in range(4)]
        ssq = stats[:, sl2:sl2 + 2]
        rsq = stats[:, sl2 + 2:sl2 + 4]
        P.op("act", lambda e: e.activation(out=junk[:, 0:AW], in_=yatt[:, ob, :], func=AF.Square, accum_out=ssq[:, 0:1]), r=["yatt"], w=[K("junk"), k2[0]])
        yield
        P.op("act", lambda e: e.activation(out=junk[:, 0:SGW], in_=ysg, func=AF.Square, accum_out=ssq[:, 1:2]), r=[K("ysg")], w=[K("junk"), k2[1]])
        yield
        rsqrt_ops(ssq, rsq, 2, 1.0 / AW, [k2[0], k2[1]], [k2[2], k2[3]])
        yield
        P.op("dve", lambda e: e.tensor_scalar(out=yn[:, 0:AW], in0=yatt[:, ob, :], scalar1=rsq[:, 0:1], scalar2=None, op0=ALU.mult), r=["yatt", k2[2]], w=[K("yn")])
        yield
        P.op("dve", lambda e: e.tensor_scalar(out=yn[:, AW:D], in0=ysg, scalar1=rsq[:, 1:2], scalar2=None, op0=ALU.mult), r=[K("ysg"), k2[3]], w=[K("yn")])
        yield
        tpv = banksb[bT][:, 0:KC * 128].rearrange("p (k t) -> p k t", t=128)
        for kc in range(KC):
            P.op("pe", lambda e, kc=kc: e.transpose(out=tpv[:, kc, :], in_=yn[:, kc * 128:(kc + 1) * 128], identity=identb), r=[K("yn"), "identb"], w=[kT])
        yield
        P.op("dve", lambda e: e.tensor_tensor(out=ynT, in0=tpv, in1=gcat.unsqueeze(2).broadcast_to([128, KC, 128]), op=ALU.mult), r=[kT, "gcat"], w=[K("ynT")])
        yield
        sl3 = stat_slot(4)
        k3 = ["st%d" % (sl3 + j) for j in range(4)]
        obanks = [bX, bT] if DH == 2 else [bX]
        for dh in range(DH):
            ob_ = obanks[dh]
            for kc in range(KC):
                P.op("pe", lambda e, kc=kc, dh=dh, ob_=ob_: e.matmul(banks[ob_][:, 0:DW], lhsT=ynT[:, kc, :], rhs=Wo[:, kc, dh * DW:(dh + 1) * DW], start=(kc == 0), stop=(kc == KC - 1)),
                     r=[K("ynT"), wok[kc]], w=["bank%d" % ob_])
            yield
            P.op("act", lambda e, dh=dh, ob_=ob_: e.activation(out=junk[:, 0:DW], in_=banks[ob_][:, 0:DW], func=AF.Square, accum_out=stats[:, sl3 + dh:sl3 + dh + 1]),
                 r=["bank%d" % ob_], w=[K("junk"), k3[dh]])
            yield
        if DH == 2:
            P.op("dve", lambda e: e.tensor_tensor(out=stats[:, sl3 + 2:sl3 + 3], in0=stats[:, sl3:sl3 + 1], in1=stats[:, sl3 + 1:sl3 + 2], op=ALU.add), r=[k3[0], k3[1]], w=[k3[2]])
            yield
            sso = stats[:, sl3 + 2:sl3 + 3]
            ssk = k3[2]
        else:
            sso = stats[:, sl3:sl3 + 1]
            ssk = k3[0]
        rso = stats[:, sl3 + 3:sl3 + 4]
        rsqrt_ops(sso, rso, 1, 1.0 / D, [ssk], [k3[3]])
        yield
        hk = K("h1")
        for dh in range(DH):
            ob_ = obanks[dh]
            P.op("dve", lambda e, dh=dh, ob_=ob_: e.scalar_tensor_tensor(out=h1[:, dh * DW:(dh + 1) * DW], in0=banks[ob_][:, 0:DW], scalar=rso, in1=gpmix[:, dh * DW:(dh + 1) * DW], op0=ALU.mult, op1=ALU.mult),
                 r=["bank%d" % ob_, k3[3], "gpmix"], w=[hk])
            yield
        P.op("dve", lambda e: e.tensor_tensor(out=h1, in0=h1, in1=xt, op=ALU.add), r=[hk, xk], w=[hk])
        yield
        P.op("sp", lambda e: e.dma_start(out=h1_d[ob * 128:(ob + 1) * 128, :], in_=h1), r=[hk], w=["h1d%d" % ob], dma=True)
        yield

    interleave((tileC1(ob) for ob in range(NO)), 2, 20)
    P.barrier()
    AR.top = const_top

    W1 = AR.alloc([128, KC, DFF], BF16)
    W2 = AR.alloc([128, FC, D], BF16)
    HT = AR.alloc([128, FC, 512], BF16)
    cT = AR.alloc([128, KC, 512], BF16)
    gpffn = AR.alloc([128, D], F32)
    fr = Front(nx=2, tpbanks=(0, 1))
    rtmp = [AR.alloc([128, 512], BF16) for _ in range(2)]
    h1r = [AR.alloc([128, D], F32) for _ in range(2)]
    o2t = [AR.alloc([128, D], F32) for _ in range(1)]
    junk2 = AR.alloc([128, 512], BF16)
    w1k = load_w(W1, w_ff1, "W1", KC)
    w2k = load_w(W2, w_ff2, "W2", FC)
    ld(gpffn, gpffn_d, "gpffn")
    tcount = 0
    for g, blocks in enumerate(c.groups):
        N = len(blocks) * 128
        for ti, ob in enumerate(blocks):
            fr.run(h1_d[ob * 128:(ob + 1) * 128, :], (gpf, "gpf"), cT[:, :, ti * 128:(ti + 1) * 128], "cT")
        for fc in range(FC):
            hb = 2 + fc % 2
            for kc in range(KC):
                P.op("pe", lambda e, kc=kc, fc=fc, hb=hb, N=N: e.matmul(banks[hb][:, 0:N], lhsT=W1[:, kc, fc * 128:(fc + 1) * 128], rhs=cT[:, kc, 0:N], start=(kc == 0), stop=(kc == KC - 1)),
                     r=["cT", w1k[kc]], w=["bank%d" % hb])
            rt = rtmp[fc % 2]
            rk = "rtmp%d" % (fc % 2)
            P.op("act", lambda e, hb=hb, rt=rt, N=N: e.activation(out=rt[:, 0:N], in_=banks[hb][:, 0:N], func=AF.Relu), r=["bank%d" % hb], w=[rk])
            P.op("dve", lambda e, fc=fc, rt=rt, N=N: e.tensor_tensor(out=HT[:, fc, 0:N], in0=rt[:, 0:N], in1=rt[:, 0:N], op=ALU.mult), r=[rk], w=["HT"])
        for ti, ob in enumerate(blocks):
            i2 = tcount % 2
            tcount += 1
            h1 = h1r[i2]
            hk = "h1r%d" % i2
            P.op("sp", lambda e, h1=h1, ob=ob: e.dma_start(out=h1, in_=h1_d[ob * 128:(ob + 1) * 128, :]), r=["h1d%d" % ob], w=[hk], dma=True)
            sl3 = stat_slot(4)
            k3 = ["st%d" % (sl3 + j) for j in range(4)]
            for dh in range(DH):
                ob_ = 4 + 2 * i2 + dh
                for fc in range(FC):
                    P.op("pe", lambda e, fc=fc, dh=dh, ob_=ob_, ti=ti: e.matmul(banks[ob_][:, 0:DW], lhsT=HT[:, fc, ti * 128:(ti + 1) * 128], rhs=W2[:, fc, dh * DW:(dh + 1) * DW], start=(fc == 0), stop=(fc == FC - 1)),
                         r=["HT", w2k[fc]], w=["bank%d" % ob_])
                P.op("act", lambda e, dh=dh, ob_=ob_, sl3=sl3: e.activation(out=junk2[:, 0:DW], in_=banks[ob_][:, 0:DW], func=AF.Square, accum_out=stats[:, sl3 + dh:sl3 + dh + 1]),
                     r=["bank%d" % ob_], w=["junk2", k3[dh]])
            if DH == 2:
                P.op("dve", lambda e, sl3=sl3: e.tensor_tensor(out=stats[:, sl3 + 2:sl3 + 3], in0=stats[:, sl3:sl3 + 1], in1=stats[:, sl3 + 1:sl3 + 2], op=ALU.add), r=[k3[0], k3[1]], w=[k3[2]])
                sso = stats[:, sl3 + 2:sl3 + 3]
                ssk = k3[2]
            else:
                sso = stats[:, sl3:sl3 + 1]
                ssk = k3[0]
            rso = stats[:, sl3 + 3:sl3 + 4]
            rsqrt_ops(sso, rso, 1, 1.0 / D, [ssk], [k3[3]])
            o2 = o2t[0]
            ok2 = "o2t0"
            for dh in range(DH):
                ob_ = 4 + 2 * i2 + dh
                P.op("dve", lambda e, dh=dh, ob_=ob_, rso=rso, o2=o2: e.scalar_tensor_tensor(out=o2[:, dh * DW:(dh + 1) * DW], in0=banks[ob_][:, 0:DW], scalar=rso, in1=gpffn[:, dh * DW:(dh + 1) * DW], op0=ALU.mult, op1=ALU.mult),
                     r=["bank%d" % ob_, k3[3], "gpffn"], w=[ok2])
            P.op("dve", lambda e, o2=o2, h1=h1: e.tensor_tensor(out=h1, in0=o2, in1=h1, op=ALU.add), r=[ok2, hk], w=[hk])
            P.op("sp", lambda e, h1=h1, ob=ob: e.dma_start(out=h2_d[ob * 128:(ob + 1) * 128, :], in_=h1), r=[hk], w=["h2d%d" % ob], dma=True)
    P.barrier()
    AR.top = const_top

    Wg = AR.alloc([128, KC, D], BF16)
    Wpe = AR.alloc([128, PK, D], BF16)
    gateb = AR.alloc([128, D], F32)
    fr = Front(nx=3, tpbanks=(0, 1))
    h2T = [AR.alloc([128, KC, 128], BF16) for _ in range(2)]
    pts_ = [AR.alloc([128, PLE], F32) for _ in range(2)]
    pbs = [AR.alloc([128, PLE], BF16) for _ in range(2)]
    pTs = [AR.alloc([128, PK, 128], BF16) for _ in range(2)]
    zt = AR.alloc([128, D], F32)
    gt = AR.alloc([128, D], F32)
    outs = [AR.alloc([128, D], F32) for _ in range(2)]
    wgk = load_w(Wg, gate_w, "Wg", KC)
    wpk = load_w(Wpe, ple_w, "Wpe", PK)
    ld(gateb, gateb_d, "gateb")
    out_dmas = []
    for ob in range(NO):
        i2 = ob % 2
        xt, xk = fr.run(h2_d[ob * 128:(ob + 1) * 128, :], None, h2T[i2], "h2T%d" % i2, norm=False)
        pt_, pb_, pT_ = pts_[i2], pbs[i2], pTs[i2]
        P.op("sp", lambda e, pt_=pt_, ob=ob: e.dma_start(out=pt_, in_=pown[ob * 128:(ob + 1) * 128, :]), w=["pt_%d" % i2], dma=True)
        P.op("dve", lambda e, pt_=pt_, pb_=pb_: e.tensor_copy(out=pb_, in_=pt_), r=["pt_%d" % i2], w=["pb%d" % i2])
        tpv3 = banksb[2][:, 0:PK * 128].rearrange("p (k t) -> p k t", t=128)
        for k2_ in range(PK):
            P.op("pe", lambda e, k2_=k2_, pb_=pb_, tpv3=tpv3: e.transpose(out=tpv3[:, k2_, :], in_=pb_[:, k2_ * 128:(k2_ + 1) * 128], identity=identb), r=["pb%d" % i2, "identb"], w=["bank2"])
        P.op("act", lambda e, pT_=pT_, tpv3=tpv3: e.activation(out=pT_, in_=tpv3, func=AF.Copy), r=["bank2"], w=["pT%d" % i2])
        for dh in range(DH):
            gb_ = 4 + dh
            pb2 = 6 + dh
            for kc in range(KC):
                P.op("pe", lambda e, kc=kc, dh=dh, gb_=gb_, i2=i2: e.matmul(banks[gb_][:, 0:DW], lhsT=h2T[i2][:, kc, :], rhs=Wg[:, kc, dh * DW:(dh + 1) * DW], start=(kc == 0), stop=(kc == KC - 1)),
                     r=["h2T%d" % i2, wgk[kc]], w=["bank%d" % gb_])
            for k2_ in range(PK):
                P.op("pe", lambda e, k2_=k2_, dh=dh, pb2=pb2, pT_=pT_: e.matmul(banks[pb2][:, 0:DW], lhsT=pT_[:, k2_, :], rhs=Wpe[:, k2_, dh * DW:(dh + 1) * DW], start=(k2_ == 0), stop=(k2_ == PK - 1)),
                     r=["pT%d" % i2, wpk[k2_]], w=["bank%d" % pb2])
            P.op("dve", lambda e, dh=dh, gb_=gb_: e.tensor_tensor(out=zt[:, dh * DW:(dh + 1) * DW], in0=banks[gb_][:, 0:DW], in1=gateb[:, dh * DW:(dh + 1) * DW], op=ALU.add),
                 r=["bank%d" % gb_, "gateb"], w=["zt"])
        P.op("act", lambda e: e.activation(out=gt, in_=zt, func=AF.Tanh, scale=0.5), r=["zt"], w=["gt"])
        P.op("dve", lambda e: e.tensor_scalar(out=gt, in0=gt, scalar1=0.5, scalar2=0.5, op0=ALU.mult, op1=ALU.add), r=["gt"], w=["gt"])
        o_ = outs[i2]
        okk = "outs%d" % i2
        for dh in range(DH):
            pb2 = 6 + dh
            P.op("dve", lambda e, dh=dh, pb2=pb2, o_=o_: e.tensor_tensor(out=o_[:, dh * DW:(dh + 1) * DW], in0=gt[:, dh * DW:(dh + 1) * DW], in1=banks[pb2][:, 0:DW], op=ALU.mult),
                 r=["gt", "bank%d" % pb2], w=[okk])
        P.op("dve", lambda e, o_=o_, xt=xt: e.tensor_tensor(out=o_, in0=o_, in1=xt, op=ALU.add), r=[okk, xk], w=[okk])
        out_dmas.append(P.op("sp", lambda e, o_=o_, ob=ob: e.dma_start(out=out_d[ob * 128:(ob + 1) * 128, :], in_=o_), r=[okk], dma=True))
    P.wait_all("sp", out_dmas)
    P.emit()
    return nc, P


def make_core_inputs(cfg, core, x, p, w_in, f_bias, sg_ln_g, sg_ln_b, sg_w, sg_b, att_out_g, sg_out_g,
                     w_out, pre_mix_g, post_mix_g, pre_ffn_g, post_ffn_g, w_ff1, w_ff2, ple_w, ple_gate_w, ple_gate_b):
    c = cfg
    b, r = core // 4, core % 4
    f32 = np.float32
    blocks = c.owned_blocks(r)
    rows = np.concatenate([np.arange(bl * 128, (bl + 1) * 128) for bl in blocks])

    def fm(v):
        return np.ascontiguousarray(np.asarray(v, f32).reshape(c.KC, 128).T)

    def rep(v):
        return np.ascontiguousarray(np.broadcast_to(np.asarray(v, f32).reshape(1, -1), (128, np.asarray(v).size)))

    k = np.arange(128)[:, None]
    q = np.arange(128)[None, :]
    tri = np.where(k > q, NEG, 0.0).astype(f32)
    full = np.full((128, 128), NEG, f32)
    zero = np.zeros((128, 128), f32)
    maskT = np.zeros((128, 8, 128), f32)
    for i in range(4):
        maskT[:, i, :] = zero if i < r else (tri if i == r else full)
        maskT[:, 4 + i, :] = zero if i < 3 - r else (tri if i == 3 - r else full)
    sel = np.zeros((c.HH, c.HH, 128), f32)
    for hh in range(c.HH):
        sel[hh, hh, :] = 1.0
    LTfull = (np.arange(c.NB)[:, None] < np.arange(c.NB)[None, :]).astype(f32)
    LTown = (np.arange(c.NB)[:, None] < np.asarray(blocks)[None, :]).astype(f32)
    xb = np.asarray(x[b], f32)
    return {
        "xfull": np.ascontiguousarray(xb),
        "xown": np.ascontiguousarray(xb[rows]),
        "pown": np.ascontiguousarray(np.asarray(p[0, b], f32)[rows]),
        "w_in": np.ascontiguousarray(np.asarray(w_in[0], f32)),
        "w_out": np.ascontiguousarray(np.asarray(w_out[0], f32)),
        "w_ff1": np.ascontiguousarray(np.asarray(w_ff1[0], f32)),
        "w_ff2": np.ascontiguousarray(np.asarray(w_ff2[0], f32)),
        "ple_w": np.ascontiguousarray(np.asarray(ple_w[0], f32)),
        "gate_w": np.ascontiguousarray(np.asarray(ple_gate_w[0], f32)),
        "gpm": fm(pre_mix_g[0]),
        "gpf": fm(pre_ffn_g[0]),
        "gcat": fm(np.concatenate([np.asarray(att_out_g[0]), np.asarray(sg_out_g[0])])),
        "gpmix": rep(post_mix_g[0]),
        "gpffn": rep(post_ffn_g[0]),
        "gateb": rep(ple_gate_b[0]),
        "lng": rep(sg_ln_g[0]),
        "lnb": rep(sg_ln_b[0]),
        "fbias": rep(f_bias[0]),
        "sgwT": np.ascontiguousarray(np.transpose(np.asarray(sg_w[0], f32), (2, 0, 1))),
        "sgbT": np.ascontiguousarray(np.asarray(sg_b[0], f32).T),
        "ident": np.eye(128, dtype=f32),
        "U": (np.arange(128)[:, None] <= np.arange(128)[None, :]).astype(f32),
        "maskT": maskT,
        "sel": sel,
        "LTfull": LTfull,
        "LTown": LTown,
    }, rows


_CACHE = {}


def kernel(**inputs):
    x = np.asarray(inputs["x"])
    B, S, D = x.shape
    PLE = np.asarray(inputs["p"]).shape[-1]
    cfg = Cfg(D=D, S=S, PLE=PLE)
    key = (D, S, PLE)
    if key not in _CACHE:
        _CACHE[key] = build_program(cfg)
    nc, _ = _CACHE[key]
    in_maps, rows_all = [], []
    for core in range(8):
        m, rows = make_core_inputs(cfg, core, **inputs)
        in_maps.append(m)
        rows_all.append(rows)
    res = run_bass_kernel_spmd(nc, in_maps, core_ids=list(range(8)))
    out = np.zeros((B, S, D), np.float32)
    for core in range(8):
        out[core // 4, rows_all[core], :] = np.asarray(res.results[core]["out"], np.float32)
    return out
```

```python
import numpy as np
import concourse.bass as bass
import concourse.mybir as mybir
from concourse.bass_utils import run_bass_kernel_spmd

F32 = mybir.dt.float32
BF16 = mybir.dt.bfloat16
AF = mybir.ActivationFunctionType
ALU = mybir.AluOpType
AX = mybir.AxisListType

EPS = 1e-6
NEG = -30000.0


class _Ins:
    __slots__ = ("eng", "idx", "fn", "deps", "signal", "is_dma", "dma_sem", "dma_val", "sig_val", "epoch")

    def __init__(self, eng, idx, fn, is_dma):
        self.eng = eng
        self.idx = idx
        self.fn = fn
        self.deps = set()
        self.signal = False
        self.is_dma = is_dma
        self.dma_sem = None
        self.dma_val = 0
        self.sig_val = 0
        self.epoch = 0


class Prog:
    ENGS = ("pe", "act", "dve", "pool", "sp")

    def __init__(self, nc, n_dma_sems=24):
        self.nc = nc
        self.q = {e: [] for e in self.ENGS}
        self.lastw = {}
        self.readers = {}
        self.n_dma_sems = n_dma_sems
        self.dma_count = 0
        self.dma_last = [None] * n_dma_sems
        self.dma_pools = {"sp": (0, n_dma_sems - 8), "act": (0, n_dma_sems - 8), "pool": (n_dma_sems - 8, 8)}
        self.dma_pool_cnt = {"sp": 0, "act": 0, "pool": 0}
        self.dma_sem_uses = [0] * n_dma_sems
        self.epoch = 0

    def op(self, eng, fn, r=(), w=(), dma=False):
        ins = _Ins(eng, len(self.q[eng]), fn, dma)
        ins.epoch = self.epoch
        deps = ins.deps
        if any(k.startswith("bank") for k in r):
            w = list(w) + [k for k in r if k.startswith("bank") and k not in w]
            r = [k for k in r if not k.startswith("bank")]
        for k in r:
            lw = self.lastw.get(k)
            if lw is not None:
                deps.add(lw)
        for k in w:
            lw = self.lastw.get(k)
            if lw is not None:
                deps.add(lw)
            rd = self.readers.get(k)
            if rd:
                for x in rd[0].values():
                    deps.add(x)
                for x in rd[1]:
                    deps.add(x)
        if dma:
            base, cnt = self.dma_pools[eng]
            pk = "sp" if eng in ("sp", "act") else "pool"
            s = base + self.dma_pool_cnt[pk] % cnt
            self.dma_pool_cnt[pk] += 1
            prev = self.dma_last[s]
            if prev is not None:
                deps.add(prev)
            self.dma_sem_uses[s] += 1
            ins.dma_sem = s
            ins.dma_val = 16 * self.dma_sem_uses[s]
            self.dma_last[s] = ins
            self.dma_count += 1
        deps.discard(ins)
        for k in w:
            self.lastw[k] = ins
            self.readers[k] = ({}, [])
        for k in r:
            rd = self.readers.setdefault(k, ({}, []))
            if dma:
                rd[1].append(ins)
            else:
                rd[0][eng] = ins
        self.q[eng].append(ins)
        return ins

    def barrier(self):
        lasts = []
        for e in self.ENGS:
            for ins in reversed(self.q[e]):
                if not ins.is_dma and ins.fn is not None:
                    lasts.append(ins)
                    break
        dmas = [d for d in self.dma_last if d is not None]
        for e in self.ENGS:
            ins = _Ins(e, len(self.q[e]), None, False)
            ins.epoch = self.epoch
            ins.deps = set(lasts) | set(dmas)
            self.q[e].append(ins)
        self.lastw.clear()
        self.readers.clear()
        self.epoch += 1

    def wait_all(self, eng, instrs):
        ins = _Ins(eng, len(self.q[eng]), None, False)
        ins.epoch = self.epoch
        ins.deps = set(instrs)
        self.q[eng].append(ins)

    def emit(self):
        nc = self.nc
        for e in self.ENGS:
            for ins in self.q[e]:
                for d in ins.deps:
                    if not d.is_dma:
                        d.signal = True
        counts = {}
        for e in self.ENGS:
            c = 0
            ep = 0
            mx = 0
            for ins in self.q[e]:
                if ins.epoch != ep:
                    ep = ins.epoch
                    c = 0
                if (not ins.is_dma) and ins.signal and ins.fn is not None:
                    c += 1
                    ins.sig_val = c
                    mx = max(mx, c)
            counts[e] = mx
        self.counts = counts
        nep = self.epoch + 1
        import contextlib

        with contextlib.ExitStack() as st:
            esem = {(e, ep): st.enter_context(nc.semaphore("s_%s%d" % (e, ep))) for e in self.ENGS for ep in range(nep)}
            dsem = [st.enter_context(nc.semaphore("s_dma%d" % i)) for i in range(self.n_dma_sems)]
            block = st.enter_context(nc.Block())

            def run(e, eng):
                known = {}
                for ins in self.q[e]:
                    waits = {}
                    for d in ins.deps:
                        if d.is_dma:
                            key = ("d", d.dma_sem)
                            val = d.dma_val
                        else:
                            if d.fn is None:
                                continue
                            if d.eng == e and not ins.is_dma:
                                if e == "pe" or ins.idx - d.idx >= 3:
                                    continue
                            key = ("e", (d.eng, d.epoch))
                            val = d.sig_val
                        if val > waits.get(key, 0):
                            waits[key] = val
                    for key, val in waits.items():
                        if known.get(key, 0) >= val:
                            continue
                        sem = dsem[key[1]] if key[0] == "d" else esem[key[1]]
                        eng.wait_ge(sem, val)
                        known[key] = val
                    if ins.fn is None:
                        continue
                    bi = ins.fn(eng)
                    if ins.is_dma:
                        bi.then_inc(dsem[ins.dma_sem], 16)
                    elif ins.signal:
                        bi.then_inc(esem[(e, ins.epoch)], 1)

            @block.tensor
            def _(eng):
                run("pe", eng)

            @block.scalar
            def _(eng):
                run("act", eng)

            @block.vector
            def _(eng):
                run("dve", eng)

            @block.gpsimd
            def _(eng):
                run("pool", eng)

            @block.sync
            def _(eng):
                run("sp", eng)


class Cfg:
    def __init__(self, D=1024, S=8192, PLE=256):
        self.D, self.S, self.PLE = D, S, PLE
        self.KC = D // 128
        self.AW = D // 2
        self.H = self.AW // 64
        self.SGW = D // 2
        self.G = self.SGW // 64
        self.DFF = 4 * D
        self.FC = self.DFF // 128
        self.IPW = 3 * self.AW + self.H + 2 * self.SGW
        self.NB = S // 128
        self.J = self.NB // 8
        self.NO = 2 * self.J
        self.HH = self.H // 2
        self.PK = PLE // 128
        self.DH = max(1, D // 512)
        self.DW = min(D, 512)
        NO, J, NB = self.NO, self.J, self.NB
        self.groups = [list(range(i, min(i + 4, NO))) for i in range(0, NO, 4)]
        self.lo = [4 * j for j in range(J)] + [NB - 4 - 4 * j for j in reversed(range(J))]
        self.mtype = [0] * J + [1] * J

    def owned_blocks(self, r):
        J, NB = self.J, self.NB
        return [4 * j + r for j in range(J)] + [NB - 1 - 4 * j - r for j in reversed(range(J))]


class Arena:
    def __init__(self, ap, total):
        self.A = ap
        self.total = total
        self.top = 0

    def alloc(self, shape, dt):
        n = int(np.prod(shape[1:]))
        ne = n * (2 if dt == F32 else 1)
        ne = (ne + 15) // 16 * 16
        assert self.top + ne <= self.total, ("SBUF arena overflow", self.top, ne, self.total)
        v = self.A[:, self.top:self.top + (n * (2 if dt == F32 else 1))]
        self.top += ne
        if dt == F32:
            v = v.bitcast(F32)
        if len(shape) > 2:
            names = " ".join("a%d" % i for i in range(len(shape) - 1))
            kw = {"a%d" % i: int(shape[i + 1]) for i in range(len(shape) - 1)}
            v = v.rearrange("p (%s) -> p %s" % (names, names), **kw)
        if shape[0] < 128:
            v = v[0:shape[0]]
        return v


def _ap(t):
    return t.ap() if hasattr(t, "ap") else t[:]


def build_program(cfg, debug=False):
    c = cfg
    D, S, KC, AW, H, HH, SGW, G, NB, NO = c.D, c.S, c.KC, c.AW, c.H, c.HH, c.SGW, c.G, c.NB, c.NO
    DFF, FC, PLE, PK, DH, DW = c.DFF, c.FC, c.PLE, c.PK, c.DH, c.DW
    NG = len(c.groups)
    HP = H // 2
    HPP = HH // 2
    nc = bass.Bass("TRN2", target_bir_lowering=False)

    def din(name, shape):
        return nc.dram_tensor(name, list(shape), F32, kind="ExternalInput").ap()

    xfull = din("xfull", [S, D])
    xown = din("xown", [NO * 128, D])
    pown = din("pown", [NO * 128, PLE])
    w_in = din("w_in", [D, c.IPW])
    w_out = din("w_out", [D, D])
    w_ff1 = din("w_ff1", [D, DFF])
    w_ff2 = din("w_ff2", [DFF, D])
    ple_w = din("ple_w", [PLE, D])
    gate_w = din("gate_w", [D, D])
    gpm_d = din("gpm", [128, KC])
    gpf_d = din("gpf", [128, KC])
    gcat_d = din("gcat", [128, KC])
    gpmix_d = din("gpmix", [128, D])
    gpffn_d = din("gpffn", [128, D])
    gateb_d = din("gateb", [128, D])
    lng_d = din("lng", [128, SGW])
    lnb_d = din("lnb", [128, SGW])
    fbias_d = din("fbias", [128, H])
    sgwT_d = din("sgwT", [128, G, 128])
    sgbT_d = din("sgbT", [128, G])
    ident_d = din("ident", [128, 128])
    U_d = din("U", [128, 128])
    maskT_d = din("maskT", [128, 8, 128])
    sel_d = din("sel", [HH, HH, 128])
    LTfull_d = din("LTfull", [NB, NB])
    LTown_d = din("LTown", [NB, NO])
    out_d = nc.dram_tensor("out", [NO * 128, D], F32, kind="ExternalOutput").ap()
    h1_d = nc.dram_tensor("h1_scr", [NO * 128, D], F32).ap()
    h2_d = nc.dram_tensor("h2_scr", [NO * 128, D], F32).ap()
    dbg = {}

    total = (nc.sbuf_bytes_remaining - 2048) // 2
    total = total // 16 * 16
    arena_t = nc.alloc_sbuf_tensor("arena", [128, total], BF16)
    AR = Arena(_ap(arena_t), total)
    banks = [_ap(nc.alloc_psum_tensor("bank%d" % i, [128, 512], F32)) for i in range(8)]
    banksb = [b.bitcast(BF16) for b in banks]

    P = Prog(nc)
    rr = [0]

    def wq():
        return "sp"

    identf = AR.alloc([128, 128], F32)
    identb = AR.alloc([128, 128], BF16)
    Uf = AR.alloc([128, 128], F32)
    onesf = AR.alloc([128, 128], F32)
    maskb = AR.alloc([128, 8, 128], BF16)
    selb = AR.alloc([128, HH, 128], BF16)
    LTfull = AR.alloc([128, NB], F32)
    LTown = AR.alloc([128, NO], F32)
    gpm = AR.alloc([128, KC], F32)
    gpf = AR.alloc([128, KC], F32)
    gcat = AR.alloc([128, KC], F32)
    fbias = AR.alloc([128, H], F32)
    stats = AR.alloc([128, 64], F32)
    const_top = AR.top
    QT = AR.alloc([128, HP, NO * 128], BF16)
    within_own = AR.alloc([128, NO, H], F32)
    yatt = AR.alloc([128, NO, AW], F32)
    persist_top = AR.top

    def ld(dst, src, key, eng="sp"):
        return P.op(eng, lambda e: e.dma_start(out=dst, in_=src), w=[key], dma=True)

    ld(identf, ident_d, "identf")
    ld(Uf, U_d, "Uf")
    ld(LTfull[0:NB], LTfull_d, "LTfull")
    ld(LTown[0:NB], LTown_d, "LTown")
    ld(gpm, gpm_d, "gpm")
    ld(gpf, gpf_d, "gpf")
    ld(gcat, gcat_d, "gcat")
    ld(fbias, fbias_d, "fbias")
    P.op("pool", lambda e: e.dma_start(out=maskb, in_=maskT_d), w=["maskb"], dma=True)
    P.op("pool", lambda e: e.dma_start(out=selb[0:HH], in_=sel_d), w=["selb"], dma=True)
    P.op("dve", lambda e: e.tensor_copy(out=identb, in_=identf), r=["identf"], w=["identb"])
    P.op("dve", lambda e: e.memset(onesf, 1.0), w=["onesf"])

    scnt = [0]

    def stat_slot(n=1):
        s = scnt[0]
        scnt[0] = (scnt[0] + n) % 60
        if s + n > 60:
            s = 0
            scnt[0] = n
        return s

    def load_w(dst, src2d, key, kcn):
        for kc in range(kcn):
            P.op("pool", lambda e, kc=kc: e.dma_start(out=dst[:, kc, :], in_=src2d[kc * 128:(kc + 1) * 128, :]),
                 w=[key + str(kc)], dma=True)
        return [key + str(kc) for kc in range(kcn)]

    def rsqrt_ops(ss_ap, out_ap, n, scale, rkeys, wkeys):
        tmpslot = stat_slot(n)
        tmp = stats[:, tmpslot:tmpslot + n]
        tks = ["st%d" % (tmpslot + j) for j in range(n)]
        P.op("act", lambda e: e.activation(out=tmp, in_=ss_ap, func=AF.Ln, scale=scale, bias=EPS), r=rkeys, w=tks)
        P.op("act", lambda e: e.activation(out=out_ap, in_=tmp, func=AF.Exp, scale=-0.5), r=tks, w=wkeys)

    class Front:
        def __init__(self, nx=3, tpbanks=(0, 1), nxn=2):
            self.xt = [AR.alloc([128, D], F32) for _ in range(nx)]
            self.xn = [AR.alloc([128, D], BF16) for _ in range(nxn)]
            self.i = 0
            self.tpb = tpbanks

        def run(self, rows_ap, gain, dst, dkey, norm=True):
            g = self.gen(rows_ap, gain, dst, dkey, norm)
            for _ in g:
                pass
            return self.last

        def gen(self, rows_ap, gain, dst, dkey, norm=True):
            i = self.i
            self.i += 1
            xt = self.xt[i % len(self.xt)]
            xk = "xt%d_%d" % (id(self) % 1000, i % len(self.xt))
            xn = self.xn[i % len(self.xn)]
            nk = "xn%d_%d" % (id(self) % 1000, i % len(self.xn))
            tb = self.tpb[i % len(self.tpb)]
            tpv = banksb[tb][:, 0:KC * 128].rearrange("p (k t) -> p k t", t=128)
            tk = "bank%d" % tb
            self.last = (xt, xk)
            P.op("sp", lambda e: e.dma_start(out=xt, in_=rows_ap), w=[xk], dma=True)
            yield
            if norm:
                sl = stat_slot(2)
                ss = stats[:, sl:sl + 1]
                rs = stats[:, sl + 1:sl + 2]
                sk = "st%d" % sl
                rk = "st%d" % (sl + 1)
                P.op("act", lambda e: e.activation(out=xn, in_=xt, func=AF.Square, accum_out=ss), r=[xk], w=[nk, sk])
                yield
                rsqrt_ops(ss, rs, 1, 1.0 / D, [sk], [rk])
                yield
                P.op("dve", lambda e: e.tensor_scalar(out=xn, in0=xt, scalar1=rs, scalar2=None, op0=ALU.mult),
                     r=[xk, rk], w=[nk])
            else:
                P.op("dve", lambda e: e.tensor_copy(out=xn, in_=xt), r=[xk], w=[nk])
            yield
            for kc in range(KC):
                P.op("pe", lambda e, kc=kc: e.transpose(out=tpv[:, kc, :], in_=xn[:, kc * 128:(kc + 1) * 128], identity=identb),
                     r=[nk, "identb"], w=[tk])
            yield
            if gain is not None:
                gk = gain[1]
                gb = gain[0].unsqueeze(2).broadcast_to([128, KC, 128])
                P.op("dve", lambda e: e.tensor_tensor(out=dst, in0=tpv, in1=gb, op=ALU.mult), r=[tk, gk], w=[dkey])
            else:
                P.op("act", lambda e: e.activation(out=dst, in_=tpv, func=AF.Copy), r=[tk], w=[dkey])
            yield

    def interleave(gens, depth, period):
        it = iter(gens)
        active = []
        rounds = 0
        done = False
        while True:
            if not done and len(active) < depth and (rounds % period == 0 or not active):
                try:
                    active.append(next(it))
                except StopIteration:
                    done = True
            if not active:
                if done:
                    break
                continue
            for g in list(active):
                try:
                    next(g)
                except StopIteration:
                    active.remove(g)
            rounds += 1

    def softplus_neg(dst, src, n_keys_r, wkey, tmp):
        P.op("act", lambda e: e.activation(out=tmp, in_=src, func=AF.Exp, scale=-1.0), r=n_keys_r, w=[wkey + "_e"])
        P.op("act", lambda e: e.activation(out=dst, in_=tmp, func=AF.Ln, scale=1.0, bias=1.0), r=[wkey + "_e"], w=[wkey])

    qcol = 0
    kcol = AW
    vcol = 2 * AW
    fcol = 3 * AW
    ucol = 3 * AW + H
    vscol = ucol + SGW
    w_in_r = w_in

    mark = AR.top
    Wq = AR.alloc([128, KC, AW], BF16)
    Wfa = AR.alloc([128, KC, H], BF16)
    fr = Front(nx=3, tpbanks=(0, 1))
    aTq = [AR.alloc([128, KC, 512], BF16) for _ in range(2)]
    fown = AR.alloc([128, NO, H], F32)
    spo = AR.alloc([128, NO, H], F32)
    spo_e = AR.alloc([128, NO, H], F32)
    wqk = load_w(Wq, w_in_r[:, qcol:qcol + AW], "Wq", KC)
    wfk = load_w(Wfa, w_in_r[:, fcol:fcol + H], "Wfa", KC)
    for g, blocks in enumerate(c.groups):
        aT = aTq[g % 2]
        ak = "aTq%d" % (g % 2)
        N = len(blocks) * 128
        for ti, ob in enumerate(blocks):
            fr.run(xown[ob * 128:(ob + 1) * 128, :], (gpm, "gpm"), aT[:, :, ti * 128:(ti + 1) * 128], ak)
            for kc in range(KC):
                P.op("pe", lambda e, kc=kc, ti=ti, aT=aT: e.matmul(banks[4][:, 0:H], lhsT=aT[:, kc, ti * 128:(ti + 1) * 128], rhs=Wfa[:, kc, :],
                                                                   start=(kc == 0), stop=(kc == KC - 1)), r=[ak, wfk[kc]], w=["bank4"])
            P.op("dve", lambda e, ob=ob: e.tensor_tensor(out=fown[:, ob, :], in0=banks[4][:, 0:H], in1=fbias, op=ALU.add),
                 r=["bank4", "fbias"], w=["fown"])
        for hp in range(HP):
            qb = 2 + (hp % 2)
            for kc in range(KC):
                P.op("pe", lambda e, kc=kc, hp=hp, aT=aT, qb=qb, N=N: e.matmul(banks[qb][:, 0:N], lhsT=Wq[:, kc, hp * 128:(hp + 1) * 128], rhs=aT[:, kc, 0:N],
                                                                              start=(kc == 0), stop=(kc == KC - 1)), r=[ak, wqk[kc]], w=["bank%d" % qb])
            P.op("act", lambda e, hp=hp, qb=qb, g=g, N=N: e.activation(out=QT[:, hp, g * 512:g * 512 + N], in_=banks[qb][:, 0:N], func=AF.Copy, scale=0.125),
                 r=["bank%d" % qb], w=["QT"])
    softplus_neg(spo, fown, ["fown"], "spo", spo_e)
    P.op("pe", lambda e: e.matmul(banks[5][:, 0:NO * H], lhsT=Uf, rhs=spo.rearrange("p a b -> p (a b)"), start=True, stop=True),
         r=["Uf", "spo"], w=["bank5"])
    P.op("dve", lambda e: e.tensor_copy(out=within_own.rearrange("p a b -> p (a b)"), in_=banks[5][:, 0:NO * H]), r=["bank5"], w=["within_own"])
    P.barrier()
    AR.top = mark

    KT = AR.alloc([128, HPP, S], BF16)
    Vaug = AR.alloc([128, NB, HH, 65], BF16)
    biasT = AR.alloc([128, NG, NB, HH], F32)
    R8 = AR.alloc([128, NO * 128], BF16)
    rbc = AR.alloc([128, HH, NO * 128], BF16)
    attn_top = AR.top
    P.op("dve", lambda e: e.memset(Vaug[:, :, :, 64:65], 1.0), w=["Vones"])

    for hs in range(2):
        mark = AR.top
        Wk = AR.alloc([128, KC, HH * 64], BF16)
        VF = HH * 64 + HH
        Wv = AR.alloc([128, KC, VF], BF16)
        fr = Front(nx=4, tpbanks=(0, 1, 4, 7), nxn=3)
        aTa = [AR.alloc([128, KC, 512], BF16) for _ in range(2)]
        fsb = AR.alloc([128, NB, HH], F32)
        spf = AR.alloc([128, NB, HH], F32)
        spf_e = AR.alloc([128, NB, HH], F32)
        wsb = AR.alloc([128, NB, HH], F32)
        Cpos = AR.alloc([128, NB, HH], F32)
        totT = AR.alloc([128, HH], F32)
        rhs_full = AR.alloc([128, NB, HH], F32)
        rhs_own = AR.alloc([128, NO, HH], F32)
        pexo = AR.alloc([128, NO, HH], F32)
        rt1 = AR.alloc([128, NO, HH], F32)
        Rtok = AR.alloc([128, NO, HH], F32)
        wkk = load_w(Wk, w_in_r[:, kcol + hs * HH * 64: kcol + (hs + 1) * HH * 64], "Wk", KC)
        wvk = []
        for kc in range(KC):
            P.op("pool", lambda e, kc=kc, hs=hs, Wv=Wv: e.dma_start(out=Wv[:, kc, 0:HH * 64], in_=w_in_r[kc * 128:(kc + 1) * 128, vcol + hs * HH * 64: vcol + (hs + 1) * HH * 64]),
                 w=["Wv%da" % kc], dma=True)
            P.op("pool", lambda e, kc=kc, hs=hs, Wv=Wv: e.dma_start(out=Wv[:, kc, HH * 64:VF], in_=w_in_r[kc * 128:(kc + 1) * 128, fcol + hs * HH: fcol + (hs + 1) * HH]),
                 w=["Wv%db" % kc], dma=True)
            wvk.append(["Wv%da" % kc, "Wv%db" % kc])
        nst = (NB + 3) // 4

        def tileA(t, hs=hs, Wk=Wk, Wv=Wv, fsb=fsb, aTa=aTa, fr=fr, wkk=wkk, wvk=wvk):
            st, ti = t // 4, t % 4
            aT = aTa[st % 2]
            aks = ["aTa%d_%d" % (st % 2, j) for j in range(4)]
            ak = aks[ti]
            for _ in fr.gen(xfull[t * 128:(t + 1) * 128, :], (gpm, "gpm"), aT[:, :, ti * 128:(ti + 1) * 128], ak):
                yield
            vb = 2 + (t % 2)
            vk = "bank%d" % vb
            for kc in range(KC):
                P.op("pe", lambda e, kc=kc: e.matmul(banks[vb][:, 0:VF], lhsT=aT[:, kc, ti * 128:(ti + 1) * 128], rhs=Wv[:, kc, :],
                                                     start=(kc == 0), stop=(kc == KC - 1)), r=[ak] + wvk[kc], w=[vk])
            yield
            P.op("act", lambda e: e.activation(out=Vaug[:, t, :, 0:64], in_=banks[vb][:, 0:HH * 64].rearrange("p (h d) -> p h d", d=64), func=AF.Copy),
                 r=[vk], w=["V%d" % t])
            yield
            P.op("act", lambda e: e.activation(out=fsb[:, t, :], in_=banks[vb][:, HH * 64:VF], func=AF.Copy), r=[vk], w=["fsb"])
            yield
            if ti == 3 or t == NB - 1:
                N = (ti + 1) * 128
                for hpl in range(HPP):
                    kb_ = 5 + (hpl % 2)
                    for kc in range(KC):
                        P.op("pe", lambda e, kc=kc, hpl=hpl, kb_=kb_: e.matmul(banks[kb_][:, 0:N], lhsT=Wk[:, kc, hpl * 128:(hpl + 1) * 128], rhs=aT[:, kc, 0:N],
                                                                               start=(kc == 0), stop=(kc == KC - 1)), r=aks[0:ti + 1] + [wkk[kc]], w=["bank%d" % kb_])
                    yield
                    P.op("act", lambda e, hpl=hpl, kb_=kb_: e.activation(out=KT[:, hpl, st * 512:st * 512 + N], in_=banks[kb_][:, 0:N], func=AF.Copy),
                         r=["bank%d" % kb_], w=["KT%d" % st])
                    yield

        interleave((tileA(t) for t in range(NB)), 4, 2)

        P.op("dve", lambda e, hs=hs, fsb=fsb: e.tensor_tensor(out=fsb, in0=fsb, in1=fbias[:, hs * HH:(hs + 1) * HH].unsqueeze(1).broadcast_to([128, NB, HH]), op=ALU.add),
             r=["fsb", "fbias"], w=["fsb"])
        softplus_neg(spf, fsb, ["fsb"], "spf", spf_e)
        spf2 = spf.rearrange("p a b -> p (a b)")
        P.op("pe", lambda e: e.matmul(banks[0][:, 0:NB * HH], lhsT=Uf, rhs=spf2, start=True, stop=True), r=["Uf", "spf"], w=["bank0"])
        P.op("dve", lambda e: e.tensor_copy(out=wsb.rearrange("p a b -> p (a b)"), in_=banks[0][:, 0:NB * HH]), r=["bank0"], w=["wsb"])
        for hh in range(HH):
            P.op("pe", lambda e, hh=hh: e.matmul(banks[1][0:NB, hh:hh + 1], lhsT=spf[:, :, hh], rhs=onesf[:, 0:1], start=True, stop=True),
                 r=["spf", "onesf"], w=["bank1"])
        P.op("dve", lambda e: e.tensor_copy(out=totT[0:NB, :], in_=banks[1][0:NB, 0:HH]), r=["bank1"], w=["totT"])
        P.op("dve", lambda e: e.tensor_tensor(out=rhs_full[0:NB], in0=LTfull[0:NB].unsqueeze(2).broadcast_to([NB, NB, HH]),
                                              in1=totT[0:NB].unsqueeze(1).broadcast_to([NB, NB, HH]), op=ALU.mult), r=["LTfull", "totT"], w=["rhs_full"])
        P.op("dve", lambda e: e.tensor_tensor(out=rhs_own[0:NB], in0=LTown[0:NB].unsqueeze(2).broadcast_to([NB, NO, HH]),
                                              in1=totT[0:NB].unsqueeze(1).broadcast_to([NB, NO, HH]), op=ALU.mult), r=["LTown", "totT"], w=["rhs_own"])
        P.op("pe", lambda e: e.matmul(banks[2][:, 0:NB * HH], lhsT=onesf[0:NB, :], rhs=rhs_full[0:NB].rearrange("p a b -> p (a b)"), start=True, stop=True),
             r=["onesf", "rhs_full"], w=["bank2"])
        P.op("pe", lambda e: e.matmul(banks[3][:, 0:NO * HH], lhsT=onesf[0:NB, :], rhs=rhs_own[0:NB].rearrange("p a b -> p (a b)"), start=True, stop=True),
             r=["onesf", "rhs_own"], w=["bank3"])
        P.op("dve", lambda e: e.tensor_tensor(out=Cpos.rearrange("p a b -> p (a b)"), in0=banks[2][:, 0:NB * HH], in1=wsb.rearrange("p a b -> p (a b)"), op=ALU.add),
             r=["bank2", "wsb"], w=["Cpos"])
        P.op("dve", lambda e: e.tensor_copy(out=pexo.rearrange("p a b -> p (a b)"), in_=banks[3][:, 0:NO * HH]), r=["bank3"], w=["pexo"])
        for g, blocks in enumerate(c.groups):
            g0 = blocks[0]
            nb = len(blocks)
            P.op("dve", lambda e, g=g, g0=g0: e.tensor_tensor(out=biasT[:, g, :, :], in0=Cpos, in1=pexo[:, g0:g0 + 1, :].broadcast_to([128, NB, HH]), op=ALU.subtract),
                 r=["Cpos", "pexo"], w=["biasT"])
            P.op("dve", lambda e, g0=g0, nb=nb: e.tensor_tensor(out=rt1[:, g0:g0 + nb, :], in0=pexo[:, g0:g0 + 1, :].broadcast_to([128, nb, HH]), in1=pexo[:, g0:g0 + nb, :], op=ALU.subtract),
                 r=["pexo"], w=["rt1"])
            P.op("dve", lambda e, g0=g0, nb=nb, hs=hs: e.tensor_tensor(out=Rtok[:, g0:g0 + nb, :], in0=rt1[:, g0:g0 + nb, :], in1=within_own[:, g0:g0 + nb, hs * HH:(hs + 1) * HH], op=ALU.subtract),
                 r=["rt1", "within_own"], w=["Rtok"])
            for ti, ob in enumerate(blocks):
                P.op("pe", lambda e, ti=ti, ob=ob: e.matmul(banks[4][0:HH, ti * 128:(ti + 1) * 128], lhsT=Rtok[:, ob, :], rhs=identf, start=True, stop=True),
                     r=["Rtok", "identf"], w=["bank4"])
            P.op("dve", lambda e, g0=g0, nb=nb: e.tensor_copy(out=R8[0:HH, g0 * 128:(g0 + nb) * 128], in_=banks[4][0:HH, 0:nb * 128]), r=["bank4"], w=["R8"])

        for hh in range(HH):
            for g, blocks in enumerate(c.groups):
                g0, nb = blocks[0], len(blocks)
                rb_ = 5 + ((hh * NG + g) % 2)
                P.op("pe", lambda e, hh=hh, g0=g0, nb=nb, rb_=rb_: e.matmul(banks[rb_][:, 0:nb * 128], lhsT=selb[0:HH, hh, :], rhs=R8[0:HH, g0 * 128:(g0 + nb) * 128], start=True, stop=True),
                     r=["selb", "R8"], w=["bank%d" % rb_])
                P.op("dve", lambda e, hh=hh, g0=g0, nb=nb, rb_=rb_: e.tensor_copy(out=rbc[:, hh, g0 * 128:(g0 + nb) * 128], in_=banks[rb_][:, 0:nb * 128]),
                     r=["bank%d" % rb_], w=["rbc"])

        mark2 = AR.top
        NPT = 6
        pts = [AR.alloc([128, 512], BF16) for _ in range(NPT)]
        osb = [AR.alloc([128, 512], F32) for _ in range(2)]
        rc = AR.alloc([128, 8], F32)
        kmaxs = [c.lo[blocks[-1]] + 3 for blocks in c.groups]
        batches = []
        for hh in range(HH):
            for kb in range(NB):
                act_g = [g for g in range(NG) if kb <= kmaxs[g]]
                for j in range(0, len(act_g), 2):
                    batches.append((hh, kb, act_g[j:j + 2]))
        epi = [0]

        def geom(g, kb):
            blocks = c.groups[g]
            fa = 0
            while c.lo[blocks[fa]] + 3 < kb:
                fa += 1
            N = (len(blocks) - fa) * 128
            c0 = blocks[fa] * 128
            msk = [(bi, kb - c.lo[ob]) for bi, ob in enumerate(blocks) if bi >= fa and c.lo[ob] <= kb <= c.lo[ob] + 3]
            return blocks, fa, N, c0, msk

        def emit_S(i):
            hh, kb, gs = batches[i]
            h = hs * HH + hh
            hpg, e_, hpl = h // 2, h % 2, hh // 2
            info = []
            for j, g in enumerate(gs):
                blocks, fa, N, c0, msk = geom(g, kb)
                sb = 2 * (i % 2) + j
                info.append((g, blocks, fa, N, c0, msk, sb))
            for (g, blocks, fa, N, c0, msk, sb) in info:
                P.op("pe", lambda e, N=N, c0=c0, sb=sb, nm=len(msk): e.matmul(banks[sb][:, 0:N], lhsT=KT[e_ * 64:(e_ + 1) * 64, hpl, kb * 128:(kb + 1) * 128],
                                                                rhs=QT[e_ * 64:(e_ + 1) * 64, hpg, c0:c0 + N], start=True, stop=(nm == 0)),
                     r=["KT%d" % (kb // 4), "QT"], w=["bank%d" % sb])
            for (g, blocks, fa, N, c0, msk, sb) in info:
                for mi, (bi, i4) in enumerate(msk):
                    mt = c.mtype[blocks[bi]] * 4 + i4
                    P.op("pe", lambda e, bi=bi, mt=mt, mi=mi, fa=fa, sb=sb, nm=len(msk): e.matmul(banks[sb][:, (bi - fa) * 128:(bi - fa + 1) * 128], lhsT=identb, rhs=maskb[:, mt, :],
                                                                                              start=False, stop=(mi == nm - 1)), r=["identb", "maskb"], w=["bank%d" % sb])
            for (g, blocks, fa, N, c0, msk, sb) in info:
                P.op("dve", lambda e, N=N, c0=c0, sb=sb: e.tensor_tensor(out=banks[sb][:, 0:N], in0=banks[sb][:, 0:N], in1=rbc[:, hh, c0:c0 + N], op=ALU.add),
                     r=["bank%d" % sb, "rbc"], w=["bank%d" % sb])
            for j, (g, blocks, fa, N, c0, msk, sb) in enumerate(info):
                pi = (2 * i + j) % NPT
                P.op("act", lambda e, N=N, sb=sb, g=g, pi=pi: e.activation(out=pts[pi][:, 0:N], in_=banks[sb][:, 0:N], func=AF.Exp, bias=biasT[:, g, kb, hh:hh + 1], scale=1.0),
                     r=["bank%d" % sb, "biasT"], w=["pt%d" % pi])

        def emit_PV(i):
            hh, kb, gs = batches[i]
            h = hs * HH + hh
            for j, g in enumerate(gs):
                blocks, fa, N, c0, msk = geom(g, kb)
                ob_ = 4 + g
                ok = "bank%d" % ob_
                pi = (2 * i + j) % NPT
                last = (kb == kmaxs[g])
                P.op("pe", lambda e, fa=fa, N=N, ob_=ob_, pi=pi, last=last: e.matmul(banks[ob_][0:65, fa * 128:fa * 128 + N], lhsT=Vaug[:, kb, hh, :], rhs=pts[pi][:, 0:N], start=(kb == 0), stop=last),
                     r=["V%d" % kb, "Vones", "pt%d" % pi], w=[ok])
            for j, g in enumerate(gs):
                if kb != kmaxs[g]:
                    continue
                blocks = c.groups[g]
                ob_ = 4 + g
                ok = "bank%d" % ob_
                nb = len(blocks)
                ei = epi[0] % 2
                epi[0] += 1
                os_ = osb[ei]
                osk = "osb%d" % ei
                tb = 2 * (i % 2)
                tk = "bank%d" % tb
                P.op("dve", lambda e, os_=os_, ob_=ob_, nb=nb: e.tensor_copy(out=os_[0:65, 0:nb * 128], in_=banks[ob_][0:65, 0:nb * 128]), r=[ok], w=[osk])
                for ti, ob in enumerate(blocks):
                    P.op("pe", lambda e, ti=ti, os_=os_, tb=tb: e.matmul(banks[tb][:, ti * 65:(ti + 1) * 65], lhsT=os_[0:65, ti * 128:(ti + 1) * 128], rhs=identf[0:65, 0:65], start=True, stop=True),
                         r=[osk, "identf"], w=[tk])
                o3 = banks[tb][:, 0:nb * 65].rearrange("p (b x) -> p b x", x=65)
                P.op("dve", lambda e, o3=o3, nb=nb: e.reciprocal(out=rc[:, 0:nb], in_=o3[:, :, 64]), r=[tk], w=["rc"])
                for ti, ob in enumerate(blocks):
                    P.op("dve", lambda e, ti=ti, ob=ob, o3=o3, h=h: e.tensor_scalar(out=yatt[:, ob, h * 64:(h + 1) * 64], in0=o3[:, ti, 0:64], scalar1=rc[:, ti:ti + 1], scalar2=None, op0=ALU.mult),
                         r=[tk, "rc"], w=["yatt"])

        for i in range(len(batches)):
            emit_S(i)
            if i >= 1:
                emit_PV(i - 1)
        emit_PV(len(batches) - 1)
        P.barrier()
        AR.top = mark

    AR.top = attn_top - 0
    AR.top = persist_top
    C0 = 0.7978845608028654
    C1_ = 0.044715
    Wu = AR.alloc([128, KC, SGW], BF16)
    Wvs = AR.alloc([128, KC, SGW], BF16)
    Wo = AR.alloc([128, KC, D], BF16)
    wsT = AR.alloc([128, G, 128], BF16)
    wsTf = AR.alloc([128, G, 128], F32)
    sgb = AR.alloc([128, G], F32)
    lng = AR.alloc([128, SGW], F32)
    lnb = AR.alloc([128, SGW], F32)
    gpmix = AR.alloc([128, D], F32)
    fr = Front(nx=3, tpbanks=(0, 1))
    aTc = [AR.alloc([128, KC, 128], BF16) for _ in range(2)]
    TN = ["x2", "tA", "tB", "gu", "gv", "xc", "vln", "tmix", "ysg"]
    TT = [{n: AR.alloc([128, SGW], F32) for n in TN} for _ in range(2)]
    for p_ in range(2):
        TT[p_]["vlnb"] = AR.alloc([128, SGW], BF16)
        TT[p_]["junk"] = AR.alloc([128, D], BF16)
        TT[p_]["yn"] = AR.alloc([128, D], BF16)
        TT[p_]["ynT"] = AR.alloc([128, KC, 128], BF16)
        TT[p_]["h1"] = AR.alloc([128, D], F32)
    wuk = load_w(Wu, w_in_r[:, ucol:ucol + SGW], "Wu", KC)
    wvsk = load_w(Wvs, w_in_r[:, vscol:vscol + SGW], "Wvs", KC)
    wok = load_w(Wo, w_out, "Wo", KC)
    ld(wsTf, sgwT_d, "wsTf")
    ld(sgb, sgbT_d, "sgb")
    ld(lng, lng_d, "lng")
    ld(lnb, lnb_d, "lnb")
    ld(gpmix, gpmix_d, "gpmix")
    P.op("dve", lambda e: e.tensor_tensor(out=wsT, in0=wsTf, in1=Uf.unsqueeze(1).broadcast_to([128, G, 128]), op=ALU.mult), r=["wsTf", "Uf"], w=["wsT"])

    def gelu2(T, p, dst, dk, ps, pk):
        x2, tA, tB = T["x2"], T["tA"], T["tB"]
        kx, ka, kb2 = "x2_%d" % p, "tA_%d" % p, "tB_%d" % p
        P.op("act", lambda e: e.activation(out=x2, in_=ps, func=AF.Square), r=[pk], w=[kx])
        yield
        P.op("dve", lambda e: e.tensor_scalar(out=tA, in0=x2, scalar1=C1_, scalar2=1.0, op0=ALU.mult, op1=ALU.add), r=[kx], w=[ka])
        yield
        P.op("dve", lambda e: e.tensor_tensor(out=tB, in0=tA, in1=ps, op=ALU.mult), r=[ka, pk], w=[kb2])
        yield
        P.op("act", lambda e: e.activation(out=tA, in_=tB, func=AF.Tanh, scale=C0), r=[kb2], w=[ka])
        yield
        P.op("dve", lambda e: e.scalar_tensor_tensor(out=dst, in0=tA, scalar=1.0, in1=ps, op0=ALU.add, op1=ALU.mult), r=[ka, pk], w=[dk])
        yield

    def tileC1(ob):
        p = ob % 2
        T = TT[p]
        aT = aTc[p]
        ak = "aTc%d" % p
        bX, bY = 2 + 3 * p, 3 + 3 * p
        bT = 4 + 3 * p
        kX, kY, kT = "bank%d" % bX, "bank%d" % bY, "bank%d" % bT
        K = lambda n: "%s_%d" % (n, p)
        gu, gv, xc, vln, vlnb, tmix, ysg, junk, yn, ynT, h1 = (T[n] for n in ("gu", "gv", "xc", "vln", "vlnb", "tmix", "ysg", "junk", "yn", "ynT", "h1"))
        for _ in fr.gen(xown[ob * 128:(ob + 1) * 128, :], (gpm, "gpm"), aT, ak):
            yield
        xt, xk = fr.last
        for kc in range(KC):
            P.op("pe", lambda e, kc=kc: e.matmul(banks[bX][:, 0:SGW], lhsT=aT[:, kc, :], rhs=Wu[:, kc, :], start=(kc == 0), stop=(kc == KC - 1)),
                 r=[ak, wuk[kc]], w=[kX])
        for kc in range(KC):
            P.op("pe", lambda e, kc=kc: e.matmul(banks[bY][:, 0:SGW], lhsT=aT[:, kc, :], rhs=Wvs[:, kc, :], start=(kc == 0), stop=(kc == KC - 1)),
                 r=[ak, wvsk[kc]], w=[kY])
        yield
        for _ in gelu2(T, p, gv, K("gv"), banks[bY][:, 0:SGW], kY):
            yield
        for _ in gelu2(T, p, gu, K("gu"), banks[bX][:, 0:SGW], kX):
            yield
        sl = stat_slot(6)
        s1 = stats[:, sl:sl + 1]
        nm = stats[:, sl + 1:sl + 2]
        s2 = stats[:, sl + 2:sl + 3]
        r2 = stats[:, sl + 3:sl + 4]
        k_ = ["st%d" % (sl + j) for j in range(6)]
        P.op("dve", lambda e: e.reduce_sum(out=s1, in_=gv, axis=AX.X), r=[K("gv")], w=[k_[0]])
        yield
        P.op("dve", lambda e: e.tensor_scalar(out=nm, in0=s1, scalar1=-0.5 / SGW, scalar2=None, op0=ALU.mult), r=[k_[0]], w=[k_[1]])
        yield
        P.op("dve", lambda e: e.tensor_scalar(out=xc, in0=gv, scalar1=0.5, scalar2=nm, op0=ALU.mult, op1=ALU.add), r=[K("gv"), k_[1]], w=[K("xc")])
        yield
        P.op("act", lambda e: e.activation(out=junk[:, 0:SGW], in_=xc, func=AF.Square, accum_out=s2), r=[K("xc")], w=[K("junk"), k_[2]])
        yield
        rsqrt_ops(s2, r2, 1, 1.0 / SGW, [k_[2]], [k_[3]])
        yield
        P.op("dve", lambda e: e.scalar_tensor_tensor(out=vln, in0=xc, scalar=r2, in1=lng, op0=ALU.mult, op1=ALU.mult), r=[K("xc"), k_[3], "lng"], w=[K("vln")])
        yield
        P.op("dve", lambda e: e.tensor_tensor(out=vlnb, in0=vln, in1=lnb, op=ALU.add), r=[K("vln"), "lnb"], w=[K("vlnb")])
        yield
        for g8 in range(G):
            P.op("pe", lambda e, g8=g8: e.matmul(banks[bY][:, g8 * 64:(g8 + 1) * 64], lhsT=wsT[:, g8, :], rhs=vlnb[:, g8 * 64:(g8 + 1) * 64], start=True, stop=True),
                 r=["wsT", K("vlnb")], w=[kY])
        yield
        P.op("dve", lambda e: e.tensor_tensor(out=tmix.rearrange("p (g d) -> p g d", d=64), in0=banks[bY][:, 0:SGW].rearrange("p (g d) -> p g d", d=64),
                                              in1=sgb.unsqueeze(2).broadcast_to([128, G, 64]), op=ALU.add), r=[kY, "sgb"], w=[K("tmix")])
        yield
        P.op("dve", lambda e: e.scalar_tensor_tensor(out=ysg, in0=gu, scalar=0.5, in1=tmix, op0=ALU.mult, op1=ALU.mult), r=[K("gu"), K("tmix")], w=[K("ysg")])
        yield
        sl2 = stat_slot(4)
        k2 = ["st%d" % (sl2 + j) for j in range(4)]
        ssq = stats[:, sl2:sl2 + 2]
        rsq = stats[:, sl2 + 2:sl2 + 4]
        P.op("act", lambda e: e.activation(out=junk[:, 0:AW], in_=yatt[:, ob, :], func=AF.Square, accum_out=ssq[:, 0:1]), r=["yatt"], w=[K("junk"), k2[0]])
        yield
        P.op("act", lambda e: e.activation(out=junk[:, 0:SGW], in_=ysg, func=AF.Square, accum_out=ssq[:, 1:2]), r=[K("ysg")], w=[K("junk"), k2[1]])
        yield
        rsqrt_ops(ssq, rsq, 2, 1.0 / AW, [k2[0], k2[1]], [k2[2], k2[3]])
        yield
        P.op("dve", lambda e: e.tensor_scalar(out=yn[:, 0:AW], in0=yatt[:, ob, :], scalar1=rsq[:, 0:1], scalar2=None, op0=ALU.mult), r=["yatt", k2[2]], w=[K("yn")])
        yield
        P.op("dve", lambda e: e.tensor_scalar(out=yn[:, AW:D], in0=ysg, scalar1=rsq[:, 1:2], scalar2=None, op0=ALU.mult), r=[K("ysg"), k2[3]], w=[K("yn")])
        yield
        tpv = banksb[bT][:, 0:KC * 128].rearrange("p (k t) -> p k t", t=128)
        for kc in range(KC):
            P.op("pe", lambda e, kc=kc: e.transpose(out=tpv[:, kc, :], in_=yn[:, kc * 128:(kc + 1) * 128], identity=identb), r=[K("yn"), "identb"], w=[kT])
        yield
        P.op("dve", lambda e: e.tensor_tensor(out=ynT, in0=tpv, in1=gcat.unsqueeze(2).broadcast_to([128, KC, 128]), op=ALU.mult), r=[kT, "gcat"], w=[K("ynT")])
        yield
        sl3 = stat_slot(4)
        k3 = ["st%d" % (sl3 + j) for j in range(4)]
        obanks = [bX, bT] if DH == 2 else [bX]
        for dh in range(DH):
            ob_ = obanks[dh]
            for kc in range(KC):
                P.op("pe", lambda e, kc=kc, dh=dh, ob_=ob_: e.matmul(banks[ob_][:, 0:DW], lhsT=ynT[:, kc, :], rhs=Wo[:, kc, dh * DW:(dh + 1) * DW], start=(kc == 0), stop=(kc == KC - 1)),
                     r=[K("ynT"), wok[kc]], w=["bank%d" % ob_])
            yield
            P.op("act", lambda e, dh=dh, ob_=ob_: e.activation(out=junk[:, 0:DW], in_=banks[ob_][:, 0:DW], func=AF.Square, accum_out=stats[:, sl3 + dh:sl3 + dh + 1]),
                 r=["bank%d" % ob_], w=[K("junk"), k3[dh]])
            yield
        if DH == 2:
            P.op("dve", lambda e: e.tensor_tensor(out=stats[:, sl3 + 2:sl3 + 3], in0=stats[:, sl3:sl3 + 1], in1=stats[:, sl3 + 1:sl3 + 2], op=ALU.add), r=[k3[0], k3[1]], w=[k3[2]])
            yield
            sso = stats[:, sl3 + 2:sl3 + 3]
            ssk = k3[2]
        else:
            sso = stats[:, sl3:sl3 + 1]
            ssk = k3[0]
        rso = stats[:, sl3 + 3:sl3 + 4]
        rsqrt_ops(sso, rso, 1, 1.0 / D, [ssk], [k3[3]])
        yield
        hk = K("h1")
        for dh in range(DH):
            ob_ = obanks[dh]
            P.op("dve", lambda e, dh=dh, ob_=ob_: e.scalar_tensor_tensor(out=h1[:, dh * DW:(dh + 1) * DW], in0=banks[ob_][:, 0:DW], scalar=rso, in1=gpmix[:, dh * DW:(dh + 1) * DW], op0=ALU.mult, op1=ALU.mult),
                 r=["bank%d" % ob_, k3[3], "gpmix"], w=[hk])
            yield
        P.op("dve", lambda e: e.tensor_tensor(out=h1, in0=h1, in1=xt, op=ALU.add), r=[hk, xk], w=[hk])
        yield
        P.op("sp", lambda e: e.dma_start(out=h1_d[ob * 128:(ob + 1) * 128, :], in_=h1), r=[hk], w=["h1d%d" % ob], dma=True)
        yield

    interleave((tileC1(ob) for ob in range(NO)), 2, 20)
    P.barrier()
    AR.top = const_top

    W1 = AR.alloc([128, KC, DFF], BF16)
    W2 = AR.alloc([128, FC, D], BF16)
    HT = AR.alloc([128, FC, 512], BF16)
    cT = AR.alloc([128, KC, 512], BF16)
    gpffn = AR.alloc([128, D], F32)
    fr = Front(nx=2, tpbanks=(0, 1))
    rtmp = [AR.alloc([128, 512], BF16) for _ in range(2)]
    h1r = [AR.alloc([128, D], F32) for _ in range(2)]
    o2t = [AR.alloc([128, D], F32) for _ in range(1)]
    junk2 = AR.alloc([128, 512], BF16)
    w1k = load_w(W1, w_ff1, "W1", KC)
    w2k = load_w(W2, w_ff2, "W2", FC)
    ld(gpffn, gpffn_d, "gpffn")
    tcount = 0
    for g, blocks in enumerate(c.groups):
        N = len(blocks) * 128
        for ti, ob in enumerate(blocks):
            fr.run(h1_d[ob * 128:(ob + 1) * 128, :], (gpf, "gpf"), cT[:, :, ti * 128:(ti + 1) * 128], "cT")
        for fc in range(FC):
            hb = 2 + fc % 2
            for kc in range(KC):
                P.op("pe", lambda e, kc=kc, fc=fc, hb=hb, N=N: e.matmul(banks[hb][:, 0:N], lhsT=W1[:, kc, fc * 128:(fc + 1) * 128], rhs=cT[:, kc, 0:N], start=(kc == 0), stop=(kc == KC - 1)),
                     r=["cT", w1k[kc]], w=["bank%d" % hb])
            rt = rtmp[fc % 2]
            rk = "rtmp%d" % (fc % 2)
            P.op("act", lambda e, hb=hb, rt=rt, N=N: e.activation(out=rt[:, 0:N], in_=banks[hb][:, 0:N], func=AF.Relu), r=["bank%d" % hb], w=[rk])
            P.op("dve", lambda e, fc=fc, rt=rt, N=N: e.tensor_tensor(out=HT[:, fc, 0:N], in0=rt[:, 0:N], in1=rt[:, 0:N], op=ALU.mult), r=[rk], w=["HT"])
        for ti, ob in enumerate(blocks):
            i2 = tcount % 2
            tcount += 1
            h1 = h1r[i2]
            hk = "h1r%d" % i2
            P.op("sp", lambda e, h1=h1, ob=ob: e.dma_start(out=h1, in_=h1_d[ob * 128:(ob + 1) * 128, :]), r=["h1d%d" % ob], w=[hk], dma=True)
            sl3 = stat_slot(4)
            k3 = ["st%d" % (sl3 + j) for j in range(4)]
            for dh in range(DH):
                ob_ = 4 + 2 * i2 + dh
                for fc in range(FC):
                    P.op("pe", lambda e, fc=fc, dh=dh, ob_=ob_, ti=ti: e.matmul(banks[ob_][:, 0:DW], lhsT=HT[:, fc, ti * 128:(ti + 1) * 128], rhs=W2[:, fc, dh * DW:(dh + 1) * DW], start=(fc == 0), stop=(fc == FC - 1)),
                         r=["HT", w2k[fc]], w=["bank%d" % ob_])
                P.op("act", lambda e, dh=dh, ob_=ob_, sl3=sl3: e.activation(out=junk2[:, 0:DW], in_=banks[ob_][:, 0:DW], func=AF.Square, accum_out=stats[:, sl3 + dh:sl3 + dh + 1]),
                     r=["bank%d" % ob_], w=["junk2", k3[dh]])
            if DH == 2:
                P.op("dve", lambda e, sl3=sl3: e.tensor_tensor(out=stats[:, sl3 + 2:sl3 + 3], in0=stats[:, sl3:sl3 + 1], in1=stats[:, sl3 + 1:sl3 + 2], op=ALU.add), r=[k3[0], k3[1]], w=[k3[2]])
                sso = stats[:, sl3 + 2:sl3 + 3]
                ssk = k3[2]
            else:
                sso = stats[:, sl3:sl3 + 1]
                ssk = k3[0]
            rso = stats[:, sl3 + 3:sl3 + 4]
            rsqrt_ops(sso, rso, 1, 1.0 / D, [ssk], [k3[3]])
            o2 = o2t[0]
            ok2 = "o2t0"
            for dh in range(DH):
                ob_ = 4 + 2 * i2 + dh
                P.op("dve", lambda e, dh=dh, ob_=ob_, rso=rso, o2=o2: e.scalar_tensor_tensor(out=o2[:, dh * DW:(dh + 1) * DW], in0=banks[ob_][:, 0:DW], scalar=rso, in1=gpffn[:, dh * DW:(dh + 1) * DW], op0=ALU.mult, op1=ALU.mult),
                     r=["bank%d" % ob_, k3[3], "gpffn"], w=[ok2])
            P.op("dve", lambda e, o2=o2, h1=h1: e.tensor_tensor(out=h1, in0=o2, in1=h1, op=ALU.add), r=[ok2, hk], w=[hk])
            P.op("sp", lambda e, h1=h1, ob=ob: e.dma_start(out=h2_d[ob * 128:(ob + 1) * 128, :], in_=h1), r=[hk], w=["h2d%d" % ob], dma=True)
    P.barrier()
    AR.top = const_top

    Wg = AR.alloc([128, KC, D], BF16)
    Wpe = AR.alloc([128, PK, D], BF16)
    gateb = AR.alloc([128, D], F32)
    fr = Front(nx=3, tpbanks=(0, 1))
    h2T = [AR.alloc([128, KC, 128], BF16) for _ in range(2)]
    pts_ = [AR.alloc([128, PLE], F32) for _ in range(2)]
    pbs = [AR.alloc([128, PLE], BF16) for _ in range(2)]
    pTs = [AR.alloc([128, PK, 128], BF16) for _ in range(2)]
    zt = AR.alloc([128, D], F32)
    gt = AR.alloc([128, D], F32)
    outs = [AR.alloc([128, D], F32) for _ in range(2)]
    wgk = load_w(Wg, gate_w, "Wg", KC)
    wpk = load_w(Wpe, ple_w, "Wpe", PK)
    ld(gateb, gateb_d, "gateb")
    out_dmas = []
    for ob in range(NO):
        i2 = ob % 2
        xt, xk = fr.run(h2_d[ob * 128:(ob + 1) * 128, :], None, h2T[i2], "h2T%d" % i2, norm=False)
        pt_, pb_, pT_ = pts_[i2], pbs[i2], pTs[i2]
        P.op("sp", lambda e, pt_=pt_, ob=ob: e.dma_start(out=pt_, in_=pown[ob * 128:(ob + 1) * 128, :]), w=["pt_%d" % i2], dma=True)
        P.op("dve", lambda e, pt_=pt_, pb_=pb_: e.tensor_copy(out=pb_, in_=pt_), r=["pt_%d" % i2], w=["pb%d" % i2])
        tpv3 = banksb[2][:, 0:PK * 128].rearrange("p (k t) -> p k t", t=128)
        for k2_ in range(PK):
            P.op("pe", lambda e, k2_=k2_, pb_=pb_, tpv3=tpv3: e.transpose(out=tpv3[:, k2_, :], in_=pb_[:, k2_ * 128:(k2_ + 1) * 128], identity=identb), r=["pb%d" % i2, "identb"], w=["bank2"])
        P.op("act", lambda e, pT_=pT_, tpv3=tpv3: e.activation(out=pT_, in_=tpv3, func=AF.Copy), r=["bank2"], w=["pT%d" % i2])
        for dh in range(DH):
            gb_ = 4 + dh
            pb2 = 6 + dh
            for kc in range(KC):
                P.op("pe", lambda e, kc=kc, dh=dh, gb_=gb_, i2=i2: e.matmul(banks[gb_][:, 0:DW], lhsT=h2T[i2][:, kc, :], rhs=Wg[:, kc, dh * DW:(dh + 1) * DW], start=(kc == 0), stop=(kc == KC - 1)),
                     r=["h2T%d" % i2, wgk[kc]], w=["bank%d" % gb_])
            for k2_ in range(PK):
                P.op("pe", lambda e, k2_=k2_, dh=dh, pb2=pb2, pT_=pT_: e.matmul(banks[pb2][:, 0:DW], lhsT=pT_[:, k2_, :], rhs=Wpe[:, k2_, dh * DW:(dh + 1) * DW], start=(k2_ == 0), stop=(k2_ == PK - 1)),
                     r=["pT%d" % i2, wpk[k2_]], w=["bank%d" % pb2])
            P.op("dve", lambda e, dh=dh, gb_=gb_: e.tensor_tensor(out=zt[:, dh * DW:(dh + 1) * DW], in0=banks[gb_][:, 0:DW], in1=gateb[:, dh * DW:(dh + 1) * DW], op=ALU.add),
                 r=["bank%d" % gb_, "gateb"], w=["zt"])
        P.op("act", lambda e: e.activation(out=gt, in_=zt, func=AF.Tanh, scale=0.5), r=["zt"], w=["gt"])
        P.op("dve", lambda e: e.tensor_scalar(out=gt, in0=gt, scalar1=0.5, scalar2=0.5, op0=ALU.mult, op1=ALU.add), r=["gt"], w=["gt"])
        o_ = outs[i2]
        okk = "outs%d" % i2
        for dh in range(DH):
            pb2 = 6 + dh
            P.op("dve", lambda e, dh=dh, pb2=pb2, o_=o_: e.tensor_tensor(out=o_[:, dh * DW:(dh + 1) * DW], in0=gt[:, dh * DW:(dh + 1) * DW], in1=banks[pb2][:, 0:DW], op=ALU.mult),
                 r=["gt", "bank%d" % pb2], w=[okk])
        P.op("dve", lambda e, o_=o_, xt=xt: e.tensor_tensor(out=o_, in0=o_, in1=xt, op=ALU.add), r=[okk, xk], w=[okk])
        out_dmas.append(P.op("sp", lambda e, o_=o_, ob=ob: e.dma_start(out=out_d[ob * 128:(ob + 1) * 128, :], in_=o_), r=[okk], dma=True))
    P.wait_all("sp", out_dmas)
    P.emit()
    return nc, P


def make_core_inputs(cfg, core, x, p, w_in, f_bias, sg_ln_g, sg_ln_b, sg_w, sg_b, att_out_g, sg_out_g,
                     w_out, pre_mix_g, post_mix_g, pre_ffn_g, post_ffn_g, w_ff1, w_ff2, ple_w, ple_gate_w, ple_gate_b):
    c = cfg
    b, r = core // 4, core % 4
    f32 = np.float32
    blocks = c.owned_blocks(r)
    rows = np.concatenate([np.arange(bl * 128, (bl + 1) * 128) for bl in blocks])

    def fm(v):
        return np.ascontiguousarray(np.asarray(v, f32).reshape(c.KC, 128).T)

    def rep(v):
        return np.ascontiguousarray(np.broadcast_to(np.asarray(v, f32).reshape(1, -1), (128, np.asarray(v).size)))

    k = np.arange(128)[:, None]
    q = np.arange(128)[None, :]
    tri = np.where(k > q, NEG, 0.0).astype(f32)
    full = np.full((128, 128), NEG, f32)
    zero = np.zeros((128, 128), f32)
    maskT = np.zeros((128, 8, 128), f32)
    for i in range(4):
        maskT[:, i, :] = zero if i < r else (tri if i == r else full)
        maskT[:, 4 + i, :] = zero if i < 3 - r else (tri if i == 3 - r else full)
    sel = np.zeros((c.HH, c.HH, 128), f32)
    for hh in range(c.HH):
        sel[hh, hh, :] = 1.0
    LTfull = (np.arange(c.NB)[:, None] < np.arange(c.NB)[None, :]).astype(f32)
    LTown = (np.arange(c.NB)[:, None] < np.asarray(blocks)[None, :]).astype(f32)
    xb = np.asarray(x[b], f32)
    return {
        "xfull": np.ascontiguousarray(xb),
        "xown": np.ascontiguousarray(xb[rows]),
        "pown": np.ascontiguousarray(np.asarray(p[0, b], f32)[rows]),
        "w_in": np.ascontiguousarray(np.asarray(w_in[0], f32)),
        "w_out": np.ascontiguousarray(np.asarray(w_out[0], f32)),
        "w_ff1": np.ascontiguousarray(np.asarray(w_ff1[0], f32)),
        "w_ff2": np.ascontiguousarray(np.asarray(w_ff2[0], f32)),
        "ple_w": np.ascontiguousarray(np.asarray(ple_w[0], f32)),
        "gate_w": np.ascontiguousarray(np.asarray(ple_gate_w[0], f32)),
        "gpm": fm(pre_mix_g[0]),
        "gpf": fm(pre_ffn_g[0]),
        "gcat": fm(np.concatenate([np.asarray(att_out_g[0]), np.asarray(sg_out_g[0])])),
        "gpmix": rep(post_mix_g[0]),
        "gpffn": rep(post_ffn_g[0]),
        "gateb": rep(ple_gate_b[0]),
        "lng": rep(sg_ln_g[0]),
        "lnb": rep(sg_ln_b[0]),
        "fbias": rep(f_bias[0]),
        "sgwT": np.ascontiguousarray(np.transpose(np.asarray(sg_w[0], f32), (2, 0, 1))),
        "sgbT": np.ascontiguousarray(np.asarray(sg_b[0], f32).T),
        "ident": np.eye(128, dtype=f32),
        "U": (np.arange(128)[:, None] <= np.arange(128)[None, :]).astype(f32),
        "maskT": maskT,
        "sel": sel,
        "LTfull": LTfull,
        "LTown": LTown,
    }, rows


_CACHE = {}


def kernel(**inputs):
    x = np.asarray(inputs["x"])
    B, S, D = x.shape
    PLE = np.asarray(inputs["p"]).shape[-1]
    cfg = Cfg(D=D, S=S, PLE=PLE)
    key = (D, S, PLE)
    if key not in _CACHE:
        _CACHE[key] = build_program(cfg)
    nc, _ = _CACHE[key]
    in_maps, rows_all = [], []
    for core in range(8):
        m, rows = make_core_inputs(cfg, core, **inputs)
        in_maps.append(m)
        rows_all.append(rows)
    res = run_bass_kernel_spmd(nc, in_maps, core_ids=list(range(8)))
    out = np.zeros((B, S, D), np.float32)
    for core in range(8):
        out[core // 4, rows_all[core], :] = np.asarray(res.results[core]["out"], np.float32)
    return out
```

```python
import numpy as np
import concourse.bass as bass
import concourse.mybir as mybir
from concourse.bass_utils import run_bass_kernel_spmd

F32 = mybir.dt.float32
BF16 = mybir.dt.bfloat16
AF = mybir.ActivationFunctionType
ALU = mybir.AluOpType
AX = mybir.AxisListType

EPS = 1e-6
NEG = -30000.0


class _Ins:
    __slots__ = ("eng", "idx", "fn", "deps", "signal", "is_dma", "dma_sem", "dma_val", "sig_val", "epoch")

    def __init__(self, eng, idx, fn, is_dma):
        self.eng = eng
        self.idx = idx
        self.fn = fn
        self.deps = set()
        self.signal = False
        self.is_dma = is_dma
        self.dma_sem = None
        self.dma_val = 0
        self.sig_val = 0
        self.epoch = 0


class Prog:
    ENGS = ("pe", "act", "dve", "pool", "sp")

    def __init__(self, nc, n_dma_sems=24):
        self.nc = nc
        self.q = {e: [] for e in self.ENGS}
        self.lastw = {}
        self.readers = {}
        self.n_dma_sems = n_dma_sems
        self.dma_count = 0
        self.dma_last = [None] * n_dma_sems
        self.dma_pools = {"sp": (0, n_dma_sems - 8), "act": (0, n_dma_sems - 8), "pool": (n_dma_sems - 8, 8)}
        self.dma_pool_cnt = {"sp": 0, "act": 0, "pool": 0}
        self.dma_sem_uses = [0] * n_dma_sems
        self.epoch = 0

    def op(self, eng, fn, r=(), w=(), dma=False):
        ins = _Ins(eng, len(self.q[eng]), fn, dma)
        ins.epoch = self.epoch
        deps = ins.deps
        if any(k.startswith("bank") for k in r):
            w = list(w) + [k for k in r if k.startswith("bank") and k not in w]
            r = [k for k in r if not k.startswith("bank")]
        for k in r:
            lw = self.lastw.get(k)
            if lw is not None:
                deps.add(lw)
        for k in w:
            lw = self.lastw.get(k)
            if lw is not None:
                deps.add(lw)
            rd = self.readers.get(k)
            if rd:
                for x in rd[0].values():
                    deps.add(x)
                for x in rd[1]:
                    deps.add(x)
        if dma:
            base, cnt = self.dma_pools[eng]
            pk = "sp" if eng in ("sp", "act") else "pool"
            s = base + self.dma_pool_cnt[pk] % cnt
            self.dma_pool_cnt[pk] += 1
            prev = self.dma_last[s]
            if prev is not None:
                deps.add(prev)
            self.dma_sem_uses[s] += 1
            ins.dma_sem = s
            ins.dma_val = 16 * self.dma_sem_uses[s]
            self.dma_last[s] = ins
            self.dma_count += 1
        deps.discard(ins)
        for k in w:
            self.lastw[k] = ins
            self.readers[k] = ({}, [])
        for k in r:
            rd = self.readers.setdefault(k, ({}, []))
            if dma:
                rd[1].append(ins)
            else:
                rd[0][eng] = ins
        self.q[eng].append(ins)
        return ins

    def barrier(self):
        lasts = []
        for e in self.ENGS:
            for ins in reversed(self.q[e]):
                if not ins.is_dma and ins.fn is not None:
                    lasts.append(ins)
                    break
        dmas = [d for d in self.dma_last if d is not None]
        for e in self.ENGS:
            ins = _Ins(e, len(self.q[e]), None, False)
            ins.epoch = self.epoch
            ins.deps = set(lasts) | set(dmas)
            self.q[e].append(ins)
        self.lastw.clear()
        self.readers.clear()
        self.epoch += 1

    def wait_all(self, eng, instrs):
        ins = _Ins(eng, len(self.q[eng]), None, False)
        ins.epoch = self.epoch
        ins.deps = set(instrs)
        self.q[eng].append(ins)

    def emit(self):
        nc = self.nc
        for e in self.ENGS:
            for ins in self.q[e]:
                for d in ins.deps:
                    if not d.is_dma:
                        d.signal = True
        counts = {}
        for e in self.ENGS:
            c = 0
            ep = 0
            mx = 0
            for ins in self.q[e]:
                if ins.epoch != ep:
                    ep = ins.epoch
                    c = 0
                if (not ins.is_dma) and ins.signal and ins.fn is not None:
                    c += 1
                    ins.sig_val = c
                    mx = max(mx, c)
            counts[e] = mx
        self.counts = counts
        nep = self.epoch + 1
        import contextlib

        with contextlib.ExitStack() as st:
            esem = {(e, ep): st.enter_context(nc.semaphore("s_%s%d" % (e, ep))) for e in self.ENGS for ep in range(nep)}
            dsem = [st.enter_context(nc.semaphore("s_dma%d" % i)) for i in range(self.n_dma_sems)]
            block = st.enter_context(nc.Block())

            def run(e, eng):
                known = {}
                for ins in self.q[e]:
                    waits = {}
                    for d in ins.deps:
                        if d.is_dma:
                            key = ("d", d.dma_sem)
                            val = d.dma_val
                        else:
                            if d.fn is None:
                                continue
                            if d.eng == e and not ins.is_dma:
                                if e == "pe" or ins.idx - d.idx >= 3:
                                    continue
                            key = ("e", (d.eng, d.epoch))
                            val = d.sig_val
                        if val > waits.get(key, 0):
                            waits[key] = val
                    for key, val in waits.items():
                        if known.get(key, 0) >= val:
                            continue
                        sem = dsem[key[1]] if key[0] == "d" else esem[key[1]]
                        eng.wait_ge(sem, val)
                        known[key] = val
                    if ins.fn is None:
                        continue
                    bi = ins.fn(eng)
                    if ins.is_dma:
                        bi.then_inc(dsem[ins.dma_sem], 16)
                    elif ins.signal:
                        bi.then_inc(esem[(e, ins.epoch)], 1)

            @block.tensor
            def _(eng):
                run("pe", eng)

            @block.scalar
            def _(eng):
                run("act", eng)

            @block.vector
            def _(eng):
                run("dve", eng)

            @block.gpsimd
            def _(eng):
                run("pool", eng)

            @block.sync
            def _(eng):
                run("sp", eng)


class Cfg:
    def __init__(self, D=1024, S=8192, PLE=256):
        self.D, self.S, self.PLE = D, S, PLE
        self.KC = D // 128
        self.AW = D // 2
        self.H = self.AW // 64
        self.SGW = D // 2
        self.G = self.SGW // 64
        self.DFF = 4 * D
        self.FC = self.DFF // 128
        self.IPW = 3 * self.AW + self.H + 2 * self.SGW
        self.NB = S // 128
        self.J = self.NB // 8
        self.NO = 2 * self.J
        self.HH = self.H // 2
        self.PK = PLE // 128
        self.DH = max(1, D // 512)
        self.DW = min(D, 512)
        NO, J, NB = self.NO, self.J, self.NB
        self.groups = [list(range(i, min(i + 4, NO))) for i in range(0, NO, 4)]
        self.lo = [4 * j for j in range(J)] + [NB - 4 - 4 * j for j in reversed(range(J))]
        self.mtype = [0] * J + [1] * J

    def owned_blocks(self, r):
        J, NB = self.J, self.NB
        return [4 * j + r for j in range(J)] + [NB - 1 - 4 * j - r for j in reversed(range(J))]


class Arena:
    def __init__(self, ap, total):
        self.A = ap
        self.total = total
        self.top = 0

    def alloc(self, shape, dt):
        n = int(np.prod(shape[1:]))
        ne = n * (2 if dt == F32 else 1)
        ne = (ne + 15) // 16 * 16
        assert self.top + ne <= self.total, ("SBUF arena overflow", self.top, ne, self.total)
        v = self.A[:, self.top:self.top + (n * (2 if dt == F32 else 1))]
        self.top += ne
        if dt == F32:
            v = v.bitcast(F32)
        if len(shape) > 2:
            names = " ".join("a%d" % i for i in range(len(shape) - 1))
            kw = {"a%d" % i: int(shape[i + 1]) for i in range(len(shape) - 1)}
            v = v.rearrange("p (%s) -> p %s" % (names, names), **kw)
        if shape[0] < 128:
            v = v[0:shape[0]]
        return v


def _ap(t):
    return t.ap() if hasattr(t, "ap") else t[:]


def build_program(cfg, debug=False):
    c = cfg
    D, S, KC, AW, H, HH, SGW, G, NB, NO = c.D, c.S, c.KC, c.AW, c.H, c.HH, c.SGW, c.G, c.NB, c.NO
    DFF, FC, PLE, PK, DH, DW = c.DFF, c.FC, c.PLE, c.PK, c.DH, c.DW
    NG = len(c.groups)
    HP = H // 2
    HPP = HH // 2
    nc = bass.Bass("TRN2", target_bir_lowering=False)

    def din(name, shape):
        return nc.dram_tensor(name, list(shape), F32, kind="ExternalInput").ap()

    xfull = din("xfull", [S, D])
    xown = din("xown", [NO * 128, D])
    pown = din("pown", [NO * 128, PLE])
    w_in = din("w_in", [D, c.IPW])
    w_out = din("w_out", [D, D])
    w_ff1 = din("w_ff1", [D, DFF])
    w_ff2 = din("w_ff2", [DFF, D])
    ple_w = din("ple_w", [PLE, D])
    gate_w = din("gate_w", [D, D])
    gpm_d = din("gpm", [128, KC])
    gpf_d = din("gpf", [128, KC])
    gcat_d = din("gcat", [128, KC])
    gpmix_d = din("gpmix", [128, D])
    gpffn_d = din("gpffn", [128, D])
    gateb_d = din("gateb", [128, D])
    lng_d = din("lng", [128, SGW])
    lnb_d = din("lnb", [128, SGW])
    fbias_d = din("fbias", [128, H])
    sgwT_d = din("sgwT", [128, G, 128])
    sgbT_d = din("sgbT", [128, G])
    ident_d = din("ident", [128, 128])
    U_d = din("U", [128, 128])
    maskT_d = din("maskT", [128, 8, 128])
    sel_d = din("sel", [HH, HH, 128])
    LTfull_d = din("LTfull", [NB, NB])
    LTown_d = din("LTown", [NB, NO])
    out_d = nc.dram_tensor("out", [NO * 128, D], F32, kind="ExternalOutput").ap()
    h1_d = nc.dram_tensor("h1_scr", [NO * 128, D], F32).ap()
    h2_d = nc.dram_tensor("h2_scr", [NO * 128, D], F32).ap()
    dbg = {}

    total = (nc.sbuf_bytes_remaining - 2048) // 2
    total = total // 16 * 16
    arena_t = nc.alloc_sbuf_tensor("arena", [128, total], BF16)
    AR = Arena(_ap(arena_t), total)
    banks = [_ap(nc.alloc_psum_tensor("bank%d" % i, [128, 512], F32)) for i in range(8)]
    banksb = [b.bitcast(BF16) for b in banks]

    P = Prog(nc)
    rr = [0]

    def wq():
        return "sp"

    identf = AR.alloc([128, 128], F32)
    identb = AR.alloc([128, 128], BF16)
    Uf = AR.alloc([128, 128], F32)
    onesf = AR.alloc([128, 128], F32)
    maskb = AR.alloc([128, 8, 128], BF16)
    selb = AR.alloc([128, HH, 128], BF16)
    LTfull = AR.alloc([128, NB], F32)
    LTown = AR.alloc([128, NO], F32)
    gpm = AR.alloc([128, KC], F32)
    gpf = AR.alloc([128, KC], F32)
    gcat = AR.alloc([128, KC], F32)
    fbias = AR.alloc([128, H], F32)
    stats = AR.alloc([128, 64], F32)
    const_top = AR.top
    QT = AR.alloc([128, HP, NO * 128], BF16)
    within_own = AR.alloc([128, NO, H], F32)
    yatt = AR.alloc([128, NO, AW], F32)
    persist_top = AR.top

    def ld(dst, src, key, eng="sp"):
        return P.op(eng, lambda e: e.dma_start(out=dst, in_=src), w=[key], dma=True)

    ld(identf, ident_d, "identf")
    ld(Uf, U_d, "Uf")
    ld(LTfull[0:NB], LTfull_d, "LTfull")
    ld(LTown[0:NB], LTown_d, "LTown")
    ld(gpm, gpm_d, "gpm")
    ld(gpf, gpf_d, "gpf")
    ld(gcat, gcat_d, "gcat")
    ld(fbias, fbias_d, "fbias")
    P.op("pool", lambda e: e.dma_start(out=maskb, in_=maskT_d), w=["maskb"], dma=True)
    P.op("pool", lambda e: e.dma_start(out=selb[0:HH], in_=sel_d), w=["selb"], dma=True)
    P.op("dve", lambda e: e.tensor_copy(out=identb, in_=identf), r=["identf"], w=["identb"])
    P.op("dve", lambda e: e.memset(onesf, 1.0), w=["onesf"])

    scnt = [0]

    def stat_slot(n=1):
        s = scnt[0]
        scnt[0] = (scnt[0] + n) % 60
        if s + n > 60:
            s = 0
            scnt[0] = n
        return s

    def load_w(dst, src2d, key, kcn):
        for kc in range(kcn):
            P.op("pool", lambda e, kc=kc: e.dma_start(out=dst[:, kc, :], in_=src2d[kc * 128:(kc + 1) * 128, :]),
                 w=[key + str(kc)], dma=True)
        return [key + str(kc) for kc in range(kcn)]

    def rsqrt_ops(ss_ap, out_ap, n, scale, rkeys, wkeys):
        tmpslot = stat_slot(n)
        tmp = stats[:, tmpslot:tmpslot + n]
        tks = ["st%d" % (tmpslot + j) for j in range(n)]
        P.op("act", lambda e: e.activation(out=tmp, in_=ss_ap, func=AF.Ln, scale=scale, bias=EPS), r=rkeys, w=tks)
        P.op("act", lambda e: e.activation(out=out_ap, in_=tmp, func=AF.Exp, scale=-0.5), r=tks, w=wkeys)

    class Front:
        def __init__(self, nx=3, tpbanks=(0, 1), nxn=2):
            self.xt = [AR.alloc([128, D], F32) for _ in range(nx)]
            self.xn = [AR.alloc([128, D], BF16) for _ in range(nxn)]
            self.i = 0
            self.tpb = tpbanks

        def run(self, rows_ap, gain, dst, dkey, norm=True):
            g = self.gen(rows_ap, gain, dst, dkey, norm)
            for _ in g:
                pass
            return self.last

        def gen(self, rows_ap, gain, dst, dkey, norm=True):
            i = self.i
            self.i += 1
            xt = self.xt[i % len(self.xt)]
            xk = "xt%d_%d" % (id(self) % 1000, i % len(self.xt))
            xn = self.xn[i % len(self.xn)]
            nk = "xn%d_%d" % (id(self) % 1000, i % len(self.xn))
            tb = self.tpb[i % len(self.tpb)]
            tpv = banksb[tb][:, 0:KC * 128].rearrange("p (k t) -> p k t", t=128)
            tk = "bank%d" % tb
            self.last = (xt, xk)
            P.op("sp", lambda e: e.dma_start(out=xt, in_=rows_ap), w=[xk], dma=True)
            yield
            if norm:
                sl = stat_slot(2)
                ss = stats[:, sl:sl + 1]
                rs = stats[:, sl + 1:sl + 2]
                sk = "st%d" % sl
                rk = "st%d" % (sl + 1)
                P.op("act", lambda e: e.activation(out=xn, in_=xt, func=AF.Square, accum_out=ss), r=[xk], w=[nk, sk])
                yield
                rsqrt_ops(ss, rs, 1, 1.0 / D, [sk], [rk])
                yield
                P.op("dve", lambda e: e.tensor_scalar(out=xn, in0=xt, scalar1=rs, scalar2=None, op0=ALU.mult),
                     r=[xk, rk], w=[nk])
            else:
                P.op("dve", lambda e: e.tensor_copy(out=xn, in_=xt), r=[xk], w=[nk])
            yield
            for kc in range(KC):
                P.op("pe", lambda e, kc=kc: e.transpose(out=tpv[:, kc, :], in_=xn[:, kc * 128:(kc + 1) * 128], identity=identb),
                     r=[nk, "identb"], w=[tk])
            yield
            if gain is not None:
                gk = gain[1]
                gb = gain[0].unsqueeze(2).broadcast_to([128, KC, 128])
                P.op("dve", lambda e: e.tensor_tensor(out=dst, in0=tpv, in1=gb, op=ALU.mult), r=[tk, gk], w=[dkey])
            else:
                P.op("act", lambda e: e.activation(out=dst, in_=tpv, func=AF.Copy), r=[tk], w=[dkey])
            yield

    def interleave(gens, depth, period):
        it = iter(gens)
        active = []
        rounds = 0
        done = False
        while True:
            if not done and len(active) < depth and (rounds % period == 0 or not active):
                try:
                    active.append(next(it))
                except StopIteration:
                    done = True
            if not active:
                if done:
                    break
                continue
            for g in list(active):
                try:
                    next(g)
                except StopIteration:
                    active.remove(g)
            rounds += 1

    def softplus_neg(dst, src, n_keys_r, wkey, tmp):
        P.op("act", lambda e: e.activation(out=tmp, in_=src, func=AF.Exp, scale=-1.0), r=n_keys_r, w=[wkey + "_e"])
        P.op("act", lambda e: e.activation(out=dst, in_=tmp, func=AF.Ln, scale=1.0, bias=1.0), r=[wkey + "_e"], w=[wkey])

    qcol = 0
    kcol = AW
    vcol = 2 * AW
    fcol = 3 * AW
    ucol = 3 * AW + H
    vscol = ucol + SGW
    w_in_r = w_in

    mark = AR.top
    Wq = AR.alloc([128, KC, AW], BF16)
    Wfa = AR.alloc([128, KC, H], BF16)
    fr = Front(nx=3, tpbanks=(0, 1))
    aTq = [AR.alloc([128, KC, 512], BF16) for _ in range(2)]
    fown = AR.alloc([128, NO, H], F32)
    spo = AR.alloc([128, NO, H], F32)
    spo_e = AR.alloc([128, NO, H], F32)
    wqk = load_w(Wq, w_in_r[:, qcol:qcol + AW], "Wq", KC)
    wfk = load_w(Wfa, w_in_r[:, fcol:fcol + H], "Wfa", KC)
    for g, blocks in enumerate(c.groups):
        aT = aTq[g % 2]
        ak = "aTq%d" % (g % 2)
        N = len(blocks) * 128
        for ti, ob in enumerate(blocks):
            fr.run(xown[ob * 128:(ob + 1) * 128, :], (gpm, "gpm"), aT[:, :, ti * 128:(ti + 1) * 128], ak)
            for kc in range(KC):
                P.op("pe", lambda e, kc=kc, ti=ti, aT=aT: e.matmul(banks[4][:, 0:H], lhsT=aT[:, kc, ti * 128:(ti + 1) * 128], rhs=Wfa[:, kc, :],
                                                                   start=(kc == 0), stop=(kc == KC - 1)), r=[ak, wfk[kc]], w=["bank4"])
            P.op("dve", lambda e, ob=ob: e.tensor_tensor(out=fown[:, ob, :], in0=banks[4][:, 0:H], in1=fbias, op=ALU.add),
                 r=["bank4", "fbias"], w=["fown"])
        for hp in range(HP):
            qb = 2 + (hp % 2)
            for kc in range(KC):
                P.op("pe", lambda e, kc=kc, hp=hp, aT=aT, qb=qb, N=N: e.matmul(banks[qb][:, 0:N], lhsT=Wq[:, kc, hp * 128:(hp + 1) * 128], rhs=aT[:, kc, 0:N],
                                                                              start=(kc == 0), stop=(kc == KC - 1)), r=[ak, wqk[kc]], w=["bank%d" % qb])
            P.op("act", lambda e, hp=hp, qb=qb, g=g, N=N: e.activation(out=QT[:, hp, g * 512:g * 512 + N], in_=banks[qb][:, 0:N], func=AF.Copy, scale=0.125),
                 r=["bank%d" % qb], w=["QT"])
    softplus_neg(spo, fown, ["fown"], "spo", spo_e)
    P.op("pe", lambda e: e.matmul(banks[5][:, 0:NO * H], lhsT=Uf, rhs=spo.rearrange("p a b -> p (a b)"), start=True, stop=True),
         r=["Uf", "spo"], w=["bank5"])
    P.op("dve", lambda e: e.tensor_copy(out=within_own.rearrange("p a b -> p (a b)"), in_=banks[5][:, 0:NO * H]), r=["bank5"], w=["within_own"])
    P.barrier()
    AR.top = mark

    KT = AR.alloc([128, HPP, S], BF16)
    Vaug = AR.alloc([128, NB, HH, 65], BF16)
    biasT = AR.alloc([128, NG, NB, HH], F32)
    R8 = AR.alloc([128, NO * 128], BF16)
    rbc = AR.alloc([128, HH, NO * 128], BF16)
    attn_top = AR.top
    P.op("dve", lambda e: e.memset(Vaug[:, :, :, 64:65], 1.0), w=["Vones"])

    for hs in range(2):
        mark = AR.top
        Wk = AR.alloc([128, KC, HH * 64], BF16)
        VF = HH * 64 + HH
        Wv = AR.alloc([128, KC, VF], BF16)
        fr = Front(nx=4, tpbanks=(0, 1, 4, 7), nxn=3)
        aTa = [AR.alloc([128, KC, 512], BF16) for _ in range(2)]
        fsb = AR.alloc([128, NB, HH], F32)
        spf = AR.alloc([128, NB, HH], F32)
        spf_e = AR.alloc([128, NB, HH], F32)
        wsb = AR.alloc([128, NB, HH], F32)
        Cpos = AR.alloc([128, NB, HH], F32)
        totT = AR.alloc([128, HH], F32)
        rhs_full = AR.alloc([128, NB, HH], F32)
        rhs_own = AR.alloc([128, NO, HH], F32)
        pexo = AR.alloc([128, NO, HH], F32)
        rt1 = AR.alloc([128, NO, HH], F32)
        Rtok = AR.alloc([128, NO, HH], F32)
        wkk = load_w(Wk, w_in_r[:, kcol + hs * HH * 64: kcol + (hs + 1) * HH * 64], "Wk", KC)
        wvk = []
        for kc in range(KC):
            P.op("pool", lambda e, kc=kc, hs=hs, Wv=Wv: e.dma_start(out=Wv[:, kc, 0:HH * 64], in_=w_in_r[kc * 128:(kc + 1) * 128, vcol + hs * HH * 64: vcol + (hs + 1) * HH * 64]),
                 w=["Wv%da" % kc], dma=True)
            P.op("pool", lambda e, kc=kc, hs=hs, Wv=Wv: e.dma_start(out=Wv[:, kc, HH * 64:VF], in_=w_in_r[kc * 128:(kc + 1) * 128, fcol + hs * HH: fcol + (hs + 1) * HH]),
                 w=["Wv%db" % kc], dma=True)
            wvk.append(["Wv%da" % kc, "Wv%db" % kc])
        nst = (NB + 3) // 4

        def tileA(t, hs=hs, Wk=Wk, Wv=Wv, fsb=fsb, aTa=aTa, fr=fr, wkk=wkk, wvk=wvk):
            st, ti = t // 4, t % 4
            aT = aTa[st % 2]
            aks = ["aTa%d_%d" % (st % 2, j) for j in range(4)]
            ak = aks[ti]
            for _ in fr.gen(xfull[t * 128:(t + 1) * 128, :], (gpm, "gpm"), aT[:, :, ti * 128:(ti + 1) * 128], ak):
                yield
            vb = 2 + (t % 2)
            vk = "bank%d" % vb
            for kc in range(KC):
                P.op("pe", lambda e, kc=kc: e.matmul(banks[vb][:, 0:VF], lhsT=aT[:, kc, ti * 128:(ti + 1) * 128], rhs=Wv[:, kc, :],
                                                     start=(kc == 0), stop=(kc == KC - 1)), r=[ak] + wvk[kc], w=[vk])
            yield
            P.op("act", lambda e: e.activation(out=Vaug[:, t, :, 0:64], in_=banks[vb][:, 0:HH * 64].rearrange("p (h d) -> p h d", d=64), func=AF.Copy),
                 r=[vk], w=["V%d" % t])
            yield
            P.op("act", lambda e: e.activation(out=fsb[:, t, :], in_=banks[vb][:, HH * 64:VF], func=AF.Copy), r=[vk], w=["fsb"])
            yield
            if ti == 3 or t == NB - 1:
                N = (ti + 1) * 128
                for hpl in range(HPP):
                    kb_ = 5 + (hpl % 2)
                    for kc in range(KC):
                        P.op("pe", lambda e, kc=kc, hpl=hpl, kb_=kb_: e.matmul(banks[kb_][:, 0:N], lhsT=Wk[:, kc, hpl * 128:(hpl + 1) * 128], rhs=aT[:, kc, 0:N],
                                                                               start=(kc == 0), stop=(kc == KC - 1)), r=aks[0:ti + 1] + [wkk[kc]], w=["bank%d" % kb_])
                    yield
                    P.op("act", lambda e, hpl=hpl, kb_=kb_: e.activation(out=KT[:, hpl, st * 512:st * 512 + N], in_=banks[kb_][:, 0:N], func=AF.Copy),
                         r=["bank%d" % kb_], w=["KT%d" % st])
                    yield

        interleave((tileA(t) for t in range(NB)), 4, 2)

        P.op("dve", lambda e, hs=hs, fsb=fsb: e.tensor_tensor(out=fsb, in0=fsb, in1=fbias[:, hs * HH:(hs + 1) * HH].unsqueeze(1).broadcast_to([128, NB, HH]), op=ALU.add),
             r=["fsb", "fbias"], w=["fsb"])
        softplus_neg(spf, fsb, ["fsb"], "spf", spf_e)
        spf2 = spf.rearrange("p a b -> p (a b)")
        P.op("pe", lambda e: e.matmul(banks[0][:, 0:NB * HH], lhsT=Uf, rhs=spf2, start=True, stop=True), r=["Uf", "spf"], w=["bank0"])
        P.op("dve", lambda e: e.tensor_copy(out=wsb.rearrange("p a b -> p (a b)"), in_=banks[0][:, 0:NB * HH]), r=["bank0"], w=["wsb"])
        for hh in range(HH):
            P.op("pe", lambda e, hh=hh: e.matmul(banks[1][0:NB, hh:hh + 1], lhsT=spf[:, :, hh], rhs=onesf[:, 0:1], start=True, stop=True),
                 r=["spf", "onesf"], w=["bank1"])
        P.op("dve", lambda e: e.tensor_copy(out=totT[0:NB, :], in_=banks[1][0:NB, 0:HH]), r=["bank1"], w=["totT"])
        P.op("dve", lambda e: e.tensor_tensor(out=rhs_full[0:NB], in0=LTfull[0:NB].unsqueeze(2).broadcast_to([NB, NB, HH]),
                                              in1=totT[0:NB].unsqueeze(1).broadcast_to([NB, NB, HH]), op=ALU.mult), r=["LTfull", "totT"], w=["rhs_full"])
        P.op("dve", lambda e: e.tensor_tensor(out=rhs_own[0:NB], in0=LTown[0:NB].unsqueeze(2).broadcast_to([NB, NO, HH]),
                                              in1=totT[0:NB].unsqueeze(1).broadcast_to([NB, NO, HH]), op=ALU.mult), r=["LTown", "totT"], w=["rhs_own"])
        P.op("pe", lambda e: e.matmul(banks[2][:, 0:NB * HH], lhsT=onesf[0:NB, :], rhs=rhs_full[0:NB].rearrange("p a b -> p (a b)"), start=True, stop=True),
             r=["onesf", "rhs_full"], w=["bank2"])
        P.op("pe", lambda e: e.matmul(banks[3][:, 0:NO * HH], lhsT=onesf[0:NB, :], rhs=rhs_own[0:NB].rearrange("p a b -> p (a b)"), start=True, stop=True),
             r=["onesf", "rhs_own"], w=["bank3"])
        P.op("dve", lambda e: e.tensor_tensor(out=Cpos.rearrange("p a b -> p (a b)"), in0=banks[2][:, 0:NB * HH], in1=wsb.rearrange("p a b -> p (a b)"), op=ALU.add),
             r=["bank2", "wsb"], w=["Cpos"])
        P.op("dve", lambda e: e.tensor_copy(out=pexo.rearrange("p a b -> p (a b)"), in_=banks[3][:, 0:NO * HH]), r=["bank3"], w=["pexo"])
        for g, blocks in enumerate(c.groups):
            g0 = blocks[0]
            nb = len(blocks)
            P.op("dve", lambda e, g=g, g0=g0: e.tensor_tensor(out=biasT[:, g, :, :], in0=Cpos, in1=pexo[:, g0:g0 + 1, :].broadcast_to([128, NB, HH]), op=ALU.subtract),
                 r=["Cpos", "pexo"], w=["biasT"])
            P.op("dve", lambda e, g0=g0, nb=nb: e.tensor_tensor(out=rt1[:, g0:g0 + nb, :], in0=pexo[:, g0:g0 + 1, :].broadcast_to([128, nb, HH]), in1=pexo[:, g0:g0 + nb, :], op=ALU.subtract),
                 r=["pexo"], w=["rt1"])
            P.op("dve", lambda e, g0=g0, nb=nb, hs=hs: e.tensor_tensor(out=Rtok[:, g0:g0 + nb, :], in0=rt1[:, g0:g0 + nb, :], in1=within_own[:, g0:g0 + nb, hs * HH:(hs + 1) * HH], op=ALU.subtract),
                 r=["rt1", "within_own"], w=["Rtok"])
            for ti, ob in enumerate(blocks):
                P.op("pe", lambda e, ti=ti, ob=ob: e.matmul(banks[4][0:HH, ti * 128:(ti + 1) * 128], lhsT=Rtok[:, ob, :], rhs=identf, start=True, stop=True),
                     r=["Rtok", "identf"], w=["bank4"])
            P.op("dve", lambda e, g0=g0, nb=nb: e.tensor_copy(out=R8[0:HH, g0 * 128:(g0 + nb) * 128], in_=banks[4][0:HH, 0:nb * 128]), r=["bank4"], w=["R8"])

        for hh in range(HH):
            for g, blocks in enumerate(c.groups):
                g0, nb = blocks[0], len(blocks)
                rb_ = 5 + ((hh * NG + g) % 2)
                P.op("pe", lambda e, hh=hh, g0=g0, nb=nb, rb_=rb_: e.matmul(banks[rb_][:, 0:nb * 128], lhsT=selb[0:HH, hh, :], rhs=R8[0:HH, g0 * 128:(g0 + nb) * 128], start=True, stop=True),
                     r=["selb", "R8"], w=["bank%d" % rb_])
                P.op("dve", lambda e, hh=hh, g0=g0, nb=nb, rb_=rb_: e.tensor_copy(out=rbc[:, hh, g0 * 128:(g0 + nb) * 128], in_=banks[rb_][:, 0:nb * 128]),
                     r=["bank%d" % rb_], w=["rbc"])

        mark2 = AR.top
        NPT = 6
        pts = [AR.alloc([128, 512], BF16) for _ in range(NPT)]
        osb = [AR.alloc([128, 512], F32) for _ in range(2)]
        rc = AR.alloc([128, 8], F32)
        kmaxs = [c.lo[blocks[-1]] + 3 for blocks in c.groups]
        batches = []
        for hh in range(HH):
            for kb in range(NB):
                act_g = [g for g in range(NG) if kb <= kmaxs[g]]
                for j in range(0, len(act_g), 1):
                    batches.append((hh, kb, act_g[j:j + 1]))
        epi = [0]

        def geom(g, kb):
            blocks = c.groups[g]
            fa = 0
            while c.lo[blocks[fa]] + 3 < kb:
                fa += 1
            N = (len(blocks) - fa) * 128
            c0 = blocks[fa] * 128
            msk = [(bi, kb - c.lo[ob]) for bi, ob in enumerate(blocks) if bi >= fa and c.lo[ob] <= kb <= c.lo[ob] + 3]
            return blocks, fa, N, c0, msk

        def emit_S(i):
            hh, kb, gs = batches[i]
            h = hs * HH + hh
            hpg, e_, hpl = h // 2, h % 2, hh // 2
            info = []
            for j, g in enumerate(gs):
                blocks, fa, N, c0, msk = geom(g, kb)
                sb = i % 4
                info.append((g, blocks, fa, N, c0, msk, sb))
            for (g, blocks, fa, N, c0, msk, sb) in info:
                P.op("pe", lambda e, N=N, c0=c0, sb=sb, nm=len(msk): e.matmul(banks[sb][:, 0:N], lhsT=KT[e_ * 64:(e_ + 1) * 64, hpl, kb * 128:(kb + 1) * 128],
                                                                rhs=QT[e_ * 64:(e_ + 1) * 64, hpg, c0:c0 + N], start=True, stop=(nm == 0)),
                     r=["KT%d" % (kb // 4), "QT"], w=["bank%d" % sb])
            for (g, blocks, fa, N, c0, msk, sb) in info:
                for mi, (bi, i4) in enumerate(msk):
                    mt = c.mtype[blocks[bi]] * 4 + i4
                    P.op("pe", lambda e, bi=bi, mt=mt, mi=mi, fa=fa, sb=sb, nm=len(msk): e.matmul(banks[sb][:, (bi - fa) * 128:(bi - fa + 1) * 128], lhsT=identb, rhs=maskb[:, mt, :],
                                                                                              start=False, stop=(mi == nm - 1)), r=["identb", "maskb"], w=["bank%d" % sb])
            for (g, blocks, fa, N, c0, msk, sb) in info:
                P.op("dve", lambda e, N=N, c0=c0, sb=sb: e.tensor_tensor(out=banks[sb][:, 0:N], in0=banks[sb][:, 0:N], in1=rbc[:, hh, c0:c0 + N], op=ALU.add),
                     r=["bank%d" % sb, "rbc"], w=["bank%d" % sb])
            for j, (g, blocks, fa, N, c0, msk, sb) in enumerate(info):
                pi = i % NPT
                P.op("act", lambda e, N=N, sb=sb, g=g, pi=pi: e.activation(out=pts[pi][:, 0:N], in_=banks[sb][:, 0:N], func=AF.Exp, bias=biasT[:, g, kb, hh:hh + 1], scale=1.0),
                     r=["bank%d" % sb, "biasT"], w=["pt%d" % pi])

        def emit_PV(i):
            hh, kb, gs = batches[i]
            h = hs * HH + hh
            for j, g in enumerate(gs):
                blocks, fa, N, c0, msk = geom(g, kb)
                ob_ = 4 + g
                ok = "bank%d" % ob_
                pi = i % NPT
                last = (kb == kmaxs[g])
                P.op("pe", lambda e, fa=fa, N=N, ob_=ob_, pi=pi, last=last: e.matmul(banks[ob_][0:65, fa * 128:fa * 128 + N], lhsT=Vaug[:, kb, hh, :], rhs=pts[pi][:, 0:N], start=(kb == 0), stop=last),
                     r=["V%d" % kb, "Vones", "pt%d" % pi], w=[ok])
            for j, g in enumerate(gs):
                if kb != kmaxs[g]:
                    continue
                blocks = c.groups[g]
                ob_ = 4 + g
                ok = "bank%d" % ob_
                nb = len(blocks)
                ei = epi[0] % 2
                epi[0] += 1
                os_ = osb[ei]
                osk = "osb%d" % ei
                tb = i % 4
                tk = "bank%d" % tb
                P.op("dve", lambda e, os_=os_, ob_=ob_, nb=nb: e.tensor_copy(out=os_[0:65, 0:nb * 128], in_=banks[ob_][0:65, 0:nb * 128]), r=[ok], w=[osk])
                for ti, ob in enumerate(blocks):
                    P.op("pe", lambda e, ti=ti, os_=os_, tb=tb: e.matmul(banks[tb][:, ti * 65:(ti + 1) * 65], lhsT=os_[0:65, ti * 128:(ti + 1) * 128], rhs=identf[0:65, 0:65], start=True, stop=True),
                         r=[osk, "identf"], w=[tk])
                o3 = banks[tb][:, 0:nb * 65].rearrange("p (b x) -> p b x", x=65)
                P.op("dve", lambda e, o3=o3, nb=nb: e.reciprocal(out=rc[:, 0:nb], in_=o3[:, :, 64]), r=[tk], w=["rc"])
                for ti, ob in enumerate(blocks):
                    P.op("dve", lambda e, ti=ti, ob=ob, o3=o3, h=h: e.tensor_scalar(out=yatt[:, ob, h * 64:(h + 1) * 64], in0=o3[:, ti, 0:64], scalar1=rc[:, ti:ti + 1], scalar2=None, op0=ALU.mult),
                         r=[tk, "rc"], w=["yatt"])

        SKEW = 3
        for i in range(len(batches)):
            emit_S(i)
            if i >= SKEW:
                emit_PV(i - SKEW)
        for i in range(max(0, len(batches) - SKEW), len(batches)):
            emit_PV(i)
        P.barrier()
        AR.top = mark

    AR.top = attn_top - 0
    AR.top = persist_top
    C0 = 0.7978845608028654
    C1_ = 0.044715
    Wu = AR.alloc([128, KC, SGW], BF16)
    Wvs = AR.alloc([128, KC, SGW], BF16)
    Wo = AR.alloc([128, KC, D], BF16)
    wsT = AR.alloc([128, G, 128], BF16)
    wsTf = AR.alloc([128, G, 128], F32)
    sgb = AR.alloc([128, G], F32)
    lng = AR.alloc([128, SGW], F32)
    lnb = AR.alloc([128, SGW], F32)
    gpmix = AR.alloc([128, D], F32)
    fr = Front(nx=3, tpbanks=(0, 1))
    aTc = [AR.alloc([128, KC, 128], BF16) for _ in range(2)]
    TN = ["x2", "tA", "tB", "gu", "gv", "xc", "vln", "tmix", "ysg"]
    TT = [{n: AR.alloc([128, SGW], F32) for n in TN} for _ in range(2)]
    for p_ in range(2):
        TT[p_]["vlnb"] = AR.alloc([128, SGW], BF16)
        TT[p_]["junk"] = AR.alloc([128, D], BF16)
        TT[p_]["yn"] = AR.alloc([128, D], BF16)
        TT[p_]["ynT"] = AR.alloc([128, KC, 128], BF16)
        TT[p_]["h1"] = AR.alloc([128, D], F32)
    wuk = load_w(Wu, w_in_r[:, ucol:ucol + SGW], "Wu", KC)
    wvsk = load_w(Wvs, w_in_r[:, vscol:vscol + SGW], "Wvs", KC)
    wok = load_w(Wo, w_out, "Wo", KC)
    ld(wsTf, sgwT_d, "wsTf")
    ld(sgb, sgbT_d, "sgb")
    ld(lng, lng_d, "lng")
    ld(lnb, lnb_d, "lnb")
    ld(gpmix, gpmix_d, "gpmix")
    P.op("dve", lambda e: e.tensor_tensor(out=wsT, in0=wsTf, in1=Uf.unsqueeze(1).broadcast_to([128, G, 128]), op=ALU.mult), r=["wsTf", "Uf"], w=["wsT"])

    def gelu2(T, p, dst, dk, ps, pk):
        x2, tA, tB = T["x2"], T["tA"], T["tB"]
        kx, ka, kb2 = "x2_%d" % p, "tA_%d" % p, "tB_%d" % p
        P.op("act", lambda e: e.activation(out=x2, in_=ps, func=AF.Square), r=[pk], w=[kx])
        yield
        P.op("dve", lambda e: e.tensor_scalar(out=tA, in0=x2, scalar1=C1_, scalar2=1.0, op0=ALU.mult, op1=ALU.add), r=[kx], w=[ka])
        yield
        P.op("dve", lambda e: e.tensor_tensor(out=tB, in0=tA, in1=ps, op=ALU.mult), r=[ka, pk], w=[kb2])
        yield
        P.op("act", lambda e: e.activation(out=tA, in_=tB, func=AF.Tanh, scale=C0), r=[kb2], w=[ka])
        yield
        P.op("dve", lambda e: e.scalar_tensor_tensor(out=dst, in0=tA, scalar=1.0, in1=ps, op0=ALU.add, op1=ALU.mult), r=[ka, pk], w=[dk])
        yield

    def tileC1(ob):
        p = ob % 2
        T = TT[p]
        aT = aTc[p]
        ak = "aTc%d" % p
        bX, bY = 2 + 3 * p, 3 + 3 * p
        bT = 4 + 3 * p
        kX, kY, kT = "bank%d" % bX, "bank%d" % bY, "bank%d" % bT
        K = lambda n: "%s_%d" % (n, p)
        gu, gv, xc, vln, vlnb, tmix, ysg, junk, yn, ynT, h1 = (T[n] for n in ("gu", "gv", "xc", "vln", "vlnb", "tmix", "ysg", "junk", "yn", "ynT", "h1"))
        for _ in fr.gen(xown[ob * 128:(ob + 1) * 128, :], (gpm, "gpm"), aT, ak):
            yield
        xt, xk = fr.last
        for kc in range(KC):
            P.op("pe", lambda e, kc=kc: e.matmul(banks[bX][:, 0:SGW], lhsT=aT[:, kc, :], rhs=Wu[:, kc, :], start=(kc == 0), stop=(kc == KC - 1)),
                 r=[ak, wuk[kc]], w=[kX])
        for kc in range(KC):
            P.op("pe", lambda e, kc=kc: e.matmul(banks[bY][:, 0:SGW], lhsT=aT[:, kc, :], rhs=Wvs[:, kc, :], start=(kc == 0), stop=(kc == KC - 1)),
                 r=[ak, wvsk[kc]], w=[kY])
        yield
        for _ in gelu2(T, p, gv, K("gv"), banks[bY][:, 0:SGW], kY):
            yield
        for _ in gelu2(T, p, gu, K("gu"), banks[bX][:, 0:SGW], kX):
            yield
        sl = stat_slot(6)
        s1 = stats[:, sl:sl + 1]
        nm = stats[:, sl + 1:sl + 2]
        s2 = stats[:, sl + 2:sl + 3]
        r2 = stats[:, sl + 3:sl + 4]
        k_ = ["st%d" % (sl + j) for j in range(6)]
        P.op("dve", lambda e: e.reduce_sum(out=s1, in_=gv, axis=AX.X), r=[K("gv")], w=[k_[0]])
        yield
        P.op("dve", lambda e: e.tensor_scalar(out=nm, in0=s1, scalar1=-0.5 / SGW, scalar2=None, op0=ALU.mult), r=[k_[0]], w=[k_[1]])
        yield
        P.op("dve", lambda e: e.tensor_scalar(out=xc, in0=gv, scalar1=0.5, scalar2=nm, op0=ALU.mult, op1=ALU.add), r=[K("gv"), k_[1]], w=[K("xc")])
        yield
        P.op("act", lambda e: e.activation(out=junk[:, 0:SGW], in_=xc, func=AF.Square, accum_out=s2), r=[K("xc")], w=[K("junk"), k_[2]])
        yield
        rsqrt_ops(s2, r2, 1, 1.0 / SGW, [k_[2]], [k_[3]])
        yield
        P.op("dve", lambda e: e.scalar_tensor_tensor(out=vln, in0=xc, scalar=r2, in1=lng, op0=ALU.mult, op1=ALU.mult), r=[K("xc"), k_[3], "lng"], w=[K("vln")])
        yield
        P.op("dve", lambda e: e.tensor_tensor(out=vlnb, in0=vln, in1=lnb, op=ALU.add), r=[K("vln"), "lnb"], w=[K("vlnb")])
        yield
        for g8 in range(G):
            P.op("pe", lambda e, g8=g8: e.matmul(banks[bY][:, g8 * 64:(g8 + 1) * 64], lhsT=wsT[:, g8, :], rhs=vlnb[:, g8 * 64:(g8 + 1) * 64], start=True, stop=True),
                 r=["wsT", K("vlnb")], w=[kY])
        yield
        P.op("dve", lambda e: e.tensor_tensor(out=tmix.rearrange("p (g d) -> p g d", d=64), in0=banks[bY][:, 0:SGW].rearrange("p (g d) -> p g d", d=64),
                                              in1=sgb.unsqueeze(2).broadcast_to([128, G, 64]), op=ALU.add), r=[kY, "sgb"], w=[K("tmix")])
        yield
        P.op("dve", lambda e: e.scalar_tensor_tensor(out=ysg, in0=gu, scalar=0.5, in1=tmix, op0=ALU.mult, op1=ALU.mult), r=[K("gu"), K("tmix")], w=[K("ysg")])
        yield
        sl2 = stat_slot(4)
        k2 = ["st%d" % (sl2 + j) for j in range(4)]
        ssq = stats[:, sl2:sl2 + 2]
        rsq = stats[:, sl2 + 2:sl2 + 4]
        P.op("act", lambda e: e.activation(out=junk[:, 0:AW], in_=yatt[:, ob, :], func=AF.Square, accum_out=ssq[:, 0:1]), r=["yatt"], w=[K("junk"), k2[0]])
        yield
        P.op("act", lambda e: e.activation(out=junk[:, 0:SGW], in_=ysg, func=AF.Square, accum_out=ssq[:, 1:2]), r=[K("ysg")], w=[K("junk"), k2[1]])
        yield
        rsqrt_ops(ssq, rsq, 2, 1.0 / AW, [k2[0], k2[1]], [k2[2], k2[3]])
        yield
        P.op("dve", lambda e: e.tensor_scalar(out=yn[:, 0:AW], in0=yatt[:, ob, :], scalar1=rsq[:, 0:1], scalar2=None, op0=ALU.mult), r=["yatt", k2[2]], w=[K("yn")])
        yield
        P.op("dve", lambda e: e.tensor_scalar(out=yn[:, AW:D], in0=ysg, scalar1=rsq[:, 1:2], scalar2=None, op0=ALU.mult), r=[K("ysg"), k2[3]], w=[K("yn")])
        yield
        tpv = banksb[bT][:, 0:KC * 128].rearrange("p (k t) -> p k t", t=128)
        for kc in range(KC):
            P.op("pe", lambda e, kc=kc: e.transpose(out=tpv[:, kc, :], in_=yn[:, kc * 128:(kc + 1) * 128], identity=identb), r=[K("yn"), "identb"], w=[kT])
        yield
        P.op("dve", lambda e: e.tensor_tensor(out=ynT, in0=tpv, in1=gcat.unsqueeze(2).broadcast_to([128, KC, 128]), op=ALU.mult), r=[kT, "gcat"], w=[K("ynT")])
        yield
        sl3 = stat_slot(4)
        k3 = ["st%d" % (sl3 + j) for j in range(4)]
        obanks = [bX, bT] if DH == 2 else [bX]
        for dh in range(DH):
            ob_ = obanks[dh]
            for kc in range(KC):
                P.op("pe", lambda e, kc=kc, dh=dh, ob_=ob_: e.matmul(banks[ob_][:, 0:DW], lhsT=ynT[:, kc, :], rhs=Wo[:, kc, dh * DW:(dh + 1) * DW], start=(kc == 0), stop=(kc == KC - 1)),
                     r=[K("ynT"), wok[kc]], w=["bank%d" % ob_])
            yield
            P.op("act", lambda e, dh=dh, ob_=ob_: e.activation(out=junk[:, 0:DW], in_=banks[ob_][:, 0:DW], func=AF.Square, accum_out=stats[:, sl3 + dh:sl3 + dh + 1]),
                 r=["bank%d" % ob_], w=[K("junk"), k3[dh]])
            yield
        if DH == 2:
            P.op("dve", lambda e: e.tensor_tensor(out=stats[:, sl3 + 2:sl3 + 3], in0=stats[:, sl3:sl3 + 1], in1=stats[:, sl3 + 1:sl3 + 2], op=ALU.add), r=[k3[0], k3[1]], w=[k3[2]])
            yield
            sso = stats[:, sl3 + 2:sl3 + 3]
            ssk = k3[2]
        else:
            sso = stats[:, sl3:sl3 + 1]
            ssk = k3[0]
        rso = stats[:, sl3 + 3:sl3 + 4]
        rsqrt_ops(sso, rso, 1, 1.0 / D, [ssk], [k3[3]])
        yield
        hk = K("h1")
        for dh in range(DH):
            ob_ = obanks[dh]
            P.op("dve", lambda e, dh=dh, ob_=ob_: e.scalar_tensor_tensor(out=h1[:, dh * DW:(dh + 1) * DW], in0=banks[ob_][:, 0:DW], scalar=rso, in1=gpmix[:, dh * DW:(dh + 1) * DW], op0=ALU.mult, op1=ALU.mult),
                 r=["bank%d" % ob_, k3[3], "gpmix"], w=[hk])
            yield
        P.op("dve", lambda e: e.tensor_tensor(out=h1, in0=h1, in1=xt, op=ALU.add), r=[hk, xk], w=[hk])
        yield
        P.op("sp", lambda e: e.dma_start(out=h1_d[ob * 128:(ob + 1) * 128, :], in_=h1), r=[hk], w=["h1d%d" % ob], dma=True)
        yield

    interleave((tileC1(ob) for ob in range(NO)), 2, 20)
    P.barrier()
    AR.top = const_top

    W1 = AR.alloc([128, KC, DFF], BF16)
    W2 = AR.alloc([128, FC, D], BF16)
    HT = AR.alloc([128, FC, 512], BF16)
    cT = AR.alloc([128, KC, 512], BF16)
    gpffn = AR.alloc([128, D], F32)
    fr = Front(nx=2, tpbanks=(0, 1))
    rtmp = [AR.alloc([128, 512], BF16) for _ in range(2)]
    h1r = [AR.alloc([128, D], F32) for _ in range(2)]
    o2t = [AR.alloc([128, D], F32) for _ in range(1)]
    junk2 = AR.alloc([128, 512], BF16)
    w1k = load_w(W1, w_ff1, "W1", KC)
    w2k = load_w(W2, w_ff2, "W2", FC)
    ld(gpffn, gpffn_d, "gpffn")
    tcount = 0
    for g, blocks in enumerate(c.groups):
        N = len(blocks) * 128
        interleave((fr.gen(h1_d[ob * 128:(ob + 1) * 128, :], (gpf, "gpf"), cT[:, :, ti * 128:(ti + 1) * 128], "cT%d" % ti) for ti, ob in enumerate(blocks)), 2, 2)
        for fc in range(FC):
            hb = 2 + fc % 2
            for kc in range(KC):
                P.op("pe", lambda e, kc=kc, fc=fc, hb=hb, N=N: e.matmul(banks[hb][:, 0:N], lhsT=W1[:, kc, fc * 128:(fc + 1) * 128], rhs=cT[:, kc, 0:N], start=(kc == 0), stop=(kc == KC - 1)),
                     r=["cT%d" % j for j in range(len(blocks))] + [w1k[kc]], w=["bank%d" % hb])
            rt = rtmp[fc % 2]
            rk = "rtmp%d" % (fc % 2)
            P.op("act", lambda e, hb=hb, rt=rt, N=N: e.activation(out=rt[:, 0:N], in_=banks[hb][:, 0:N], func=AF.Relu), r=["bank%d" % hb], w=[rk])
            P.op("dve", lambda e, fc=fc, rt=rt, N=N: e.tensor_tensor(out=HT[:, fc, 0:N], in0=rt[:, 0:N], in1=rt[:, 0:N], op=ALU.mult), r=[rk], w=["HT"])
        for ti, ob in enumerate(blocks):
            i2 = tcount % 2
            tcount += 1
            h1 = h1r[i2]
            hk = "h1r%d" % i2
            P.op("sp", lambda e, h1=h1, ob=ob: e.dma_start(out=h1, in_=h1_d[ob * 128:(ob + 1) * 128, :]), r=["h1d%d" % ob], w=[hk], dma=True)
            sl3 = stat_slot(4)
            k3 = ["st%d" % (sl3 + j) for j in range(4)]
            for dh in range(DH):
                ob_ = 4 + 2 * i2 + dh
                for fc in range(FC):
                    P.op("pe", lambda e, fc=fc, dh=dh, ob_=ob_, ti=ti: e.matmul(banks[ob_][:, 0:DW], lhsT=HT[:, fc, ti * 128:(ti + 1) * 128], rhs=W2[:, fc, dh * DW:(dh + 1) * DW], start=(fc == 0), stop=(fc == FC - 1)),
                         r=["HT", w2k[fc]], w=["bank%d" % ob_])
                P.op("act", lambda e, dh=dh, ob_=ob_, sl3=sl3: e.activation(out=junk2[:, 0:DW], in_=banks[ob_][:, 0:DW], func=AF.Square, accum_out=stats[:, sl3 + dh:sl3 + dh + 1]),
                     r=["bank%d" % ob_], w=["junk2", k3[dh]])
            if DH == 2:
                P.op("dve", lambda e, sl3=sl3: e.tensor_tensor(out=stats[:, sl3 + 2:sl3 + 3], in0=stats[:, sl3:sl3 + 1], in1=stats[:, sl3 + 1:sl3 + 2], op=ALU.add), r=[k3[0], k3[1]], w=[k3[2]])
                sso = stats[:, sl3 + 2:sl3 + 3]
                ssk = k3[2]
            else:
                sso = stats[:, sl3:sl3 + 1]
                ssk = k3[0]
            rso = stats[:, sl3 + 3:sl3 + 4]
            rsqrt_ops(sso, rso, 1, 1.0 / D, [ssk], [k3[3]])
            o2 = o2t[0]
            ok2 = "o2t0"
            for dh in range(DH):
                ob_ = 4 + 2 * i2 + dh
                P.op("dve", lambda e, dh=dh, ob_=ob_, rso=rso, o2=o2: e.scalar_tensor_tensor(out=o2[:, dh * DW:(dh + 1) * DW], in0=banks[ob_][:, 0:DW], scalar=rso, in1=gpffn[:, dh * DW:(dh + 1) * DW], op0=ALU.mult, op1=ALU.mult),
                     r=["bank%d" % ob_, k3[3], "gpffn"], w=[ok2])
            P.op("dve", lambda e, o2=o2, h1=h1: e.tensor_tensor(out=h1, in0=o2, in1=h1, op=ALU.add), r=[ok2, hk], w=[hk])
            P.op("sp", lambda e, h1=h1, ob=ob: e.dma_start(out=h2_d[ob * 128:(ob + 1) * 128, :], in_=h1), r=[hk], w=["h2d%d" % ob], dma=True)
    P.barrier()
    AR.top = const_top

    Wg = AR.alloc([128, KC, D], BF16)
    Wpe = AR.alloc([128, PK, D], BF16)
    gateb = AR.alloc([128, D], F32)
    ND3 = 3
    fr3 = Front(nx=ND3, tpbanks=(0, 1), nxn=ND3)
    S3 = []
    for _ in range(ND3):
        S3.append(dict(h2T=AR.alloc([128, KC, 128], BF16), pt=AR.alloc([128, PLE], F32), pb=AR.alloc([128, PLE], BF16),
                       pT=AR.alloc([128, PK, 128], BF16), z=AR.alloc([128, D], F32), pp=AR.alloc([128, D], F32), o=AR.alloc([128, D], F32)))
    wgk = load_w(Wg, gate_w, "Wg", KC)
    wpk = load_w(Wpe, ple_w, "Wpe", PK)
    ld(gateb, gateb_d, "gateb")
    out_dmas = []
    brot = [0]

    def nbank():
        b_ = 2 + brot[0] % 6
        brot[0] += 1
        return b_

    def tileC3(ob):
        p = ob % ND3
        T = S3[p]
        K = lambda n: "%s3_%d" % (n, p)
        h2T, pt_, pb_, pT_, z, pp, o_ = T["h2T"], T["pt"], T["pb"], T["pT"], T["z"], T["pp"], T["o"]
        P.op("sp", lambda e: e.dma_start(out=pt_, in_=pown[ob * 128:(ob + 1) * 128, :]), w=[K("pt")], dma=True)
        for _ in fr3.gen(h2_d[ob * 128:(ob + 1) * 128, :], None, h2T, K("h2T"), norm=False):
            yield
        xt, xk = fr3.last
        P.op("dve", lambda e: e.tensor_copy(out=pb_, in_=pt_), r=[K("pt")], w=[K("pb")])
        yield
        tb_ = nbank()
        tpv3 = banksb[tb_][:, 0:PK * 128].rearrange("p (k t) -> p k t", t=128)
        for k2_ in range(PK):
            P.op("pe", lambda e, k2_=k2_: e.transpose(out=tpv3[:, k2_, :], in_=pb_[:, k2_ * 128:(k2_ + 1) * 128], identity=identb), r=[K("pb"), "identb"], w=["bank%d" % tb_])
        yield
        P.op("act", lambda e: e.activation(out=pT_, in_=tpv3, func=AF.Copy), r=["bank%d" % tb_], w=[K("pT")])
        yield
        for dh in range(DH):
            gb_ = nbank()
            for kc in range(KC):
                P.op("pe", lambda e, kc=kc, dh=dh, gb_=gb_: e.matmul(banks[gb_][:, 0:DW], lhsT=h2T[:, kc, :], rhs=Wg[:, kc, dh * DW:(dh + 1) * DW], start=(kc == 0), stop=(kc == KC - 1)),
                     r=[K("h2T"), wgk[kc]], w=["bank%d" % gb_])
            yield
            P.op("dve", lambda e, dh=dh, gb_=gb_: e.tensor_tensor(out=z[:, dh * DW:(dh + 1) * DW], in0=banks[gb_][:, 0:DW], in1=gateb[:, dh * DW:(dh + 1) * DW], op=ALU.add),
                 r=["bank%d" % gb_, "gateb"], w=[K("z")])
            yield
            pb2 = nbank()
            for k2_ in range(PK):
                P.op("pe", lambda e, k2_=k2_, dh=dh, pb2=pb2: e.matmul(banks[pb2][:, 0:DW], lhsT=pT_[:, k2_, :], rhs=Wpe[:, k2_, dh * DW:(dh + 1) * DW], start=(k2_ == 0), stop=(k2_ == PK - 1)),
                     r=[K("pT"), wpk[k2_]], w=["bank%d" % pb2])
            yield
            P.op("act", lambda e, dh=dh, pb2=pb2: e.activation(out=pp[:, dh * DW:(dh + 1) * DW], in_=banks[pb2][:, 0:DW], func=AF.Copy), r=["bank%d" % pb2], w=[K("pp")])
            yield
        P.op("act", lambda e: e.activation(out=z, in_=z, func=AF.Tanh, scale=0.5), r=[K("z")], w=[K("z")])
        yield
        P.op("dve", lambda e: e.tensor_scalar(out=z, in0=z, scalar1=0.5, scalar2=0.5, op0=ALU.mult, op1=ALU.add), r=[K("z")], w=[K("z")])
        yield
        P.op("dve", lambda e: e.tensor_tensor(out=o_, in0=z, in1=pp, op=ALU.mult), r=[K("z"), K("pp")], w=[K("o")])
        yield
        P.op("dve", lambda e: e.tensor_tensor(out=o_, in0=o_, in1=xt, op=ALU.add), r=[K("o"), xk], w=[K("o")])
        yield
        out_dmas.append(P.op("sp", lambda e: e.dma_start(out=out_d[ob * 128:(ob + 1) * 128, :], in_=o_), r=[K("o")], dma=True))
        yield

    interleave((tileC3(ob) for ob in range(NO)), ND3, 7)
    P.wait_all("sp", out_dmas)
    P.emit()
    return nc, P


def make_core_inputs(cfg, core, x, p, w_in, f_bias, sg_ln_g, sg_ln_b, sg_w, sg_b, att_out_g, sg_out_g,
                     w_out, pre_mix_g, post_mix_g, pre_ffn_g, post_ffn_g, w_ff1, w_ff2, ple_w, ple_gate_w, ple_gate_b):
    c = cfg
    b, r = core // 4, core % 4
    f32 = np.float32
    blocks = c.owned_blocks(r)
    rows = np.concatenate([np.arange(bl * 128, (bl + 1) * 128) for bl in blocks])

    def fm(v):
        return np.ascontiguousarray(np.asarray(v, f32).reshape(c.KC, 128).T)

    def rep(v):
        return np.ascontiguousarray(np.broadcast_to(np.asarray(v, f32).reshape(1, -1), (128, np.asarray(v).size)))

    k = np.arange(128)[:, None]
    q = np.arange(128)[None, :]
    tri = np.where(k > q, NEG, 0.0).astype(f32)
    full = np.full((128, 128), NEG, f32)
    zero = np.zeros((128, 128), f32)
    maskT = np.zeros((128, 8, 128), f32)
    for i in range(4):
        maskT[:, i, :] = zero if i < r else (tri if i == r else full)
        maskT[:, 4 + i, :] = zero if i < 3 - r else (tri if i == 3 - r else full)
    sel = np.zeros((c.HH, c.HH, 128), f32)
    for hh in range(c.HH):
        sel[hh, hh, :] = 1.0
    LTfull = (np.arange(c.NB)[:, None] < np.arange(c.NB)[None, :]).astype(f32)
    LTown = (np.arange(c.NB)[:, None] < np.asarray(blocks)[None, :]).astype(f32)
    xb = np.asarray(x[b], f32)
    return {
        "xfull": np.ascontiguousarray(xb),
        "xown": np.ascontiguousarray(xb[rows]),
        "pown": np.ascontiguousarray(np.asarray(p[0, b], f32)[rows]),
        "w_in": np.ascontiguousarray(np.asarray(w_in[0], f32)),
        "w_out": np.ascontiguousarray(np.asarray(w_out[0], f32)),
        "w_ff1": np.ascontiguousarray(np.asarray(w_ff1[0], f32)),
        "w_ff2": np.ascontiguousarray(np.asarray(w_ff2[0], f32)),
        "ple_w": np.ascontiguousarray(np.asarray(ple_w[0], f32)),
        "gate_w": np.ascontiguousarray(np.asarray(ple_gate_w[0], f32)),
        "gpm": fm(pre_mix_g[0]),
        "gpf": fm(pre_ffn_g[0]),
        "gcat": fm(np.concatenate([np.asarray(att_out_g[0]), np.asarray(sg_out_g[0])])),
        "gpmix": rep(post_mix_g[0]),
        "gpffn": rep(post_ffn_g[0]),
        "gateb": rep(ple_gate_b[0]),
        "lng": rep(sg_ln_g[0]),
        "lnb": rep(sg_ln_b[0]),
        "fbias": rep(f_bias[0]),
        "sgwT": np.ascontiguousarray(np.transpose(np.asarray(sg_w[0], f32), (2, 0, 1))),
        "sgbT": np.ascontiguousarray(np.asarray(sg_b[0], f32).T),
        "ident": np.eye(128, dtype=f32),
        "U": (np.arange(128)[:, None] <= np.arange(128)[None, :]).astype(f32),
        "maskT": maskT,
        "sel": sel,
        "LTfull": LTfull,
        "LTown": LTown,
    }, rows


_CACHE = {}


def kernel(**inputs):
    x = np.asarray(inputs["x"])
    B, S, D = x.shape
    PLE = np.asarray(inputs["p"]).shape[-1]
    cfg = Cfg(D=D, S=S, PLE=PLE)
    key = (D, S, PLE)
    if key not in _CACHE:
        _CACHE[key] = build_program(cfg)
    nc, _ = _CACHE[key]
    in_maps, rows_all = [], []
    for core in range(8):
        m, rows = make_core_inputs(cfg, core, **inputs)
        in_maps.append(m)
        rows_all.append(rows)
    res = run_bass_kernel_spmd(nc, in_maps, core_ids=list(range(8)))
    out = np.zeros((B, S, D), np.float32)
    for core in range(8):
        out[core // 4, rows_all[core], :] = np.asarray(res.results[core]["out"], np.float32)
    return out
```

```python
import numpy as np
import concourse.bass as bass
import concourse.mybir as mybir
from concourse.bass_utils import run_bass_kernel_spmd

F32 = mybir.dt.float32
BF16 = mybir.dt.bfloat16
AF = mybir.ActivationFunctionType
ALU = mybir.AluOpType
AX = mybir.AxisListType

EPS = 1e-6
NEG = -30000.0


class _Ins:
    __slots__ = ("eng", "idx", "fn", "deps", "signal", "is_dma", "dma_sem", "dma_val", "sig_val", "epoch")

    def __init__(self, eng, idx, fn, is_dma):
        self.eng = eng
        self.idx = idx
        self.fn = fn
        self.deps = set()
        self.signal = False
        self.is_dma = is_dma
        self.dma_sem = None
        self.dma_val = 0
        self.sig_val = 0
        self.epoch = 0


class Prog:
    ENGS = ("pe", "act", "dve", "pool", "sp")

    def __init__(self, nc, n_dma_sems=24):
        self.nc = nc
        self.q = {e: [] for e in self.ENGS}
        self.lastw = {}
        self.readers = {}
        self.n_dma_sems = n_dma_sems
        self.dma_count = 0
        self.dma_last = [None] * n_dma_sems
        self.dma_pools = {"sp": (0, n_dma_sems - 8), "act": (0, n_dma_sems - 8), "pool": (n_dma_sems - 8, 8)}
        self.dma_pool_cnt = {"sp": 0, "act": 0, "pool": 0}
        self.dma_sem_uses = [0] * n_dma_sems
        self.epoch = 0

    def op(self, eng, fn, r=(), w=(), dma=False):
        ins = _Ins(eng, len(self.q[eng]), fn, dma)
        ins.epoch = self.epoch
        deps = ins.deps
        if any(k.startswith("bank") for k in r):
            w = list(w) + [k for k in r if k.startswith("bank") and k not in w]
            r = [k for k in r if not k.startswith("bank")]
        for k in r:
            lw = self.lastw.get(k)
            if lw is not None:
                deps.add(lw)
        for k in w:
            lw = self.lastw.get(k)
            if lw is not None:
                deps.add(lw)
            rd = self.readers.get(k)
            if rd:
                for x in rd[0].values():
                    deps.add(x)
                for x in rd[1]:
                    deps.add(x)
        if dma:
            base, cnt = self.dma_pools[eng]
            pk = "sp" if eng in ("sp", "act") else "pool"
            s = base + self.dma_pool_cnt[pk] % cnt
            self.dma_pool_cnt[pk] += 1
            prev = self.dma_last[s]
            if prev is not None:
                deps.add(prev)
            self.dma_sem_uses[s] += 1
            ins.dma_sem = s
            ins.dma_val = 16 * self.dma_sem_uses[s]
            self.dma_last[s] = ins
            self.dma_count += 1
        deps.discard(ins)
        for k in w:
            self.lastw[k] = ins
            self.readers[k] = ({}, [])
        for k in r:
            rd = self.readers.setdefault(k, ({}, []))
            if dma:
                rd[1].append(ins)
            else:
                rd[0][eng] = ins
        self.q[eng].append(ins)
        return ins

    def barrier(self):
        lasts = []
        for e in self.ENGS:
            for ins in reversed(self.q[e]):
                if not ins.is_dma and ins.fn is not None:
                    lasts.append(ins)
                    break
        dmas = [d for d in self.dma_last if d is not None]
        for e in self.ENGS:
            ins = _Ins(e, len(self.q[e]), None, False)
            ins.epoch = self.epoch
            ins.deps = set(lasts) | set(dmas)
            self.q[e].append(ins)
        self.lastw.clear()
        self.readers.clear()
        self.epoch += 1

    def wait_all(self, eng, instrs):
        ins = _Ins(eng, len(self.q[eng]), None, False)
        ins.epoch = self.epoch
        ins.deps = set(instrs)
        self.q[eng].append(ins)

    def emit(self):
        nc = self.nc
        for e in self.ENGS:
            for ins in self.q[e]:
                for d in ins.deps:
                    if not d.is_dma:
                        d.signal = True
        counts = {}
        for e in self.ENGS:
            c = 0
            ep = 0
            mx = 0
            for ins in self.q[e]:
                if ins.epoch != ep:
                    ep = ins.epoch
                    c = 0
                if (not ins.is_dma) and ins.signal and ins.fn is not None:
                    c += 1
                    ins.sig_val = c
                    mx = max(mx, c)
            counts[e] = mx
        self.counts = counts
        nep = self.epoch + 1
        import contextlib

        with contextlib.ExitStack() as st:
            esem = {(e, ep): st.enter_context(nc.semaphore("s_%s%d" % (e, ep))) for e in self.ENGS for ep in range(nep)}
            dsem = [st.enter_context(nc.semaphore("s_dma%d" % i)) for i in range(self.n_dma_sems)]
            block = st.enter_context(nc.Block())

            def run(e, eng):
                known = {}
                for ins in self.q[e]:
                    waits = {}
                    for d in ins.deps:
                        if d.is_dma:
                            key = ("d", d.dma_sem)
                            val = d.dma_val
                        else:
                            if d.fn is None:
                                continue
                            if d.eng == e and not ins.is_dma:
                                if e == "pe" or ins.idx - d.idx >= 3:
                                    continue
                            key = ("e", (d.eng, d.epoch))
                            val = d.sig_val
                        if val > waits.get(key, 0):
                            waits[key] = val
                    for key, val in waits.items():
                        if known.get(key, 0) >= val:
                            continue
                        sem = dsem[key[1]] if key[0] == "d" else esem[key[1]]
                        eng.wait_ge(sem, val)
                        known[key] = val
                    if ins.fn is None:
                        continue
                    bi = ins.fn(eng)
                    if ins.is_dma:
                        bi.then_inc(dsem[ins.dma_sem], 16)
                    elif ins.signal:
                        bi.then_inc(esem[(e, ins.epoch)], 1)

            @block.tensor
            def _(eng):
                run("pe", eng)

            @block.scalar
            def _(eng):
                run("act", eng)

            @block.vector
            def _(eng):
                run("dve", eng)

            @block.gpsimd
            def _(eng):
                run("pool", eng)

            @block.sync
            def _(eng):
                run("sp", eng)


class Cfg:
    def __init__(self, D=1024, S=8192, PLE=256):
        self.D, self.S, self.PLE = D, S, PLE
        self.KC = D // 128
        self.AW = D // 2
        self.H = self.AW // 64
        self.SGW = D // 2
        self.G = self.SGW // 64
        self.DFF = 4 * D
        self.FC = self.DFF // 128
        self.IPW = 3 * self.AW + self.H + 2 * self.SGW
        self.NB = S // 128
        self.J = self.NB // 8
        self.NO = 2 * self.J
        self.HH = self.H // 2
        self.PK = PLE // 128
        self.DH = max(1, D // 512)
        self.DW = min(D, 512)
        NO, J, NB = self.NO, self.J, self.NB
        self.groups = [list(range(i, min(i + 4, NO))) for i in range(0, NO, 4)]
        self.lo = [4 * j for j in range(J)] + [NB - 4 - 4 * j for j in reversed(range(J))]
        self.mtype = [0] * J + [1] * J

    def owned_blocks(self, r):
        J, NB = self.J, self.NB
        return [4 * j + r for j in range(J)] + [NB - 1 - 4 * j - r for j in reversed(range(J))]


class Arena:
    def __init__(self, ap, total):
        self.A = ap
        self.total = total
        self.top = 0

    def alloc(self, shape, dt):
        n = int(np.prod(shape[1:]))
        ne = n * (2 if dt == F32 else 1)
        ne = (ne + 15) // 16 * 16
        assert self.top + ne <= self.total, ("SBUF arena overflow", self.top, ne, self.total)
        v = self.A[:, self.top:self.top + (n * (2 if dt == F32 else 1))]
        self.top += ne
        if dt == F32:
            v = v.bitcast(F32)
        if len(shape) > 2:
            names = " ".join("a%d" % i for i in range(len(shape) - 1))
            kw = {"a%d" % i: int(shape[i + 1]) for i in range(len(shape) - 1)}
            v = v.rearrange("p (%s) -> p %s" % (names, names), **kw)
        if shape[0] < 128:
            v = v[0:shape[0]]
        return v


def _ap(t):
    return t.ap() if hasattr(t, "ap") else t[:]


def build_program(cfg, debug=False):
    c = cfg
    D, S, KC, AW, H, HH, SGW, G, NB, NO = c.D, c.S, c.KC, c.AW, c.H, c.HH, c.SGW, c.G, c.NB, c.NO
    DFF, FC, PLE, PK, DH, DW = c.DFF, c.FC, c.PLE, c.PK, c.DH, c.DW
    NG = len(c.groups)
    HP = H // 2
    HPP = HH // 2
    nc = bass.Bass("TRN2", target_bir_lowering=False)

    def din(name, shape):
        return nc.dram_tensor(name, list(shape), F32, kind="ExternalInput").ap()

    xfull = din("xfull", [S, D])
    xown = din("xown", [NO * 128, D])
    pown = din("pown", [NO * 128, PLE])
    w_in = din("w_in", [D, c.IPW])
    w_out = din("w_out", [D, D])
    w_ff1 = din("w_ff1", [D, DFF])
    w_ff2 = din("w_ff2", [DFF, D])
    ple_w = din("ple_w", [PLE, D])
    gate_w = din("gate_w", [D, D])
    gpm_d = din("gpm", [128, KC])
    gpf_d = din("gpf", [128, KC])
    gcat_d = din("gcat", [128, KC])
    gpmix_d = din("gpmix", [128, D])
    gpffn_d = din("gpffn", [128, D])
    gateb_d = din("gateb", [128, D])
    lng_d = din("lng", [128, SGW])
    lnb_d = din("lnb", [128, SGW])
    fbias_d = din("fbias", [128, H])
    sgwT_d = din("sgwT", [128, G, 128])
    sgbT_d = din("sgbT", [128, G])
    ident_d = din("ident", [128, 128])
    U_d = din("U", [128, 128])
    maskT_d = din("maskT", [128, 8, 128])
    sel_d = din("sel", [HH, HH, 128])
    LTfull_d = din("LTfull", [NB, NB])
    LTown_d = din("LTown", [NB, NO])
    out_d = nc.dram_tensor("out", [NO * 128, D], F32, kind="ExternalOutput").ap()
    h1_d = nc.dram_tensor("h1_scr", [NO * 128, D], F32).ap()
    h2_d = nc.dram_tensor("h2_scr", [NO * 128, D], F32).ap()
    dbg = {}

    total = (nc.sbuf_bytes_remaining - 2048) // 2
    total = total // 16 * 16
    arena_t = nc.alloc_sbuf_tensor("arena", [128, total], BF16)
    AR = Arena(_ap(arena_t), total)
    banks = [_ap(nc.alloc_psum_tensor("bank%d" % i, [128, 512], F32)) for i in range(8)]
    banksb = [b.bitcast(BF16) for b in banks]

    P = Prog(nc)
    rr = [0]

    def wq():
        return "sp"

    identf = AR.alloc([128, 128], F32)
    identb = AR.alloc([128, 128], BF16)
    Uf = AR.alloc([128, 128], F32)
    onesf = AR.alloc([128, 128], F32)
    maskb = AR.alloc([128, 8, 128], BF16)
    selb = AR.alloc([128, HH, 128], BF16)
    LTfull = AR.alloc([128, NB], F32)
    LTown = AR.alloc([128, NO], F32)
    gpm = AR.alloc([128, KC], F32)
    gpf = AR.alloc([128, KC], F32)
    gcat = AR.alloc([128, KC], F32)
    fbias = AR.alloc([128, H], F32)
    stats = AR.alloc([128, 64], F32)
    const_top = AR.top
    QT = AR.alloc([128, HP, NO * 128], BF16)
    within_own = AR.alloc([128, NO, H], F32)
    yatt = AR.alloc([128, NO, AW], F32)
    persist_top = AR.top

    def ld(dst, src, key, eng="sp"):
        return P.op(eng, lambda e: e.dma_start(out=dst, in_=src), w=[key], dma=True)

    ld(identf, ident_d, "identf")
    ld(Uf, U_d, "Uf")
    ld(LTfull[0:NB], LTfull_d, "LTfull")
    ld(LTown[0:NB], LTown_d, "LTown")
    ld(gpm, gpm_d, "gpm")
    ld(gpf, gpf_d, "gpf")
    ld(gcat, gcat_d, "gcat")
    ld(fbias, fbias_d, "fbias")
    P.op("pool", lambda e: e.dma_start(out=maskb, in_=maskT_d), w=["maskb"], dma=True)
    P.op("pool", lambda e: e.dma_start(out=selb[0:HH], in_=sel_d), w=["selb"], dma=True)
    P.op("dve", lambda e: e.tensor_copy(out=identb, in_=identf), r=["identf"], w=["identb"])
    P.op("dve", lambda e: e.memset(onesf, 1.0), w=["onesf"])

    scnt = [0]

    def stat_slot(n=1):
        s = scnt[0]
        scnt[0] = (scnt[0] + n) % 60
        if s + n > 60:
            s = 0
            scnt[0] = n
        return s

    def load_w(dst, src2d, key, kcn):
        for kc in range(kcn):
            P.op("pool", lambda e, kc=kc: e.dma_start(out=dst[:, kc, :], in_=src2d[kc * 128:(kc + 1) * 128, :]),
                 w=[key + str(kc)], dma=True)
        return [key + str(kc) for kc in range(kcn)]

    def rsqrt_ops(ss_ap, out_ap, n, scale, rkeys, wkeys):
        tmpslot = stat_slot(n)
        tmp = stats[:, tmpslot:tmpslot + n]
        tks = ["st%d" % (tmpslot + j) for j in range(n)]
        P.op("act", lambda e: e.activation(out=tmp, in_=ss_ap, func=AF.Ln, scale=scale, bias=EPS), r=rkeys, w=tks)
        P.op("act", lambda e: e.activation(out=out_ap, in_=tmp, func=AF.Exp, scale=-0.5), r=tks, w=wkeys)

    class Front:
        def __init__(self, nx=3, tpbanks=(0, 1), nxn=2):
            self.xt = [AR.alloc([128, D], F32) for _ in range(nx)]
            self.xn = [AR.alloc([128, D], BF16) for _ in range(nxn)]
            self.i = 0
            self.tpb = tpbanks

        def run(self, rows_ap, gain, dst, dkey, norm=True):
            g = self.gen(rows_ap, gain, dst, dkey, norm)
            for _ in g:
                pass
            return self.last

        def gen(self, rows_ap, gain, dst, dkey, norm=True):
            i = self.i
            self.i += 1
            xt = self.xt[i % len(self.xt)]
            xk = "xt%d_%d" % (id(self) % 1000, i % len(self.xt))
            xn = self.xn[i % len(self.xn)]
            nk = "xn%d_%d" % (id(self) % 1000, i % len(self.xn))
            tb = self.tpb[i % len(self.tpb)]
            tpv = banksb[tb][:, 0:KC * 128].rearrange("p (k t) -> p k t", t=128)
            tk = "bank%d" % tb
            self.last = (xt, xk)
            P.op("sp", lambda e: e.dma_start(out=xt, in_=rows_ap), w=[xk], dma=True)
            yield
            if norm:
                sl = stat_slot(2)
                ss = stats[:, sl:sl + 1]
                rs = stats[:, sl + 1:sl + 2]
                sk = "st%d" % sl
                rk = "st%d" % (sl + 1)
                P.op("act", lambda e: e.activation(out=xn, in_=xt, func=AF.Square, accum_out=ss), r=[xk], w=[nk, sk])
                yield
                rsqrt_ops(ss, rs, 1, 1.0 / D, [sk], [rk])
                yield
                P.op("dve", lambda e: e.tensor_scalar(out=xn, in0=xt, scalar1=rs, scalar2=None, op0=ALU.mult),
                     r=[xk, rk], w=[nk])
            else:
                P.op("dve", lambda e: e.tensor_copy(out=xn, in_=xt), r=[xk], w=[nk])
            yield
            for kc in range(KC):
                P.op("pe", lambda e, kc=kc: e.transpose(out=tpv[:, kc, :], in_=xn[:, kc * 128:(kc + 1) * 128], identity=identb),
                     r=[nk, "identb"], w=[tk])
            yield
            if gain is not None:
                gk = gain[1]
                gb = gain[0].unsqueeze(2).broadcast_to([128, KC, 128])
                P.op("dve", lambda e: e.tensor_tensor(out=dst, in0=tpv, in1=gb, op=ALU.mult), r=[tk, gk], w=[dkey])
            else:
                P.op("act", lambda e: e.activation(out=dst, in_=tpv, func=AF.Copy), r=[tk], w=[dkey])
            yield

    def interleave(gens, depth, period):
        it = iter(gens)
        active = []
        rounds = 0
        done = False
        while True:
            if not done and len(active) < depth and (rounds % period == 0 or not active):
                try:
                    active.append(next(it))
                except StopIteration:
                    done = True
            if not active:
                if done:
                    break
                continue
            for g in list(active):
                try:
                    next(g)
                except StopIteration:
                    active.remove(g)
            rounds += 1

    def softplus_neg(dst, src, n_keys_r, wkey, tmp):
        P.op("act", lambda e: e.activation(out=tmp, in_=src, func=AF.Exp, scale=-1.0), r=n_keys_r, w=[wkey + "_e"])
        P.op("act", lambda e: e.activation(out=dst, in_=tmp, func=AF.Ln, scale=1.0, bias=1.0), r=[wkey + "_e"], w=[wkey])

    qcol = 0
    kcol = AW
    vcol = 2 * AW
    fcol = 3 * AW
    ucol = 3 * AW + H
    vscol = ucol + SGW
    w_in_r = w_in

    mark = AR.top
    Wq = AR.alloc([128, KC, AW], BF16)
    Wfa = AR.alloc([128, KC, H], BF16)
    fr = Front(nx=3, tpbanks=(0, 1))
    aTq = [AR.alloc([128, KC, 512], BF16) for _ in range(2)]
    fown = AR.alloc([128, NO, H], F32)
    spo = AR.alloc([128, NO, H], F32)
    spo_e = AR.alloc([128, NO, H], F32)
    wqk = load_w(Wq, w_in_r[:, qcol:qcol + AW], "Wq", KC)
    wfk = load_w(Wfa, w_in_r[:, fcol:fcol + H], "Wfa", KC)
    for g, blocks in enumerate(c.groups):
        aT = aTq[g % 2]
        ak = "aTq%d" % (g % 2)
        N = len(blocks) * 128
        for ti, ob in enumerate(blocks):
            fr.run(xown[ob * 128:(ob + 1) * 128, :], (gpm, "gpm"), aT[:, :, ti * 128:(ti + 1) * 128], ak)
            for kc in range(KC):
                P.op("pe", lambda e, kc=kc, ti=ti, aT=aT: e.matmul(banks[4][:, 0:H], lhsT=aT[:, kc, ti * 128:(ti + 1) * 128], rhs=Wfa[:, kc, :],
                                                                   start=(kc == 0), stop=(kc == KC - 1)), r=[ak, wfk[kc]], w=["bank4"])
            P.op("dve", lambda e, ob=ob: e.tensor_tensor(out=fown[:, ob, :], in0=banks[4][:, 0:H], in1=fbias, op=ALU.add),
                 r=["bank4", "fbias"], w=["fown"])
        for hp in range(HP):
            qb = 2 + (hp % 2)
            for kc in range(KC):
                P.op("pe", lambda e, kc=kc, hp=hp, aT=aT, qb=qb, N=N: e.matmul(banks[qb][:, 0:N], lhsT=Wq[:, kc, hp * 128:(hp + 1) * 128], rhs=aT[:, kc, 0:N],
                                                                              start=(kc == 0), stop=(kc == KC - 1)), r=[ak, wqk[kc]], w=["bank%d" % qb])
            P.op("act", lambda e, hp=hp, qb=qb, g=g, N=N: e.activation(out=QT[:, hp, g * 512:g * 512 + N], in_=banks[qb][:, 0:N], func=AF.Copy, scale=0.125),
                 r=["bank%d" % qb], w=["QT"])
    softplus_neg(spo, fown, ["fown"], "spo", spo_e)
    P.op("pe", lambda e: e.matmul(banks[5][:, 0:NO * H], lhsT=Uf, rhs=spo.rearrange("p a b -> p (a b)"), start=True, stop=True),
         r=["Uf", "spo"], w=["bank5"])
    P.op("dve", lambda e: e.tensor_copy(out=within_own.rearrange("p a b -> p (a b)"), in_=banks[5][:, 0:NO * H]), r=["bank5"], w=["within_own"])
    P.barrier()
    AR.top = mark

    KT = AR.alloc([128, HPP, S], BF16)
    Vflat = AR.alloc([128, NB * HH * 65 + 64], BF16)
    Vaug = Vflat[:, 0:NB * HH * 65].rearrange("p (a b c) -> p a b c", a=NB, b=HH)
    biasT = AR.alloc([128, NG, NB, HH], F32)
    R8 = AR.alloc([128, NO * 128], BF16)
    rbc = AR.alloc([128, HH, NO * 128], BF16)
    attn_top = AR.top
    P.op("dve", lambda e: e.memset(Vaug[:, :, :, 64:65], 1.0), w=["Vones"])
    P.op("dve", lambda e: e.memset(Vflat[:, NB * HH * 65:NB * HH * 65 + 64], 0.0), w=["Vpad"])

    for hs in range(2):
        mark = AR.top
        fsb = AR.alloc([128, NB, HH], F32)
        spf = AR.alloc([128, NB, HH], F32)
        spf_e = AR.alloc([128, NB, HH], F32)
        wsb = AR.alloc([128, NB, HH], F32)
        Cpos = AR.alloc([128, NB, HH], F32)
        totT = AR.alloc([128, HH], F32)
        rhs_full = AR.alloc([128, NB, HH], F32)
        rhs_own = AR.alloc([128, NO, HH], F32)
        pexo = AR.alloc([128, NO, HH], F32)
        rt1 = AR.alloc([128, NO, HH], F32)
        Rtok = AR.alloc([128, NO, HH], F32)
        markA = AR.top
        Wk = AR.alloc([128, KC, HH * 64], BF16)
        VF = HH * 64 + HH
        Wv = AR.alloc([128, KC, VF], BF16)
        fr = Front(nx=4, tpbanks=(0, 1, 4, 7), nxn=3)
        aTa = [AR.alloc([128, KC, 512], BF16) for _ in range(2)]
        wkk = load_w(Wk, w_in_r[:, kcol + hs * HH * 64: kcol + (hs + 1) * HH * 64], "Wk", KC)
        wvk = []
        for kc in range(KC):
            P.op("pool", lambda e, kc=kc, hs=hs, Wv=Wv: e.dma_start(out=Wv[:, kc, 0:HH * 64], in_=w_in_r[kc * 128:(kc + 1) * 128, vcol + hs * HH * 64: vcol + (hs + 1) * HH * 64]),
                 w=["Wv%da" % kc], dma=True)
            P.op("pool", lambda e, kc=kc, hs=hs, Wv=Wv: e.dma_start(out=Wv[:, kc, HH * 64:VF], in_=w_in_r[kc * 128:(kc + 1) * 128, fcol + hs * HH: fcol + (hs + 1) * HH]),
                 w=["Wv%db" % kc], dma=True)
            wvk.append(["Wv%da" % kc, "Wv%db" % kc])
        nst = (NB + 3) // 4

        def tileA(t, hs=hs, Wk=Wk, Wv=Wv, fsb=fsb, aTa=aTa, fr=fr, wkk=wkk, wvk=wvk):
            st, ti = t // 4, t % 4
            aT = aTa[st % 2]
            aks = ["aTa%d_%d" % (st % 2, j) for j in range(4)]
            ak = aks[ti]
            for _ in fr.gen(xfull[t * 128:(t + 1) * 128, :], (gpm, "gpm"), aT[:, :, ti * 128:(ti + 1) * 128], ak):
                yield
            vb = 2 + (t % 2)
            vk = "bank%d" % vb
            for kc in range(KC):
                P.op("pe", lambda e, kc=kc: e.matmul(banks[vb][:, 0:VF], lhsT=aT[:, kc, ti * 128:(ti + 1) * 128], rhs=Wv[:, kc, :],
                                                     start=(kc == 0), stop=(kc == KC - 1)), r=[ak] + wvk[kc], w=[vk])
            yield
            P.op("act", lambda e: e.activation(out=Vaug[:, t, :, 0:64], in_=banks[vb][:, 0:HH * 64].rearrange("p (h d) -> p h d", d=64), func=AF.Copy),
                 r=[vk], w=["V%d" % t])
            yield
            P.op("act", lambda e: e.activation(out=fsb[:, t, :], in_=banks[vb][:, HH * 64:VF], func=AF.Copy), r=[vk], w=["fsb"])
            yield
            if ti == 3 or t == NB - 1:
                N = (ti + 1) * 128
                for hpl in range(HPP):
                    kb_ = 5 + (hpl % 2)
                    for kc in range(KC):
                        P.op("pe", lambda e, kc=kc, hpl=hpl, kb_=kb_: e.matmul(banks[kb_][:, 0:N], lhsT=Wk[:, kc, hpl * 128:(hpl + 1) * 128], rhs=aT[:, kc, 0:N],
                                                                               start=(kc == 0), stop=(kc == KC - 1)), r=aks[0:ti + 1] + [wkk[kc]], w=["bank%d" % kb_])
                    yield
                    P.op("act", lambda e, hpl=hpl, kb_=kb_: e.activation(out=KT[:, hpl, st * 512:st * 512 + N], in_=banks[kb_][:, 0:N], func=AF.Copy),
                         r=["bank%d" % kb_], w=["KT%d" % st])
                    yield

        interleave((tileA(t) for t in range(NB)), 4, 2)

        P.op("dve", lambda e, hs=hs, fsb=fsb: e.tensor_tensor(out=fsb, in0=fsb, in1=fbias[:, hs * HH:(hs + 1) * HH].unsqueeze(1).broadcast_to([128, NB, HH]), op=ALU.add),
             r=["fsb", "fbias"], w=["fsb"])
        softplus_neg(spf, fsb, ["fsb"], "spf", spf_e)
        spf2 = spf.rearrange("p a b -> p (a b)")
        P.op("pe", lambda e: e.matmul(banks[0][:, 0:NB * HH], lhsT=Uf, rhs=spf2, start=True, stop=True), r=["Uf", "spf"], w=["bank0"])
        P.op("dve", lambda e: e.tensor_copy(out=wsb.rearrange("p a b -> p (a b)"), in_=banks[0][:, 0:NB * HH]), r=["bank0"], w=["wsb"])
        for hh in range(HH):
            P.op("pe", lambda e, hh=hh: e.matmul(banks[1][0:NB, hh:hh + 1], lhsT=spf[:, :, hh], rhs=onesf[:, 0:1], start=True, stop=True),
                 r=["spf", "onesf"], w=["bank1"])
        P.op("dve", lambda e: e.tensor_copy(out=totT[0:NB, :], in_=banks[1][0:NB, 0:HH]), r=["bank1"], w=["totT"])
        P.op("dve", lambda e: e.tensor_tensor(out=rhs_full[0:NB], in0=LTfull[0:NB].unsqueeze(2).broadcast_to([NB, NB, HH]),
                                              in1=totT[0:NB].unsqueeze(1).broadcast_to([NB, NB, HH]), op=ALU.mult), r=["LTfull", "totT"], w=["rhs_full"])
        P.op("dve", lambda e: e.tensor_tensor(out=rhs_own[0:NB], in0=LTown[0:NB].unsqueeze(2).broadcast_to([NB, NO, HH]),
                                              in1=totT[0:NB].unsqueeze(1).broadcast_to([NB, NO, HH]), op=ALU.mult), r=["LTown", "totT"], w=["rhs_own"])
        P.op("pe", lambda e: e.matmul(banks[2][:, 0:NB * HH], lhsT=onesf[0:NB, :], rhs=rhs_full[0:NB].rearrange("p a b -> p (a b)"), start=True, stop=True),
             r=["onesf", "rhs_full"], w=["bank2"])
        P.op("pe", lambda e: e.matmul(banks[3][:, 0:NO * HH], lhsT=onesf[0:NB, :], rhs=rhs_own[0:NB].rearrange("p a b -> p (a b)"), start=True, stop=True),
             r=["onesf", "rhs_own"], w=["bank3"])
        P.op("dve", lambda e: e.tensor_tensor(out=Cpos.rearrange("p a b -> p (a b)"), in0=banks[2][:, 0:NB * HH], in1=wsb.rearrange("p a b -> p (a b)"), op=ALU.add),
             r=["bank2", "wsb"], w=["Cpos"])
        P.op("dve", lambda e: e.tensor_copy(out=pexo.rearrange("p a b -> p (a b)"), in_=banks[3][:, 0:NO * HH]), r=["bank3"], w=["pexo"])
        for g, blocks in enumerate(c.groups):
            g0 = blocks[0]
            nb = len(blocks)
            P.op("dve", lambda e, g=g, g0=g0: e.tensor_tensor(out=biasT[:, g, :, :], in0=Cpos, in1=pexo[:, g0:g0 + 1, :].broadcast_to([128, NB, HH]), op=ALU.subtract),
                 r=["Cpos", "pexo"], w=["biasT"])
            P.op("dve", lambda e, g0=g0, nb=nb: e.tensor_tensor(out=rt1[:, g0:g0 + nb, :], in0=pexo[:, g0:g0 + 1, :].broadcast_to([128, nb, HH]), in1=pexo[:, g0:g0 + nb, :], op=ALU.subtract),
                 r=["pexo"], w=["rt1"])
            P.op("dve", lambda e, g0=g0, nb=nb, hs=hs: e.tensor_tensor(out=Rtok[:, g0:g0 + nb, :], in0=rt1[:, g0:g0 + nb, :], in1=within_own[:, g0:g0 + nb, hs * HH:(hs + 1) * HH], op=ALU.subtract),
                 r=["rt1", "within_own"], w=["Rtok"])
            for ti, ob in enumerate(blocks):
                P.op("pe", lambda e, ti=ti, ob=ob: e.matmul(banks[4][0:HH, ti * 128:(ti + 1) * 128], lhsT=Rtok[:, ob, :], rhs=identf, start=True, stop=True),
                     r=["Rtok", "identf"], w=["bank4"])
            P.op("dve", lambda e, g0=g0, nb=nb: e.tensor_copy(out=R8[0:HH, g0 * 128:(g0 + nb) * 128], in_=banks[4][0:HH, 0:nb * 128]), r=["bank4"], w=["R8"])

        for hh in range(HH):
            for g, blocks in enumerate(c.groups):
                g0, nb = blocks[0], len(blocks)
                rb_ = 5 + ((hh * NG + g) % 2)
                P.op("pe", lambda e, hh=hh, g0=g0, nb=nb, rb_=rb_: e.matmul(banks[rb_][:, 0:nb * 128], lhsT=selb[0:HH, hh, :], rhs=R8[0:HH, g0 * 128:(g0 + nb) * 128], start=True, stop=True),
                     r=["selb", "R8"], w=["bank%d" % rb_])
                P.op("dve", lambda e, hh=hh, g0=g0, nb=nb, rb_=rb_: e.tensor_copy(out=rbc[:, hh, g0 * 128:(g0 + nb) * 128], in_=banks[rb_][:, 0:nb * 128]),
                     r=["bank%d" % rb_], w=["rbc"])

        P.barrier()
        AR.top = markA
        NPT = 6
        QTp = AR.alloc([128, HH, NO * 128], BF16)
        P.op("pool", lambda e, QTp=QTp: e.memset(QTp, 0.0), w=["QTp"])
        for hh in range(HH):
            h_ = hs * HH + hh
            e2 = h_ % 2
            P.op("dve", lambda e, hh=hh, h_=h_, e2=e2, QTp=QTp: e.tensor_copy(out=QTp[e2 * 64:(e2 + 1) * 64, hh, :], in_=QT[e2 * 64:(e2 + 1) * 64, h_ // 2, :]),
                 r=["QT", "QTp"], w=["QTp"])
        pts = [AR.alloc([128, 512], BF16) for _ in range(NPT)]
        osb = [AR.alloc([128, 512], F32) for _ in range(2)]
        rc = AR.alloc([128, 8], F32)
        kmaxs = [c.lo[blocks[-1]] + 3 for blocks in c.groups]
        batches = []
        for hh in range(HH):
            for kb in range(NB):
                act_g = [g for g in range(NG) if kb <= kmaxs[g]]
                for j in range(0, len(act_g), 1):
                    batches.append((hh, kb, act_g[j:j + 1]))
        epi = [0]

        def geom(g, kb):
            blocks = c.groups[g]
            fa = 0
            while c.lo[blocks[fa]] + 3 < kb:
                fa += 1
            N = (len(blocks) - fa) * 128
            c0 = blocks[fa] * 128
            msk = [(bi, kb - c.lo[ob]) for bi, ob in enumerate(blocks) if bi >= fa and c.lo[ob] <= kb <= c.lo[ob] + 3]
            return blocks, fa, N, c0, msk

        def emit_S(i):
            hh, kb, gs = batches[i]
            h = hs * HH + hh
            hpg, e_, hpl = h // 2, h % 2, hh // 2
            info = []
            for j, g in enumerate(gs):
                blocks, fa, N, c0, msk = geom(g, kb)
                sb = i % 4
                info.append((g, blocks, fa, N, c0, msk, sb))
            for (g, blocks, fa, N, c0, msk, sb) in info:
                P.op("pe", lambda e, N=N, c0=c0, sb=sb, nm=len(msk): e.matmul(banks[sb][:, 0:N], lhsT=KT[:, hpl, kb * 128:(kb + 1) * 128],
                                                                rhs=QTp[:, hh, c0:c0 + N], start=True, stop=(nm == 0)),
                     r=["KT%d" % (kb // 4), "QTp"], w=["bank%d" % sb])
            for (g, blocks, fa, N, c0, msk, sb) in info:
                for mi, (bi, i4) in enumerate(msk):
                    mt = c.mtype[blocks[bi]] * 4 + i4
                    P.op("pe", lambda e, bi=bi, mt=mt, mi=mi, fa=fa, sb=sb, nm=len(msk): e.matmul(banks[sb][:, (bi - fa) * 128:(bi - fa + 1) * 128], lhsT=identb, rhs=maskb[:, mt, :],
                                                                                              start=False, stop=(mi == nm - 1)), r=["identb", "maskb"], w=["bank%d" % sb])
            for (g, blocks, fa, N, c0, msk, sb) in info:
                P.op("dve", lambda e, N=N, c0=c0, sb=sb: e.tensor_tensor(out=banks[sb][:, 0:N], in0=banks[sb][:, 0:N], in1=rbc[:, hh, c0:c0 + N], op=ALU.add),
                     r=["bank%d" % sb, "rbc"], w=["bank%d" % sb])
            for j, (g, blocks, fa, N, c0, msk, sb) in enumerate(info):
                pi = i % NPT
                P.op("act", lambda e, N=N, sb=sb, g=g, pi=pi: e.activation(out=pts[pi][:, 0:N], in_=banks[sb][:, 0:N], func=AF.Exp, bias=biasT[:, g, kb, hh:hh + 1], scale=1.0),
                     r=["bank%d" % sb, "biasT"], w=["pt%d" % pi])

        def emit_PV(i):
            hh, kb, gs = batches[i]
            h = hs * HH + hh
            for j, g in enumerate(gs):
                blocks, fa, N, c0, msk = geom(g, kb)
                ob_ = 4 + g
                ok = "bank%d" % ob_
                pi = i % NPT
                last = (kb == kmaxs[g])
                vo = (kb * HH + hh) * 65
                vkeys = ["V%d" % kb, "Vones", "Vpad"] + (["V%d" % (kb + 1)] if kb + 1 < NB else [])
                P.op("pe", lambda e, fa=fa, N=N, ob_=ob_, pi=pi, last=last, vo=vo: e.matmul(banks[ob_][:, fa * 128:fa * 128 + N], lhsT=Vflat[:, vo:vo + 128], rhs=pts[pi][:, 0:N], start=(kb == 0), stop=last),
                     r=vkeys + ["pt%d" % pi], w=[ok])
            for j, g in enumerate(gs):
                if kb != kmaxs[g]:
                    continue
                blocks = c.groups[g]
                ob_ = 4 + g
                ok = "bank%d" % ob_
                nb = len(blocks)
                ei = epi[0] % 2
                epi[0] += 1
                os_ = osb[ei]
                osk = "osb%d" % ei
                tb = i % 4
                tk = "bank%d" % tb
                P.op("dve", lambda e, os_=os_, ob_=ob_, nb=nb: e.tensor_copy(out=os_[0:65, 0:nb * 128], in_=banks[ob_][0:65, 0:nb * 128]), r=[ok], w=[osk])
                for ti, ob in enumerate(blocks):
                    P.op("pe", lambda e, ti=ti, os_=os_, tb=tb: e.matmul(banks[tb][:, ti * 65:(ti + 1) * 65], lhsT=os_[0:65, ti * 128:(ti + 1) * 128], rhs=identf[0:65, 0:65], start=True, stop=True),
                         r=[osk, "identf"], w=[tk])
                o3 = banks[tb][:, 0:nb * 65].rearrange("p (b x) -> p b x", x=65)
                P.op("dve", lambda e, o3=o3, nb=nb: e.reciprocal(out=rc[:, 0:nb], in_=o3[:, :, 64]), r=[tk], w=["rc"])
                for ti, ob in enumerate(blocks):
                    P.op("dve", lambda e, ti=ti, ob=ob, o3=o3, h=h: e.tensor_scalar(out=yatt[:, ob, h * 64:(h + 1) * 64], in0=o3[:, ti, 0:64], scalar1=rc[:, ti:ti + 1], scalar2=None, op0=ALU.mult),
                         r=[tk, "rc"], w=["yatt"])

        SKEW = 3
        for i in range(len(batches)):
            emit_S(i)
            if i >= SKEW:
                emit_PV(i - SKEW)
        for i in range(max(0, len(batches) - SKEW), len(batches)):
            emit_PV(i)
        P.barrier()
        AR.top = mark

    AR.top = attn_top - 0
    AR.top = persist_top
    C0 = 0.7978845608028654
    C1_ = 0.044715
    Wu = AR.alloc([128, KC, SGW], BF16)
    Wvs = AR.alloc([128, KC, SGW], BF16)
    Wo = AR.alloc([128, KC, D], BF16)
    wsT = AR.alloc([128, G, 128], BF16)
    wsTf = AR.alloc([128, G, 128], F32)
    sgb = AR.alloc([128, G], F32)
    lng = AR.alloc([128, SGW], F32)
    lnb = AR.alloc([128, SGW], F32)
    gpmix = AR.alloc([128, D], F32)
    fr = Front(nx=3, tpbanks=(0, 1))
    aTc = [AR.alloc([128, KC, 128], BF16) for _ in range(2)]
    TN = ["x2", "tA", "tB", "gu", "gv", "xc", "vln", "tmix", "ysg"]
    TT = [{n: AR.alloc([128, SGW], F32) for n in TN} for _ in range(2)]
    for p_ in range(2):
        TT[p_]["vlnb"] = AR.alloc([128, SGW], BF16)
        TT[p_]["junk"] = AR.alloc([128, D], BF16)
        TT[p_]["yn"] = AR.alloc([128, D], BF16)
        TT[p_]["ynT"] = AR.alloc([128, KC, 128], BF16)
        TT[p_]["h1"] = AR.alloc([128, D], F32)
    wuk = load_w(Wu, w_in_r[:, ucol:ucol + SGW], "Wu", KC)
    wvsk = load_w(Wvs, w_in_r[:, vscol:vscol + SGW], "Wvs", KC)
    wok = load_w(Wo, w_out, "Wo", KC)
    ld(wsTf, sgwT_d, "wsTf")
    ld(sgb, sgbT_d, "sgb")
    ld(lng, lng_d, "lng")
    ld(lnb, lnb_d, "lnb")
    ld(gpmix, gpmix_d, "gpmix")
    P.op("dve", lambda e: e.tensor_tensor(out=wsT, in0=wsTf, in1=Uf.unsqueeze(1).broadcast_to([128, G, 128]), op=ALU.mult), r=["wsTf", "Uf"], w=["wsT"])

    def gelu2(T, p, dst, dk, ps, pk):
        x2, tA, tB = T["x2"], T["tA"], T["tB"]
        kx, ka, kb2 = "x2_%d" % p, "tA_%d" % p, "tB_%d" % p
        P.op("act", lambda e: e.activation(out=x2, in_=ps, func=AF.Square), r=[pk], w=[kx])
        yield
        P.op("dve", lambda e: e.tensor_scalar(out=tA, in0=x2, scalar1=C1_, scalar2=1.0, op0=ALU.mult, op1=ALU.add), r=[kx], w=[ka])
        yield
        P.op("dve", lambda e: e.tensor_tensor(out=tB, in0=tA, in1=ps, op=ALU.mult), r=[ka, pk], w=[kb2])
        yield
        P.op("act", lambda e: e.activation(out=tA, in_=tB, func=AF.Tanh, scale=C0), r=[kb2], w=[ka])
        yield
        P.op("dve", lambda e: e.scalar_tensor_tensor(out=dst, in0=tA, scalar=1.0, in1=ps, op0=ALU.add, op1=ALU.mult), r=[ka, pk], w=[dk])
        yield

    def tileC1(ob):
        p = ob % 2
        T = TT[p]
        aT = aTc[p]
        ak = "aTc%d" % p
        bX, bY = 2 + 3 * p, 3 + 3 * p
        bT = 4 + 3 * p
        kX, kY, kT = "bank%d" % bX, "bank%d" % bY, "bank%d" % bT
        K = lambda n: "%s_%d" % (n, p)
        gu, gv, xc, vln, vlnb, tmix, ysg, junk, yn, ynT, h1 = (T[n] for n in ("gu", "gv", "xc", "vln", "vlnb", "tmix", "ysg", "junk", "yn", "ynT", "h1"))
        for _ in fr.gen(xown[ob * 128:(ob + 1) * 128, :], (gpm, "gpm"), aT, ak):
            yield
        xt, xk = fr.last
        for kc in range(KC):
            P.op("pe", lambda e, kc=kc: e.matmul(banks[bX][:, 0:SGW], lhsT=aT[:, kc, :], rhs=Wu[:, kc, :], start=(kc == 0), stop=(kc == KC - 1)),
                 r=[ak, wuk[kc]], w=[kX])
        for kc in range(KC):
            P.op("pe", lambda e, kc=kc: e.matmul(banks[bY][:, 0:SGW], lhsT=aT[:, kc, :], rhs=Wvs[:, kc, :], start=(kc == 0), stop=(kc == KC - 1)),
                 r=[ak, wvsk[kc]], w=[kY])
        yield
        for _ in gelu2(T, p, gv, K("gv"), banks[bY][:, 0:SGW], kY):
            yield
        for _ in gelu2(T, p, gu, K("gu"), banks[bX][:, 0:SGW], kX):
            yield
        sl = stat_slot(6)
        s1 = stats[:, sl:sl + 1]
        nm = stats[:, sl + 1:sl + 2]
        s2 = stats[:, sl + 2:sl + 3]
        r2 = stats[:, sl + 3:sl + 4]
        k_ = ["st%d" % (sl + j) for j in range(6)]
        P.op("dve", lambda e: e.reduce_sum(out=s1, in_=gv, axis=AX.X), r=[K("gv")], w=[k_[0]])
        yield
        P.op("dve", lambda e: e.tensor_scalar(out=nm, in0=s1, scalar1=-0.5 / SGW, scalar2=None, op0=ALU.mult), r=[k_[0]], w=[k_[1]])
        yield
        P.op("dve", lambda e: e.tensor_scalar(out=xc, in0=gv, scalar1=0.5, scalar2=nm, op0=ALU.mult, op1=ALU.add), r=[K("gv"), k_[1]], w=[K("xc")])
        yield
        P.op("act", lambda e: e.activation(out=junk[:, 0:SGW], in_=xc, func=AF.Square, accum_out=s2), r=[K("xc")], w=[K("junk"), k_[2]])
        yield
        rsqrt_ops(s2, r2, 1, 1.0 / SGW, [k_[2]], [k_[3]])
        yield
        P.op("dve", lambda e: e.scalar_tensor_tensor(out=vln, in0=xc, scalar=r2, in1=lng, op0=ALU.mult, op1=ALU.mult), r=[K("xc"), k_[3], "lng"], w=[K("vln")])
        yield
        P.op("dve", lambda e: e.tensor_tensor(out=vlnb, in0=vln, in1=lnb, op=ALU.add), r=[K("vln"), "lnb"], w=[K("vlnb")])
        yield
        for g8 in range(G):
            P.op("pe", lambda e, g8=g8: e.matmul(banks[bY][:, g8 * 64:(g8 + 1) * 64], lhsT=wsT[:, g8, :], rhs=vlnb[:, g8 * 64:(g8 + 1) * 64], start=True, stop=True),
                 r=["wsT", K("vlnb")], w=[kY])
        yield
        P.op("dve", lambda e: e.tensor_tensor(out=tmix.rearrange("p (g d) -> p g d", d=64), in0=banks[bY][:, 0:SGW].rearrange("p (g d) -> p g d", d=64),
                                              in1=sgb.unsqueeze(2).broadcast_to([128, G, 64]), op=ALU.add), r=[kY, "sgb"], w=[K("tmix")])
        yield
        P.op("dve", lambda e: e.scalar_tensor_tensor(out=ysg, in0=gu, scalar=0.5, in1=tmix, op0=ALU.mult, op1=ALU.mult), r=[K("gu"), K("tmix")], w=[K("ysg")])
        yield
        sl2 = stat_slot(4)
        k2 = ["st%d" % (sl2 + j) for j in range(4)]
        ssq = stats[:, sl2:sl2 + 2]
        rsq = stats[:, sl2 + 2:sl2 + 4]
        P.op("act", lambda e: e.activation(out=junk[:, 0:AW], in_=yatt[:, ob, :], func=AF.Square, accum_out=ssq[:, 0:1]), r=["yatt"], w=[K("junk"), k2[0]])
        yield
        P.op("act", lambda e: e.activation(out=junk[:, 0:SGW], in_=ysg, func=AF.Square, accum_out=ssq[:, 1:2]), r=[K("ysg")], w=[K("junk"), k2[1]])
        yield
        rsqrt_ops(ssq, rsq, 2, 1.0 / AW, [k2[0], k2[1]], [k2[2], k2[3]])
        yield
        P.op("dve", lambda e: e.tensor_scalar(out=yn[:, 0:AW], in0=yatt[:, ob, :], scalar1=rsq[:, 0:1], scalar2=None, op0=ALU.mult), r=["yatt", k2[2]], w=[K("yn")])
        yield
        P.op("dve", lambda e: e.tensor_scalar(out=yn[:, AW:D], in0=ysg, scalar1=rsq[:, 1:2], scalar2=None, op0=ALU.mult), r=[K("ysg"), k2[3]], w=[K("yn")])
        yield
        tpv = banksb[bT][:, 0:KC * 128].rearrange("p (k t) -> p k t", t=128)
        for kc in range(KC):
            P.op("pe", lambda e, kc=kc: e.transpose(out=tpv[:, kc, :], in_=yn[:, kc * 128:(kc + 1) * 128], identity=identb), r=[K("yn"), "identb"], w=[kT])
        yield
        P.op("dve", lambda e: e.tensor_tensor(out=ynT, in0=tpv, in1=gcat.unsqueeze(2).broadcast_to([128, KC, 128]), op=ALU.mult), r=[kT, "gcat"], w=[K("ynT")])
        yield
        sl3 = stat_slot(4)
        k3 = ["st%d" % (sl3 + j) for j in range(4)]
        obanks = [bX, bT] if DH == 2 else [bX]
        for dh in range(DH):
            ob_ = obanks[dh]
            for kc in range(KC):
                P.op("pe", lambda e, kc=kc, dh=dh, ob_=ob_: e.matmul(banks[ob_][:, 0:DW], lhsT=ynT[:, kc, :], rhs=Wo[:, kc, dh * DW:(dh + 1) * DW], start=(kc == 0), stop=(kc == KC - 1)),
                     r=[K("ynT"), wok[kc]], w=["bank%d" % ob_])
            yield
            P.op("act", lambda e, dh=dh, ob_=ob_: e.activation(out=junk[:, 0:DW], in_=banks[ob_][:, 0:DW], func=AF.Square, accum_out=stats[:, sl3 + dh:sl3 + dh + 1]),
                 r=["bank%d" % ob_], w=[K("junk"), k3[dh]])
            yield
        if DH == 2:
            P.op("dve", lambda e: e.tensor_tensor(out=stats[:, sl3 + 2:sl3 + 3], in0=stats[:, sl3:sl3 + 1], in1=stats[:, sl3 + 1:sl3 + 2], op=ALU.add), r=[k3[0], k3[1]], w=[k3[2]])
            yield
            sso = stats[:, sl3 + 2:sl3 + 3]
            ssk = k3[2]
        else:
            sso = stats[:, sl3:sl3 + 1]
            ssk = k3[0]
        rso = stats[:, sl3 + 3:sl3 + 4]
        rsqrt_ops(sso, rso, 1, 1.0 / D, [ssk], [k3[3]])
        yield
        hk = K("h1")
        for dh in range(DH):
            ob_ = obanks[dh]
            P.op("dve", lambda e, dh=dh, ob_=ob_: e.scalar_tensor_tensor(out=h1[:, dh * DW:(dh + 1) * DW], in0=banks[ob_][:, 0:DW], scalar=rso, in1=gpmix[:, dh * DW:(dh + 1) * DW], op0=ALU.mult, op1=ALU.mult),
                 r=["bank%d" % ob_, k3[3], "gpmix"], w=[hk])
            yield
        P.op("dve", lambda e: e.tensor_tensor(out=h1, in0=h1, in1=xt, op=ALU.add), r=[hk, xk], w=[hk])
        yield
        P.op("sp", lambda e: e.dma_start(out=h1_d[ob * 128:(ob + 1) * 128, :], in_=h1), r=[hk], w=["h1d%d" % ob], dma=True)
        yield

    interleave((tileC1(ob) for ob in range(NO)), 2, 20)
    P.barrier()
    AR.top = const_top

    W1 = AR.alloc([128, KC, DFF], BF16)
    W2 = AR.alloc([128, FC, D], BF16)
    HT = AR.alloc([128, FC, 512], BF16)
    cT = AR.alloc([128, KC, 512], BF16)
    gpffn = AR.alloc([128, D], F32)
    fr = Front(nx=2, tpbanks=(0, 1))
    rtmp = [AR.alloc([128, 512], BF16) for _ in range(2)]
    h1r = [AR.alloc([128, D], F32) for _ in range(2)]
    o2t = [AR.alloc([128, D], F32) for _ in range(1)]
    junk2 = AR.alloc([128, 512], BF16)
    w1k = load_w(W1, w_ff1, "W1", KC)
    w2k = load_w(W2, w_ff2, "W2", FC)
    ld(gpffn, gpffn_d, "gpffn")
    tcount = 0
    for g, blocks in enumerate(c.groups):
        N = len(blocks) * 128
        interleave((fr.gen(h1_d[ob * 128:(ob + 1) * 128, :], (gpf, "gpf"), cT[:, :, ti * 128:(ti + 1) * 128], "cT%d" % ti) for ti, ob in enumerate(blocks)), 2, 2)
        for fc in range(FC):
            hb = 2 + fc % 2
            for kc in range(KC):
                P.op("pe", lambda e, kc=kc, fc=fc, hb=hb, N=N: e.matmul(banks[hb][:, 0:N], lhsT=W1[:, kc, fc * 128:(fc + 1) * 128], rhs=cT[:, kc, 0:N], start=(kc == 0), stop=(kc == KC - 1)),
                     r=["cT%d" % j for j in range(len(blocks))] + [w1k[kc]], w=["bank%d" % hb])
            rt = rtmp[fc % 2]
            rk = "rtmp%d" % (fc % 2)
            P.op("act", lambda e, hb=hb, rt=rt, N=N: e.activation(out=rt[:, 0:N], in_=banks[hb][:, 0:N], func=AF.Relu), r=["bank%d" % hb], w=[rk])
            P.op("dve", lambda e, fc=fc, rt=rt, N=N: e.tensor_tensor(out=HT[:, fc, 0:N], in0=rt[:, 0:N], in1=rt[:, 0:N], op=ALU.mult), r=[rk], w=["HT"])
        for ti, ob in enumerate(blocks):
            i2 = tcount % 2
            tcount += 1
            h1 = h1r[i2]
            hk = "h1r%d" % i2
            P.op("sp", lambda e, h1=h1, ob=ob: e.dma_start(out=h1, in_=h1_d[ob * 128:(ob + 1) * 128, :]), r=["h1d%d" % ob], w=[hk], dma=True)
            sl3 = stat_slot(4)
            k3 = ["st%d" % (sl3 + j) for j in range(4)]
            for dh in range(DH):
                ob_ = 4 + 2 * i2 + dh
                for fc in range(FC):
                    P.op("pe", lambda e, fc=fc, dh=dh, ob_=ob_, ti=ti: e.matmul(banks[ob_][:, 0:DW], lhsT=HT[:, fc, ti * 128:(ti + 1) * 128], rhs=W2[:, fc, dh * DW:(dh + 1) * DW], start=(fc == 0), stop=(fc == FC - 1)),
                         r=["HT", w2k[fc]], w=["bank%d" % ob_])
                P.op("act", lambda e, dh=dh, ob_=ob_, sl3=sl3: e.activation(out=junk2[:, 0:DW], in_=banks[ob_][:, 0:DW], func=AF.Square, accum_out=stats[:, sl3 + dh:sl3 + dh + 1]),
                     r=["bank%d" % ob_], w=["junk2", k3[dh]])
            if DH == 2:
                P.op("dve", lambda e, sl3=sl3: e.tensor_tensor(out=stats[:, sl3 + 2:sl3 + 3], in0=stats[:, sl3:sl3 + 1], in1=stats[:, sl3 + 1:sl3 + 2], op=ALU.add), r=[k3[0], k3[1]], w=[k3[2]])
                sso = stats[:, sl3 + 2:sl3 + 3]
                ssk = k3[2]
            else:
                sso = stats[:, sl3:sl3 + 1]
                ssk = k3[0]
            rso = stats[:, sl3 + 3:sl3 + 4]
            rsqrt_ops(sso, rso, 1, 1.0 / D, [ssk], [k3[3]])
            o2 = o2t[0]
            ok2 = "o2t0"
            for dh in range(DH):
                ob_ = 4 + 2 * i2 + dh
                P.op("dve", lambda e, dh=dh, ob_=ob_, rso=rso, o2=o2: e.scalar_tensor_tensor(out=o2[:, dh * DW:(dh + 1) * DW], in0=banks[ob_][:, 0:DW], scalar=rso, in1=gpffn[:, dh * DW:(dh + 1) * DW], op0=ALU.mult, op1=ALU.mult),
                     r=["bank%d" % ob_, k3[3], "gpffn"], w=[ok2])
            P.op("dve", lambda e, o2=o2, h1=h1: e.tensor_tensor(out=h1, in0=o2, in1=h1, op=ALU.add), r=[ok2, hk], w=[hk])
            P.op("sp", lambda e, h1=h1, ob=ob: e.dma_start(out=h2_d[ob * 128:(ob + 1) * 128, :], in_=h1), r=[hk], w=["h2d%d" % ob], dma=True)
    P.barrier()
    AR.top = const_top

    Wg = AR.alloc([128, KC, D], BF16)
    Wpe = AR.alloc([128, PK, D], BF16)
    gateb = AR.alloc([128, D], F32)
    ND3 = 3
    fr3 = Front(nx=ND3, tpbanks=(0, 1), nxn=ND3)
    S3 = []
    for _ in range(ND3):
        S3.append(dict(h2T=AR.alloc([128, KC, 128], BF16), pt=AR.alloc([128, PLE], F32), pb=AR.alloc([128, PLE], BF16),
                       pT=AR.alloc([128, PK, 128], BF16), z=AR.alloc([128, D], F32), pp=AR.alloc([128, D], F32), o=AR.alloc([128, D], F32)))
    wgk = load_w(Wg, gate_w, "Wg", KC)
    wpk = load_w(Wpe, ple_w, "Wpe", PK)
    ld(gateb, gateb_d, "gateb")
    out_dmas = []
    brot = [0]

    def nbank():
        b_ = 2 + brot[0] % 6
        brot[0] += 1
        return b_

    def tileC3(ob):
        p = ob % ND3
        T = S3[p]
        K = lambda n: "%s3_%d" % (n, p)
        h2T, pt_, pb_, pT_, z, pp, o_ = T["h2T"], T["pt"], T["pb"], T["pT"], T["z"], T["pp"], T["o"]
        P.op("sp", lambda e: e.dma_start(out=pt_, in_=pown[ob * 128:(ob + 1) * 128, :]), w=[K("pt")], dma=True)
        for _ in fr3.gen(h2_d[ob * 128:(ob + 1) * 128, :], None, h2T, K("h2T"), norm=False):
            yield
        xt, xk = fr3.last
        P.op("dve", lambda e: e.tensor_copy(out=pb_, in_=pt_), r=[K("pt")], w=[K("pb")])
        yield
        tb_ = nbank()
        tpv3 = banksb[tb_][:, 0:PK * 128].rearrange("p (k t) -> p k t", t=128)
        for k2_ in range(PK):
            P.op("pe", lambda e, k2_=k2_: e.transpose(out=tpv3[:, k2_, :], in_=pb_[:, k2_ * 128:(k2_ + 1) * 128], identity=identb), r=[K("pb"), "identb"], w=["bank%d" % tb_])
        yield
        P.op("act", lambda e: e.activation(out=pT_, in_=tpv3, func=AF.Copy), r=["bank%d" % tb_], w=[K("pT")])
        yield
        for dh in range(DH):
            gb_ = nbank()
            for kc in range(KC):
                P.op("pe", lambda e, kc=kc, dh=dh, gb_=gb_: e.matmul(banks[gb_][:, 0:DW], lhsT=h2T[:, kc, :], rhs=Wg[:, kc, dh * DW:(dh + 1) * DW], start=(kc == 0), stop=(kc == KC - 1)),
                     r=[K("h2T"), wgk[kc]], w=["bank%d" % gb_])
            yield
            P.op("dve", lambda e, dh=dh, gb_=gb_: e.tensor_tensor(out=z[:, dh * DW:(dh + 1) * DW], in0=banks[gb_][:, 0:DW], in1=gateb[:, dh * DW:(dh + 1) * DW], op=ALU.add),
                 r=["bank%d" % gb_, "gateb"], w=[K("z")])
            yield
            pb2 = nbank()
            for k2_ in range(PK):
                P.op("pe", lambda e, k2_=k2_, dh=dh, pb2=pb2: e.matmul(banks[pb2][:, 0:DW], lhsT=pT_[:, k2_, :], rhs=Wpe[:, k2_, dh * DW:(dh + 1) * DW], start=(k2_ == 0), stop=(k2_ == PK - 1)),
                     r=[K("pT"), wpk[k2_]], w=["bank%d" % pb2])
            yield
            P.op("act", lambda e, dh=dh, pb2=pb2: e.activation(out=pp[:, dh * DW:(dh + 1) * DW], in_=banks[pb2][:, 0:DW], func=AF.Copy), r=["bank%d" % pb2], w=[K("pp")])
            yield
        P.op("act", lambda e: e.activation(out=z, in_=z, func=AF.Tanh, scale=0.5), r=[K("z")], w=[K("z")])
        yield
        P.op("dve", lambda e: e.tensor_scalar(out=z, in0=z, scalar1=0.5, scalar2=0.5, op0=ALU.mult, op1=ALU.add), r=[K("z")], w=[K("z")])
        yield
        P.op("dve", lambda e: e.tensor_tensor(out=o_, in0=z, in1=pp, op=ALU.mult), r=[K("z"), K("pp")], w=[K("o")])
        yield
        P.op("dve", lambda e: e.tensor_tensor(out=o_, in0=o_, in1=xt, op=ALU.add), r=[K("o"), xk], w=[K("o")])
        yield
        out_dmas.append(P.op("sp", lambda e: e.dma_start(out=out_d[ob * 128:(ob + 1) * 128, :], in_=o_), r=[K("o")], dma=True))
        yield

    interleave((tileC3(ob) for ob in range(NO)), ND3, 7)
    P.wait_all("sp", out_dmas)
    P.emit()
    return nc, P


def make_core_inputs(cfg, core, x, p, w_in, f_bias, sg_ln_g, sg_ln_b, sg_w, sg_b, att_out_g, sg_out_g,
                     w_out, pre_mix_g, post_mix_g, pre_ffn_g, post_ffn_g, w_ff1, w_ff2, ple_w, ple_gate_w, ple_gate_b):
    c = cfg
    b, r = core // 4, core % 4
    f32 = np.float32
    blocks = c.owned_blocks(r)
    rows = np.concatenate([np.arange(bl * 128, (bl + 1) * 128) for bl in blocks])

    def fm(v):
        return np.ascontiguousarray(np.asarray(v, f32).reshape(c.KC, 128).T)

    def rep(v):
        return np.ascontiguousarray(np.broadcast_to(np.asarray(v, f32).reshape(1, -1), (128, np.asarray(v).size)))

    k = np.arange(128)[:, None]
    q = np.arange(128)[None, :]
    tri = np.where(k > q, NEG, 0.0).astype(f32)
    full = np.full((128, 128), NEG, f32)
    zero = np.zeros((128, 128), f32)
    maskT = np.zeros((128, 8, 128), f32)
    for i in range(4):
        maskT[:, i, :] = zero if i < r else (tri if i == r else full)
        maskT[:, 4 + i, :] = zero if i < 3 - r else (tri if i == 3 - r else full)
    sel = np.zeros((c.HH, c.HH, 128), f32)
    for hh in range(c.HH):
        sel[hh, hh, :] = 1.0
    LTfull = (np.arange(c.NB)[:, None] < np.arange(c.NB)[None, :]).astype(f32)
    LTown = (np.arange(c.NB)[:, None] < np.asarray(blocks)[None, :]).astype(f32)
    xb = np.asarray(x[b], f32)
    return {
        "xfull": np.ascontiguousarray(xb),
        "xown": np.ascontiguousarray(xb[rows]),
        "pown": np.ascontiguousarray(np.asarray(p[0, b], f32)[rows]),
        "w_in": np.ascontiguousarray(np.asarray(w_in[0], f32)),
        "w_out": np.ascontiguousarray(np.asarray(w_out[0], f32)),
        "w_ff1": np.ascontiguousarray(np.asarray(w_ff1[0], f32)),
        "w_ff2": np.ascontiguousarray(np.asarray(w_ff2[0], f32)),
        "ple_w": np.ascontiguousarray(np.asarray(ple_w[0], f32)),
        "gate_w": np.ascontiguousarray(np.asarray(ple_gate_w[0], f32)),
        "gpm": fm(pre_mix_g[0]),
        "gpf": fm(pre_ffn_g[0]),
        "gcat": fm(np.concatenate([np.asarray(att_out_g[0]), np.asarray(sg_out_g[0])])),
        "gpmix": rep(post_mix_g[0]),
        "gpffn": rep(post_ffn_g[0]),
        "gateb": rep(ple_gate_b[0]),
        "lng": rep(sg_ln_g[0]),
        "lnb": rep(sg_ln_b[0]),
        "fbias": rep(f_bias[0]),
        "sgwT": np.ascontiguousarray(np.transpose(np.asarray(sg_w[0], f32), (2, 0, 1))),
        "sgbT": np.ascontiguousarray(np.asarray(sg_b[0], f32).T),
        "ident": np.eye(128, dtype=f32),
        "U": (np.arange(128)[:, None] <= np.arange(128)[None, :]).astype(f32),
        "maskT": maskT,
        "sel": sel,
        "LTfull": LTfull,
        "LTown": LTown,
    }, rows


_CACHE = {}


def kernel(**inputs):
    x = np.asarray(inputs["x"])
    B, S, D = x.shape
    PLE = np.asarray(inputs["p"]).shape[-1]
    cfg = Cfg(D=D, S=S, PLE=PLE)
    key = (D, S, PLE)
    if key not in _CACHE:
        _CACHE[key] = build_program(cfg)
    nc, _ = _CACHE[key]
    in_maps, rows_all = [], []
    for core in range(8):
        m, rows = make_core_inputs(cfg, core, **inputs)
        in_maps.append(m)
        rows_all.append(rows)
    res = run_bass_kernel_spmd(nc, in_maps, core_ids=list(range(8)))
    out = np.zeros((B, S, D), np.float32)
    for core in range(8):
        out[core // 4, rows_all[core], :] = np.asarray(res.results[core]["out"], np.float32)
    return out
```

```python
import numpy as np
import concourse.bass as bass
import concourse.mybir as mybir
from concourse.bass_utils import run_bass_kernel_spmd

F32 = mybir.dt.float32
BF16 = mybir.dt.bfloat16
AF = mybir.ActivationFunctionType
ALU = mybir.AluOpType
AX = mybir.AxisListType

EPS = 1e-6
NEG = -30000.0


class _Ins:
    __slots__ = ("eng", "idx", "fn", "deps", "signal", "is_dma", "dma_sem", "dma_val", "sig_val", "epoch")

    def __init__(self, eng, idx, fn, is_dma):
        self.eng = eng
        self.idx = idx
        self.fn = fn
        self.deps = set()
        self.signal = False
        self.is_dma = is_dma
        self.dma_sem = None
        self.dma_val = 0
        self.sig_val = 0
        self.epoch = 0


class Prog:
    ENGS = ("pe", "act", "dve", "pool", "sp")

    def __init__(self, nc, n_dma_sems=24):
        self.nc = nc
        self.q = {e: [] for e in self.ENGS}
        self.lastw = {}
        self.readers = {}
        self.n_dma_sems = n_dma_sems
        self.dma_count = 0
        self.dma_last = [None] * n_dma_sems
        self.dma_pools = {"sp": (0, n_dma_sems - 8), "act": (0, n_dma_sems - 8), "pool": (n_dma_sems - 8, 8)}
        self.dma_pool_cnt = {"sp": 0, "act": 0, "pool": 0}
        self.dma_sem_uses = [0] * n_dma_sems
        self.epoch = 0

    def op(self, eng, fn, r=(), w=(), dma=False):
        ins = _Ins(eng, len(self.q[eng]), fn, dma)
        ins.epoch = self.epoch
        deps = ins.deps
        if any(k.startswith("bank") for k in r):
            w = list(w) + [k for k in r if k.startswith("bank") and k not in w]
            r = [k for k in r if not k.startswith("bank")]
        for k in r:
            lw = self.lastw.get(k)
            if lw is not None:
                deps.add(lw)
        for k in w:
            lw = self.lastw.get(k)
            if lw is not None:
                deps.add(lw)
            rd = self.readers.get(k)
            if rd:
                for x in rd[0].values():
                    deps.add(x)
                for x in rd[1]:
                    deps.add(x)
        if dma:
            base, cnt = self.dma_pools[eng]
            pk = "sp" if eng in ("sp", "act") else "pool"
            s = base + self.dma_pool_cnt[pk] % cnt
            self.dma_pool_cnt[pk] += 1
            prev = self.dma_last[s]
            if prev is not None:
                deps.add(prev)
            self.dma_sem_uses[s] += 1
            ins.dma_sem = s
            ins.dma_val = 16 * self.dma_sem_uses[s]
            self.dma_last[s] = ins
            self.dma_count += 1
        deps.discard(ins)
        for k in w:
            self.lastw[k] = ins
            self.readers[k] = ({}, [])
        for k in r:
            rd = self.readers.setdefault(k, ({}, []))
            if dma:
                rd[1].append(ins)
            else:
                rd[0][eng] = ins
        self.q[eng].append(ins)
        return ins

    def barrier(self):
        lasts = []
        for e in self.ENGS:
            for ins in reversed(self.q[e]):
                if not ins.is_dma and ins.fn is not None:
                    lasts.append(ins)
                    break
        dmas = [d for d in self.dma_last if d is not None]
        for e in self.ENGS:
            ins = _Ins(e, len(self.q[e]), None, False)
            ins.epoch = self.epoch
            ins.deps = set(lasts) | set(dmas)
            self.q[e].append(ins)
        self.lastw.clear()
        self.readers.clear()
        self.epoch += 1

    def wait_all(self, eng, instrs):
        ins = _Ins(eng, len(self.q[eng]), None, False)
        ins.epoch = self.epoch
        ins.deps = set(instrs)
        self.q[eng].append(ins)

    def emit(self):
        nc = self.nc
        for e in self.ENGS:
            for ins in self.q[e]:
                for d in ins.deps:
                    if not d.is_dma:
                        d.signal = True
        counts = {}
        for e in self.ENGS:
            c = 0
            ep = 0
            mx = 0
            for ins in self.q[e]:
                if ins.epoch != ep:
                    ep = ins.epoch
                    c = 0
                if (not ins.is_dma) and ins.signal and ins.fn is not None:
                    c += 1
                    ins.sig_val = c
                    mx = max(mx, c)
            counts[e] = mx
        self.counts = counts
        nep = self.epoch + 1
        import contextlib

        with contextlib.ExitStack() as st:
            esem = {(e, ep): st.enter_context(nc.semaphore("s_%s%d" % (e, ep))) for e in self.ENGS for ep in range(nep)}
            dsem = [st.enter_context(nc.semaphore("s_dma%d" % i)) for i in range(self.n_dma_sems)]
            block = st.enter_context(nc.Block())

            def run(e, eng):
                known = {}
                for ins in self.q[e]:
                    waits = {}
                    for d in ins.deps:
                        if d.is_dma:
                            key = ("d", d.dma_sem)
                            val = d.dma_val
                        else:
                            if d.fn is None:
                                continue
                            if d.eng == e and not ins.is_dma:
                                if e == "pe":
                                    continue
                            key = ("e", (d.eng, d.epoch))
                            val = d.sig_val
                        if val > waits.get(key, 0):
                            waits[key] = val
                    for key, val in waits.items():
                        if known.get(key, 0) >= val:
                            continue
                        sem = dsem[key[1]] if key[0] == "d" else esem[key[1]]
                        eng.wait_ge(sem, val)
                        known[key] = val
                    if ins.fn is None:
                        continue
                    bi = ins.fn(eng)
                    if ins.is_dma:
                        bi.then_inc(dsem[ins.dma_sem], 16)
                    elif ins.signal:
                        bi.then_inc(esem[(e, ins.epoch)], 1)

            @block.tensor
            def _(eng):
                run("pe", eng)

            @block.scalar
            def _(eng):
                run("act", eng)

            @block.vector
            def _(eng):
                run("dve", eng)

            @block.gpsimd
            def _(eng):
                run("pool", eng)

            @block.sync
            def _(eng):
                run("sp", eng)


class Cfg:
    def __init__(self, D=1024, S=8192, PLE=256):
        self.D, self.S, self.PLE = D, S, PLE
        self.KC = D // 128
        self.AW = D // 2
        self.H = self.AW // 64
        self.SGW = D // 2
        self.G = self.SGW // 64
        self.DFF = 4 * D
        self.FC = self.DFF // 128
        self.IPW = 3 * self.AW + self.H + 2 * self.SGW
        self.NB = S // 128
        self.J = self.NB // 8
        self.NO = 2 * self.J
        self.HH = self.H // 2
        self.PK = PLE // 128
        self.DH = max(1, D // 512)
        self.DW = min(D, 512)
        NO, J, NB = self.NO, self.J, self.NB
        self.groups = [list(range(i, min(i + 4, NO))) for i in range(0, NO, 4)]
        self.lo = [4 * j for j in range(J)] + [NB - 4 - 4 * j for j in reversed(range(J))]
        self.mtype = [0] * J + [1] * J

    def owned_blocks(self, r):
        J, NB = self.J, self.NB
        return [4 * j + r for j in range(J)] + [NB - 1 - 4 * j - r for j in reversed(range(J))]


class Arena:
    def __init__(self, ap, total):
        self.A = ap
        self.total = total
        self.top = 0

    def alloc(self, shape, dt):
        n = int(np.prod(shape[1:]))
        ne = n * (2 if dt == F32 else 1)
        ne = (ne + 15) // 16 * 16
        assert self.top + ne <= self.total, ("SBUF arena overflow", self.top, ne, self.total)
        v = self.A[:, self.top:self.top + (n * (2 if dt == F32 else 1))]
        self.top += ne
        if dt == F32:
            v = v.bitcast(F32)
        if len(shape) > 2:
            names = " ".join("a%d" % i for i in range(len(shape) - 1))
            kw = {"a%d" % i: int(shape[i + 1]) for i in range(len(shape) - 1)}
            v = v.rearrange("p (%s) -> p %s" % (names, names), **kw)
        if shape[0] < 128:
            v = v[0:shape[0]]
        return v


def _ap(t):
    return t.ap() if hasattr(t, "ap") else t[:]


def build_program(cfg, debug=False):
    c = cfg
    D, S, KC, AW, H, HH, SGW, G, NB, NO = c.D, c.S, c.KC, c.AW, c.H, c.HH, c.SGW, c.G, c.NB, c.NO
    DFF, FC, PLE, PK, DH, DW = c.DFF, c.FC, c.PLE, c.PK, c.DH, c.DW
    NG = len(c.groups)
    HP = H // 2
    HPP = HH // 2
    nc = bass.Bass("TRN2", target_bir_lowering=False)

    def din(name, shape):
        return nc.dram_tensor(name, list(shape), F32, kind="ExternalInput").ap()

    xfull = din("xfull", [S, D])
    xown = din("xown", [NO * 128, D])
    pown = din("pown", [NO * 128, PLE])
    w_in = din("w_in", [D, c.IPW])
    w_out = din("w_out", [D, D])
    w_ff1 = din("w_ff1", [D, DFF])
    w_ff2 = din("w_ff2", [DFF, D])
    ple_w = din("ple_w", [PLE, D])
    gate_w = din("gate_w", [D, D])
    gpm_d = din("gpm", [128, KC])
    gpf_d = din("gpf", [128, KC])
    gcat_d = din("gcat", [128, KC])
    gpmix_d = din("gpmix", [128, D])
    gpffn_d = din("gpffn", [128, D])
    gateb_d = din("gateb", [128, D])
    lng_d = din("lng", [128, SGW])
    lnb_d = din("lnb", [128, SGW])
    fbias_d = din("fbias", [128, H])
    sgwT_d = din("sgwT", [128, G, 128])
    sgbT_d = din("sgbT", [128, G])
    ident_d = din("ident", [128, 128])
    U_d = din("U", [128, 128])
    maskT_d = din("maskT", [128, 8, 128])
    sel_d = din("sel", [HH, HH, 128])
    LTfull_d = din("LTfull", [NB, NB])
    LTown_d = din("LTown", [NB, NO])
    out_d = nc.dram_tensor("out", [NO * 128, D], F32, kind="ExternalOutput").ap()
    h1_d = nc.dram_tensor("h1_scr", [NO * 128, D], F32).ap()
    h2_d = nc.dram_tensor("h2_scr", [NO * 128, D], F32).ap()
    dbg = {}

    total = (nc.sbuf_bytes_remaining - 2048) // 2
    total = total // 16 * 16
    arena_t = nc.alloc_sbuf_tensor("arena", [128, total], BF16)
    AR = Arena(_ap(arena_t), total)
    banks = [_ap(nc.alloc_psum_tensor("bank%d" % i, [128, 512], F32)) for i in range(8)]
    banksb = [b.bitcast(BF16) for b in banks]

    P = Prog(nc)
    rr = [0]

    def wq():
        return "sp"

    identf = AR.alloc([128, 128], F32)
    identb = AR.alloc([128, 128], BF16)
    Uf = AR.alloc([128, 128], F32)
    onesf = AR.alloc([128, 128], F32)
    maskb = AR.alloc([128, 8, 128], BF16)
    selb = AR.alloc([128, HH, 128], BF16)
    LTfull = AR.alloc([128, NB], F32)
    LTown = AR.alloc([128, NO], F32)
    gpm = AR.alloc([128, KC], F32)
    gpf = AR.alloc([128, KC], F32)
    gcat = AR.alloc([128, KC], F32)
    fbias = AR.alloc([128, H], F32)
    stats = AR.alloc([128, 64], F32)
    const_top = AR.top
    QT = AR.alloc([128, HP, NO * 128], BF16)
    within_own = AR.alloc([128, NO, H], F32)
    yatt = AR.alloc([128, NO, AW], F32)
    persist_top = AR.top

    def ld(dst, src, key, eng="sp"):
        return P.op(eng, lambda e: e.dma_start(out=dst, in_=src), w=[key], dma=True)

    ld(identf, ident_d, "identf")
    ld(Uf, U_d, "Uf")
    ld(LTfull[0:NB], LTfull_d, "LTfull")
    ld(LTown[0:NB], LTown_d, "LTown")
    ld(gpm, gpm_d, "gpm")
    ld(gpf, gpf_d, "gpf")
    ld(gcat, gcat_d, "gcat")
    ld(fbias, fbias_d, "fbias")
    P.op("pool", lambda e: e.dma_start(out=maskb, in_=maskT_d), w=["maskb"], dma=True)
    P.op("pool", lambda e: e.dma_start(out=selb[0:HH], in_=sel_d), w=["selb"], dma=True)
    P.op("dve", lambda e: e.tensor_copy(out=identb, in_=identf), r=["identf"], w=["identb"])
    P.op("dve", lambda e: e.memset(onesf, 1.0), w=["onesf"])

    scnt = [0]

    def stat_slot(n=1):
        s = scnt[0]
        scnt[0] = (scnt[0] + n) % 60
        if s + n > 60:
            s = 0
            scnt[0] = n
        return s

    def load_w(dst, src2d, key, kcn):
        for kc in range(kcn):
            P.op("pool", lambda e, kc=kc: e.dma_start(out=dst[:, kc, :], in_=src2d[kc * 128:(kc + 1) * 128, :]),
                 w=[key + str(kc)], dma=True)
        return [key + str(kc) for kc in range(kcn)]

    def rsqrt_ops(ss_ap, out_ap, n, scale, rkeys, wkeys):
        tmpslot = stat_slot(n)
        tmp = stats[:, tmpslot:tmpslot + n]
        tks = ["st%d" % (tmpslot + j) for j in range(n)]
        P.op("act", lambda e: e.activation(out=tmp, in_=ss_ap, func=AF.Ln, scale=scale, bias=EPS), r=rkeys, w=tks)
        P.op("act", lambda e: e.activation(out=out_ap, in_=tmp, func=AF.Exp, scale=-0.5), r=tks, w=wkeys)

    class Front:
        def __init__(self, nx=3, tpbanks=(0, 1), nxn=2):
            self.xt = [AR.alloc([128, D], F32) for _ in range(nx)]
            self.xn = [AR.alloc([128, D], BF16) for _ in range(nxn)]
            self.i = 0
            self.tpb = tpbanks

        def run(self, rows_ap, gain, dst, dkey, norm=True):
            g = self.gen(rows_ap, gain, dst, dkey, norm)
            for _ in g:
                pass
            return self.last

        def gen(self, rows_ap, gain, dst, dkey, norm=True):
            i = self.i
            self.i += 1
            xt = self.xt[i % len(self.xt)]
            xk = "xt%d_%d" % (id(self) % 1000, i % len(self.xt))
            xn = self.xn[i % len(self.xn)]
            nk = "xn%d_%d" % (id(self) % 1000, i % len(self.xn))
            tb = self.tpb[i % len(self.tpb)]
            tpv = banksb[tb][:, 0:KC * 128].rearrange("p (k t) -> p k t", t=128)
            tk = "bank%d" % tb
            self.last = (xt, xk)
            P.op("sp", lambda e: e.dma_start(out=xt, in_=rows_ap), w=[xk], dma=True)
            yield
            if norm:
                sl = stat_slot(2)
                ss = stats[:, sl:sl + 1]
                rs = stats[:, sl + 1:sl + 2]
                sk = "st%d" % sl
                rk = "st%d" % (sl + 1)
                P.op("act", lambda e: e.activation(out=xn, in_=xt, func=AF.Square, accum_out=ss), r=[xk], w=[nk, sk])
                yield
                rsqrt_ops(ss, rs, 1, 1.0 / D, [sk], [rk])
                yield
                P.op("dve", lambda e: e.tensor_scalar(out=xn, in0=xt, scalar1=rs, scalar2=None, op0=ALU.mult),
                     r=[xk, rk], w=[nk])
            else:
                P.op("dve", lambda e: e.tensor_copy(out=xn, in_=xt), r=[xk], w=[nk])
            yield
            for kc in range(KC):
                P.op("pe", lambda e, kc=kc: e.transpose(out=tpv[:, kc, :], in_=xn[:, kc * 128:(kc + 1) * 128], identity=identb),
                     r=[nk, "identb"], w=[tk])
            yield
            if gain is not None:
                gk = gain[1]
                gb = gain[0].unsqueeze(2).broadcast_to([128, KC, 128])
                P.op("dve", lambda e: e.tensor_tensor(out=dst, in0=tpv, in1=gb, op=ALU.mult), r=[tk, gk], w=[dkey])
            else:
                P.op("act", lambda e: e.activation(out=dst, in_=tpv, func=AF.Copy), r=[tk], w=[dkey])
            yield

    def interleave(gens, depth, period):
        it = iter(gens)
        active = []
        rounds = 0
        done = False
        while True:
            if not done and len(active) < depth and (rounds % period == 0 or not active):
                try:
                    active.append(next(it))
                except StopIteration:
                    done = True
            if not active:
                if done:
                    break
                continue
            for g in list(active):
                try:
                    next(g)
                except StopIteration:
                    active.remove(g)
            rounds += 1

    def softplus_neg(dst, src, n_keys_r, wkey, tmp):
        P.op("act", lambda e: e.activation(out=tmp, in_=src, func=AF.Exp, scale=-1.0), r=n_keys_r, w=[wkey + "_e"])
        P.op("act", lambda e: e.activation(out=dst, in_=tmp, func=AF.Ln, scale=1.0, bias=1.0), r=[wkey + "_e"], w=[wkey])

    qcol = 0
    kcol = AW
    vcol = 2 * AW
    fcol = 3 * AW
    ucol = 3 * AW + H
    vscol = ucol + SGW
    w_in_r = w_in

    mark = AR.top
    Wq = AR.alloc([128, KC, AW], BF16)
    Wfa = AR.alloc([128, KC, H], BF16)
    fr = Front(nx=3, tpbanks=(0, 1))
    aTq = [AR.alloc([128, KC, 512], BF16) for _ in range(2)]
    fown = AR.alloc([128, NO, H], F32)
    spo = AR.alloc([128, NO, H], F32)
    spo_e = AR.alloc([128, NO, H], F32)
    wqk = load_w(Wq, w_in_r[:, qcol:qcol + AW], "Wq", KC)
    wfk = load_w(Wfa, w_in_r[:, fcol:fcol + H], "Wfa", KC)
    for g, blocks in enumerate(c.groups):
        aT = aTq[g % 2]
        ak = "aTq%d" % (g % 2)
        N = len(blocks) * 128
        for ti, ob in enumerate(blocks):
            fr.run(xown[ob * 128:(ob + 1) * 128, :], (gpm, "gpm"), aT[:, :, ti * 128:(ti + 1) * 128], ak)
            for kc in range(KC):
                P.op("pe", lambda e, kc=kc, ti=ti, aT=aT: e.matmul(banks[4][:, 0:H], lhsT=aT[:, kc, ti * 128:(ti + 1) * 128], rhs=Wfa[:, kc, :],
                                                                   start=(kc == 0), stop=(kc == KC - 1)), r=[ak, wfk[kc]], w=["bank4"])
            P.op("dve", lambda e, ob=ob: e.tensor_tensor(out=fown[:, ob, :], in0=banks[4][:, 0:H], in1=fbias, op=ALU.add),
                 r=["bank4", "fbias"], w=["fown"])
        for hp in range(HP):
            qb = 2 + (hp % 2)
            for kc in range(KC):
                P.op("pe", lambda e, kc=kc, hp=hp, aT=aT, qb=qb, N=N: e.matmul(banks[qb][:, 0:N], lhsT=Wq[:, kc, hp * 128:(hp + 1) * 128], rhs=aT[:, kc, 0:N],
                                                                              start=(kc == 0), stop=(kc == KC - 1)), r=[ak, wqk[kc]], w=["bank%d" % qb])
            P.op("act", lambda e, hp=hp, qb=qb, g=g, N=N: e.activation(out=QT[:, hp, g * 512:g * 512 + N], in_=banks[qb][:, 0:N], func=AF.Copy, scale=0.125),
                 r=["bank%d" % qb], w=["QT"])
    softplus_neg(spo, fown, ["fown"], "spo", spo_e)
    P.op("pe", lambda e: e.matmul(banks[5][:, 0:NO * H], lhsT=Uf, rhs=spo.rearrange("p a b -> p (a b)"), start=True, stop=True),
         r=["Uf", "spo"], w=["bank5"])
    P.op("dve", lambda e: e.tensor_copy(out=within_own.rearrange("p a b -> p (a b)"), in_=banks[5][:, 0:NO * H]), r=["bank5"], w=["within_own"])
    P.barrier()
    AR.top = mark

    KT = AR.alloc([128, HPP, S], BF16)
    Vflat = AR.alloc([128, NB * HH * 65 + 64], BF16)
    Vaug = Vflat[:, 0:NB * HH * 65].rearrange("p (a b c) -> p a b c", a=NB, b=HH)
    biasT = AR.alloc([128, NG, NB, HH], F32)
    R8 = AR.alloc([128, NO * 128], BF16)
    rbc = AR.alloc([128, HH, NO * 128], BF16)
    attn_top = AR.top
    P.op("dve", lambda e: e.memset(Vaug[:, :, :, 64:65], 1.0), w=["Vones"])
    P.op("dve", lambda e: e.memset(Vflat[:, NB * HH * 65:NB * HH * 65 + 64], 0.0), w=["Vpad"])

    for hs in range(2):
        mark = AR.top
        fsb = AR.alloc([128, NB, HH], F32)
        spf = AR.alloc([128, NB, HH], F32)
        spf_e = AR.alloc([128, NB, HH], F32)
        wsb = AR.alloc([128, NB, HH], F32)
        Cpos = AR.alloc([128, NB, HH], F32)
        totT = AR.alloc([128, HH], F32)
        rhs_full = AR.alloc([128, NB, HH], F32)
        rhs_own = AR.alloc([128, NO, HH], F32)
        pexo = AR.alloc([128, NO, HH], F32)
        rt1 = AR.alloc([128, NO, HH], F32)
        Rtok = AR.alloc([128, NO, HH], F32)
        markA = AR.top
        Wk = AR.alloc([128, KC, HH * 64], BF16)
        VF = HH * 64 + HH
        Wv = AR.alloc([128, KC, VF], BF16)
        fr = Front(nx=4, tpbanks=(0, 1, 4, 7), nxn=3)
        aTa = [AR.alloc([128, KC, 512], BF16) for _ in range(2)]
        wkk = load_w(Wk, w_in_r[:, kcol + hs * HH * 64: kcol + (hs + 1) * HH * 64], "Wk", KC)
        wvk = []
        for kc in range(KC):
            P.op("pool", lambda e, kc=kc, hs=hs, Wv=Wv: e.dma_start(out=Wv[:, kc, 0:HH * 64], in_=w_in_r[kc * 128:(kc + 1) * 128, vcol + hs * HH * 64: vcol + (hs + 1) * HH * 64]),
                 w=["Wv%da" % kc], dma=True)
            P.op("pool", lambda e, kc=kc, hs=hs, Wv=Wv: e.dma_start(out=Wv[:, kc, HH * 64:VF], in_=w_in_r[kc * 128:(kc + 1) * 128, fcol + hs * HH: fcol + (hs + 1) * HH]),
                 w=["Wv%db" % kc], dma=True)
            wvk.append(["Wv%da" % kc, "Wv%db" % kc])
        nst = (NB + 3) // 4

        def tileA(t, hs=hs, Wk=Wk, Wv=Wv, fsb=fsb, aTa=aTa, fr=fr, wkk=wkk, wvk=wvk):
            st, ti = t // 4, t % 4
            aT = aTa[st % 2]
            aks = ["aTa%d_%d" % (st % 2, j) for j in range(4)]
            ak = aks[ti]
            for _ in fr.gen(xfull[t * 128:(t + 1) * 128, :], (gpm, "gpm"), aT[:, :, ti * 128:(ti + 1) * 128], ak):
                yield
            vb = 2 + (t % 2)
            vk = "bank%d" % vb
            for kc in range(KC):
                P.op("pe", lambda e, kc=kc: e.matmul(banks[vb][:, 0:VF], lhsT=aT[:, kc, ti * 128:(ti + 1) * 128], rhs=Wv[:, kc, :],
                                                     start=(kc == 0), stop=(kc == KC - 1)), r=[ak] + wvk[kc], w=[vk])
            yield
            P.op("act", lambda e: e.activation(out=Vaug[:, t, :, 0:64], in_=banks[vb][:, 0:HH * 64].rearrange("p (h d) -> p h d", d=64), func=AF.Copy),
                 r=[vk], w=["V%d" % t])
            yield
            P.op("act", lambda e: e.activation(out=fsb[:, t, :], in_=banks[vb][:, HH * 64:VF], func=AF.Copy), r=[vk], w=["fsb"])
            yield
            if ti == 3 or t == NB - 1:
                N = (ti + 1) * 128
                for hpl in range(HPP):
                    kb_ = 5 + (hpl % 2)
                    for kc in range(KC):
                        P.op("pe", lambda e, kc=kc, hpl=hpl, kb_=kb_: e.matmul(banks[kb_][:, 0:N], lhsT=Wk[:, kc, hpl * 128:(hpl + 1) * 128], rhs=aT[:, kc, 0:N],
                                                                               start=(kc == 0), stop=(kc == KC - 1)), r=aks[0:ti + 1] + [wkk[kc]], w=["bank%d" % kb_])
                    yield
                    P.op("act", lambda e, hpl=hpl, kb_=kb_: e.activation(out=KT[:, hpl, st * 512:st * 512 + N], in_=banks[kb_][:, 0:N], func=AF.Copy),
                         r=["bank%d" % kb_], w=["KT%d" % st])
                    yield

        interleave((tileA(t) for t in range(NB)), 4, 2)

        P.op("dve", lambda e, hs=hs, fsb=fsb: e.tensor_tensor(out=fsb, in0=fsb, in1=fbias[:, hs * HH:(hs + 1) * HH].unsqueeze(1).broadcast_to([128, NB, HH]), op=ALU.add),
             r=["fsb", "fbias"], w=["fsb"])
        softplus_neg(spf, fsb, ["fsb"], "spf", spf_e)
        spf2 = spf.rearrange("p a b -> p (a b)")
        P.op("pe", lambda e: e.matmul(banks[0][:, 0:NB * HH], lhsT=Uf, rhs=spf2, start=True, stop=True), r=["Uf", "spf"], w=["bank0"])
        P.op("dve", lambda e: e.tensor_copy(out=wsb.rearrange("p a b -> p (a b)"), in_=banks[0][:, 0:NB * HH]), r=["bank0"], w=["wsb"])
        for hh in range(HH):
            P.op("pe", lambda e, hh=hh: e.matmul(banks[1][0:NB, hh:hh + 1], lhsT=spf[:, :, hh], rhs=onesf[:, 0:1], start=True, stop=True),
                 r=["spf", "onesf"], w=["bank1"])
        P.op("dve", lambda e: e.tensor_copy(out=totT[0:NB, :], in_=banks[1][0:NB, 0:HH]), r=["bank1"], w=["totT"])
        P.op("dve", lambda e: e.tensor_tensor(out=rhs_full[0:NB], in0=LTfull[0:NB].unsqueeze(2).broadcast_to([NB, NB, HH]),
                                              in1=totT[0:NB].unsqueeze(1).broadcast_to([NB, NB, HH]), op=ALU.mult), r=["LTfull", "totT"], w=["rhs_full"])
        P.op("dve", lambda e: e.tensor_tensor(out=rhs_own[0:NB], in0=LTown[0:NB].unsqueeze(2).broadcast_to([NB, NO, HH]),
                                              in1=totT[0:NB].unsqueeze(1).broadcast_to([NB, NO, HH]), op=ALU.mult), r=["LTown", "totT"], w=["rhs_own"])
        P.op("pe", lambda e: e.matmul(banks[2][:, 0:NB * HH], lhsT=onesf[0:NB, :], rhs=rhs_full[0:NB].rearrange("p a b -> p (a b)"), start=True, stop=True),
             r=["onesf", "rhs_full"], w=["bank2"])
        P.op("pe", lambda e: e.matmul(banks[3][:, 0:NO * HH], lhsT=onesf[0:NB, :], rhs=rhs_own[0:NB].rearrange("p a b -> p (a b)"), start=True, stop=True),
             r=["onesf", "rhs_own"], w=["bank3"])
        P.op("dve", lambda e: e.tensor_tensor(out=Cpos.rearrange("p a b -> p (a b)"), in0=banks[2][:, 0:NB * HH], in1=wsb.rearrange("p a b -> p (a b)"), op=ALU.add),
             r=["bank2", "wsb"], w=["Cpos"])
        P.op("dve", lambda e: e.tensor_copy(out=pexo.rearrange("p a b -> p (a b)"), in_=banks[3][:, 0:NO * HH]), r=["bank3"], w=["pexo"])
        for g, blocks in enumerate(c.groups):
            g0 = blocks[0]
            nb = len(blocks)
            P.op("dve", lambda e, g=g, g0=g0: e.tensor_tensor(out=biasT[:, g, :, :], in0=Cpos, in1=pexo[:, g0:g0 + 1, :].broadcast_to([128, NB, HH]), op=ALU.subtract),
                 r=["Cpos", "pexo"], w=["biasT"])
            P.op("dve", lambda e, g0=g0, nb=nb: e.tensor_tensor(out=rt1[:, g0:g0 + nb, :], in0=pexo[:, g0:g0 + 1, :].broadcast_to([128, nb, HH]), in1=pexo[:, g0:g0 + nb, :], op=ALU.subtract),
                 r=["pexo"], w=["rt1"])
            P.op("dve", lambda e, g0=g0, nb=nb, hs=hs: e.tensor_tensor(out=Rtok[:, g0:g0 + nb, :], in0=rt1[:, g0:g0 + nb, :], in1=within_own[:, g0:g0 + nb, hs * HH:(hs + 1) * HH], op=ALU.subtract),
                 r=["rt1", "within_own"], w=["Rtok"])
            for ti, ob in enumerate(blocks):
                P.op("pe", lambda e, ti=ti, ob=ob: e.matmul(banks[4][0:HH, ti * 128:(ti + 1) * 128], lhsT=Rtok[:, ob, :], rhs=identf, start=True, stop=True),
                     r=["Rtok", "identf"], w=["bank4"])
            P.op("dve", lambda e, g0=g0, nb=nb: e.tensor_copy(out=R8[0:HH, g0 * 128:(g0 + nb) * 128], in_=banks[4][0:HH, 0:nb * 128]), r=["bank4"], w=["R8"])

        for hh in range(HH):
            for g, blocks in enumerate(c.groups):
                g0, nb = blocks[0], len(blocks)
                rb_ = 5 + ((hh * NG + g) % 2)
                P.op("pe", lambda e, hh=hh, g0=g0, nb=nb, rb_=rb_: e.matmul(banks[rb_][:, 0:nb * 128], lhsT=selb[0:HH, hh, :], rhs=R8[0:HH, g0 * 128:(g0 + nb) * 128], start=True, stop=True),
                     r=["selb", "R8"], w=["bank%d" % rb_])
                P.op("dve", lambda e, hh=hh, g0=g0, nb=nb, rb_=rb_: e.tensor_copy(out=rbc[:, hh, g0 * 128:(g0 + nb) * 128], in_=banks[rb_][:, 0:nb * 128]),
                     r=["bank%d" % rb_], w=["rbc"])

        P.barrier()
        AR.top = markA
        NPT = 6
        QTp = AR.alloc([128, HH, NO * 128], BF16)
        P.op("pool", lambda e, QTp=QTp: e.memset(QTp, 0.0), w=["QTp"])
        for hh in range(HH):
            h_ = hs * HH + hh
            e2 = h_ % 2
            P.op("dve", lambda e, hh=hh, h_=h_, e2=e2, QTp=QTp: e.tensor_copy(out=QTp[e2 * 64:(e2 + 1) * 64, hh, :], in_=QT[e2 * 64:(e2 + 1) * 64, h_ // 2, :]),
                 r=["QT", "QTp"], w=["QTp"])
        pts = [AR.alloc([128, 512], BF16) for _ in range(NPT)]
        osb = [AR.alloc([128, 512], F32) for _ in range(2)]
        rc = AR.alloc([128, 8], F32)
        kmaxs = [c.lo[blocks[-1]] + 3 for blocks in c.groups]
        batches = []
        for hh in range(HH):
            for kb in range(NB):
                act_g = [g for g in range(NG) if kb <= kmaxs[g]]
                for j in range(0, len(act_g), 1):
                    batches.append((hh, kb, act_g[j:j + 1]))
        epi = [0]

        def geom(g, kb):
            blocks = c.groups[g]
            fa = 0
            while c.lo[blocks[fa]] + 3 < kb:
                fa += 1
            N = (len(blocks) - fa) * 128
            c0 = blocks[fa] * 128
            msk = [(bi, kb - c.lo[ob]) for bi, ob in enumerate(blocks) if bi >= fa and c.lo[ob] <= kb <= c.lo[ob] + 3]
            return blocks, fa, N, c0, msk

        def emit_S(i):
            hh, kb, gs = batches[i]
            h = hs * HH + hh
            hpg, e_, hpl = h // 2, h % 2, hh // 2
            info = []
            for j, g in enumerate(gs):
                blocks, fa, N, c0, msk = geom(g, kb)
                sb = i % 4
                info.append((g, blocks, fa, N, c0, msk, sb))
            for (g, blocks, fa, N, c0, msk, sb) in info:
                P.op("pe", lambda e, N=N, c0=c0, sb=sb, nm=len(msk): e.matmul(banks[sb][:, 0:N], lhsT=KT[:, hpl, kb * 128:(kb + 1) * 128],
                                                                rhs=QTp[:, hh, c0:c0 + N], start=True, stop=(nm == 0)),
                     r=["KT%d" % (kb // 4), "QTp"], w=["bank%d" % sb])
            for (g, blocks, fa, N, c0, msk, sb) in info:
                for mi, (bi, i4) in enumerate(msk):
                    mt = c.mtype[blocks[bi]] * 4 + i4
                    P.op("pe", lambda e, bi=bi, mt=mt, mi=mi, fa=fa, sb=sb, nm=len(msk): e.matmul(banks[sb][:, (bi - fa) * 128:(bi - fa + 1) * 128], lhsT=identb, rhs=maskb[:, mt, :],
                                                                                              start=False, stop=(mi == nm - 1)), r=["identb", "maskb"], w=["bank%d" % sb])
            for (g, blocks, fa, N, c0, msk, sb) in info:
                P.op("dve", lambda e, N=N, c0=c0, sb=sb: e.tensor_tensor(out=banks[sb][:, 0:N], in0=banks[sb][:, 0:N], in1=rbc[:, hh, c0:c0 + N], op=ALU.add),
                     r=["bank%d" % sb, "rbc"], w=["bank%d" % sb])
            for j, (g, blocks, fa, N, c0, msk, sb) in enumerate(info):
                pi = i % NPT
                P.op("act", lambda e, N=N, sb=sb, g=g, pi=pi: e.activation(out=pts[pi][:, 0:N], in_=banks[sb][:, 0:N], func=AF.Exp, bias=biasT[:, g, kb, hh:hh + 1], scale=1.0),
                     r=["bank%d" % sb, "biasT"], w=["pt%d" % pi])

        def emit_PV(i):
            hh, kb, gs = batches[i]
            h = hs * HH + hh
            for j, g in enumerate(gs):
                blocks, fa, N, c0, msk = geom(g, kb)
                ob_ = 4 + g
                ok = "bank%d" % ob_
                pi = i % NPT
                last = (kb == kmaxs[g])
                vo = (kb * HH + hh) * 65
                vkeys = ["V%d" % kb, "Vones", "Vpad"] + (["V%d" % (kb + 1)] if kb + 1 < NB else [])
                P.op("pe", lambda e, fa=fa, N=N, ob_=ob_, pi=pi, last=last, vo=vo: e.matmul(banks[ob_][:, fa * 128:fa * 128 + N], lhsT=Vflat[:, vo:vo + 128], rhs=pts[pi][:, 0:N], start=(kb == 0), stop=last),
                     r=vkeys + ["pt%d" % pi], w=[ok])
            for j, g in enumerate(gs):
                if kb != kmaxs[g]:
                    continue
                blocks = c.groups[g]
                ob_ = 4 + g
                ok = "bank%d" % ob_
                nb = len(blocks)
                ei = epi[0] % 2
                epi[0] += 1
                os_ = osb[ei]
                osk = "osb%d" % ei
                tb = i % 4
                tk = "bank%d" % tb
                P.op("dve", lambda e, os_=os_, ob_=ob_, nb=nb: e.tensor_copy(out=os_[0:65, 0:nb * 128], in_=banks[ob_][0:65, 0:nb * 128]), r=[ok], w=[osk])
                for ti, ob in enumerate(blocks):
                    P.op("pe", lambda e, ti=ti, os_=os_, tb=tb: e.matmul(banks[tb][:, ti * 65:(ti + 1) * 65], lhsT=os_[0:65, ti * 128:(ti + 1) * 128], rhs=identf[0:65, 0:65], start=True, stop=True),
                         r=[osk, "identf"], w=[tk])
                o3 = banks[tb][:, 0:nb * 65].rearrange("p (b x) -> p b x", x=65)
                P.op("dve", lambda e, o3=o3, nb=nb: e.reciprocal(out=rc[:, 0:nb], in_=o3[:, :, 64]), r=[tk], w=["rc"])
                for ti, ob in enumerate(blocks):
                    P.op("dve", lambda e, ti=ti, ob=ob, o3=o3, h=h: e.tensor_scalar(out=yatt[:, ob, h * 64:(h + 1) * 64], in0=o3[:, ti, 0:64], scalar1=rc[:, ti:ti + 1], scalar2=None, op0=ALU.mult),
                         r=[tk, "rc"], w=["yatt"])

        SKEW = 3
        for i in range(len(batches)):
            emit_S(i)
            if i >= SKEW:
                emit_PV(i - SKEW)
        for i in range(max(0, len(batches) - SKEW), len(batches)):
            emit_PV(i)
        P.barrier()
        AR.top = mark

    AR.top = attn_top - 0
    AR.top = persist_top
    C0 = 0.7978845608028654
    C1_ = 0.044715
    Wu = AR.alloc([128, KC, SGW], BF16)
    Wvs = AR.alloc([128, KC, SGW], BF16)
    Wo = AR.alloc([128, KC, D], BF16)
    wsT = AR.alloc([128, G, 128], BF16)
    wsTf = AR.alloc([128, G, 128], F32)
    sgb = AR.alloc([128, G], F32)
    lng = AR.alloc([128, SGW], F32)
    lnb = AR.alloc([128, SGW], F32)
    gpmix = AR.alloc([128, D], F32)
    ND1 = 3
    fr = Front(nx=ND1, tpbanks=(0, 1), nxn=ND1)
    aTc = [AR.alloc([128, KC, 128], BF16) for _ in range(ND1)]
    TN = ["us", "vs", "x2", "tA", "tB", "tmix"]
    TT = [{n: AR.alloc([128, SGW], F32) for n in TN} for _ in range(ND1)]
    for p_ in range(ND1):
        TT[p_]["vlnb"] = AR.alloc([128, SGW], BF16)
        TT[p_]["junk"] = AR.alloc([128, D], BF16)
        TT[p_]["yn"] = AR.alloc([128, D], BF16)
        TT[p_]["ynT"] = AR.alloc([128, KC, 128], BF16)
        TT[p_]["h1"] = AR.alloc([128, D], F32)
    wuk = load_w(Wu, w_in_r[:, ucol:ucol + SGW], "Wu", KC)
    wvsk = load_w(Wvs, w_in_r[:, vscol:vscol + SGW], "Wvs", KC)
    wok = load_w(Wo, w_out, "Wo", KC)
    ld(wsTf, sgwT_d, "wsTf")
    ld(sgb, sgbT_d, "sgb")
    ld(lng, lng_d, "lng")
    ld(lnb, lnb_d, "lnb")
    ld(gpmix, gpmix_d, "gpmix")
    P.op("dve", lambda e: e.tensor_tensor(out=wsT, in0=wsTf, in1=Uf.unsqueeze(1).broadcast_to([128, G, 128]), op=ALU.mult), r=["wsTf", "Uf"], w=["wsT"])

    def gelu2(T, p, dst, dk, ps, pk):
        x2, tA, tB = T["x2"], T["tA"], T["tB"]
        kx, ka, kb2 = "x2_%d" % p, "tA_%d" % p, "tB_%d" % p
        P.op("act", lambda e: e.activation(out=x2, in_=ps, func=AF.Square), r=[pk], w=[kx])
        yield
        P.op("dve", lambda e: e.tensor_scalar(out=tA, in0=x2, scalar1=C1_, scalar2=1.0, op0=ALU.mult, op1=ALU.add), r=[kx], w=[ka])
        yield
        P.op("dve", lambda e: e.tensor_tensor(out=tB, in0=tA, in1=ps, op=ALU.mult), r=[ka, pk], w=[kb2])
        yield
        P.op("act", lambda e: e.activation(out=tA, in_=tB, func=AF.Tanh, scale=C0), r=[kb2], w=[ka])
        yield
        P.op("dve", lambda e: e.scalar_tensor_tensor(out=dst, in0=tA, scalar=1.0, in1=ps, op0=ALU.add, op1=ALU.mult), r=[ka, pk], w=[dk])
        yield

    def tileC1(ob):
        p = ob % ND1
        T = TT[p]
        aT = aTc[p]
        ak = "aTc%d" % p
        bA, bB = 2 + 2 * p, 3 + 2 * p
        kA, kB = "bank%d" % bA, "bank%d" % bB
        K = lambda n: "%s_%d" % (n, p)
        us, vs, tmix, vlnb, junk, yn, ynT, h1 = (T[n] for n in ("us", "vs", "tmix", "vlnb", "junk", "yn", "ynT", "h1"))
        for _ in fr.gen(xown[ob * 128:(ob + 1) * 128, :], (gpm, "gpm"), aT, ak):
            yield
        xt, xk = fr.last
        for kc in range(KC):
            P.op("pe", lambda e, kc=kc: e.matmul(banks[bB][:, 0:SGW], lhsT=aT[:, kc, :], rhs=Wvs[:, kc, :], start=(kc == 0), stop=(kc == KC - 1)),
                 r=[ak, wvsk[kc]], w=[kB])
        for kc in range(KC):
            P.op("pe", lambda e, kc=kc: e.matmul(banks[bA][:, 0:SGW], lhsT=aT[:, kc, :], rhs=Wu[:, kc, :], start=(kc == 0), stop=(kc == KC - 1)),
                 r=[ak, wuk[kc]], w=[kA])
        yield
        P.op("act", lambda e: e.activation(out=vs, in_=banks[bB][:, 0:SGW], func=AF.Copy), r=[kB], w=[K("vs")])
        yield
        P.op("act", lambda e: e.activation(out=us, in_=banks[bA][:, 0:SGW], func=AF.Copy), r=[kA], w=[K("us")])
        yield
        for _ in gelu2(T, p, vs, K("vs"), vs, K("vs")):
            yield
        for _ in gelu2(T, p, us, K("us"), us, K("us")):
            yield
        gv, gu = vs, us
        sl = stat_slot(6)
        s1 = stats[:, sl:sl + 1]
        nm = stats[:, sl + 1:sl + 2]
        s2 = stats[:, sl + 2:sl + 3]
        r2 = stats[:, sl + 3:sl + 4]
        k_ = ["st%d" % (sl + j) for j in range(6)]
        P.op("dve", lambda e: e.reduce_sum(out=s1, in_=gv, axis=AX.X), r=[K("vs")], w=[k_[0]])
        yield
        P.op("dve", lambda e: e.tensor_scalar(out=nm, in0=s1, scalar1=-0.5 / SGW, scalar2=None, op0=ALU.mult), r=[k_[0]], w=[k_[1]])
        yield
        P.op("dve", lambda e: e.tensor_scalar(out=gv, in0=gv, scalar1=0.5, scalar2=nm, op0=ALU.mult, op1=ALU.add), r=[K("vs"), k_[1]], w=[K("vs")])
        yield
        P.op("act", lambda e: e.activation(out=junk[:, 0:SGW], in_=gv, func=AF.Square, accum_out=s2), r=[K("vs")], w=[K("junk"), k_[2]])
        yield
        rsqrt_ops(s2, r2, 1, 1.0 / SGW, [k_[2]], [k_[3]])
        yield
        P.op("dve", lambda e: e.scalar_tensor_tensor(out=gv, in0=gv, scalar=r2, in1=lng, op0=ALU.mult, op1=ALU.mult), r=[K("vs"), k_[3], "lng"], w=[K("vs")])
        yield
        P.op("dve", lambda e: e.tensor_tensor(out=vlnb, in0=gv, in1=lnb, op=ALU.add), r=[K("vs"), "lnb"], w=[K("vlnb")])
        yield
        for g8 in range(G):
            P.op("pe", lambda e, g8=g8: e.matmul(banks[bA][:, g8 * 64:(g8 + 1) * 64], lhsT=wsT[:, g8, :], rhs=vlnb[:, g8 * 64:(g8 + 1) * 64], start=True, stop=True),
                 r=["wsT", K("vlnb")], w=[kA])
        yield
        P.op("dve", lambda e: e.tensor_tensor(out=tmix.rearrange("p (g d) -> p g d", d=64), in0=banks[bA][:, 0:SGW].rearrange("p (g d) -> p g d", d=64),
                                              in1=sgb.unsqueeze(2).broadcast_to([128, G, 64]), op=ALU.add), r=[kA, "sgb"], w=[K("tmix")])
        yield
        ysg = us
        P.op("dve", lambda e: e.scalar_tensor_tensor(out=ysg, in0=gu, scalar=0.5, in1=tmix, op0=ALU.mult, op1=ALU.mult), r=[K("us"), K("tmix")], w=[K("us")])
        yield
        sl2 = stat_slot(4)
        k2 = ["st%d" % (sl2 + j) for j in range(4)]
        ssq = stats[:, sl2:sl2 + 2]
        rsq = stats[:, sl2 + 2:sl2 + 4]
        P.op("act", lambda e: e.activation(out=junk[:, 0:AW], in_=yatt[:, ob, :], func=AF.Square, accum_out=ssq[:, 0:1]), r=["yatt"], w=[K("junk"), k2[0]])
        yield
        P.op("act", lambda e: e.activation(out=junk[:, 0:SGW], in_=ysg, func=AF.Square, accum_out=ssq[:, 1:2]), r=[K("us")], w=[K("junk"), k2[1]])
        yield
        rsqrt_ops(ssq, rsq, 2, 1.0 / AW, [k2[0], k2[1]], [k2[2], k2[3]])
        yield
        P.op("dve", lambda e: e.tensor_scalar(out=yn[:, 0:AW], in0=yatt[:, ob, :], scalar1=rsq[:, 0:1], scalar2=None, op0=ALU.mult), r=["yatt", k2[2]], w=[K("yn")])
        yield
        P.op("dve", lambda e: e.tensor_scalar(out=yn[:, AW:D], in0=ysg, scalar1=rsq[:, 1:2], scalar2=None, op0=ALU.mult), r=[K("us"), k2[3]], w=[K("yn")])
        yield
        tpv = banksb[bB][:, 0:KC * 128].rearrange("p (k t) -> p k t", t=128)
        for kc in range(KC):
            P.op("pe", lambda e, kc=kc: e.transpose(out=tpv[:, kc, :], in_=yn[:, kc * 128:(kc + 1) * 128], identity=identb), r=[K("yn"), "identb"], w=[kB])
        yield
        P.op("dve", lambda e: e.tensor_tensor(out=ynT, in0=tpv, in1=gcat.unsqueeze(2).broadcast_to([128, KC, 128]), op=ALU.mult), r=[kB, "gcat"], w=[K("ynT")])
        yield
        sl3 = stat_slot(4)
        k3 = ["st%d" % (sl3 + j) for j in range(4)]
        obanks = [bA, bB] if DH == 2 else [bA]
        for dh in range(DH):
            ob_ = obanks[dh]
            for kc in range(KC):
                P.op("pe", lambda e, kc=kc, dh=dh, ob_=ob_: e.matmul(banks[ob_][:, 0:DW], lhsT=ynT[:, kc, :], rhs=Wo[:, kc, dh * DW:(dh + 1) * DW], start=(kc == 0), stop=(kc == KC - 1)),
                     r=[K("ynT"), wok[kc]], w=["bank%d" % ob_])
            yield
            P.op("act", lambda e, dh=dh, ob_=ob_: e.activation(out=junk[:, 0:DW], in_=banks[ob_][:, 0:DW], func=AF.Square, accum_out=stats[:, sl3 + dh:sl3 + dh + 1]),
                 r=["bank%d" % ob_], w=[K("junk"), k3[dh]])
            yield
        if DH == 2:
            P.op("dve", lambda e: e.tensor_tensor(out=stats[:, sl3 + 2:sl3 + 3], in0=stats[:, sl3:sl3 + 1], in1=stats[:, sl3 + 1:sl3 + 2], op=ALU.add), r=[k3[0], k3[1]], w=[k3[2]])
            yield
            sso = stats[:, sl3 + 2:sl3 + 3]
            ssk = k3[2]
        else:
            sso = stats[:, sl3:sl3 + 1]
            ssk = k3[0]
        rso = stats[:, sl3 + 3:sl3 + 4]
        rsqrt_ops(sso, rso, 1, 1.0 / D, [ssk], [k3[3]])
        yield
        hk = K("h1")
        for dh in range(DH):
            ob_ = obanks[dh]
            P.op("dve", lambda e, dh=dh, ob_=ob_: e.scalar_tensor_tensor(out=h1[:, dh * DW:(dh + 1) * DW], in0=banks[ob_][:, 0:DW], scalar=rso, in1=gpmix[:, dh * DW:(dh + 1) * DW], op0=ALU.mult, op1=ALU.mult),
                 r=["bank%d" % ob_, k3[3], "gpmix"], w=[hk])
            yield
        P.op("dve", lambda e: e.tensor_tensor(out=h1, in0=h1, in1=xt, op=ALU.add), r=[hk, xk], w=[hk])
        yield
        P.op("sp", lambda e: e.dma_start(out=h1_d[ob * 128:(ob + 1) * 128, :], in_=h1), r=[hk], w=["h1d%d" % ob], dma=True)
        yield

    interleave((tileC1(ob) for ob in range(NO)), ND1, 18)
    P.barrier()
    AR.top = const_top

    W1 = AR.alloc([128, KC, DFF], BF16)
    W2 = AR.alloc([128, FC, D], BF16)
    HT = AR.alloc([128, FC, 512], BF16)
    cT = AR.alloc([128, KC, 512], BF16)
    gpffn = AR.alloc([128, D], F32)
    fr = Front(nx=2, tpbanks=(0, 1))
    rtmp = [AR.alloc([128, 512], BF16) for _ in range(2)]
    h1r = [AR.alloc([128, D], F32) for _ in range(2)]
    o2t = [AR.alloc([128, D], F32) for _ in range(1)]
    junk2 = AR.alloc([128, 512], BF16)
    w1src = w_ff1.rearrange("(kc p) n -> p kc n", p=128)
    for j in range(DFF // 512):
        P.op("pool", lambda e, j=j: e.dma_start(out=W1[:, :, j * 512:(j + 1) * 512], in_=w1src[:, :, j * 512:(j + 1) * 512]), w=["W1c%d" % j], dma=True)
    w2k = load_w(W2, w_ff2, "W2", FC)
    ld(gpffn, gpffn_d, "gpffn")
    tcount = 0
    def c2_fronts(blocks):
        interleave((fr.gen(h1_d[ob * 128:(ob + 1) * 128, :], (gpf, "gpf"), cT[:, :, ti * 128:(ti + 1) * 128], "cT%d" % ti) for ti, ob in enumerate(blocks)), 2, 2)

    c2_fronts(c.groups[0])
    for g, blocks in enumerate(c.groups):
        N = len(blocks) * 128
        for fc in range(FC):
            hb = 2 + fc % 2
            for kc in range(KC):
                P.op("pe", lambda e, kc=kc, fc=fc, hb=hb, N=N: e.matmul(banks[hb][:, 0:N], lhsT=W1[:, kc, fc * 128:(fc + 1) * 128], rhs=cT[:, kc, 0:N], start=(kc == 0), stop=(kc == KC - 1)),
                     r=["cT%d" % j for j in range(len(blocks))] + ["W1c%d" % (fc // 4)], w=["bank%d" % hb])
            rt = rtmp[fc % 2]
            rk = "rtmp%d" % (fc % 2)
            P.op("act", lambda e, hb=hb, rt=rt, N=N: e.activation(out=rt[:, 0:N], in_=banks[hb][:, 0:N], func=AF.Relu), r=["bank%d" % hb], w=[rk])
            P.op("dve", lambda e, fc=fc, rt=rt, N=N: e.tensor_tensor(out=HT[:, fc, 0:N], in0=rt[:, 0:N], in1=rt[:, 0:N], op=ALU.mult), r=[rk], w=["HT"])
        if g + 1 < NG:
            c2_fronts(c.groups[g + 1])
        for ti, ob in enumerate(blocks):
            i2 = tcount % 2
            tcount += 1
            h1 = h1r[i2]
            hk = "h1r%d" % i2
            P.op("sp", lambda e, h1=h1, ob=ob: e.dma_start(out=h1, in_=h1_d[ob * 128:(ob + 1) * 128, :]), r=["h1d%d" % ob], w=[hk], dma=True)
            sl3 = stat_slot(4)
            k3 = ["st%d" % (sl3 + j) for j in range(4)]
            for dh in range(DH):
                ob_ = 4 + 2 * i2 + dh
                for fc in range(FC):
                    P.op("pe", lambda e, fc=fc, dh=dh, ob_=ob_, ti=ti: e.matmul(banks[ob_][:, 0:DW], lhsT=HT[:, fc, ti * 128:(ti + 1) * 128], rhs=W2[:, fc, dh * DW:(dh + 1) * DW], start=(fc == 0), stop=(fc == FC - 1)),
                         r=["HT", w2k[fc]], w=["bank%d" % ob_])
                P.op("act", lambda e, dh=dh, ob_=ob_, sl3=sl3: e.activation(out=junk2[:, 0:DW], in_=banks[ob_][:, 0:DW], func=AF.Square, accum_out=stats[:, sl3 + dh:sl3 + dh + 1]),
                     r=["bank%d" % ob_], w=["junk2", k3[dh]])
            if DH == 2:
                P.op("dve", lambda e, sl3=sl3: e.tensor_tensor(out=stats[:, sl3 + 2:sl3 + 3], in0=stats[:, sl3:sl3 + 1], in1=stats[:, sl3 + 1:sl3 + 2], op=ALU.add), r=[k3[0], k3[1]], w=[k3[2]])
                sso = stats[:, sl3 + 2:sl3 + 3]
                ssk = k3[2]
            else:
                sso = stats[:, sl3:sl3 + 1]
                ssk = k3[0]
            rso = stats[:, sl3 + 3:sl3 + 4]
            rsqrt_ops(sso, rso, 1, 1.0 / D, [ssk], [k3[3]])
            o2 = o2t[0]
            ok2 = "o2t0"
            for dh in range(DH):
                ob_ = 4 + 2 * i2 + dh
                P.op("dve", lambda e, dh=dh, ob_=ob_, rso=rso, o2=o2: e.scalar_tensor_tensor(out=o2[:, dh * DW:(dh + 1) * DW], in0=banks[ob_][:, 0:DW], scalar=rso, in1=gpffn[:, dh * DW:(dh + 1) * DW], op0=ALU.mult, op1=ALU.mult),
                     r=["bank%d" % ob_, k3[3], "gpffn"], w=[ok2])
            P.op("dve", lambda e, o2=o2, h1=h1: e.tensor_tensor(out=h1, in0=o2, in1=h1, op=ALU.add), r=[ok2, hk], w=[hk])
            P.op("sp", lambda e, h1=h1, ob=ob: e.dma_start(out=h2_d[ob * 128:(ob + 1) * 128, :], in_=h1), r=[hk], w=["h2d%d" % ob], dma=True)
    P.barrier()
    AR.top = const_top

    Wg = AR.alloc([128, KC, D], BF16)
    Wpe = AR.alloc([128, PK, D], BF16)
    gateb = AR.alloc([128, D], F32)
    ND3 = 3
    fr3 = Front(nx=ND3, tpbanks=(0, 1), nxn=ND3)
    S3 = []
    for _ in range(ND3):
        S3.append(dict(h2T=AR.alloc([128, KC, 128], BF16), pt=AR.alloc([128, PLE], F32), pb=AR.alloc([128, PLE], BF16),
                       pT=AR.alloc([128, PK, 128], BF16), z=AR.alloc([128, D], F32), pp=AR.alloc([128, D], F32), o=AR.alloc([128, D], F32)))
    wgk = load_w(Wg, gate_w, "Wg", KC)
    wpk = load_w(Wpe, ple_w, "Wpe", PK)
    ld(gateb, gateb_d, "gateb")
    out_dmas = []
    brot = [0]

    def nbank():
        b_ = 2 + brot[0] % 6
        brot[0] += 1
        return b_

    def tileC3(ob):
        p = ob % ND3
        T = S3[p]
        K = lambda n: "%s3_%d" % (n, p)
        h2T, pt_, pb_, pT_, z, pp, o_ = T["h2T"], T["pt"], T["pb"], T["pT"], T["z"], T["pp"], T["o"]
        P.op("sp", lambda e: e.dma_start(out=pt_, in_=pown[ob * 128:(ob + 1) * 128, :]), w=[K("pt")], dma=True)
        for _ in fr3.gen(h2_d[ob * 128:(ob + 1) * 128, :], None, h2T, K("h2T"), norm=False):
            yield
        xt, xk = fr3.last
        P.op("dve", lambda e: e.tensor_copy(out=pb_, in_=pt_), r=[K("pt")], w=[K("pb")])
        yield
        tb_ = nbank()
        tpv3 = banksb[tb_][:, 0:PK * 128].rearrange("p (k t) -> p k t", t=128)
        for k2_ in range(PK):
            P.op("pe", lambda e, k2_=k2_: e.transpose(out=tpv3[:, k2_, :], in_=pb_[:, k2_ * 128:(k2_ + 1) * 128], identity=identb), r=[K("pb"), "identb"], w=["bank%d" % tb_])
        yield
        P.op("act", lambda e: e.activation(out=pT_, in_=tpv3, func=AF.Copy), r=["bank%d" % tb_], w=[K("pT")])
        yield
        for dh in range(DH):
            gb_ = nbank()
            for kc in range(KC):
                P.op("pe", lambda e, kc=kc, dh=dh, gb_=gb_: e.matmul(banks[gb_][:, 0:DW], lhsT=h2T[:, kc, :], rhs=Wg[:, kc, dh * DW:(dh + 1) * DW], start=(kc == 0), stop=(kc == KC - 1)),
                     r=[K("h2T"), wgk[kc]], w=["bank%d" % gb_])
            yield
            P.op("dve", lambda e, dh=dh, gb_=gb_: e.tensor_tensor(out=z[:, dh * DW:(dh + 1) * DW], in0=banks[gb_][:, 0:DW], in1=gateb[:, dh * DW:(dh + 1) * DW], op=ALU.add),
                 r=["bank%d" % gb_, "gateb"], w=[K("z")])
            yield
            pb2 = nbank()
            for k2_ in range(PK):
                P.op("pe", lambda e, k2_=k2_, dh=dh, pb2=pb2: e.matmul(banks[pb2][:, 0:DW], lhsT=pT_[:, k2_, :], rhs=Wpe[:, k2_, dh * DW:(dh + 1) * DW], start=(k2_ == 0), stop=(k2_ == PK - 1)),
                     r=[K("pT"), wpk[k2_]], w=["bank%d" % pb2])
            yield
            P.op("act", lambda e, dh=dh, pb2=pb2: e.activation(out=pp[:, dh * DW:(dh + 1) * DW], in_=banks[pb2][:, 0:DW], func=AF.Copy), r=["bank%d" % pb2], w=[K("pp")])
            yield
        P.op("act", lambda e: e.activation(out=z, in_=z, func=AF.Tanh, scale=0.5), r=[K("z")], w=[K("z")])
        yield
        P.op("dve", lambda e: e.tensor_scalar(out=z, in0=z, scalar1=0.5, scalar2=0.5, op0=ALU.mult, op1=ALU.add), r=[K("z")], w=[K("z")])
        yield
        P.op("dve", lambda e: e.tensor_tensor(out=o_, in0=z, in1=pp, op=ALU.mult), r=[K("z"), K("pp")], w=[K("o")])
        yield
        P.op("dve", lambda e: e.tensor_tensor(out=o_, in0=o_, in1=xt, op=ALU.add), r=[K("o"), xk], w=[K("o")])
        yield
        out_dmas.append(P.op("sp", lambda e: e.dma_start(out=out_d[ob * 128:(ob + 1) * 128, :], in_=o_), r=[K("o")], dma=True))
        yield

    interleave((tileC3(ob) for ob in range(NO)), ND3, 7)
    P.wait_all("sp", out_dmas)
    P.emit()
    return nc, P


def make_core_inputs(cfg, core, x, p, w_in, f_bias, sg_ln_g, sg_ln_b, sg_w, sg_b, att_out_g, sg_out_g,
                     w_out, pre_mix_g, post_mix_g, pre_ffn_g, post_ffn_g, w_ff1, w_ff2, ple_w, ple_gate_w, ple_gate_b):
    c = cfg
    b, r = core // 4, core % 4
    f32 = np.float32
    blocks = c.owned_blocks(r)
    rows = np.concatenate([np.arange(bl * 128, (bl + 1) * 128) for bl in blocks])

    def fm(v):
        return np.ascontiguousarray(np.asarray(v, f32).reshape(c.KC, 128).T)

    def rep(v):
        return np.ascontiguousarray(np.broadcast_to(np.asarray(v, f32).reshape(1, -1), (128, np.asarray(v).size)))

    k = np.arange(128)[:, None]
    q = np.arange(128)[None, :]
    tri = np.where(k > q, NEG, 0.0).astype(f32)
    full = np.full((128, 128), NEG, f32)
    zero = np.zeros((128, 128), f32)
    maskT = np.zeros((128, 8, 128), f32)
    for i in range(4):
        maskT[:, i, :] = zero if i < r else (tri if i == r else full)
        maskT[:, 4 + i, :] = zero if i < 3 - r else (tri if i == 3 - r else full)
    sel = np.zeros((c.HH, c.HH, 128), f32)
    for hh in range(c.HH):
        sel[hh, hh, :] = 1.0
    LTfull = (np.arange(c.NB)[:, None] < np.arange(c.NB)[None, :]).astype(f32)
    LTown = (np.arange(c.NB)[:, None] < np.asarray(blocks)[None, :]).astype(f32)
    xb = np.asarray(x[b], f32)
    return {
        "xfull": np.ascontiguousarray(xb),
        "xown": np.ascontiguousarray(xb[rows]),
        "pown": np.ascontiguousarray(np.asarray(p[0, b], f32)[rows]),
        "w_in": np.ascontiguousarray(np.asarray(w_in[0], f32)),
        "w_out": np.ascontiguousarray(np.asarray(w_out[0], f32)),
        "w_ff1": np.ascontiguousarray(np.asarray(w_ff1[0], f32)),
        "w_ff2": np.ascontiguousarray(np.asarray(w_ff2[0], f32)),
        "ple_w": np.ascontiguousarray(np.asarray(ple_w[0], f32)),
        "gate_w": np.ascontiguousarray(np.asarray(ple_gate_w[0], f32)),
        "gpm": fm(pre_mix_g[0]),
        "gpf": fm(pre_ffn_g[0]),
        "gcat": fm(np.concatenate([np.asarray(att_out_g[0]), np.asarray(sg_out_g[0])])),
        "gpmix": rep(post_mix_g[0]),
        "gpffn": rep(post_ffn_g[0]),
        "gateb": rep(ple_gate_b[0]),
        "lng": rep(sg_ln_g[0]),
        "lnb": rep(sg_ln_b[0]),
        "fbias": rep(f_bias[0]),
        "sgwT": np.ascontiguousarray(np.transpose(np.asarray(sg_w[0], f32), (2, 0, 1))),
        "sgbT": np.ascontiguousarray(np.asarray(sg_b[0], f32).T),
        "ident": np.eye(128, dtype=f32),
        "U": (np.arange(128)[:, None] <= np.arange(128)[None, :]).astype(f32),
        "maskT": maskT,
        "sel": sel,
        "LTfull": LTfull,
        "LTown": LTown,
    }, rows


_CACHE = {}


def kernel(**inputs):
    x = np.asarray(inputs["x"])
    B, S, D = x.shape
    PLE = np.asarray(inputs["p"]).shape[-1]
    cfg = Cfg(D=D, S=S, PLE=PLE)
    key = (D, S, PLE)
    if key not in _CACHE:
        _CACHE[key] = build_program(cfg)
    nc, _ = _CACHE[key]
    in_maps, rows_all = [], []
    for core in range(8):
        m, rows = make_core_inputs(cfg, core, **inputs)
        in_maps.append(m)
        rows_all.append(rows)
    res = run_bass_kernel_spmd(nc, in_maps, core_ids=list(range(8)))
    out = np.zeros((B, S, D), np.float32)
    for core in range(8):
        out[core // 4, rows_all[core], :] = np.asarray(res.results[core]["out"], np.float32)
    return out
```

```python
import numpy as np
import concourse.bass as bass
import concourse.mybir as mybir
from concourse.bass_utils import run_bass_kernel_spmd

F32 = mybir.dt.float32
BF16 = mybir.dt.bfloat16
AF = mybir.ActivationFunctionType
ALU = mybir.AluOpType
AX = mybir.AxisListType

EPS = 1e-6
NEG = -30000.0


class _Ins:
    __slots__ = ("eng", "idx", "fn", "deps", "signal", "is_dma", "dma_sem", "dma_val", "sig_val", "epoch")

    def __init__(self, eng, idx, fn, is_dma):
        self.eng = eng
        self.idx = idx
        self.fn = fn
        self.deps = set()
        self.signal = False
        self.is_dma = is_dma
        self.dma_sem = None
        self.dma_val = 0
        self.sig_val = 0
        self.epoch = 0


class Prog:
    ENGS = ("pe", "act", "dve", "pool", "sp")

    def __init__(self, nc, n_dma_sems=24):
        self.nc = nc
        self.q = {e: [] for e in self.ENGS}
        self.lastw = {}
        self.readers = {}
        self.n_dma_sems = n_dma_sems
        self.dma_count = 0
        self.dma_last = [None] * n_dma_sems
        self.dma_pools = {"sp": (0, n_dma_sems - 8), "act": (0, n_dma_sems - 8), "pool": (n_dma_sems - 8, 8)}
        self.dma_pool_cnt = {"sp": 0, "act": 0, "pool": 0}
        self.dma_sem_uses = [0] * n_dma_sems
        self.epoch = 0

    def op(self, eng, fn, r=(), w=(), dma=False):
        ins = _Ins(eng, len(self.q[eng]), fn, dma)
        ins.epoch = self.epoch
        deps = ins.deps
        if any(k.startswith("bank") for k in r):
            w = list(w) + [k for k in r if k.startswith("bank") and k not in w]
            r = [k for k in r if not k.startswith("bank")]
        for k in r:
            lw = self.lastw.get(k)
            if lw is not None:
                deps.add(lw)
        for k in w:
            lw = self.lastw.get(k)
            if lw is not None:
                deps.add(lw)
            rd = self.readers.get(k)
            if rd:
                for x in rd[0].values():
                    deps.add(x)
                for x in rd[1]:
                    deps.add(x)
        if dma:
            base, cnt = self.dma_pools[eng]
            pk = "sp" if eng in ("sp", "act") else "pool"
            s = base + self.dma_pool_cnt[pk] % cnt
            self.dma_pool_cnt[pk] += 1
            prev = self.dma_last[s]
            if prev is not None:
                deps.add(prev)
            self.dma_sem_uses[s] += 1
            ins.dma_sem = s
            ins.dma_val = 16 * self.dma_sem_uses[s]
            self.dma_last[s] = ins
            self.dma_count += 1
        deps.discard(ins)
        for k in w:
            self.lastw[k] = ins
            self.readers[k] = ({}, [])
        for k in r:
            rd = self.readers.setdefault(k, ({}, []))
            if dma:
                rd[1].append(ins)
            else:
                rd[0][eng] = ins
        self.q[eng].append(ins)
        return ins

    def barrier(self):
        lasts = []
        for e in self.ENGS:
            for ins in reversed(self.q[e]):
                if not ins.is_dma and ins.fn is not None:
                    lasts.append(ins)
                    break
        dmas = [d for d in self.dma_last if d is not None]
        for e in self.ENGS:
            ins = _Ins(e, len(self.q[e]), None, False)
            ins.epoch = self.epoch
            ins.deps = set(lasts) | set(dmas)
            self.q[e].append(ins)
        self.lastw.clear()
        self.readers.clear()
        self.epoch += 1

    def wait_all(self, eng, instrs):
        ins = _Ins(eng, len(self.q[eng]), None, False)
        ins.epoch = self.epoch
        ins.deps = set(instrs)
        self.q[eng].append(ins)

    def emit(self):
        nc = self.nc
        for e in self.ENGS:
            for ins in self.q[e]:
                for d in ins.deps:
                    if not d.is_dma:
                        d.signal = True
        counts = {}
        for e in self.ENGS:
            c = 0
            ep = 0
            mx = 0
            for ins in self.q[e]:
                if ins.epoch != ep:
                    ep = ins.epoch
                    c = 0
                if (not ins.is_dma) and ins.signal and ins.fn is not None:
                    c += 1
                    ins.sig_val = c
                    mx = max(mx, c)
            counts[e] = mx
        self.counts = counts
        nep = self.epoch + 1
        import contextlib

        with contextlib.ExitStack() as st:
            esem = {(e, ep): st.enter_context(nc.semaphore("s_%s%d" % (e, ep))) for e in self.ENGS for ep in range(nep)}
            dsem = [st.enter_context(nc.semaphore("s_dma%d" % i)) for i in range(self.n_dma_sems)]
            block = st.enter_context(nc.Block())

            def run(e, eng):
                known = {}
                for ins in self.q[e]:
                    waits = {}
                    for d in ins.deps:
                        if d.is_dma:
                            key = ("d", d.dma_sem)
                            val = d.dma_val
                        else:
                            if d.fn is None:
                                continue
                            if d.eng == e and not ins.is_dma:
                                if e == "pe":
                                    continue
                            key = ("e", (d.eng, d.epoch))
                            val = d.sig_val
                        if val > waits.get(key, 0):
                            waits[key] = val
                    for key, val in waits.items():
                        if known.get(key, 0) >= val:
                            continue
                        sem = dsem[key[1]] if key[0] == "d" else esem[key[1]]
                        eng.wait_ge(sem, val)
                        known[key] = val
                    if ins.fn is None:
                        continue
                    bi = ins.fn(eng)
                    if ins.is_dma:
                        bi.then_inc(dsem[ins.dma_sem], 16)
                    elif ins.signal:
                        bi.then_inc(esem[(e, ins.epoch)], 1)

            @block.tensor
            def _(eng):
                run("pe", eng)

            @block.scalar
            def _(eng):
                run("act", eng)

            @block.vector
            def _(eng):
                run("dve", eng)

            @block.gpsimd
            def _(eng):
                run("pool", eng)

            @block.sync
            def _(eng):
                run("sp", eng)


class Cfg:
    def __init__(self, D=1024, S=8192, PLE=256):
        self.D, self.S, self.PLE = D, S, PLE
        self.KC = D // 128
        self.AW = D // 2
        self.H = self.AW // 64
        self.SGW = D // 2
        self.G = self.SGW // 64
        self.DFF = 4 * D
        self.FC = self.DFF // 128
        self.IPW = 3 * self.AW + self.H + 2 * self.SGW
        self.NB = S // 128
        self.J = self.NB // 8
        self.NO = 2 * self.J
        self.HH = self.H // 2
        self.PK = PLE // 128
        self.DH = max(1, D // 512)
        self.DW = min(D, 512)
        NO, J, NB = self.NO, self.J, self.NB
        self.groups = [list(range(i, min(i + 4, NO))) for i in range(0, NO, 4)]
        self.lo = [4 * j for j in range(J)] + [NB - 4 - 4 * j for j in reversed(range(J))]
        self.mtype = [0] * J + [1] * J

    def owned_blocks(self, r):
        J, NB = self.J, self.NB
        return [4 * j + r for j in range(J)] + [NB - 1 - 4 * j - r for j in reversed(range(J))]


class Arena:
    def __init__(self, ap, total):
        self.A = ap
        self.total = total
        self.top = 0

    def alloc(self, shape, dt):
        n = int(np.prod(shape[1:]))
        ne = n * (2 if dt == F32 else 1)
        ne = (ne + 15) // 16 * 16
        assert self.top + ne <= self.total, ("SBUF arena overflow", self.top, ne, self.total)
        v = self.A[:, self.top:self.top + (n * (2 if dt == F32 else 1))]
        self.top += ne
        if dt == F32:
            v = v.bitcast(F32)
        if len(shape) > 2:
            names = " ".join("a%d" % i for i in range(len(shape) - 1))
            kw = {"a%d" % i: int(shape[i + 1]) for i in range(len(shape) - 1)}
            v = v.rearrange("p (%s) -> p %s" % (names, names), **kw)
        if shape[0] < 128:
            v = v[0:shape[0]]
        return v


def _ap(t):
    return t.ap() if hasattr(t, "ap") else t[:]


def build_program(cfg, debug=False):
    c = cfg
    D, S, KC, AW, H, HH, SGW, G, NB, NO = c.D, c.S, c.KC, c.AW, c.H, c.HH, c.SGW, c.G, c.NB, c.NO
    DFF, FC, PLE, PK, DH, DW = c.DFF, c.FC, c.PLE, c.PK, c.DH, c.DW
    NG = len(c.groups)
    HP = H // 2
    HPP = HH // 2
    nc = bass.Bass("TRN2", target_bir_lowering=False)

    def din(name, shape):
        return nc.dram_tensor(name, list(shape), F32, kind="ExternalInput").ap()

    xfull = din("xfull", [S, D])
    xown = din("xown", [NO * 128, D])
    pown = din("pown", [NO * 128, PLE])
    w_in = din("w_in", [D, c.IPW])
    w_out = din("w_out", [D, D])
    w_ff1 = din("w_ff1", [D, DFF])
    w_ff2 = din("w_ff2", [DFF, D])
    ple_w = din("ple_w", [PLE, D])
    gate_w = din("gate_w", [D, D])
    gpm_d = din("gpm", [128, KC])
    gpf_d = din("gpf", [128, KC])
    gcat_d = din("gcat", [128, KC])
    gpmix_d = din("gpmix", [128, D])
    gpffn_d = din("gpffn", [128, D])
    gateb_d = din("gateb", [128, D])
    lng_d = din("lng", [128, SGW])
    lnb_d = din("lnb", [128, SGW])
    fbias_d = din("fbias", [128, H])
    sgwT_d = din("sgwT", [128, G, 128])
    sgbT_d = din("sgbT", [128, G])
    ident_d = din("ident", [128, 128])
    U_d = din("U", [128, 128])
    maskT_d = din("maskT", [128, 8, 128])
    sel_d = din("sel", [HH, HH, 128])
    LTfull_d = din("LTfull", [NB, NB])
    LTown_d = din("LTown", [NB, NO])
    out_d = nc.dram_tensor("out", [NO * 128, D], F32, kind="ExternalOutput").ap()
    h1_d = nc.dram_tensor("h1_scr", [NO * 128, D], F32).ap()
    h2_d = nc.dram_tensor("h2_scr", [NO * 128, D], F32).ap()
    aT_d = nc.dram_tensor("aT_scr", [NB, 128, KC * 128], BF16).ap()
    dbg = {}

    total = (nc.sbuf_bytes_remaining - 2048) // 2
    total = total // 16 * 16
    arena_t = nc.alloc_sbuf_tensor("arena", [128, total], BF16)
    AR = Arena(_ap(arena_t), total)
    banks = [_ap(nc.alloc_psum_tensor("bank%d" % i, [128, 512], F32)) for i in range(8)]
    banksb = [b.bitcast(BF16) for b in banks]

    P = Prog(nc)
    rr = [0]

    def wq():
        return "sp"

    identf = AR.alloc([128, 128], F32)
    identb = AR.alloc([128, 128], BF16)
    Uf = AR.alloc([128, 128], F32)
    onesf = AR.alloc([128, 128], F32)
    maskb = AR.alloc([128, 8, 128], BF16)
    selb = AR.alloc([128, HH, 128], BF16)
    LTfull = AR.alloc([128, NB], F32)
    LTown = AR.alloc([128, NO], F32)
    gpm = AR.alloc([128, KC], F32)
    gpf = AR.alloc([128, KC], F32)
    gcat = AR.alloc([128, KC], F32)
    fbias = AR.alloc([128, H], F32)
    stats = AR.alloc([128, 64], F32)
    const_top = AR.top
    QT = AR.alloc([128, HP, NO * 128], BF16)
    within_own = AR.alloc([128, NO, H], F32)
    yatt = AR.alloc([128, NO, AW], F32)
    persist_top = AR.top

    def ld(dst, src, key, eng="sp"):
        return P.op(eng, lambda e: e.dma_start(out=dst, in_=src), w=[key], dma=True)

    ld(identf, ident_d, "identf")
    ld(Uf, U_d, "Uf")
    ld(LTfull[0:NB], LTfull_d, "LTfull")
    ld(LTown[0:NB], LTown_d, "LTown")
    ld(gpm, gpm_d, "gpm")
    ld(gpf, gpf_d, "gpf")
    ld(gcat, gcat_d, "gcat")
    ld(fbias, fbias_d, "fbias")
    P.op("pool", lambda e: e.dma_start(out=maskb, in_=maskT_d), w=["maskb"], dma=True)
    P.op("pool", lambda e: e.dma_start(out=selb[0:HH], in_=sel_d), w=["selb"], dma=True)
    P.op("dve", lambda e: e.tensor_copy(out=identb, in_=identf), r=["identf"], w=["identb"])
    P.op("dve", lambda e: e.memset(onesf, 1.0), w=["onesf"])

    scnt = [0]

    def stat_slot(n=1):
        s = scnt[0]
        scnt[0] = (scnt[0] + n) % 60
        if s + n > 60:
            s = 0
            scnt[0] = n
        return s

    def load_w(dst, src2d, key, kcn):
        for kc in range(kcn):
            P.op("pool", lambda e, kc=kc: e.dma_start(out=dst[:, kc, :], in_=src2d[kc * 128:(kc + 1) * 128, :]),
                 w=[key + str(kc)], dma=True)
        return [key + str(kc) for kc in range(kcn)]

    def rsqrt_ops(ss_ap, out_ap, n, scale, rkeys, wkeys):
        tmpslot = stat_slot(n)
        tmp = stats[:, tmpslot:tmpslot + n]
        tks = ["st%d" % (tmpslot + j) for j in range(n)]
        P.op("act", lambda e: e.activation(out=tmp, in_=ss_ap, func=AF.Ln, scale=scale, bias=EPS), r=rkeys, w=tks)
        P.op("act", lambda e: e.activation(out=out_ap, in_=tmp, func=AF.Exp, scale=-0.5), r=tks, w=wkeys)

    class Front:
        def __init__(self, nx=3, tpbanks=(0, 1), nxn=2):
            self.xt = [AR.alloc([128, D], F32) for _ in range(nx)]
            self.xn = [AR.alloc([128, D], BF16) for _ in range(nxn)]
            self.i = 0
            self.tpb = tpbanks

        def run(self, rows_ap, gain, dst, dkey, norm=True):
            g = self.gen(rows_ap, gain, dst, dkey, norm)
            for _ in g:
                pass
            return self.last

        def gen(self, rows_ap, gain, dst, dkey, norm=True):
            i = self.i
            self.i += 1
            xt = self.xt[i % len(self.xt)]
            xk = "xt%d_%d" % (id(self) % 1000, i % len(self.xt))
            xn = self.xn[i % len(self.xn)]
            nk = "xn%d_%d" % (id(self) % 1000, i % len(self.xn))
            tb = self.tpb[i % len(self.tpb)]
            tpv = banksb[tb][:, 0:KC * 128].rearrange("p (k t) -> p k t", t=128)
            tk = "bank%d" % tb
            self.last = (xt, xk)
            P.op("sp", lambda e: e.dma_start(out=xt, in_=rows_ap), w=[xk], dma=True)
            yield
            if norm:
                sl = stat_slot(2)
                ss = stats[:, sl:sl + 1]
                rs = stats[:, sl + 1:sl + 2]
                sk = "st%d" % sl
                rk = "st%d" % (sl + 1)
                P.op("act", lambda e: e.activation(out=xn, in_=xt, func=AF.Square, accum_out=ss), r=[xk], w=[nk, sk])
                yield
                rsqrt_ops(ss, rs, 1, 1.0 / D, [sk], [rk])
                yield
                P.op("dve", lambda e: e.tensor_scalar(out=xn, in0=xt, scalar1=rs, scalar2=None, op0=ALU.mult),
                     r=[xk, rk], w=[nk])
            else:
                P.op("dve", lambda e: e.tensor_copy(out=xn, in_=xt), r=[xk], w=[nk])
            yield
            for kc in range(KC):
                P.op("pe", lambda e, kc=kc: e.transpose(out=tpv[:, kc, :], in_=xn[:, kc * 128:(kc + 1) * 128], identity=identb),
                     r=[nk, "identb"], w=[tk])
            yield
            if gain is not None:
                gk = gain[1]
                gb = gain[0].unsqueeze(2).broadcast_to([128, KC, 128])
                P.op("dve", lambda e: e.tensor_tensor(out=dst, in0=tpv, in1=gb, op=ALU.mult), r=[tk, gk], w=[dkey])
            else:
                P.op("act", lambda e: e.activation(out=dst, in_=tpv, func=AF.Copy), r=[tk], w=[dkey])
            yield

    def interleave(gens, depth, period):
        it = iter(gens)
        active = []
        rounds = 0
        done = False
        while True:
            if not done and len(active) < depth and (rounds % period == 0 or not active):
                try:
                    active.append(next(it))
                except StopIteration:
                    done = True
            if not active:
                if done:
                    break
                continue
            for g in list(active):
                try:
                    next(g)
                except StopIteration:
                    active.remove(g)
            rounds += 1

    def softplus_neg(dst, src, n_keys_r, wkey, tmp):
        P.op("act", lambda e: e.activation(out=tmp, in_=src, func=AF.Exp, scale=-1.0), r=n_keys_r, w=[wkey + "_e"])
        P.op("act", lambda e: e.activation(out=dst, in_=tmp, func=AF.Ln, scale=1.0, bias=1.0), r=[wkey + "_e"], w=[wkey])

    qcol = 0
    kcol = AW
    vcol = 2 * AW
    fcol = 3 * AW
    ucol = 3 * AW + H
    vscol = ucol + SGW
    w_in_r = w_in

    mark = AR.top
    Wq = AR.alloc([128, KC, AW], BF16)
    Wfa = AR.alloc([128, KC, H], BF16)
    fr = Front(nx=3, tpbanks=(0, 1))
    aTq = [AR.alloc([128, KC, 512], BF16) for _ in range(2)]
    fown = AR.alloc([128, NO, H], F32)
    spo = AR.alloc([128, NO, H], F32)
    spo_e = AR.alloc([128, NO, H], F32)
    wqk = load_w(Wq, w_in_r[:, qcol:qcol + AW], "Wq", KC)
    wfk = load_w(Wfa, w_in_r[:, fcol:fcol + H], "Wfa", KC)
    for g, blocks in enumerate(c.groups):
        aT = aTq[g % 2]
        ak = "aTq%d" % (g % 2)
        N = len(blocks) * 128
        for ti, ob in enumerate(blocks):
            fr.run(xown[ob * 128:(ob + 1) * 128, :], (gpm, "gpm"), aT[:, :, ti * 128:(ti + 1) * 128], ak)
            for kc in range(KC):
                P.op("pe", lambda e, kc=kc, ti=ti, aT=aT: e.matmul(banks[4][:, 0:H], lhsT=aT[:, kc, ti * 128:(ti + 1) * 128], rhs=Wfa[:, kc, :],
                                                                   start=(kc == 0), stop=(kc == KC - 1)), r=[ak, wfk[kc]], w=["bank4"])
            P.op("dve", lambda e, ob=ob: e.tensor_tensor(out=fown[:, ob, :], in0=banks[4][:, 0:H], in1=fbias, op=ALU.add),
                 r=["bank4", "fbias"], w=["fown"])
        for hp in range(HP):
            qb = 2 + (hp % 2)
            for kc in range(KC):
                P.op("pe", lambda e, kc=kc, hp=hp, aT=aT, qb=qb, N=N: e.matmul(banks[qb][:, 0:N], lhsT=Wq[:, kc, hp * 128:(hp + 1) * 128], rhs=aT[:, kc, 0:N],
                                                                              start=(kc == 0), stop=(kc == KC - 1)), r=[ak, wqk[kc]], w=["bank%d" % qb])
            P.op("act", lambda e, hp=hp, qb=qb, g=g, N=N: e.activation(out=QT[:, hp, g * 512:g * 512 + N], in_=banks[qb][:, 0:N], func=AF.Copy, scale=0.125),
                 r=["bank%d" % qb], w=["QT"])
    softplus_neg(spo, fown, ["fown"], "spo", spo_e)
    P.op("pe", lambda e: e.matmul(banks[5][:, 0:NO * H], lhsT=Uf, rhs=spo.rearrange("p a b -> p (a b)"), start=True, stop=True),
         r=["Uf", "spo"], w=["bank5"])
    P.op("dve", lambda e: e.tensor_copy(out=within_own.rearrange("p a b -> p (a b)"), in_=banks[5][:, 0:NO * H]), r=["bank5"], w=["within_own"])
    P.barrier()
    AR.top = mark

    KT = AR.alloc([128, HPP, S], BF16)
    Vflat = AR.alloc([128, NB * HH * 65 + 64], BF16)
    Vaug = Vflat[:, 0:NB * HH * 65].rearrange("p (a b c) -> p a b c", a=NB, b=HH)
    biasT = AR.alloc([128, NG, NB, HH], F32)
    R8 = AR.alloc([128, NO * 128], BF16)
    rbc = AR.alloc([128, HH, NO * 128], BF16)
    attn_top = AR.top
    P.op("dve", lambda e: e.memset(Vaug[:, :, :, 64:65], 1.0), w=["Vones"])
    P.op("dve", lambda e: e.memset(Vflat[:, NB * HH * 65:NB * HH * 65 + 64], 0.0), w=["Vpad"])

    for hs in range(2):
        mark = AR.top
        fsb = AR.alloc([128, NB, HH], F32)
        spf = AR.alloc([128, NB, HH], F32)
        spf_e = AR.alloc([128, NB, HH], F32)
        wsb = AR.alloc([128, NB, HH], F32)
        Cpos = AR.alloc([128, NB, HH], F32)
        totT = AR.alloc([128, HH], F32)
        rhs_full = AR.alloc([128, NB, HH], F32)
        rhs_own = AR.alloc([128, NO, HH], F32)
        pexo = AR.alloc([128, NO, HH], F32)
        rt1 = AR.alloc([128, NO, HH], F32)
        Rtok = AR.alloc([128, NO, HH], F32)
        markA = AR.top
        Wk = AR.alloc([128, KC, HH * 64], BF16)
        VF = HH * 64 + HH
        Wv = AR.alloc([128, KC, VF], BF16)
        fr = Front(nx=4, tpbanks=(0, 1, 4, 7), nxn=3)
        aTa = [AR.alloc([128, KC, 512], BF16) for _ in range(2)]
        wkk = load_w(Wk, w_in_r[:, kcol + hs * HH * 64: kcol + (hs + 1) * HH * 64], "Wk", KC)
        wvk = []
        for kc in range(KC):
            P.op("pool", lambda e, kc=kc, hs=hs, Wv=Wv: e.dma_start(out=Wv[:, kc, 0:HH * 64], in_=w_in_r[kc * 128:(kc + 1) * 128, vcol + hs * HH * 64: vcol + (hs + 1) * HH * 64]),
                 w=["Wv%da" % kc], dma=True)
            P.op("pool", lambda e, kc=kc, hs=hs, Wv=Wv: e.dma_start(out=Wv[:, kc, HH * 64:VF], in_=w_in_r[kc * 128:(kc + 1) * 128, fcol + hs * HH: fcol + (hs + 1) * HH]),
                 w=["Wv%db" % kc], dma=True)
            wvk.append(["Wv%da" % kc, "Wv%db" % kc])
        nst = (NB + 3) // 4

        def tileA(t, hs=hs, Wk=Wk, Wv=Wv, fsb=fsb, aTa=aTa, fr=fr, wkk=wkk, wvk=wvk):
            st, ti = t // 4, t % 4
            aT = aTa[st % 2]
            aks = ["aTa%d_%d" % (st % 2, j) for j in range(4)]
            ak = aks[ti]
            if hs == 0:
                for _ in fr.gen(xfull[t * 128:(t + 1) * 128, :], (gpm, "gpm"), aT[:, :, ti * 128:(ti + 1) * 128], ak):
                    yield
                P.op("sp", lambda e: e.dma_start(out=aT_d[t].rearrange("p (k t) -> p k t", t=128), in_=aT[:, :, ti * 128:(ti + 1) * 128]), r=[ak], w=["aTd%d" % t], dma=True)
                yield
            else:
                P.op("sp", lambda e: e.dma_start(out=aT[:, :, ti * 128:(ti + 1) * 128], in_=aT_d[t].rearrange("p (k t) -> p k t", t=128)), w=[ak], dma=True)
                yield
            vb = 2 + (t % 2)
            vk = "bank%d" % vb
            for kc in range(KC):
                P.op("pe", lambda e, kc=kc: e.matmul(banks[vb][:, 0:VF], lhsT=aT[:, kc, ti * 128:(ti + 1) * 128], rhs=Wv[:, kc, :],
                                                     start=(kc == 0), stop=(kc == KC - 1)), r=[ak] + wvk[kc], w=[vk])
            yield
            P.op("act", lambda e: e.activation(out=Vaug[:, t, :, 0:64], in_=banks[vb][:, 0:HH * 64].rearrange("p (h d) -> p h d", d=64), func=AF.Copy),
                 r=[vk], w=["V%d" % t])
            yield
            P.op("act", lambda e: e.activation(out=fsb[:, t, :], in_=banks[vb][:, HH * 64:VF], func=AF.Copy), r=[vk], w=["fsb"])
            yield
            if ti == 3 or t == NB - 1:
                N = (ti + 1) * 128
                for hpl in range(HPP):
                    kb_ = 5 + (hpl % 2)
                    for kc in range(KC):
                        P.op("pe", lambda e, kc=kc, hpl=hpl, kb_=kb_: e.matmul(banks[kb_][:, 0:N], lhsT=Wk[:, kc, hpl * 128:(hpl + 1) * 128], rhs=aT[:, kc, 0:N],
                                                                               start=(kc == 0), stop=(kc == KC - 1)), r=aks[0:ti + 1] + [wkk[kc]], w=["bank%d" % kb_])
                    yield
                    P.op("act", lambda e, hpl=hpl, kb_=kb_: e.activation(out=KT[:, hpl, st * 512:st * 512 + N], in_=banks[kb_][:, 0:N], func=AF.Copy),
                         r=["bank%d" % kb_], w=["KT%d" % st])
                    yield

        interleave((tileA(t) for t in range(NB)), 4, 2)

        P.op("dve", lambda e, hs=hs, fsb=fsb: e.tensor_tensor(out=fsb, in0=fsb, in1=fbias[:, hs * HH:(hs + 1) * HH].unsqueeze(1).broadcast_to([128, NB, HH]), op=ALU.add),
             r=["fsb", "fbias"], w=["fsb"])
        softplus_neg(spf, fsb, ["fsb"], "spf", spf_e)
        spf2 = spf.rearrange("p a b -> p (a b)")
        P.op("pe", lambda e: e.matmul(banks[0][:, 0:NB * HH], lhsT=Uf, rhs=spf2, start=True, stop=True), r=["Uf", "spf"], w=["bank0"])
        P.op("dve", lambda e: e.tensor_copy(out=wsb.rearrange("p a b -> p (a b)"), in_=banks[0][:, 0:NB * HH]), r=["bank0"], w=["wsb"])
        for hh in range(HH):
            P.op("pe", lambda e, hh=hh: e.matmul(banks[1][0:NB, hh:hh + 1], lhsT=spf[:, :, hh], rhs=onesf[:, 0:1], start=True, stop=True),
                 r=["spf", "onesf"], w=["bank1"])
        P.op("dve", lambda e: e.tensor_copy(out=totT[0:NB, :], in_=banks[1][0:NB, 0:HH]), r=["bank1"], w=["totT"])
        P.op("dve", lambda e: e.tensor_tensor(out=rhs_full[0:NB], in0=LTfull[0:NB].unsqueeze(2).broadcast_to([NB, NB, HH]),
                                              in1=totT[0:NB].unsqueeze(1).broadcast_to([NB, NB, HH]), op=ALU.mult), r=["LTfull", "totT"], w=["rhs_full"])
        P.op("dve", lambda e: e.tensor_tensor(out=rhs_own[0:NB], in0=LTown[0:NB].unsqueeze(2).broadcast_to([NB, NO, HH]),
                                              in1=totT[0:NB].unsqueeze(1).broadcast_to([NB, NO, HH]), op=ALU.mult), r=["LTown", "totT"], w=["rhs_own"])
        P.op("pe", lambda e: e.matmul(banks[2][:, 0:NB * HH], lhsT=onesf[0:NB, :], rhs=rhs_full[0:NB].rearrange("p a b -> p (a b)"), start=True, stop=True),
             r=["onesf", "rhs_full"], w=["bank2"])
        P.op("pe", lambda e: e.matmul(banks[3][:, 0:NO * HH], lhsT=onesf[0:NB, :], rhs=rhs_own[0:NB].rearrange("p a b -> p (a b)"), start=True, stop=True),
             r=["onesf", "rhs_own"], w=["bank3"])
        P.op("dve", lambda e: e.tensor_tensor(out=Cpos.rearrange("p a b -> p (a b)"), in0=banks[2][:, 0:NB * HH], in1=wsb.rearrange("p a b -> p (a b)"), op=ALU.add),
             r=["bank2", "wsb"], w=["Cpos"])
        P.op("dve", lambda e: e.tensor_copy(out=pexo.rearrange("p a b -> p (a b)"), in_=banks[3][:, 0:NO * HH]), r=["bank3"], w=["pexo"])
        for g, blocks in enumerate(c.groups):
            g0 = blocks[0]
            nb = len(blocks)
            P.op("dve", lambda e, g=g, g0=g0: e.tensor_tensor(out=biasT[:, g, :, :], in0=Cpos, in1=pexo[:, g0:g0 + 1, :].broadcast_to([128, NB, HH]), op=ALU.subtract),
                 r=["Cpos", "pexo"], w=["biasT"])
            P.op("dve", lambda e, g0=g0, nb=nb: e.tensor_tensor(out=rt1[:, g0:g0 + nb, :], in0=pexo[:, g0:g0 + 1, :].broadcast_to([128, nb, HH]), in1=pexo[:, g0:g0 + nb, :], op=ALU.subtract),
                 r=["pexo"], w=["rt1"])
            P.op("dve", lambda e, g0=g0, nb=nb, hs=hs: e.tensor_tensor(out=Rtok[:, g0:g0 + nb, :], in0=rt1[:, g0:g0 + nb, :], in1=within_own[:, g0:g0 + nb, hs * HH:(hs + 1) * HH], op=ALU.subtract),
                 r=["rt1", "within_own"], w=["Rtok"])
            for ti, ob in enumerate(blocks):
                P.op("pe", lambda e, ti=ti, ob=ob: e.matmul(banks[4][0:HH, ti * 128:(ti + 1) * 128], lhsT=Rtok[:, ob, :], rhs=identf, start=True, stop=True),
                     r=["Rtok", "identf"], w=["bank4"])
            P.op("dve", lambda e, g0=g0, nb=nb: e.tensor_copy(out=R8[0:HH, g0 * 128:(g0 + nb) * 128], in_=banks[4][0:HH, 0:nb * 128]), r=["bank4"], w=["R8"])

        for hh in range(HH):
            for g, blocks in enumerate(c.groups):
                g0, nb = blocks[0], len(blocks)
                rb_ = 5 + ((hh * NG + g) % 2)
                P.op("pe", lambda e, hh=hh, g0=g0, nb=nb, rb_=rb_: e.matmul(banks[rb_][:, 0:nb * 128], lhsT=selb[0:HH, hh, :], rhs=R8[0:HH, g0 * 128:(g0 + nb) * 128], start=True, stop=True),
                     r=["selb", "R8"], w=["bank%d" % rb_])
                P.op("dve", lambda e, hh=hh, g0=g0, nb=nb, rb_=rb_: e.tensor_copy(out=rbc[:, hh, g0 * 128:(g0 + nb) * 128], in_=banks[rb_][:, 0:nb * 128]),
                     r=["bank%d" % rb_], w=["rbc"])

        P.barrier()
        AR.top = markA
        NPT = 6
        QTp = AR.alloc([128, HH, NO * 128], BF16)
        P.op("pool", lambda e, QTp=QTp: e.memset(QTp, 0.0), w=["QTp"])
        for hh in range(HH):
            h_ = hs * HH + hh
            e2 = h_ % 2
            P.op("dve", lambda e, hh=hh, h_=h_, e2=e2, QTp=QTp: e.tensor_copy(out=QTp[e2 * 64:(e2 + 1) * 64, hh, :], in_=QT[e2 * 64:(e2 + 1) * 64, h_ // 2, :]),
                 r=["QT", "QTp"], w=["QTp"])
        pts = [AR.alloc([128, 512], BF16) for _ in range(NPT)]
        osb = [AR.alloc([128, 512], F32) for _ in range(2)]
        rc = AR.alloc([128, 8], F32)
        kmaxs = [c.lo[blocks[-1]] + 3 for blocks in c.groups]
        batches = []
        for hh in range(HH):
            for kb in range(NB):
                act_g = [g for g in range(NG) if kb <= kmaxs[g]]
                for j in range(0, len(act_g), 1):
                    batches.append((hh, kb, act_g[j:j + 1]))
        epi = [0]

        def geom(g, kb):
            blocks = c.groups[g]
            fa = 0
            while c.lo[blocks[fa]] + 3 < kb:
                fa += 1
            N = (len(blocks) - fa) * 128
            c0 = blocks[fa] * 128
            msk = [(bi, kb - c.lo[ob]) for bi, ob in enumerate(blocks) if bi >= fa and c.lo[ob] <= kb <= c.lo[ob] + 3]
            return blocks, fa, N, c0, msk

        def emit_S(i):
            hh, kb, gs = batches[i]
            h = hs * HH + hh
            hpg, e_, hpl = h // 2, h % 2, hh // 2
            info = []
            for j, g in enumerate(gs):
                blocks, fa, N, c0, msk = geom(g, kb)
                sb = i % 4
                info.append((g, blocks, fa, N, c0, msk, sb))
            for (g, blocks, fa, N, c0, msk, sb) in info:
                P.op("pe", lambda e, N=N, c0=c0, sb=sb, nm=len(msk): e.matmul(banks[sb][:, 0:N], lhsT=KT[:, hpl, kb * 128:(kb + 1) * 128],
                                                                rhs=QTp[:, hh, c0:c0 + N], start=True, stop=(nm == 0)),
                     r=["KT%d" % (kb // 4), "QTp"], w=["bank%d" % sb])
            for (g, blocks, fa, N, c0, msk, sb) in info:
                for mi, (bi, i4) in enumerate(msk):
                    mt = c.mtype[blocks[bi]] * 4 + i4
                    P.op("pe", lambda e, bi=bi, mt=mt, mi=mi, fa=fa, sb=sb, nm=len(msk): e.matmul(banks[sb][:, (bi - fa) * 128:(bi - fa + 1) * 128], lhsT=identb, rhs=maskb[:, mt, :],
                                                                                              start=False, stop=(mi == nm - 1)), r=["identb", "maskb"], w=["bank%d" % sb])
            for (g, blocks, fa, N, c0, msk, sb) in info:
                P.op("dve", lambda e, N=N, c0=c0, sb=sb: e.tensor_tensor(out=banks[sb][:, 0:N], in0=banks[sb][:, 0:N], in1=rbc[:, hh, c0:c0 + N], op=ALU.add),
                     r=["bank%d" % sb, "rbc"], w=["bank%d" % sb])
            for j, (g, blocks, fa, N, c0, msk, sb) in enumerate(info):
                pi = i % NPT
                P.op("act", lambda e, N=N, sb=sb, g=g, pi=pi: e.activation(out=pts[pi][:, 0:N], in_=banks[sb][:, 0:N], func=AF.Exp, bias=biasT[:, g, kb, hh:hh + 1], scale=1.0),
                     r=["bank%d" % sb, "biasT"], w=["pt%d" % pi])

        def emit_PV(i):
            hh, kb, gs = batches[i]
            h = hs * HH + hh
            for j, g in enumerate(gs):
                blocks, fa, N, c0, msk = geom(g, kb)
                ob_ = 4 + g
                ok = "bank%d" % ob_
                pi = i % NPT
                last = (kb == kmaxs[g])
                vo = (kb * HH + hh) * 65
                vkeys = ["V%d" % kb, "Vones", "Vpad"] + (["V%d" % (kb + 1)] if kb + 1 < NB else [])
                P.op("pe", lambda e, fa=fa, N=N, ob_=ob_, pi=pi, last=last, vo=vo: e.matmul(banks[ob_][:, fa * 128:fa * 128 + N], lhsT=Vflat[:, vo:vo + 128], rhs=pts[pi][:, 0:N], start=(kb == 0), stop=last),
                     r=vkeys + ["pt%d" % pi], w=[ok])
            for j, g in enumerate(gs):
                if kb != kmaxs[g]:
                    continue
                blocks = c.groups[g]
                ob_ = 4 + g
                ok = "bank%d" % ob_
                nb = len(blocks)
                ei = epi[0] % 2
                epi[0] += 1
                os_ = osb[ei]
                osk = "osb%d" % ei
                tb = i % 4
                tk = "bank%d" % tb
                P.op("dve", lambda e, os_=os_, ob_=ob_, nb=nb: e.tensor_copy(out=os_[0:65, 0:nb * 128], in_=banks[ob_][0:65, 0:nb * 128]), r=[ok], w=[osk])
                for ti, ob in enumerate(blocks):
                    P.op("pe", lambda e, ti=ti, os_=os_, tb=tb: e.matmul(banks[tb][:, ti * 65:(ti + 1) * 65], lhsT=os_[0:65, ti * 128:(ti + 1) * 128], rhs=identf[0:65, 0:65], start=True, stop=True),
                         r=[osk, "identf"], w=[tk])
                o3 = banks[tb][:, 0:nb * 65].rearrange("p (b x) -> p b x", x=65)
                P.op("dve", lambda e, o3=o3, nb=nb: e.reciprocal(out=rc[:, 0:nb], in_=o3[:, :, 64]), r=[tk], w=["rc"])
                for ti, ob in enumerate(blocks):
                    P.op("dve", lambda e, ti=ti, ob=ob, o3=o3, h=h: e.tensor_scalar(out=yatt[:, ob, h * 64:(h + 1) * 64], in0=o3[:, ti, 0:64], scalar1=rc[:, ti:ti + 1], scalar2=None, op0=ALU.mult),
                         r=[tk, "rc"], w=["yatt"])

        SKEW = 3
        for i in range(len(batches)):
            emit_S(i)
            if i >= SKEW:
                emit_PV(i - SKEW)
        for i in range(max(0, len(batches) - SKEW), len(batches)):
            emit_PV(i)
        P.barrier()
        AR.top = mark

    AR.top = attn_top - 0
    AR.top = persist_top
    C0 = 0.7978845608028654
    C1_ = 0.044715
    Wu = AR.alloc([128, KC, SGW], BF16)
    Wvs = AR.alloc([128, KC, SGW], BF16)
    Wo = AR.alloc([128, KC, D], BF16)
    wsT = AR.alloc([128, G, 128], BF16)
    wsTf = AR.alloc([128, G, 128], F32)
    sgb = AR.alloc([128, G], F32)
    lng = AR.alloc([128, SGW], F32)
    lnb = AR.alloc([128, SGW], F32)
    gpmix = AR.alloc([128, D], F32)
    ND1 = 3
    fr = Front(nx=ND1, tpbanks=(0, 1), nxn=ND1)
    aTc = [AR.alloc([128, KC, 128], BF16) for _ in range(ND1)]
    TN = ["us", "vs", "x2u", "x2v", "tmix"]
    TT = [{n: AR.alloc([128, SGW], F32) for n in TN} for _ in range(ND1)]
    for p_ in range(ND1):
        TT[p_]["vlnb"] = AR.alloc([128, SGW], BF16)
        TT[p_]["junk"] = AR.alloc([128, D], BF16)
        TT[p_]["yn"] = AR.alloc([128, D], BF16)
        TT[p_]["ynT"] = AR.alloc([128, KC, 128], BF16)
        TT[p_]["h1"] = AR.alloc([128, D], F32)
    wuk = load_w(Wu, w_in_r[:, ucol:ucol + SGW], "Wu", KC)
    wvsk = load_w(Wvs, w_in_r[:, vscol:vscol + SGW], "Wvs", KC)
    wok = load_w(Wo, w_out, "Wo", KC)
    ld(wsTf, sgwT_d, "wsTf")
    ld(sgb, sgbT_d, "sgb")
    ld(lng, lng_d, "lng")
    ld(lnb, lnb_d, "lnb")
    ld(gpmix, gpmix_d, "gpmix")
    P.op("dve", lambda e: e.tensor_tensor(out=wsT, in0=wsTf, in1=Uf.unsqueeze(1).broadcast_to([128, G, 128]), op=ALU.mult), r=["wsTf", "Uf"], w=["wsT"])

    def gelu2(T, p, dst, dk, ps, pk, tag):
        x2 = T["x2" + tag]
        kx = "x2%s_%d" % (tag, p)
        P.op("act", lambda e: e.activation(out=x2, in_=ps, func=AF.Square), r=[pk], w=[kx])
        yield
        P.op("dve", lambda e: e.tensor_scalar(out=x2, in0=x2, scalar1=C1_, scalar2=1.0, op0=ALU.mult, op1=ALU.add), r=[kx], w=[kx])
        yield
        P.op("dve", lambda e: e.tensor_tensor(out=x2, in0=x2, in1=ps, op=ALU.mult), r=[kx, pk], w=[kx])
        yield
        P.op("act", lambda e: e.activation(out=x2, in_=x2, func=AF.Tanh, scale=C0), r=[kx], w=[kx])
        yield
        P.op("dve", lambda e: e.scalar_tensor_tensor(out=dst, in0=x2, scalar=1.0, in1=ps, op0=ALU.add, op1=ALU.mult), r=[kx, pk], w=[dk])
        yield

    def tileC1(ob):
        p = ob % ND1
        T = TT[p]
        aT = aTc[p]
        ak = "aTc%d" % p
        bA, bB = 2 + 2 * p, 3 + 2 * p
        kA, kB = "bank%d" % bA, "bank%d" % bB
        K = lambda n: "%s_%d" % (n, p)
        us, vs, tmix, vlnb, junk, yn, ynT, h1 = (T[n] for n in ("us", "vs", "tmix", "vlnb", "junk", "yn", "ynT", "h1"))
        for _ in fr.gen(xown[ob * 128:(ob + 1) * 128, :], (gpm, "gpm"), aT, ak):
            yield
        xt, xk = fr.last
        for kc in range(KC):
            P.op("pe", lambda e, kc=kc: e.matmul(banks[bB][:, 0:SGW], lhsT=aT[:, kc, :], rhs=Wvs[:, kc, :], start=(kc == 0), stop=(kc == KC - 1)),
                 r=[ak, wvsk[kc]], w=[kB])
        for kc in range(KC):
            P.op("pe", lambda e, kc=kc: e.matmul(banks[bA][:, 0:SGW], lhsT=aT[:, kc, :], rhs=Wu[:, kc, :], start=(kc == 0), stop=(kc == KC - 1)),
                 r=[ak, wuk[kc]], w=[kA])
        yield
        P.op("act", lambda e: e.activation(out=vs, in_=banks[bB][:, 0:SGW], func=AF.Copy), r=[kB], w=[K("vs")])
        yield
        P.op("act", lambda e: e.activation(out=us, in_=banks[bA][:, 0:SGW], func=AF.Copy), r=[kA], w=[K("us")])
        yield
        g1 = gelu2(T, p, vs, K("vs"), vs, K("vs"), "v")
        g2 = gelu2(T, p, us, K("us"), us, K("us"), "u")
        for _ in g1:
            next(g2, None)
            yield
        for _ in g2:
            yield
        gv, gu = vs, us
        sl = stat_slot(6)
        s1 = stats[:, sl:sl + 1]
        nm = stats[:, sl + 1:sl + 2]
        s2 = stats[:, sl + 2:sl + 3]
        r2 = stats[:, sl + 3:sl + 4]
        k_ = ["st%d" % (sl + j) for j in range(6)]
        P.op("dve", lambda e: e.reduce_sum(out=s1, in_=gv, axis=AX.X), r=[K("vs")], w=[k_[0]])
        yield
        P.op("dve", lambda e: e.tensor_scalar(out=nm, in0=s1, scalar1=-0.5 / SGW, scalar2=None, op0=ALU.mult), r=[k_[0]], w=[k_[1]])
        yield
        P.op("dve", lambda e: e.tensor_scalar(out=gv, in0=gv, scalar1=0.5, scalar2=nm, op0=ALU.mult, op1=ALU.add), r=[K("vs"), k_[1]], w=[K("vs")])
        yield
        P.op("act", lambda e: e.activation(out=junk[:, 0:SGW], in_=gv, func=AF.Square, accum_out=s2), r=[K("vs")], w=[K("junk"), k_[2]])
        yield
        rsqrt_ops(s2, r2, 1, 1.0 / SGW, [k_[2]], [k_[3]])
        yield
        P.op("dve", lambda e: e.scalar_tensor_tensor(out=gv, in0=gv, scalar=r2, in1=lng, op0=ALU.mult, op1=ALU.mult), r=[K("vs"), k_[3], "lng"], w=[K("vs")])
        yield
        P.op("dve", lambda e: e.tensor_tensor(out=vlnb, in0=gv, in1=lnb, op=ALU.add), r=[K("vs"), "lnb"], w=[K("vlnb")])
        yield
        for g8 in range(G):
            P.op("pe", lambda e, g8=g8: e.matmul(banks[bA][:, g8 * 64:(g8 + 1) * 64], lhsT=wsT[:, g8, :], rhs=vlnb[:, g8 * 64:(g8 + 1) * 64], start=True, stop=True),
                 r=["wsT", K("vlnb")], w=[kA])
        yield
        P.op("dve", lambda e: e.tensor_tensor(out=tmix.rearrange("p (g d) -> p g d", d=64), in0=banks[bA][:, 0:SGW].rearrange("p (g d) -> p g d", d=64),
                                              in1=sgb.unsqueeze(2).broadcast_to([128, G, 64]), op=ALU.add), r=[kA, "sgb"], w=[K("tmix")])
        yield
        ysg = us
        P.op("dve", lambda e: e.scalar_tensor_tensor(out=ysg, in0=gu, scalar=0.5, in1=tmix, op0=ALU.mult, op1=ALU.mult), r=[K("us"), K("tmix")], w=[K("us")])
        yield
        sl2 = stat_slot(4)
        k2 = ["st%d" % (sl2 + j) for j in range(4)]
        ssq = stats[:, sl2:sl2 + 2]
        rsq = stats[:, sl2 + 2:sl2 + 4]
        P.op("act", lambda e: e.activation(out=junk[:, 0:AW], in_=yatt[:, ob, :], func=AF.Square, accum_out=ssq[:, 0:1]), r=["yatt"], w=[K("junk"), k2[0]])
        yield
        P.op("act", lambda e: e.activation(out=junk[:, 0:SGW], in_=ysg, func=AF.Square, accum_out=ssq[:, 1:2]), r=[K("us")], w=[K("junk"), k2[1]])
        yield
        rsqrt_ops(ssq, rsq, 2, 1.0 / AW, [k2[0], k2[1]], [k2[2], k2[3]])
        yield
        P.op("dve", lambda e: e.tensor_scalar(out=yn[:, 0:AW], in0=yatt[:, ob, :], scalar1=rsq[:, 0:1], scalar2=None, op0=ALU.mult), r=["yatt", k2[2]], w=[K("yn")])
        yield
        P.op("dve", lambda e: e.tensor_scalar(out=yn[:, AW:D], in0=ysg, scalar1=rsq[:, 1:2], scalar2=None, op0=ALU.mult), r=[K("us"), k2[3]], w=[K("yn")])
        yield
        tpv = banksb[bB][:, 0:KC * 128].rearrange("p (k t) -> p k t", t=128)
        for kc in range(KC):
            P.op("pe", lambda e, kc=kc: e.transpose(out=tpv[:, kc, :], in_=yn[:, kc * 128:(kc + 1) * 128], identity=identb), r=[K("yn"), "identb"], w=[kB])
        yield
        P.op("dve", lambda e: e.tensor_tensor(out=ynT, in0=tpv, in1=gcat.unsqueeze(2).broadcast_to([128, KC, 128]), op=ALU.mult), r=[kB, "gcat"], w=[K("ynT")])
        yield
        sl3 = stat_slot(4)
        k3 = ["st%d" % (sl3 + j) for j in range(4)]
        obanks = [bA, bB] if DH == 2 else [bA]
        for dh in range(DH):
            ob_ = obanks[dh]
            for kc in range(KC):
                P.op("pe", lambda e, kc=kc, dh=dh, ob_=ob_: e.matmul(banks[ob_][:, 0:DW], lhsT=ynT[:, kc, :], rhs=Wo[:, kc, dh * DW:(dh + 1) * DW], start=(kc == 0), stop=(kc == KC - 1)),
                     r=[K("ynT"), wok[kc]], w=["bank%d" % ob_])
            yield
            P.op("act", lambda e, dh=dh, ob_=ob_: e.activation(out=junk[:, 0:DW], in_=banks[ob_][:, 0:DW], func=AF.Square, accum_out=stats[:, sl3 + dh:sl3 + dh + 1]),
                 r=["bank%d" % ob_], w=[K("junk"), k3[dh]])
            yield
        if DH == 2:
            P.op("dve", lambda e: e.tensor_tensor(out=stats[:, sl3 + 2:sl3 + 3], in0=stats[:, sl3:sl3 + 1], in1=stats[:, sl3 + 1:sl3 + 2], op=ALU.add), r=[k3[0], k3[1]], w=[k3[2]])
            yield
            sso = stats[:, sl3 + 2:sl3 + 3]
            ssk = k3[2]
        else:
            sso = stats[:, sl3:sl3 + 1]
            ssk = k3[0]
        rso = stats[:, sl3 + 3:sl3 + 4]
        rsqrt_ops(sso, rso, 1, 1.0 / D, [ssk], [k3[3]])
        yield
        hk = K("h1")
        for dh in range(DH):
            ob_ = obanks[dh]
            P.op("dve", lambda e, dh=dh, ob_=ob_: e.scalar_tensor_tensor(out=h1[:, dh * DW:(dh + 1) * DW], in0=banks[ob_][:, 0:DW], scalar=rso, in1=gpmix[:, dh * DW:(dh + 1) * DW], op0=ALU.mult, op1=ALU.mult),
                 r=["bank%d" % ob_, k3[3], "gpmix"], w=[hk])
            yield
        P.op("dve", lambda e: e.tensor_tensor(out=h1, in0=h1, in1=xt, op=ALU.add), r=[hk, xk], w=[hk])
        yield
        P.op("sp", lambda e: e.dma_start(out=h1_d[ob * 128:(ob + 1) * 128, :], in_=h1), r=[hk], w=["h1d%d" % ob], dma=True)
        yield

    interleave((tileC1(ob) for ob in range(NO)), ND1, 18)
    P.barrier()
    AR.top = const_top

    W1 = AR.alloc([128, KC, DFF], BF16)
    W2 = AR.alloc([128, FC, D], BF16)
    HT = AR.alloc([128, FC, 512], BF16)
    cT = AR.alloc([128, KC, 512], BF16)
    gpffn = AR.alloc([128, D], F32)
    fr = Front(nx=2, tpbanks=(0, 1))
    rtmp = [AR.alloc([128, 512], BF16) for _ in range(2)]
    h1r = [AR.alloc([128, D], F32) for _ in range(2)]
    o2t = [AR.alloc([128, D], F32) for _ in range(1)]
    junk2 = AR.alloc([128, 512], BF16)
    w1src = w_ff1.rearrange("(kc p) n -> p kc n", p=128)
    for j in range(DFF // 512):
        P.op("pool", lambda e, j=j: e.dma_start(out=W1[:, :, j * 512:(j + 1) * 512], in_=w1src[:, :, j * 512:(j + 1) * 512]), w=["W1c%d" % j], dma=True)
    w2k = load_w(W2, w_ff2, "W2", FC)
    ld(gpffn, gpffn_d, "gpffn")
    tcount = 0
    def c2_fronts(blocks):
        interleave((fr.gen(h1_d[ob * 128:(ob + 1) * 128, :], (gpf, "gpf"), cT[:, :, ti * 128:(ti + 1) * 128], "cT%d" % ti) for ti, ob in enumerate(blocks)), 2, 2)

    c2_fronts(c.groups[0])
    for g, blocks in enumerate(c.groups):
        N = len(blocks) * 128
        for fc in range(FC):
            hb = 2 + fc % 2
            for kc in range(KC):
                P.op("pe", lambda e, kc=kc, fc=fc, hb=hb, N=N: e.matmul(banks[hb][:, 0:N], lhsT=W1[:, kc, fc * 128:(fc + 1) * 128], rhs=cT[:, kc, 0:N], start=(kc == 0), stop=(kc == KC - 1)),
                     r=["cT%d" % j for j in range(len(blocks))] + ["W1c%d" % (fc // 4)], w=["bank%d" % hb])
            rt = rtmp[fc % 2]
            rk = "rtmp%d" % (fc % 2)
            P.op("act", lambda e, hb=hb, rt=rt, N=N: e.activation(out=rt[:, 0:N], in_=banks[hb][:, 0:N], func=AF.Relu), r=["bank%d" % hb], w=[rk])
            P.op("dve", lambda e, fc=fc, rt=rt, N=N: e.tensor_tensor(out=HT[:, fc, 0:N], in0=rt[:, 0:N], in1=rt[:, 0:N], op=ALU.mult), r=[rk], w=["HT"])
        if g + 1 < NG:
            c2_fronts(c.groups[g + 1])
        for ti, ob in enumerate(blocks):
            i2 = tcount % 2
            tcount += 1
            h1 = h1r[i2]
            hk = "h1r%d" % i2
            P.op("sp", lambda e, h1=h1, ob=ob: e.dma_start(out=h1, in_=h1_d[ob * 128:(ob + 1) * 128, :]), r=["h1d%d" % ob], w=[hk], dma=True)
            sl3 = stat_slot(4)
            k3 = ["st%d" % (sl3 + j) for j in range(4)]
            for dh in range(DH):
                ob_ = 4 + 2 * i2 + dh
                for fc in range(FC):
                    P.op("pe", lambda e, fc=fc, dh=dh, ob_=ob_, ti=ti: e.matmul(banks[ob_][:, 0:DW], lhsT=HT[:, fc, ti * 128:(ti + 1) * 128], rhs=W2[:, fc, dh * DW:(dh + 1) * DW], start=(fc == 0), stop=(fc == FC - 1)),
                         r=["HT", w2k[fc]], w=["bank%d" % ob_])
                P.op("act", lambda e, dh=dh, ob_=ob_, sl3=sl3: e.activation(out=junk2[:, 0:DW], in_=banks[ob_][:, 0:DW], func=AF.Square, accum_out=stats[:, sl3 + dh:sl3 + dh + 1]),
                     r=["bank%d" % ob_], w=["junk2", k3[dh]])
            if DH == 2:
                P.op("dve", lambda e, sl3=sl3: e.tensor_tensor(out=stats[:, sl3 + 2:sl3 + 3], in0=stats[:, sl3:sl3 + 1], in1=stats[:, sl3 + 1:sl3 + 2], op=ALU.add), r=[k3[0], k3[1]], w=[k3[2]])
                sso = stats[:, sl3 + 2:sl3 + 3]
                ssk = k3[2]
            else:
                sso = stats[:, sl3:sl3 + 1]
                ssk = k3[0]
            rso = stats[:, sl3 + 3:sl3 + 4]
            rsqrt_ops(sso, rso, 1, 1.0 / D, [ssk], [k3[3]])
            o2 = o2t[0]
            ok2 = "o2t0"
            for dh in range(DH):
                ob_ = 4 + 2 * i2 + dh
                P.op("dve", lambda e, dh=dh, ob_=ob_, rso=rso, o2=o2: e.scalar_tensor_tensor(out=o2[:, dh * DW:(dh + 1) * DW], in0=banks[ob_][:, 0:DW], scalar=rso, in1=gpffn[:, dh * DW:(dh + 1) * DW], op0=ALU.mult, op1=ALU.mult),
                     r=["bank%d" % ob_, k3[3], "gpffn"], w=[ok2])
            P.op("dve", lambda e, o2=o2, h1=h1: e.tensor_tensor(out=h1, in0=o2, in1=h1, op=ALU.add), r=[ok2, hk], w=[hk])
            P.op("sp", lambda e, h1=h1, ob=ob: e.dma_start(out=h2_d[ob * 128:(ob + 1) * 128, :], in_=h1), r=[hk], w=["h2d%d" % ob], dma=True)
    P.barrier()
    AR.top = const_top

    Wg = AR.alloc([128, KC, D], BF16)
    Wpe = AR.alloc([128, PK, D], BF16)
    gateb = AR.alloc([128, D], F32)
    ND3 = 3
    fr3 = Front(nx=ND3, tpbanks=(0, 1), nxn=ND3)
    S3 = []
    for _ in range(ND3):
        S3.append(dict(h2T=AR.alloc([128, KC, 128], BF16), pt=AR.alloc([128, PLE], F32), pb=AR.alloc([128, PLE], BF16),
                       pT=AR.alloc([128, PK, 128], BF16), z=AR.alloc([128, D], F32), pp=AR.alloc([128, D], F32), o=AR.alloc([128, D], F32)))
    wgk = load_w(Wg, gate_w, "Wg", KC)
    wpk = load_w(Wpe, ple_w, "Wpe", PK)
    ld(gateb, gateb_d, "gateb")
    out_dmas = []
    brot = [0]

    def nbank():
        b_ = 2 + brot[0] % 6
        brot[0] += 1
        return b_

    def tileC3(ob):
        p = ob % ND3
        T = S3[p]
        K = lambda n: "%s3_%d" % (n, p)
        h2T, pt_, pb_, pT_, z, pp, o_ = T["h2T"], T["pt"], T["pb"], T["pT"], T["z"], T["pp"], T["o"]
        P.op("sp", lambda e: e.dma_start(out=pt_, in_=pown[ob * 128:(ob + 1) * 128, :]), w=[K("pt")], dma=True)
        for _ in fr3.gen(h2_d[ob * 128:(ob + 1) * 128, :], None, h2T, K("h2T"), norm=False):
            yield
        xt, xk = fr3.last
        P.op("dve", lambda e: e.tensor_copy(out=pb_, in_=pt_), r=[K("pt")], w=[K("pb")])
        yield
        tb_ = nbank()
        tpv3 = banksb[tb_][:, 0:PK * 128].rearrange("p (k t) -> p k t", t=128)
        for k2_ in range(PK):
            P.op("pe", lambda e, k2_=k2_: e.transpose(out=tpv3[:, k2_, :], in_=pb_[:, k2_ * 128:(k2_ + 1) * 128], identity=identb), r=[K("pb"), "identb"], w=["bank%d" % tb_])
        yield
        P.op("act", lambda e: e.activation(out=pT_, in_=tpv3, func=AF.Copy), r=["bank%d" % tb_], w=[K("pT")])
        yield
        for dh in range(DH):
            gb_ = nbank()
            for kc in range(KC):
                P.op("pe", lambda e, kc=kc, dh=dh, gb_=gb_: e.matmul(banks[gb_][:, 0:DW], lhsT=h2T[:, kc, :], rhs=Wg[:, kc, dh * DW:(dh + 1) * DW], start=(kc == 0), stop=(kc == KC - 1)),
                     r=[K("h2T"), wgk[kc]], w=["bank%d" % gb_])
            yield
            P.op("dve", lambda e, dh=dh, gb_=gb_: e.tensor_tensor(out=z[:, dh * DW:(dh + 1) * DW], in0=banks[gb_][:, 0:DW], in1=gateb[:, dh * DW:(dh + 1) * DW], op=ALU.add),
                 r=["bank%d" % gb_, "gateb"], w=[K("z")])
            yield
            pb2 = nbank()
            for k2_ in range(PK):
                P.op("pe", lambda e, k2_=k2_, dh=dh, pb2=pb2: e.matmul(banks[pb2][:, 0:DW], lhsT=pT_[:, k2_, :], rhs=Wpe[:, k2_, dh * DW:(dh + 1) * DW], start=(k2_ == 0), stop=(k2_ == PK - 1)),
                     r=[K("pT"), wpk[k2_]], w=["bank%d" % pb2])
            yield
            P.op("act", lambda e, dh=dh, pb2=pb2: e.activation(out=pp[:, dh * DW:(dh + 1) * DW], in_=banks[pb2][:, 0:DW], func=AF.Copy), r=["bank%d" % pb2], w=[K("pp")])
            yield
        P.op("act", lambda e: e.activation(out=z, in_=z, func=AF.Tanh, scale=0.5), r=[K("z")], w=[K("z")])
        yield
        P.op("dve", lambda e: e.tensor_scalar(out=z, in0=z, scalar1=0.5, scalar2=0.5, op0=ALU.mult, op1=ALU.add), r=[K("z")], w=[K("z")])
        yield
        P.op("dve", lambda e: e.tensor_tensor(out=o_, in0=z, in1=pp, op=ALU.mult), r=[K("z"), K("pp")], w=[K("o")])
        yield
        P.op("dve", lambda e: e.tensor_tensor(out=o_, in0=o_, in1=xt, op=ALU.add), r=[K("o"), xk], w=[K("o")])
        yield
        out_dmas.append(P.op("sp", lambda e: e.dma_start(out=out_d[ob * 128:(ob + 1) * 128, :], in_=o_), r=[K("o")], dma=True))
        yield

    interleave((tileC3(ob) for ob in range(NO)), ND3, 7)
    P.wait_all("sp", out_dmas)
    P.emit()
    return nc, P


def make_core_inputs(cfg, core, x, p, w_in, f_bias, sg_ln_g, sg_ln_b, sg_w, sg_b, att_out_g, sg_out_g,
                     w_out, pre_mix_g, post_mix_g, pre_ffn_g, post_ffn_g, w_ff1, w_ff2, ple_w, ple_gate_w, ple_gate_b):
    c = cfg
    b, r = core // 4, core % 4
    f32 = np.float32
    blocks = c.owned_blocks(r)
    rows = np.concatenate([np.arange(bl * 128, (bl + 1) * 128) for bl in blocks])

    def fm(v):
        return np.ascontiguousarray(np.asarray(v, f32).reshape(c.KC, 128).T)

    def rep(v):
        return np.ascontiguousarray(np.broadcast_to(np.asarray(v, f32).reshape(1, -1), (128, np.asarray(v).size)))

    k = np.arange(128)[:, None]
    q = np.arange(128)[None, :]
    tri = np.where(k > q, NEG, 0.0).astype(f32)
    full = np.full((128, 128), NEG, f32)
    zero = np.zeros((128, 128), f32)
    maskT = np.zeros((128, 8, 128), f32)
    for i in range(4):
        maskT[:, i, :] = zero if i < r else (tri if i == r else full)
        maskT[:, 4 + i, :] = zero if i < 3 - r else (tri if i == 3 - r else full)
    sel = np.zeros((c.HH, c.HH, 128), f32)
    for hh in range(c.HH):
        sel[hh, hh, :] = 1.0
    LTfull = (np.arange(c.NB)[:, None] < np.arange(c.NB)[None, :]).astype(f32)
    LTown = (np.arange(c.NB)[:, None] < np.asarray(blocks)[None, :]).astype(f32)
    xb = np.asarray(x[b], f32)
    return {
        "xfull": np.ascontiguousarray(xb),
        "xown": np.ascontiguousarray(xb[rows]),
        "pown": np.ascontiguousarray(np.asarray(p[0, b], f32)[rows]),
        "w_in": np.ascontiguousarray(np.asarray(w_in[0], f32)),
        "w_out": np.ascontiguousarray(np.asarray(w_out[0], f32)),
        "w_ff1": np.ascontiguousarray(np.asarray(w_ff1[0], f32)),
        "w_ff2": np.ascontiguousarray(np.asarray(w_ff2[0], f32)),
        "ple_w": np.ascontiguousarray(np.asarray(ple_w[0], f32)),
        "gate_w": np.ascontiguousarray(np.asarray(ple_gate_w[0], f32)),
        "gpm": fm(pre_mix_g[0]),
        "gpf": fm(pre_ffn_g[0]),
        "gcat": fm(np.concatenate([np.asarray(att_out_g[0]), np.asarray(sg_out_g[0])])),
        "gpmix": rep(post_mix_g[0]),
        "gpffn": rep(post_ffn_g[0]),
        "gateb": rep(ple_gate_b[0]),
        "lng": rep(sg_ln_g[0]),
        "lnb": rep(sg_ln_b[0]),
        "fbias": rep(f_bias[0]),
        "sgwT": np.ascontiguousarray(np.transpose(np.asarray(sg_w[0], f32), (2, 0, 1))),
        "sgbT": np.ascontiguousarray(np.asarray(sg_b[0], f32).T),
        "ident": np.eye(128, dtype=f32),
        "U": (np.arange(128)[:, None] <= np.arange(128)[None, :]).astype(f32),
        "maskT": maskT,
        "sel": sel,
        "LTfull": LTfull,
        "LTown": LTown,
    }, rows


_CACHE = {}


def kernel(**inputs):
    x = np.asarray(inputs["x"])
    B, S, D = x.shape
    PLE = np.asarray(inputs["p"]).shape[-1]
    cfg = Cfg(D=D, S=S, PLE=PLE)
    key = (D, S, PLE)
    if key not in _CACHE:
        _CACHE[key] = build_program(cfg)
    nc, _ = _CACHE[key]
    in_maps, rows_all = [], []
    for core in range(8):
        m, rows = make_core_inputs(cfg, core, **inputs)
        in_maps.append(m)
        rows_all.append(rows)
    res = run_bass_kernel_spmd(nc, in_maps, core_ids=list(range(8)))
    out = np.zeros((B, S, D), np.float32)
    for core in range(8):
        out[core // 4, rows_all[core], :] = np.asarray(res.results[core]["out"], np.float32)
    return out
```

```python
import numpy as np
import concourse.bass as bass
import concourse.mybir as mybir
from concourse.bass_utils import run_bass_kernel_spmd

F32 = mybir.dt.float32
BF16 = mybir.dt.bfloat16
AF = mybir.ActivationFunctionType
ALU = mybir.AluOpType
AX = mybir.AxisListType

EPS = 1e-6
NEG = -30000.0


class _Ins:
    __slots__ = ("eng", "idx", "fn", "deps", "signal", "is_dma", "dma_sem", "dma_val", "sig_val", "epoch")

    def __init__(self, eng, idx, fn, is_dma):
        self.eng = eng
        self.idx = idx
        self.fn = fn
        self.deps = set()
        self.signal = False
        self.is_dma = is_dma
        self.dma_sem = None
        self.dma_val = 0
        self.sig_val = 0
        self.epoch = 0


class Prog:
    ENGS = ("pe", "act", "dve", "pool", "sp")

    def __init__(self, nc, n_dma_sems=24):
        self.nc = nc
        self.q = {e: [] for e in self.ENGS}
        self.lastw = {}
        self.readers = {}
        self.n_dma_sems = n_dma_sems
        self.dma_count = 0
        self.dma_last = [None] * n_dma_sems
        self.dma_pools = {"sp": (0, n_dma_sems - 8), "act": (0, n_dma_sems - 8), "pool": (n_dma_sems - 8, 8)}
        self.dma_pool_cnt = {"sp": 0, "act": 0, "pool": 0}
        self.dma_sem_uses = [0] * n_dma_sems
        self.epoch = 0

    def op(self, eng, fn, r=(), w=(), dma=False):
        ins = _Ins(eng, len(self.q[eng]), fn, dma)
        ins.epoch = self.epoch
        deps = ins.deps
        if any(k.startswith("bank") for k in r):
            w = list(w) + [k for k in r if k.startswith("bank") and k not in w]
            r = [k for k in r if not k.startswith("bank")]
        for k in r:
            lw = self.lastw.get(k)
            if lw is not None:
                deps.add(lw)
        for k in w:
            lw = self.lastw.get(k)
            if lw is not None:
                deps.add(lw)
            rd = self.readers.get(k)
            if rd:
                for x in rd[0].values():
                    deps.add(x)
                for x in rd[1]:
                    deps.add(x)
        if dma:
            base, cnt = self.dma_pools[eng]
            pk = "sp" if eng in ("sp", "act") else "pool"
            s = base + self.dma_pool_cnt[pk] % cnt
            self.dma_pool_cnt[pk] += 1
            prev = self.dma_last[s]
            if prev is not None:
                deps.add(prev)
            self.dma_sem_uses[s] += 1
            ins.dma_sem = s
            ins.dma_val = 16 * self.dma_sem_uses[s]
            self.dma_last[s] = ins
            self.dma_count += 1
        deps.discard(ins)
        for k in w:
            self.lastw[k] = ins
            self.readers[k] = ({}, [])
        for k in r:
            rd = self.readers.setdefault(k, ({}, []))
            if dma:
                rd[1].append(ins)
            else:
                rd[0][eng] = ins
        self.q[eng].append(ins)
        return ins

    def barrier(self):
        lasts = []
        for e in self.ENGS:
            for ins in reversed(self.q[e]):
                if not ins.is_dma and ins.fn is not None:
                    lasts.append(ins)
                    break
        dmas = [d for d in self.dma_last if d is not None]
        for e in self.ENGS:
            ins = _Ins(e, len(self.q[e]), None, False)
            ins.epoch = self.epoch
            ins.deps = set(lasts) | set(dmas)
            self.q[e].append(ins)
        self.lastw.clear()
        self.readers.clear()
        self.epoch += 1

    def wait_all(self, eng, instrs):
        ins = _Ins(eng, len(self.q[eng]), None, False)
        ins.epoch = self.epoch
        ins.deps = set(instrs)
        self.q[eng].append(ins)

    def emit(self):
        nc = self.nc
        for e in self.ENGS:
            for ins in self.q[e]:
                for d in ins.deps:
                    if not d.is_dma:
                        d.signal = True
        counts = {}
        for e in self.ENGS:
            c = 0
            ep = 0
            mx = 0
            for ins in self.q[e]:
                if ins.epoch != ep:
                    ep = ins.epoch
                    c = 0
                if (not ins.is_dma) and ins.signal and ins.fn is not None:
                    c += 1
                    ins.sig_val = c
                    mx = max(mx, c)
            counts[e] = mx
        self.counts = counts
        nep = self.epoch + 1
        import contextlib

        with contextlib.ExitStack() as st:
            esem = {(e, ep): st.enter_context(nc.semaphore("s_%s%d" % (e, ep))) for e in self.ENGS for ep in range(nep)}
            dsem = [st.enter_context(nc.semaphore("s_dma%d" % i)) for i in range(self.n_dma_sems)]
            block = st.enter_context(nc.Block())

            def run(e, eng):
                known = {}
                for ins in self.q[e]:
                    waits = {}
                    for d in ins.deps:
                        if d.is_dma:
                            key = ("d", d.dma_sem)
                            val = d.dma_val
                        else:
                            if d.fn is None:
                                continue
                            if d.eng == e and not ins.is_dma:
                                if e == "pe":
                                    continue
                            key = ("e", (d.eng, d.epoch))
                            val = d.sig_val
                        if val > waits.get(key, 0):
                            waits[key] = val
                    for key, val in waits.items():
                        if known.get(key, 0) >= val:
                            continue
                        sem = dsem[key[1]] if key[0] == "d" else esem[key[1]]
                        eng.wait_ge(sem, val)
                        known[key] = val
                    if ins.fn is None:
                        continue
                    bi = ins.fn(eng)
                    if ins.is_dma:
                        bi.then_inc(dsem[ins.dma_sem], 16)
                    elif ins.signal:
                        bi.then_inc(esem[(e, ins.epoch)], 1)

            @block.tensor
            def _(eng):
                run("pe", eng)

            @block.scalar
            def _(eng):
                run("act", eng)

            @block.vector
            def _(eng):
                run("dve", eng)

            @block.gpsimd
            def _(eng):
                run("pool", eng)

            @block.sync
            def _(eng):
                run("sp", eng)


class Cfg:
    def __init__(self, D=1024, S=8192, PLE=256):
        self.D, self.S, self.PLE = D, S, PLE
        self.KC = D // 128
        self.AW = D // 2
        self.H = self.AW // 64
        self.SGW = D // 2
        self.G = self.SGW // 64
        self.DFF = 4 * D
        self.FC = self.DFF // 128
        self.IPW = 3 * self.AW + self.H + 2 * self.SGW
        self.NB = S // 128
        self.J = self.NB // 8
        self.NO = 2 * self.J
        self.HH = self.H // 2
        self.PK = PLE // 128
        self.DH = max(1, D // 512)
        self.DW = min(D, 512)
        NO, J, NB = self.NO, self.J, self.NB
        self.groups = [list(range(i, min(i + 4, NO))) for i in range(0, NO, 4)]
        self.lo = [4 * j for j in range(J)] + [NB - 4 - 4 * j for j in reversed(range(J))]
        self.mtype = [0] * J + [1] * J

    def owned_blocks(self, r):
        J, NB = self.J, self.NB
        return [4 * j + r for j in range(J)] + [NB - 1 - 4 * j - r for j in reversed(range(J))]


class Arena:
    def __init__(self, ap, total):
        self.A = ap
        self.total = total
        self.top = 0

    def alloc(self, shape, dt):
        n = int(np.prod(shape[1:]))
        ne = n * (2 if dt == F32 else 1)
        ne = (ne + 15) // 16 * 16
        assert self.top + ne <= self.total, ("SBUF arena overflow", self.top, ne, self.total)
        v = self.A[:, self.top:self.top + (n * (2 if dt == F32 else 1))]
        self.top += ne
        if dt == F32:
            v = v.bitcast(F32)
        if len(shape) > 2:
            names = " ".join("a%d" % i for i in range(len(shape) - 1))
            kw = {"a%d" % i: int(shape[i + 1]) for i in range(len(shape) - 1)}
            v = v.rearrange("p (%s) -> p %s" % (names, names), **kw)
        if shape[0] < 128:
            v = v[0:shape[0]]
        return v


def _ap(t):
    return t.ap() if hasattr(t, "ap") else t[:]


def build_program(cfg, debug=False):
    c = cfg
    D, S, KC, AW, H, HH, SGW, G, NB, NO = c.D, c.S, c.KC, c.AW, c.H, c.HH, c.SGW, c.G, c.NB, c.NO
    DFF, FC, PLE, PK, DH, DW = c.DFF, c.FC, c.PLE, c.PK, c.DH, c.DW
    NG = len(c.groups)
    HP = H // 2
    HPP = HH // 2
    nc = bass.Bass("TRN2", target_bir_lowering=False)

    def din(name, shape):
        return nc.dram_tensor(name, list(shape), F32, kind="ExternalInput").ap()

    xfull = din("xfull", [S, D])
    xown = din("xown", [NO * 128, D])
    pown = din("pown", [NO * 128, PLE])
    w_in = din("w_in", [D, c.IPW])
    w_out = din("w_out", [D, D])
    w_ff1 = din("w_ff1", [D, DFF])
    w_ff2 = din("w_ff2", [DFF, D])
    ple_w = din("ple_w", [PLE, D])
    gate_w = din("gate_w", [D, D])
    gpm_d = din("gpm", [128, KC])
    gpf_d = din("gpf", [128, KC])
    gcat_d = din("gcat", [128, KC])
    gpmix_d = din("gpmix", [128, D])
    gpffn_d = din("gpffn", [128, D])
    gateb_d = din("gateb", [128, D])
    lng_d = din("lng", [128, SGW])
    lnb_d = din("lnb", [128, SGW])
    fbias_d = din("fbias", [128, H])
    sgwT_d = din("sgwT", [128, G, 128])
    sgbT_d = din("sgbT", [128, G])
    ident_d = din("ident", [128, 128])
    U_d = din("U", [128, 128])
    maskT_d = din("maskT", [128, 8, 128])
    sel_d = din("sel", [HH, HH, 128])
    LTfull_d = din("LTfull", [NB, NB])
    LTown_d = din("LTown", [NB, NO])
    out_d = nc.dram_tensor("out", [NO * 128, D], F32, kind="ExternalOutput").ap()
    h1_d = nc.dram_tensor("h1_scr", [NO * 128, D], F32).ap()
    h2_d = nc.dram_tensor("h2_scr", [NO * 128, D], F32).ap()
    aT_d = nc.dram_tensor("aT_scr", [NB, 128, KC * 128], BF16).ap()
    dbg = {}

    total = (nc.sbuf_bytes_remaining - 2048) // 2
    total = total // 16 * 16
    arena_t = nc.alloc_sbuf_tensor("arena", [128, total], BF16)
    AR = Arena(_ap(arena_t), total)
    banks = [_ap(nc.alloc_psum_tensor("bank%d" % i, [128, 512], F32)) for i in range(8)]
    banksb = [b.bitcast(BF16) for b in banks]

    P = Prog(nc)
    rr = [0]

    def wq():
        return "sp"

    identf = AR.alloc([128, 128], F32)
    identb = AR.alloc([128, 128], BF16)
    Uf = AR.alloc([128, 128], F32)
    onesf = AR.alloc([128, 128], F32)
    maskb = AR.alloc([128, 8, 128], BF16)
    selb = AR.alloc([128, HH, 128], BF16)
    LTfull = AR.alloc([128, NB], F32)
    LTown = AR.alloc([128, NO], F32)
    gpm = AR.alloc([128, KC], F32)
    gpf = AR.alloc([128, KC], F32)
    gcat = AR.alloc([128, KC], F32)
    fbias = AR.alloc([128, H], F32)
    stats = AR.alloc([128, 64], F32)
    const_top = AR.top
    QT = AR.alloc([128, HP, NO * 128], BF16)
    within_own = AR.alloc([128, NO, H], F32)
    yatt = AR.alloc([128, NO, AW], F32)
    persist_top = AR.top

    def ld(dst, src, key, eng="sp"):
        return P.op(eng, lambda e: e.dma_start(out=dst, in_=src), w=[key], dma=True)

    ld(identf, ident_d, "identf")
    ld(Uf, U_d, "Uf")
    ld(LTfull[0:NB], LTfull_d, "LTfull")
    ld(LTown[0:NB], LTown_d, "LTown")
    ld(gpm, gpm_d, "gpm")
    ld(gpf, gpf_d, "gpf")
    ld(gcat, gcat_d, "gcat")
    ld(fbias, fbias_d, "fbias")
    P.op("pool", lambda e: e.dma_start(out=maskb, in_=maskT_d), w=["maskb"], dma=True)
    P.op("pool", lambda e: e.dma_start(out=selb[0:HH], in_=sel_d), w=["selb"], dma=True)
    P.op("dve", lambda e: e.tensor_copy(out=identb, in_=identf), r=["identf"], w=["identb"])
    P.op("dve", lambda e: e.memset(onesf, 1.0), w=["onesf"])

    scnt = [0]

    def stat_slot(n=1):
        s = scnt[0]
        scnt[0] = (scnt[0] + n) % 60
        if s + n > 60:
            s = 0
            scnt[0] = n
        return s

    def load_w(dst, src2d, key, kcn):
        for kc in range(kcn):
            P.op("pool", lambda e, kc=kc: e.dma_start(out=dst[:, kc, :], in_=src2d[kc * 128:(kc + 1) * 128, :]),
                 w=[key + str(kc)], dma=True)
        return [key + str(kc) for kc in range(kcn)]

    def rsqrt_ops(ss_ap, out_ap, n, scale, rkeys, wkeys):
        tmpslot = stat_slot(n)
        tmp = stats[:, tmpslot:tmpslot + n]
        tks = ["st%d" % (tmpslot + j) for j in range(n)]
        P.op("act", lambda e: e.activation(out=tmp, in_=ss_ap, func=AF.Ln, scale=scale, bias=EPS), r=rkeys, w=tks)
        P.op("act", lambda e: e.activation(out=out_ap, in_=tmp, func=AF.Exp, scale=-0.5), r=tks, w=wkeys)

    class Front:
        def __init__(self, nx=3, tpbanks=(0, 1), nxn=2):
            self.xt = [AR.alloc([128, D], F32) for _ in range(nx)]
            self.xn = [AR.alloc([128, D], BF16) for _ in range(nxn)]
            self.i = 0
            self.tpb = tpbanks

        def run(self, rows_ap, gain, dst, dkey, norm=True):
            g = self.gen(rows_ap, gain, dst, dkey, norm)
            for _ in g:
                pass
            return self.last

        def gen(self, rows_ap, gain, dst, dkey, norm=True):
            i = self.i
            self.i += 1
            xt = self.xt[i % len(self.xt)]
            xk = "xt%d_%d" % (id(self) % 1000, i % len(self.xt))
            xn = self.xn[i % len(self.xn)]
            nk = "xn%d_%d" % (id(self) % 1000, i % len(self.xn))
            tb = self.tpb[i % len(self.tpb)]
            tpv = banksb[tb][:, 0:KC * 128].rearrange("p (k t) -> p k t", t=128)
            tk = "bank%d" % tb
            self.last = (xt, xk)
            P.op("sp", lambda e: e.dma_start(out=xt, in_=rows_ap), w=[xk], dma=True)
            yield
            if norm:
                sl = stat_slot(2)
                ss = stats[:, sl:sl + 1]
                rs = stats[:, sl + 1:sl + 2]
                sk = "st%d" % sl
                rk = "st%d" % (sl + 1)
                P.op("act", lambda e: e.activation(out=xn, in_=xt, func=AF.Square, accum_out=ss), r=[xk], w=[nk, sk])
                yield
                rsqrt_ops(ss, rs, 1, 1.0 / D, [sk], [rk])
                yield
                P.op("dve", lambda e: e.tensor_scalar(out=xn, in0=xt, scalar1=rs, scalar2=None, op0=ALU.mult),
                     r=[xk, rk], w=[nk])
            else:
                P.op("dve", lambda e: e.tensor_copy(out=xn, in_=xt), r=[xk], w=[nk])
            yield
            for kc in range(KC):
                P.op("pe", lambda e, kc=kc: e.transpose(out=tpv[:, kc, :], in_=xn[:, kc * 128:(kc + 1) * 128], identity=identb),
                     r=[nk, "identb"], w=[tk])
            yield
            if gain is not None:
                gk = gain[1]
                gb = gain[0].unsqueeze(2).broadcast_to([128, KC, 128])
                P.op("dve", lambda e: e.tensor_tensor(out=dst, in0=tpv, in1=gb, op=ALU.mult), r=[tk, gk], w=[dkey])
            else:
                P.op("act", lambda e: e.activation(out=dst, in_=tpv, func=AF.Copy), r=[tk], w=[dkey])
            yield

    def interleave(gens, depth, period):
        it = iter(gens)
        active = []
        rounds = 0
        done = False
        while True:
            if not done and len(active) < depth and (rounds % period == 0 or not active):
                try:
                    active.append(next(it))
                except StopIteration:
                    done = True
            if not active:
                if done:
                    break
                continue
            for g in list(active):
                try:
                    next(g)
                except StopIteration:
                    active.remove(g)
            rounds += 1

    def softplus_neg(dst, src, n_keys_r, wkey, tmp):
        P.op("act", lambda e: e.activation(out=tmp, in_=src, func=AF.Exp, scale=-1.0), r=n_keys_r, w=[wkey + "_e"])
        P.op("act", lambda e: e.activation(out=dst, in_=tmp, func=AF.Ln, scale=1.0, bias=1.0), r=[wkey + "_e"], w=[wkey])

    qcol = 0
    kcol = AW
    vcol = 2 * AW
    fcol = 3 * AW
    ucol = 3 * AW + H
    vscol = ucol + SGW
    w_in_r = w_in

    mark = AR.top
    Wq = AR.alloc([128, KC, AW], BF16)
    Wfa = AR.alloc([128, KC, H], BF16)
    fr = Front(nx=3, tpbanks=(0, 1))
    aTq = [AR.alloc([128, KC, 512], BF16) for _ in range(2)]
    fown = AR.alloc([128, NO, H], F32)
    spo = AR.alloc([128, NO, H], F32)
    spo_e = AR.alloc([128, NO, H], F32)
    wqk = load_w(Wq, w_in_r[:, qcol:qcol + AW], "Wq", KC)
    wfk = load_w(Wfa, w_in_r[:, fcol:fcol + H], "Wfa", KC)
    for g, blocks in enumerate(c.groups):
        aT = aTq[g % 2]
        ak = "aTq%d" % (g % 2)
        N = len(blocks) * 128
        for ti, ob in enumerate(blocks):
            fr.run(xown[ob * 128:(ob + 1) * 128, :], (gpm, "gpm"), aT[:, :, ti * 128:(ti + 1) * 128], ak)
            for kc in range(KC):
                P.op("pe", lambda e, kc=kc, ti=ti, aT=aT: e.matmul(banks[4][:, 0:H], lhsT=aT[:, kc, ti * 128:(ti + 1) * 128], rhs=Wfa[:, kc, :],
                                                                   start=(kc == 0), stop=(kc == KC - 1)), r=[ak, wfk[kc]], w=["bank4"])
            P.op("dve", lambda e, ob=ob: e.tensor_tensor(out=fown[:, ob, :], in0=banks[4][:, 0:H], in1=fbias, op=ALU.add),
                 r=["bank4", "fbias"], w=["fown"])
        for hp in range(HP):
            qb = 2 + (hp % 2)
            for kc in range(KC):
                P.op("pe", lambda e, kc=kc, hp=hp, aT=aT, qb=qb, N=N: e.matmul(banks[qb][:, 0:N], lhsT=Wq[:, kc, hp * 128:(hp + 1) * 128], rhs=aT[:, kc, 0:N],
                                                                              start=(kc == 0), stop=(kc == KC - 1)), r=[ak, wqk[kc]], w=["bank%d" % qb])
            P.op("act", lambda e, hp=hp, qb=qb, g=g, N=N: e.activation(out=QT[:, hp, g * 512:g * 512 + N], in_=banks[qb][:, 0:N], func=AF.Copy, scale=0.125),
                 r=["bank%d" % qb], w=["QT"])
    softplus_neg(spo, fown, ["fown"], "spo", spo_e)
    P.op("pe", lambda e: e.matmul(banks[5][:, 0:NO * H], lhsT=Uf, rhs=spo.rearrange("p a b -> p (a b)"), start=True, stop=True),
         r=["Uf", "spo"], w=["bank5"])
    P.op("dve", lambda e: e.tensor_copy(out=within_own.rearrange("p a b -> p (a b)"), in_=banks[5][:, 0:NO * H]), r=["bank5"], w=["within_own"])
    P.barrier()
    AR.top = mark

    KT = AR.alloc([128, HPP, S], BF16)
    Vflat = AR.alloc([128, NB * HH * 65 + 64], BF16)
    Vaug = Vflat[:, 0:NB * HH * 65].rearrange("p (a b c) -> p a b c", a=NB, b=HH)
    biasT = AR.alloc([128, NG, NB, HH], F32)
    R8 = AR.alloc([128, NO * 128], BF16)
    rbc = AR.alloc([128, HH, NO * 128], BF16)
    attn_top = AR.top
    P.op("dve", lambda e: e.memset(Vaug[:, :, :, 64:65], 1.0), w=["Vones"])
    P.op("dve", lambda e: e.memset(Vflat[:, NB * HH * 65:NB * HH * 65 + 64], 0.0), w=["Vpad"])

    for hs in range(2):
        mark = AR.top
        fsb = AR.alloc([128, NB, HH], F32)
        spf = AR.alloc([128, NB, HH], F32)
        spf_e = AR.alloc([128, NB, HH], F32)
        wsb = AR.alloc([128, NB, HH], F32)
        Cpos = AR.alloc([128, NB, HH], F32)
        totT = AR.alloc([128, HH], F32)
        rhs_full = AR.alloc([128, NB, HH], F32)
        rhs_own = AR.alloc([128, NO, HH], F32)
        pexo = AR.alloc([128, NO, HH], F32)
        rt1 = AR.alloc([128, NO, HH], F32)
        Rtok = AR.alloc([128, NO, HH], F32)
        markA = AR.top
        Wk = AR.alloc([128, KC, HH * 64], BF16)
        VF = HH * 64 + HH
        Wv = AR.alloc([128, KC, VF], BF16)
        fr = Front(nx=4, tpbanks=(0, 1, 4, 7), nxn=3)
        aTa = [AR.alloc([128, KC, 512], BF16) for _ in range(2)]
        wkk = load_w(Wk, w_in_r[:, kcol + hs * HH * 64: kcol + (hs + 1) * HH * 64], "Wk", KC)
        wvk = []
        for kc in range(KC):
            P.op("pool", lambda e, kc=kc, hs=hs, Wv=Wv: e.dma_start(out=Wv[:, kc, 0:HH * 64], in_=w_in_r[kc * 128:(kc + 1) * 128, vcol + hs * HH * 64: vcol + (hs + 1) * HH * 64]),
                 w=["Wv%da" % kc], dma=True)
            P.op("pool", lambda e, kc=kc, hs=hs, Wv=Wv: e.dma_start(out=Wv[:, kc, HH * 64:VF], in_=w_in_r[kc * 128:(kc + 1) * 128, fcol + hs * HH: fcol + (hs + 1) * HH]),
                 w=["Wv%db" % kc], dma=True)
            wvk.append(["Wv%da" % kc, "Wv%db" % kc])
        nst = (NB + 3) // 4

        def tileA(t, hs=hs, Wk=Wk, Wv=Wv, fsb=fsb, aTa=aTa, fr=fr, wkk=wkk, wvk=wvk):
            st, ti = t // 4, t % 4
            aT = aTa[st % 2]
            aks = ["aTa%d_%d" % (st % 2, j) for j in range(4)]
            ak = aks[ti]
            if hs == 0:
                for _ in fr.gen(xfull[t * 128:(t + 1) * 128, :], (gpm, "gpm"), aT[:, :, ti * 128:(ti + 1) * 128], ak):
                    yield
                P.op("sp", lambda e: e.dma_start(out=aT_d[t].rearrange("p (k t) -> p k t", t=128), in_=aT[:, :, ti * 128:(ti + 1) * 128]), r=[ak], w=["aTd%d" % t], dma=True)
                yield
                for _ in range(6):
                    yield
            else:
                P.op("sp", lambda e: e.dma_start(out=aT[:, :, ti * 128:(ti + 1) * 128], in_=aT_d[t].rearrange("p (k t) -> p k t", t=128)), w=[ak], dma=True)
                yield
            vb = 2 + (t % 2)
            vk = "bank%d" % vb
            for kc in range(KC):
                P.op("pe", lambda e, kc=kc: e.matmul(banks[vb][:, 0:VF], lhsT=aT[:, kc, ti * 128:(ti + 1) * 128], rhs=Wv[:, kc, :],
                                                     start=(kc == 0), stop=(kc == KC - 1)), r=[ak] + wvk[kc], w=[vk])
            yield
            P.op("act", lambda e: e.activation(out=Vaug[:, t, :, 0:64], in_=banks[vb][:, 0:HH * 64].rearrange("p (h d) -> p h d", d=64), func=AF.Copy),
                 r=[vk], w=["V%d" % t])
            yield
            P.op("act", lambda e: e.activation(out=fsb[:, t, :], in_=banks[vb][:, HH * 64:VF], func=AF.Copy), r=[vk], w=["fsb"])
            yield
            if ti == 3 or t == NB - 1:
                N = (ti + 1) * 128
                for hpl in range(HPP):
                    kb_ = 5 + (hpl % 2)
                    for kc in range(KC):
                        P.op("pe", lambda e, kc=kc, hpl=hpl, kb_=kb_: e.matmul(banks[kb_][:, 0:N], lhsT=Wk[:, kc, hpl * 128:(hpl + 1) * 128], rhs=aT[:, kc, 0:N],
                                                                               start=(kc == 0), stop=(kc == KC - 1)), r=aks[0:ti + 1] + [wkk[kc]], w=["bank%d" % kb_])
                    yield
                    P.op("act", lambda e, hpl=hpl, kb_=kb_: e.activation(out=KT[:, hpl, st * 512:st * 512 + N], in_=banks[kb_][:, 0:N], func=AF.Copy),
                         r=["bank%d" % kb_], w=["KT%d" % st])
                    yield

        interleave((tileA(t) for t in range(NB)), 6 if hs == 0 else 4, 2)

        P.op("dve", lambda e, hs=hs, fsb=fsb: e.tensor_tensor(out=fsb, in0=fsb, in1=fbias[:, hs * HH:(hs + 1) * HH].unsqueeze(1).broadcast_to([128, NB, HH]), op=ALU.add),
             r=["fsb", "fbias"], w=["fsb"])
        softplus_neg(spf, fsb, ["fsb"], "spf", spf_e)
        spf2 = spf.rearrange("p a b -> p (a b)")
        P.op("pe", lambda e: e.matmul(banks[0][:, 0:NB * HH], lhsT=Uf, rhs=spf2, start=True, stop=True), r=["Uf", "spf"], w=["bank0"])
        P.op("dve", lambda e: e.tensor_copy(out=wsb.rearrange("p a b -> p (a b)"), in_=banks[0][:, 0:NB * HH]), r=["bank0"], w=["wsb"])
        for hh in range(HH):
            P.op("pe", lambda e, hh=hh: e.matmul(banks[1][0:NB, hh:hh + 1], lhsT=spf[:, :, hh], rhs=onesf[:, 0:1], start=True, stop=True),
                 r=["spf", "onesf"], w=["bank1"])
        P.op("dve", lambda e: e.tensor_copy(out=totT[0:NB, :], in_=banks[1][0:NB, 0:HH]), r=["bank1"], w=["totT"])
        P.op("dve", lambda e: e.tensor_tensor(out=rhs_full[0:NB], in0=LTfull[0:NB].unsqueeze(2).broadcast_to([NB, NB, HH]),
                                              in1=totT[0:NB].unsqueeze(1).broadcast_to([NB, NB, HH]), op=ALU.mult), r=["LTfull", "totT"], w=["rhs_full"])
        P.op("dve", lambda e: e.tensor_tensor(out=rhs_own[0:NB], in0=LTown[0:NB].unsqueeze(2).broadcast_to([NB, NO, HH]),
                                              in1=totT[0:NB].unsqueeze(1).broadcast_to([NB, NO, HH]), op=ALU.mult), r=["LTown", "totT"], w=["rhs_own"])
        P.op("pe", lambda e: e.matmul(banks[2][:, 0:NB * HH], lhsT=onesf[0:NB, :], rhs=rhs_full[0:NB].rearrange("p a b -> p (a b)"), start=True, stop=True),
             r=["onesf", "rhs_full"], w=["bank2"])
        P.op("pe", lambda e: e.matmul(banks[3][:, 0:NO * HH], lhsT=onesf[0:NB, :], rhs=rhs_own[0:NB].rearrange("p a b -> p (a b)"), start=True, stop=True),
             r=["onesf", "rhs_own"], w=["bank3"])
        P.op("dve", lambda e: e.tensor_tensor(out=Cpos.rearrange("p a b -> p (a b)"), in0=banks[2][:, 0:NB * HH], in1=wsb.rearrange("p a b -> p (a b)"), op=ALU.add),
             r=["bank2", "wsb"], w=["Cpos"])
        P.op("dve", lambda e: e.tensor_copy(out=pexo.rearrange("p a b -> p (a b)"), in_=banks[3][:, 0:NO * HH]), r=["bank3"], w=["pexo"])
        for g, blocks in enumerate(c.groups):
            g0 = blocks[0]
            nb = len(blocks)
            P.op("dve", lambda e, g=g, g0=g0: e.tensor_tensor(out=biasT[:, g, :, :], in0=Cpos, in1=pexo[:, g0:g0 + 1, :].broadcast_to([128, NB, HH]), op=ALU.subtract),
                 r=["Cpos", "pexo"], w=["biasT"])
            P.op("dve", lambda e, g0=g0, nb=nb: e.tensor_tensor(out=rt1[:, g0:g0 + nb, :], in0=pexo[:, g0:g0 + 1, :].broadcast_to([128, nb, HH]), in1=pexo[:, g0:g0 + nb, :], op=ALU.subtract),
                 r=["pexo"], w=["rt1"])
            P.op("dve", lambda e, g0=g0, nb=nb, hs=hs: e.tensor_tensor(out=Rtok[:, g0:g0 + nb, :], in0=rt1[:, g0:g0 + nb, :], in1=within_own[:, g0:g0 + nb, hs * HH:(hs + 1) * HH], op=ALU.subtract),
                 r=["rt1", "within_own"], w=["Rtok"])
            for ti, ob in enumerate(blocks):
                P.op("pe", lambda e, ti=ti, ob=ob: e.matmul(banks[4][0:HH, ti * 128:(ti + 1) * 128], lhsT=Rtok[:, ob, :], rhs=identf, start=True, stop=True),
                     r=["Rtok", "identf"], w=["bank4"])
            P.op("dve", lambda e, g0=g0, nb=nb: e.tensor_copy(out=R8[0:HH, g0 * 128:(g0 + nb) * 128], in_=banks[4][0:HH, 0:nb * 128]), r=["bank4"], w=["R8"])

        for hh in range(HH):
            for g, blocks in enumerate(c.groups):
                g0, nb = blocks[0], len(blocks)
                rb_ = 5 + ((hh * NG + g) % 2)
                P.op("pe", lambda e, hh=hh, g0=g0, nb=nb, rb_=rb_: e.matmul(banks[rb_][:, 0:nb * 128], lhsT=selb[0:HH, hh, :], rhs=R8[0:HH, g0 * 128:(g0 + nb) * 128], start=True, stop=True),
                     r=["selb", "R8"], w=["bank%d" % rb_])
                P.op("dve", lambda e, hh=hh, g0=g0, nb=nb, rb_=rb_: e.tensor_copy(out=rbc[:, hh, g0 * 128:(g0 + nb) * 128], in_=banks[rb_][:, 0:nb * 128]),
                     r=["bank%d" % rb_], w=["rbc"])

        P.barrier()
        AR.top = markA
        NPT = 6
        QTp = AR.alloc([128, HH, NO * 128], BF16)
        P.op("pool", lambda e, QTp=QTp: e.memset(QTp, 0.0), w=["QTp"])
        for hh in range(HH):
            h_ = hs * HH + hh
            e2 = h_ % 2
            P.op("dve", lambda e, hh=hh, h_=h_, e2=e2, QTp=QTp: e.tensor_copy(out=QTp[e2 * 64:(e2 + 1) * 64, hh, :], in_=QT[e2 * 64:(e2 + 1) * 64, h_ // 2, :]),
                 r=["QT", "QTp"], w=["QTp"])
        pts = [AR.alloc([128, 512], BF16) for _ in range(NPT)]
        osb = [AR.alloc([128, 512], F32) for _ in range(2)]
        rc = AR.alloc([128, 8], F32)
        kmaxs = [c.lo[blocks[-1]] + 3 for blocks in c.groups]
        batches = []
        for hh in range(HH):
            for kb in range(NB):
                act_g = [g for g in range(NG) if kb <= kmaxs[g]]
                for j in range(0, len(act_g), 1):
                    batches.append((hh, kb, act_g[j:j + 1]))
        epi = [0]

        def geom(g, kb):
            blocks = c.groups[g]
            fa = 0
            while c.lo[blocks[fa]] + 3 < kb:
                fa += 1
            N = (len(blocks) - fa) * 128
            c0 = blocks[fa] * 128
            msk = [(bi, kb - c.lo[ob]) for bi, ob in enumerate(blocks) if bi >= fa and c.lo[ob] <= kb <= c.lo[ob] + 3]
            return blocks, fa, N, c0, msk

        def emit_S(i):
            hh, kb, gs = batches[i]
            h = hs * HH + hh
            hpg, e_, hpl = h // 2, h % 2, hh // 2
            info = []
            for j, g in enumerate(gs):
                blocks, fa, N, c0, msk = geom(g, kb)
                sb = i % 4
                info.append((g, blocks, fa, N, c0, msk, sb))
            for (g, blocks, fa, N, c0, msk, sb) in info:
                P.op("pe", lambda e, N=N, c0=c0, sb=sb, nm=len(msk): e.matmul(banks[sb][:, 0:N], lhsT=KT[:, hpl, kb * 128:(kb + 1) * 128],
                                                                rhs=QTp[:, hh, c0:c0 + N], start=True, stop=(nm == 0)),
                     r=["KT%d" % (kb // 4), "QTp"], w=["bank%d" % sb])
            for (g, blocks, fa, N, c0, msk, sb) in info:
                for mi, (bi, i4) in enumerate(msk):
                    mt = c.mtype[blocks[bi]] * 4 + i4
                    P.op("pe", lambda e, bi=bi, mt=mt, mi=mi, fa=fa, sb=sb, nm=len(msk): e.matmul(banks[sb][:, (bi - fa) * 128:(bi - fa + 1) * 128], lhsT=identb, rhs=maskb[:, mt, :],
                                                                                              start=False, stop=(mi == nm - 1)), r=["identb", "maskb"], w=["bank%d" % sb])
            for (g, blocks, fa, N, c0, msk, sb) in info:
                P.op("dve", lambda e, N=N, c0=c0, sb=sb: e.tensor_tensor(out=banks[sb][:, 0:N], in0=banks[sb][:, 0:N], in1=rbc[:, hh, c0:c0 + N], op=ALU.add),
                     r=["bank%d" % sb, "rbc"], w=["bank%d" % sb])
            for j, (g, blocks, fa, N, c0, msk, sb) in enumerate(info):
                pi = i % NPT
                P.op("act", lambda e, N=N, sb=sb, g=g, pi=pi: e.activation(out=pts[pi][:, 0:N], in_=banks[sb][:, 0:N], func=AF.Exp, bias=biasT[:, g, kb, hh:hh + 1], scale=1.0),
                     r=["bank%d" % sb, "biasT"], w=["pt%d" % pi])

        def emit_PV(i):
            hh, kb, gs = batches[i]
            h = hs * HH + hh
            for j, g in enumerate(gs):
                blocks, fa, N, c0, msk = geom(g, kb)
                ob_ = 4 + g
                ok = "bank%d" % ob_
                pi = i % NPT
                last = (kb == kmaxs[g])
                vo = (kb * HH + hh) * 65
                vkeys = ["V%d" % kb, "Vones", "Vpad"] + (["V%d" % (kb + 1)] if kb + 1 < NB else [])
                P.op("pe", lambda e, fa=fa, N=N, ob_=ob_, pi=pi, last=last, vo=vo: e.matmul(banks[ob_][:, fa * 128:fa * 128 + N], lhsT=Vflat[:, vo:vo + 128], rhs=pts[pi][:, 0:N], start=(kb == 0), stop=last),
                     r=vkeys + ["pt%d" % pi], w=[ok])
            for j, g in enumerate(gs):
                if kb != kmaxs[g]:
                    continue
                blocks = c.groups[g]
                ob_ = 4 + g
                ok = "bank%d" % ob_
                nb = len(blocks)
                ei = epi[0] % 2
                epi[0] += 1
                os_ = osb[ei]
                osk = "osb%d" % ei
                tb = i % 4
                tk = "bank%d" % tb
                P.op("dve", lambda e, os_=os_, ob_=ob_, nb=nb: e.tensor_copy(out=os_[0:65, 0:nb * 128], in_=banks[ob_][0:65, 0:nb * 128]), r=[ok], w=[osk])
                for ti, ob in enumerate(blocks):
                    P.op("pe", lambda e, ti=ti, os_=os_, tb=tb: e.matmul(banks[tb][:, ti * 65:(ti + 1) * 65], lhsT=os_[0:65, ti * 128:(ti + 1) * 128], rhs=identf[0:65, 0:65], start=True, stop=True),
                         r=[osk, "identf"], w=[tk])
                o3 = banks[tb][:, 0:nb * 65].rearrange("p (b x) -> p b x", x=65)
                P.op("dve", lambda e, o3=o3, nb=nb: e.reciprocal(out=rc[:, 0:nb], in_=o3[:, :, 64]), r=[tk], w=["rc"])
                for ti, ob in enumerate(blocks):
                    P.op("dve", lambda e, ti=ti, ob=ob, o3=o3, h=h: e.tensor_scalar(out=yatt[:, ob, h * 64:(h + 1) * 64], in0=o3[:, ti, 0:64], scalar1=rc[:, ti:ti + 1], scalar2=None, op0=ALU.mult),
                         r=[tk, "rc"], w=["yatt"])

        SKEW = 3
        for i in range(len(batches)):
            emit_S(i)
            if i >= SKEW:
                emit_PV(i - SKEW)
        for i in range(max(0, len(batches) - SKEW), len(batches)):
            emit_PV(i)
        P.barrier()
        AR.top = mark

    AR.top = attn_top - 0
    AR.top = persist_top
    C0 = 0.7978845608028654
    C1_ = 0.044715
    Wu = AR.alloc([128, KC, SGW], BF16)
    Wvs = AR.alloc([128, KC, SGW], BF16)
    Wo = AR.alloc([128, KC, D], BF16)
    wsT = AR.alloc([128, G, 128], BF16)
    wsTf = AR.alloc([128, G, 128], F32)
    sgb = AR.alloc([128, G], F32)
    lng = AR.alloc([128, SGW], F32)
    lnb = AR.alloc([128, SGW], F32)
    gpmix = AR.alloc([128, D], F32)
    ND1 = 3
    fr = Front(nx=ND1, tpbanks=(0, 1), nxn=ND1)
    aTc = [AR.alloc([128, KC, 128], BF16) for _ in range(ND1)]
    TN = ["us", "vs", "x2u", "x2v", "tmix"]
    TT = [{n: AR.alloc([128, SGW], F32) for n in TN} for _ in range(ND1)]
    for p_ in range(ND1):
        TT[p_]["vlnb"] = AR.alloc([128, SGW], BF16)
        TT[p_]["junk"] = AR.alloc([128, D], BF16)
        TT[p_]["yn"] = AR.alloc([128, D], BF16)
        TT[p_]["ynT"] = AR.alloc([128, KC, 128], BF16)
        TT[p_]["h1"] = AR.alloc([128, D], F32)
    wuk = load_w(Wu, w_in_r[:, ucol:ucol + SGW], "Wu", KC)
    wvsk = load_w(Wvs, w_in_r[:, vscol:vscol + SGW], "Wvs", KC)
    wok = load_w(Wo, w_out, "Wo", KC)
    ld(wsTf, sgwT_d, "wsTf")
    ld(sgb, sgbT_d, "sgb")
    ld(lng, lng_d, "lng")
    ld(lnb, lnb_d, "lnb")
    ld(gpmix, gpmix_d, "gpmix")
    P.op("dve", lambda e: e.tensor_tensor(out=wsT, in0=wsTf, in1=Uf.unsqueeze(1).broadcast_to([128, G, 128]), op=ALU.mult), r=["wsTf", "Uf"], w=["wsT"])

    def gelu2(T, p, dst, dk, ps, pk, tag):
        x2 = T["x2" + tag]
        kx = "x2%s_%d" % (tag, p)
        P.op("act", lambda e: e.activation(out=x2, in_=ps, func=AF.Square), r=[pk], w=[kx])
        yield
        P.op("dve", lambda e: e.tensor_scalar(out=x2, in0=x2, scalar1=C1_, scalar2=1.0, op0=ALU.mult, op1=ALU.add), r=[kx], w=[kx])
        yield
        P.op("dve", lambda e: e.tensor_tensor(out=x2, in0=x2, in1=ps, op=ALU.mult), r=[kx, pk], w=[kx])
        yield
        P.op("act", lambda e: e.activation(out=x2, in_=x2, func=AF.Tanh, scale=C0), r=[kx], w=[kx])
        yield
        P.op("dve", lambda e: e.scalar_tensor_tensor(out=dst, in0=x2, scalar=1.0, in1=ps, op0=ALU.add, op1=ALU.mult), r=[kx, pk], w=[dk])
        yield

    def tileC1(ob):
        p = ob % ND1
        T = TT[p]
        aT = aTc[p]
        ak = "aTc%d" % p
        bA, bB = 2 + 2 * p, 3 + 2 * p
        kA, kB = "bank%d" % bA, "bank%d" % bB
        K = lambda n: "%s_%d" % (n, p)
        us, vs, tmix, vlnb, junk, yn, ynT, h1 = (T[n] for n in ("us", "vs", "tmix", "vlnb", "junk", "yn", "ynT", "h1"))
        for _ in fr.gen(xown[ob * 128:(ob + 1) * 128, :], (gpm, "gpm"), aT, ak):
            yield
        xt, xk = fr.last
        for kc in range(KC):
            P.op("pe", lambda e, kc=kc: e.matmul(banks[bB][:, 0:SGW], lhsT=aT[:, kc, :], rhs=Wvs[:, kc, :], start=(kc == 0), stop=(kc == KC - 1)),
                 r=[ak, wvsk[kc]], w=[kB])
        for kc in range(KC):
            P.op("pe", lambda e, kc=kc: e.matmul(banks[bA][:, 0:SGW], lhsT=aT[:, kc, :], rhs=Wu[:, kc, :], start=(kc == 0), stop=(kc == KC - 1)),
                 r=[ak, wuk[kc]], w=[kA])
        yield
        P.op("act", lambda e: e.activation(out=vs, in_=banks[bB][:, 0:SGW], func=AF.Copy), r=[kB], w=[K("vs")])
        yield
        P.op("act", lambda e: e.activation(out=us, in_=banks[bA][:, 0:SGW], func=AF.Copy), r=[kA], w=[K("us")])
        yield
        g1 = gelu2(T, p, vs, K("vs"), vs, K("vs"), "v")
        g2 = gelu2(T, p, us, K("us"), us, K("us"), "u")
        for _ in g1:
            next(g2, None)
            yield
        for _ in g2:
            yield
        gv, gu = vs, us
        sl = stat_slot(6)
        s1 = stats[:, sl:sl + 1]
        nm = stats[:, sl + 1:sl + 2]
        s2 = stats[:, sl + 2:sl + 3]
        r2 = stats[:, sl + 3:sl + 4]
        k_ = ["st%d" % (sl + j) for j in range(6)]
        P.op("dve", lambda e: e.reduce_sum(out=s1, in_=gv, axis=AX.X), r=[K("vs")], w=[k_[0]])
        yield
        P.op("dve", lambda e: e.tensor_scalar(out=nm, in0=s1, scalar1=-0.5 / SGW, scalar2=None, op0=ALU.mult), r=[k_[0]], w=[k_[1]])
        yield
        P.op("dve", lambda e: e.tensor_scalar(out=gv, in0=gv, scalar1=0.5, scalar2=nm, op0=ALU.mult, op1=ALU.add), r=[K("vs"), k_[1]], w=[K("vs")])
        yield
        P.op("act", lambda e: e.activation(out=junk[:, 0:SGW], in_=gv, func=AF.Square, accum_out=s2), r=[K("vs")], w=[K("junk"), k_[2]])
        yield
        rsqrt_ops(s2, r2, 1, 1.0 / SGW, [k_[2]], [k_[3]])
        yield
        P.op("dve", lambda e: e.scalar_tensor_tensor(out=gv, in0=gv, scalar=r2, in1=lng, op0=ALU.mult, op1=ALU.mult), r=[K("vs"), k_[3], "lng"], w=[K("vs")])
        yield
        P.op("dve", lambda e: e.tensor_tensor(out=vlnb, in0=gv, in1=lnb, op=ALU.add), r=[K("vs"), "lnb"], w=[K("vlnb")])
        yield
        for g8 in range(G):
            P.op("pe", lambda e, g8=g8: e.matmul(banks[bA][:, g8 * 64:(g8 + 1) * 64], lhsT=wsT[:, g8, :], rhs=vlnb[:, g8 * 64:(g8 + 1) * 64], start=True, stop=True),
                 r=["wsT", K("vlnb")], w=[kA])
        yield
        P.op("dve", lambda e: e.tensor_tensor(out=tmix.rearrange("p (g d) -> p g d", d=64), in0=banks[bA][:, 0:SGW].rearrange("p (g d) -> p g d", d=64),
                                              in1=sgb.unsqueeze(2).broadcast_to([128, G, 64]), op=ALU.add), r=[kA, "sgb"], w=[K("tmix")])
        yield
        ysg = us
        P.op("dve", lambda e: e.scalar_tensor_tensor(out=ysg, in0=gu, scalar=0.5, in1=tmix, op0=ALU.mult, op1=ALU.mult), r=[K("us"), K("tmix")], w=[K("us")])
        yield
        sl2 = stat_slot(4)
        k2 = ["st%d" % (sl2 + j) for j in range(4)]
        ssq = stats[:, sl2:sl2 + 2]
        rsq = stats[:, sl2 + 2:sl2 + 4]
        P.op("act", lambda e: e.activation(out=junk[:, 0:AW], in_=yatt[:, ob, :], func=AF.Square, accum_out=ssq[:, 0:1]), r=["yatt"], w=[K("junk"), k2[0]])
        yield
        P.op("act", lambda e: e.activation(out=junk[:, 0:SGW], in_=ysg, func=AF.Square, accum_out=ssq[:, 1:2]), r=[K("us")], w=[K("junk"), k2[1]])
        yield
        rsqrt_ops(ssq, rsq, 2, 1.0 / AW, [k2[0], k2[1]], [k2[2], k2[3]])
        yield
        P.op("dve", lambda e: e.tensor_scalar(out=yn[:, 0:AW], in0=yatt[:, ob, :], scalar1=rsq[:, 0:1], scalar2=None, op0=ALU.mult), r=["yatt", k2[2]], w=[K("yn")])
        yield
        P.op("dve", lambda e: e.tensor_scalar(out=yn[:, AW:D], in0=ysg, scalar1=rsq[:, 1:2], scalar2=None, op0=ALU.mult), r=[K("us"), k2[3]], w=[K("yn")])
        yield
        tpv = banksb[bB][:, 0:KC * 128].rearrange("p (k t) -> p k t", t=128)
        for kc in range(KC):
            P.op("pe", lambda e, kc=kc: e.transpose(out=tpv[:, kc, :], in_=yn[:, kc * 128:(kc + 1) * 128], identity=identb), r=[K("yn"), "identb"], w=[kB])
        yield
        P.op("dve", lambda e: e.tensor_tensor(out=ynT, in0=tpv, in1=gcat.unsqueeze(2).broadcast_to([128, KC, 128]), op=ALU.mult), r=[kB, "gcat"], w=[K("ynT")])
        yield
        sl3 = stat_slot(4)
        k3 = ["st%d" % (sl3 + j) for j in range(4)]
        obanks = [bA, bB] if DH == 2 else [bA]
        for dh in range(DH):
            ob_ = obanks[dh]
            for kc in range(KC):
                P.op("pe", lambda e, kc=kc, dh=dh, ob_=ob_: e.matmul(banks[ob_][:, 0:DW], lhsT=ynT[:, kc, :], rhs=Wo[:, kc, dh * DW:(dh + 1) * DW], start=(kc == 0), stop=(kc == KC - 1)),
                     r=[K("ynT"), wok[kc]], w=["bank%d" % ob_])
            yield
            P.op("act", lambda e, dh=dh, ob_=ob_: e.activation(out=junk[:, 0:DW], in_=banks[ob_][:, 0:DW], func=AF.Square, accum_out=stats[:, sl3 + dh:sl3 + dh + 1]),
                 r=["bank%d" % ob_], w=[K("junk"), k3[dh]])
            yield
        if DH == 2:
            P.op("dve", lambda e: e.tensor_tensor(out=stats[:, sl3 + 2:sl3 + 3], in0=stats[:, sl3:sl3 + 1], in1=stats[:, sl3 + 1:sl3 + 2], op=ALU.add), r=[k3[0], k3[1]], w=[k3[2]])
            yield
            sso = stats[:, sl3 + 2:sl3 + 3]
            ssk = k3[2]
        else:
            sso = stats[:, sl3:sl3 + 1]
            ssk = k3[0]
        rso = stats[:, sl3 + 3:sl3 + 4]
        rsqrt_ops(sso, rso, 1, 1.0 / D, [ssk], [k3[3]])
        yield
        hk = K("h1")
        for dh in range(DH):
            ob_ = obanks[dh]
            P.op("dve", lambda e, dh=dh, ob_=ob_: e.scalar_tensor_tensor(out=h1[:, dh * DW:(dh + 1) * DW], in0=banks[ob_][:, 0:DW], scalar=rso, in1=gpmix[:, dh * DW:(dh + 1) * DW], op0=ALU.mult, op1=ALU.mult),
                 r=["bank%d" % ob_, k3[3], "gpmix"], w=[hk])
            yield
        P.op("dve", lambda e: e.tensor_tensor(out=h1, in0=h1, in1=xt, op=ALU.add), r=[hk, xk], w=[hk])
        yield
        P.op("sp", lambda e: e.dma_start(out=h1_d[ob * 128:(ob + 1) * 128, :], in_=h1), r=[hk], w=["h1d%d" % ob], dma=True)
        yield

    interleave((tileC1(ob) for ob in range(NO)), ND1, 18)
    P.barrier()
    AR.top = const_top

    W1 = AR.alloc([128, KC, DFF], BF16)
    W2 = AR.alloc([128, FC, D], BF16)
    HT = AR.alloc([128, FC, 512], BF16)
    cT = AR.alloc([128, KC, 512], BF16)
    gpffn = AR.alloc([128, D], F32)
    fr = Front(nx=2, tpbanks=(0, 1))
    rtmp = [AR.alloc([128, 512], BF16) for _ in range(2)]
    h1r = [AR.alloc([128, D], F32) for _ in range(2)]
    o2t = [AR.alloc([128, D], F32) for _ in range(1)]
    junk2 = AR.alloc([128, 512], BF16)
    w1src = w_ff1.rearrange("(kc p) n -> p kc n", p=128)
    for j in range(DFF // 512):
        P.op("pool", lambda e, j=j: e.dma_start(out=W1[:, :, j * 512:(j + 1) * 512], in_=w1src[:, :, j * 512:(j + 1) * 512]), w=["W1c%d" % j], dma=True)
    w2k = load_w(W2, w_ff2, "W2", FC)
    ld(gpffn, gpffn_d, "gpffn")
    tcount = 0
    def c2_fronts(blocks):
        interleave((fr.gen(h1_d[ob * 128:(ob + 1) * 128, :], (gpf, "gpf"), cT[:, :, ti * 128:(ti + 1) * 128], "cT%d" % ti) for ti, ob in enumerate(blocks)), 2, 2)

    c2_fronts(c.groups[0])
    for g, blocks in enumerate(c.groups):
        N = len(blocks) * 128
        for fc in range(FC):
            hb = 2 + fc % 2
            for kc in range(KC):
                P.op("pe", lambda e, kc=kc, fc=fc, hb=hb, N=N: e.matmul(banks[hb][:, 0:N], lhsT=W1[:, kc, fc * 128:(fc + 1) * 128], rhs=cT[:, kc, 0:N], start=(kc == 0), stop=(kc == KC - 1)),
                     r=["cT%d" % j for j in range(len(blocks))] + ["W1c%d" % (fc // 4)], w=["bank%d" % hb])
            rt = rtmp[fc % 2]
            rk = "rtmp%d" % (fc % 2)
            P.op("act", lambda e, hb=hb, rt=rt, N=N: e.activation(out=rt[:, 0:N], in_=banks[hb][:, 0:N], func=AF.Relu), r=["bank%d" % hb], w=[rk])
            P.op("dve", lambda e, fc=fc, rt=rt, N=N: e.tensor_tensor(out=HT[:, fc, 0:N], in0=rt[:, 0:N], in1=rt[:, 0:N], op=ALU.mult), r=[rk], w=["HT"])
        if g + 1 < NG:
            c2_fronts(c.groups[g + 1])
        for ti, ob in enumerate(blocks):
            i2 = tcount % 2
            tcount += 1
            h1 = h1r[i2]
            hk = "h1r%d" % i2
            P.op("sp", lambda e, h1=h1, ob=ob: e.dma_start(out=h1, in_=h1_d[ob * 128:(ob + 1) * 128, :]), r=["h1d%d" % ob], w=[hk], dma=True)
            sl3 = stat_slot(4)
            k3 = ["st%d" % (sl3 + j) for j in range(4)]
            for dh in range(DH):
                ob_ = 4 + 2 * i2 + dh
                for fc in range(FC):
                    P.op("pe", lambda e, fc=fc, dh=dh, ob_=ob_, ti=ti: e.matmul(banks[ob_][:, 0:DW], lhsT=HT[:, fc, ti * 128:(ti + 1) * 128], rhs=W2[:, fc, dh * DW:(dh + 1) * DW], start=(fc == 0), stop=(fc == FC - 1)),
                         r=["HT", w2k[fc]], w=["bank%d" % ob_])
                P.op("act", lambda e, dh=dh, ob_=ob_, sl3=sl3: e.activation(out=junk2[:, 0:DW], in_=banks[ob_][:, 0:DW], func=AF.Square, accum_out=stats[:, sl3 + dh:sl3 + dh + 1]),
                     r=["bank%d" % ob_], w=["junk2", k3[dh]])
            if DH == 2:
                P.op("dve", lambda e, sl3=sl3: e.tensor_tensor(out=stats[:, sl3 + 2:sl3 + 3], in0=stats[:, sl3:sl3 + 1], in1=stats[:, sl3 + 1:sl3 + 2], op=ALU.add), r=[k3[0], k3[1]], w=[k3[2]])
                sso = stats[:, sl3 + 2:sl3 + 3]
                ssk = k3[2]
            else:
                sso = stats[:, sl3:sl3 + 1]
                ssk = k3[0]
            rso = stats[:, sl3 + 3:sl3 + 4]
            rsqrt_ops(sso, rso, 1, 1.0 / D, [ssk], [k3[3]])
            o2 = o2t[0]
            ok2 = "o2t0"
            for dh in range(DH):
                ob_ = 4 + 2 * i2 + dh
                P.op("dve", lambda e, dh=dh, ob_=ob_, rso=rso, o2=o2: e.scalar_tensor_tensor(out=o2[:, dh * DW:(dh + 1) * DW], in0=banks[ob_][:, 0:DW], scalar=rso, in1=gpffn[:, dh * DW:(dh + 1) * DW], op0=ALU.mult, op1=ALU.mult),
                     r=["bank%d" % ob_, k3[3], "gpffn"], w=[ok2])
            P.op("dve", lambda e, o2=o2, h1=h1: e.tensor_tensor(out=h1, in0=o2, in1=h1, op=ALU.add), r=[ok2, hk], w=[hk])
            P.op("sp", lambda e, h1=h1, ob=ob: e.dma_start(out=h2_d[ob * 128:(ob + 1) * 128, :], in_=h1), r=[hk], w=["h2d%d" % ob], dma=True)
    P.barrier()
    AR.top = const_top

    Wg = AR.alloc([128, KC, D], BF16)
    Wpe = AR.alloc([128, PK, D], BF16)
    gateb = AR.alloc([128, D], F32)
    ND3 = 3
    fr3 = Front(nx=ND3, tpbanks=(0, 1), nxn=ND3)
    S3 = []
    for _ in range(ND3):
        S3.append(dict(h2T=AR.alloc([128, KC, 128], BF16), pt=AR.alloc([128, PLE], F32), pb=AR.alloc([128, PLE], BF16),
                       pT=AR.alloc([128, PK, 128], BF16), z=AR.alloc([128, D], F32), pp=AR.alloc([128, D], F32), o=AR.alloc([128, D], F32)))
    wgk = load_w(Wg, gate_w, "Wg", KC)
    wpk = load_w(Wpe, ple_w, "Wpe", PK)
    ld(gateb, gateb_d, "gateb")
    out_dmas = []
    brot = [0]

    def nbank():
        b_ = 2 + brot[0] % 6
        brot[0] += 1
        return b_

    def tileC3(ob):
        p = ob % ND3
        T = S3[p]
        K = lambda n: "%s3_%d" % (n, p)
        h2T, pt_, pb_, pT_, z, pp, o_ = T["h2T"], T["pt"], T["pb"], T["pT"], T["z"], T["pp"], T["o"]
        P.op("sp", lambda e: e.dma_start(out=pt_, in_=pown[ob * 128:(ob + 1) * 128, :]), w=[K("pt")], dma=True)
        for _ in fr3.gen(h2_d[ob * 128:(ob + 1) * 128, :], None, h2T, K("h2T"), norm=False):
            yield
        xt, xk = fr3.last
        P.op("dve", lambda e: e.tensor_copy(out=pb_, in_=pt_), r=[K("pt")], w=[K("pb")])
        yield
        tb_ = nbank()
        tpv3 = banksb[tb_][:, 0:PK * 128].rearrange("p (k t) -> p k t", t=128)
        for k2_ in range(PK):
            P.op("pe", lambda e, k2_=k2_: e.transpose(out=tpv3[:, k2_, :], in_=pb_[:, k2_ * 128:(k2_ + 1) * 128], identity=identb), r=[K("pb"), "identb"], w=["bank%d" % tb_])
        yield
        P.op("act", lambda e: e.activation(out=pT_, in_=tpv3, func=AF.Copy), r=["bank%d" % tb_], w=[K("pT")])
        yield
        for dh in range(DH):
            gb_ = nbank()
            for kc in range(KC):
                P.op("pe", lambda e, kc=kc, dh=dh, gb_=gb_: e.matmul(banks[gb_][:, 0:DW], lhsT=h2T[:, kc, :], rhs=Wg[:, kc, dh * DW:(dh + 1) * DW], start=(kc == 0), stop=(kc == KC - 1)),
                     r=[K("h2T"), wgk[kc]], w=["bank%d" % gb_])
            yield
            P.op("dve", lambda e, dh=dh, gb_=gb_: e.tensor_tensor(out=z[:, dh * DW:(dh + 1) * DW], in0=banks[gb_][:, 0:DW], in1=gateb[:, dh * DW:(dh + 1) * DW], op=ALU.add),
                 r=["bank%d" % gb_, "gateb"], w=[K("z")])
            yield
            pb2 = nbank()
            for k2_ in range(PK):
                P.op("pe", lambda e, k2_=k2_, dh=dh, pb2=pb2: e.matmul(banks[pb2][:, 0:DW], lhsT=pT_[:, k2_, :], rhs=Wpe[:, k2_, dh * DW:(dh + 1) * DW], start=(k2_ == 0), stop=(k2_ == PK - 1)),
                     r=[K("pT"), wpk[k2_]], w=["bank%d" % pb2])
            yield
            P.op("act", lambda e, dh=dh, pb2=pb2: e.activation(out=pp[:, dh * DW:(dh + 1) * DW], in_=banks[pb2][:, 0:DW], func=AF.Copy), r=["bank%d" % pb2], w=[K("pp")])
            yield
        P.op("act", lambda e: e.activation(out=z, in_=z, func=AF.Tanh, scale=0.5), r=[K("z")], w=[K("z")])
        yield
        P.op("dve", lambda e: e.tensor_scalar(out=z, in0=z, scalar1=0.5, scalar2=0.5, op0=ALU.mult, op1=ALU.add), r=[K("z")], w=[K("z")])
        yield
        P.op("dve", lambda e: e.tensor_tensor(out=o_, in0=z, in1=pp, op=ALU.mult), r=[K("z"), K("pp")], w=[K("o")])
        yield
        P.op("dve", lambda e: e.tensor_tensor(out=o_, in0=o_, in1=xt, op=ALU.add), r=[K("o"), xk], w=[K("o")])
        yield
        out_dmas.append(P.op("sp", lambda e: e.dma_start(out=out_d[ob * 128:(ob + 1) * 128, :], in_=o_), r=[K("o")], dma=True))
        yield

    interleave((tileC3(ob) for ob in range(NO)), ND3, 7)
    P.wait_all("sp", out_dmas)
    P.emit()
    return nc, P


def make_core_inputs(cfg, core, x, p, w_in, f_bias, sg_ln_g, sg_ln_b, sg_w, sg_b, att_out_g, sg_out_g,
                     w_out, pre_mix_g, post_mix_g, pre_ffn_g, post_ffn_g, w_ff1, w_ff2, ple_w, ple_gate_w, ple_gate_b):
    c = cfg
    b, r = core // 4, core % 4
    f32 = np.float32
    blocks = c.owned_blocks(r)
    rows = np.concatenate([np.arange(bl * 128, (bl + 1) * 128) for bl in blocks])

    def fm(v):
        return np.ascontiguousarray(np.asarray(v, f32).reshape(c.KC, 128).T)

    def rep(v):
        return np.ascontiguousarray(np.broadcast_to(np.asarray(v, f32).reshape(1, -1), (128, np.asarray(v).size)))

    k = np.arange(128)[:, None]
    q = np.arange(128)[None, :]
    tri = np.where(k > q, NEG, 0.0).astype(f32)
    full = np.full((128, 128), NEG, f32)
    zero = np.zeros((128, 128), f32)
    maskT = np.zeros((128, 8, 128), f32)
    for i in range(4):
        maskT[:, i, :] = zero if i < r else (tri if i == r else full)
        maskT[:, 4 + i, :] = zero if i < 3 - r else (tri if i == 3 - r else full)
    sel = np.zeros((c.HH, c.HH, 128), f32)
    for hh in range(c.HH):
        sel[hh, hh, :] = 1.0
    LTfull = (np.arange(c.NB)[:, None] < np.arange(c.NB)[None, :]).astype(f32)
    LTown = (np.arange(c.NB)[:, None] < np.asarray(blocks)[None, :]).astype(f32)
    xb = np.asarray(x[b], f32)
    return {
        "xfull": np.ascontiguousarray(xb),
        "xown": np.ascontiguousarray(xb[rows]),
        "pown": np.ascontiguousarray(np.asarray(p[0, b], f32)[rows]),
        "w_in": np.ascontiguousarray(np.asarray(w_in[0], f32)),
        "w_out": np.ascontiguousarray(np.asarray(w_out[0], f32)),
        "w_ff1": np.ascontiguousarray(np.asarray(w_ff1[0], f32)),
        "w_ff2": np.ascontiguousarray(np.asarray(w_ff2[0], f32)),
        "ple_w": np.ascontiguousarray(np.asarray(ple_w[0], f32)),
        "gate_w": np.ascontiguousarray(np.asarray(ple_gate_w[0], f32)),
        "gpm": fm(pre_mix_g[0]),
        "gpf": fm(pre_ffn_g[0]),
        "gcat": fm(np.concatenate([np.asarray(att_out_g[0]), np.asarray(sg_out_g[0])])),
        "gpmix": rep(post_mix_g[0]),
        "gpffn": rep(post_ffn_g[0]),
        "gateb": rep(ple_gate_b[0]),
        "lng": rep(sg_ln_g[0]),
        "lnb": rep(sg_ln_b[0]),
        "fbias": rep(f_bias[0]),
        "sgwT": np.ascontiguousarray(np.transpose(np.asarray(sg_w[0], f32), (2, 0, 1))),
        "sgbT": np.ascontiguousarray(np.asarray(sg_b[0], f32).T),
        "ident": np.eye(128, dtype=f32),
        "U": (np.arange(128)[:, None] <= np.arange(128)[None, :]).astype(f32),
        "maskT": maskT,
        "sel": sel,
        "LTfull": LTfull,
        "LTown": LTown,
    }, rows


_CACHE = {}


def kernel(**inputs):
    x = np.asarray(inputs["x"])
    B, S, D = x.shape
    PLE = np.asarray(inputs["p"]).shape[-1]
    cfg = Cfg(D=D, S=S, PLE=PLE)
    key = (D, S, PLE)
    if key not in _CACHE:
        _CACHE[key] = build_program(cfg)
    nc, _ = _CACHE[key]
    in_maps, rows_all = [], []
    for core in range(8):
        m, rows = make_core_inputs(cfg, core, **inputs)
        in_maps.append(m)
        rows_all.append(rows)
    res = run_bass_kernel_spmd(nc, in_maps, core_ids=list(range(8)))
    out = np.zeros((B, S, D), np.float32)
    for core in range(8):
        out[core // 4, rows_all[core], :] = np.asarray(res.results[core]["out"], np.float32)
    return out
```
